# Optimizing a Trainium2 kernel written in Bass

```python
import math
import jax, jax.numpy as jnp
from jax import lax
import numpy as np

D_MODEL = 2048
BATCH = 8
SEQ = 2048
DEPTH = 1

N_META = 16
ATTN_WIDTH = D_MODEL // 2
CONV_WIDTH = D_MODEL - ATTN_WIDTH
N_ATTN_HEADS = 8
V_HEAD_DIM = ATTN_WIDTH // N_ATTN_HEADS
QK_HEAD_DIM = V_HEAD_DIM // 2
N_CONV_GROUPS = 8
CONV_K = 3
QK_WIDTH = N_ATTN_HEADS * 2 * QK_HEAD_DIM
IN_WIDTH = 2 * QK_WIDTH + ATTN_WIDTH + 3 * CONV_WIDTH
D_FF = int(math.ceil(8 * D_MODEL / 3 / 256) * 256)
Q_BLOCK = 128
NORM_EPS = 1e-6
HEAD_NORM_EPS = 1e-5

kernel_name = "hymba_diffattn_shortconv_swiglu"


def rmsnorm(x, g, eps=NORM_EPS):
    xf = x.astype(jnp.float32)
    y = xf * lax.rsqrt(jnp.mean(xf * xf, axis=-1, keepdims=True) + eps)
    return (y * g.astype(jnp.float32)).astype(x.dtype)


def alibi_slopes(n_heads):
    return jnp.asarray(2.0 ** (-8.0 * (np.arange(n_heads) + 1) / n_heads), dtype=jnp.float32)


def diff_attention(q, k, v, lam, head_g, lambda_init):
    b, t, h, _, dk = q.shape
    dv = v.shape[-1]
    n_blk = t // Q_BLOCK
    scale = 1.0 / math.sqrt(dk)
    slopes = alibi_slopes(h)
    kf = k.astype(jnp.float32)
    vf = v.astype(jnp.float32)
    kpos = jnp.arange(t)
    lam = lam.astype(jnp.float32)

    def block(i):
        start = i * Q_BLOCK
        qb = lax.dynamic_slice_in_dim(q, start, Q_BLOCK, axis=1).astype(jnp.float32)
        s = jnp.einsum('bqhmd,bkhmd->bhmqk', qb, kf) * scale
        qpos = start + jnp.arange(Q_BLOCK)
        dist = (qpos[:, None] - kpos[None, :]).astype(jnp.float32)
        bias = jnp.where(dist >= 0, -slopes[:, None, None] * dist, -jnp.inf)
        p = jax.nn.softmax(s + bias[None, :, None], axis=-1)
        w = p[:, :, 0] - lam * p[:, :, 1]
        return jnp.einsum('bhqk,bkhd->bqhd', w, vf)

    o = lax.map(block, jnp.arange(n_blk))
    o = jnp.transpose(o, (1, 0, 2, 3, 4)).reshape(b, t, h, dv)
    o = o * lax.rsqrt(jnp.mean(o * o, axis=-1, keepdims=True) + HEAD_NORM_EPS)
    o = o * head_g.astype(jnp.float32) * (1.0 - lambda_init)
    return o.reshape(b, t, h * dv).astype(v.dtype)


def short_gated_conv(bg, cg, hin, conv_w):
    u = cg * hin
    t = u.shape[1]
    up = jnp.pad(u, ((0, 0), (CONV_K - 1, 0), (0, 0)))
    y = sum(up[:, j:j + t, :] * conv_w[j] for j in range(CONV_K))
    return bg * y


def setup_inputs(seed: int = 0) -> dict:
    key = jax.random.key(seed)
    ks = jax.random.split(key, 16)
    f32 = jnp.float32
    L, D = DEPTH, D_MODEL
    x = jax.random.normal(ks[0], (BATCH, SEQ, D), f32)
    meta = jax.random.normal(ks[1], (N_META, D), f32)
    norm1_g = 1.0 + 0.02 * jax.random.normal(ks[2], (L, D), f32)
    w_in = jax.random.normal(ks[3], (L, D, IN_WIDTH), f32) * D ** -0.5
    lambda_q1 = 0.1 * jax.random.normal(ks[4], (L, QK_HEAD_DIM), f32)
    lambda_k1 = 0.1 * jax.random.normal(ks[5], (L, QK_HEAD_DIM), f32)
    lambda_q2 = 0.1 * jax.random.normal(ks[6], (L, QK_HEAD_DIM), f32)
    lambda_k2 = 0.1 * jax.random.normal(ks[7], (L, QK_HEAD_DIM), f32)
    head_g = 1.0 + 0.02 * jax.random.normal(ks[8], (L, V_HEAD_DIM), f32)
    conv_w = jax.random.normal(ks[9], (L, CONV_K, CONV_WIDTH), f32) * CONV_K ** -0.5
    w_out = jax.random.normal(ks[10], (L, D, D), f32) * D ** -0.5
    norm2_g = 1.0 + 0.02 * jax.random.normal(ks[11], (L, D), f32)
    w_gate = jax.random.normal(ks[12], (L, D, D_FF), f32) * D ** -0.5
    w_up = jax.random.normal(ks[13], (L, D, D_FF), f32) * D ** -0.5
    w_down = jax.random.normal(ks[14], (L, D_FF, D), f32) * D_FF ** -0.5
    norm_f_g = 1.0 + 0.02 * jax.random.normal(ks[15], (D,), f32)
    return {"x": x, "meta": meta, "norm1_g": norm1_g, "w_in": w_in,
            "lambda_q1": lambda_q1, "lambda_k1": lambda_k1,
            "lambda_q2": lambda_q2, "lambda_k2": lambda_k2,
            "head_g": head_g, "conv_w": conv_w, "w_out": w_out,
            "norm2_g": norm2_g, "w_gate": w_gate, "w_up": w_up,
            "w_down": w_down, "norm_f_g": norm_f_g}


def reference(x, meta, norm1_g, w_in, lambda_q1, lambda_k1, lambda_q2, lambda_k2,
              head_g, conv_w, w_out, norm2_g, w_gate, w_up, w_down, norm_f_g):
    b, s, d = x.shape
    t = s + N_META
    t_pad = ((t + Q_BLOCK - 1) // Q_BLOCK) * Q_BLOCK
    h = jnp.concatenate([jnp.broadcast_to(meta[None].astype(x.dtype), (b, N_META, d)), x], axis=1)
    h = jnp.pad(h, ((0, 0), (0, t_pad - t), (0, 0)))
    splits = np.cumsum([QK_WIDTH, QK_WIDTH, ATTN_WIDTH, CONV_WIDTH, CONV_WIDTH]).tolist()

    for l in range(DEPTH):
        lambda_init = 0.8 - 0.6 * math.exp(-0.3 * l)
        a = rmsnorm(h, norm1_g[l])
        proj = a @ w_in[l]
        q, k, v, bg, cg, hin = jnp.split(proj, splits, axis=-1)
        q = q.reshape(b, t_pad, N_ATTN_HEADS, 2, QK_HEAD_DIM)
        k = k.reshape(b, t_pad, N_ATTN_HEADS, 2, QK_HEAD_DIM)
        v = v.reshape(b, t_pad, N_ATTN_HEADS, V_HEAD_DIM)
        lam = (jnp.exp(jnp.sum(lambda_q1[l].astype(jnp.float32) * lambda_k1[l].astype(jnp.float32)))
               - jnp.exp(jnp.sum(lambda_q2[l].astype(jnp.float32) * lambda_k2[l].astype(jnp.float32)))
               + lambda_init)
        attn_out = diff_attention(q, k, v, lam, head_g[l], lambda_init)
        conv_out = short_gated_conv(bg, cg, hin, conv_w[l])
        h = h + jnp.concatenate([attn_out, conv_out], axis=-1) @ w_out[l]
        f = rmsnorm(h, norm2_g[l])
        h = h + (jax.nn.silu(f @ w_gate[l]) * (f @ w_up[l])) @ w_down[l]

    out = rmsnorm(h, norm_f_g)
    return out[:, N_META:N_META + s, :]
```

```python
import contextlib
import numpy as np
import concourse.bass as bass
import concourse.mybir as mybir
from concourse.bass_utils import run_bass_kernel_spmd

F32 = mybir.dt.float32
BF16 = mybir.dt.bfloat16
AF = mybir.ActivationFunctionType
ALU = mybir.AluOpType

D = 2048
S = 2048
NM = 16
T = S + NM
DFF = 5632
NH = 8
INW = 6144
NFF = DFF // 128
LAMBDA_INIT = 0.8 - 0.6 * 1.0
EPS = 1e-6
HEPS = 1e-5

OFF_A = 0
SZ_A = 16 * T
OFF_M = OFF_A + SZ_A
SZ_M = 16 * S
OFF_R = OFF_M + SZ_M
RING = 6
OFF_X = OFF_R + RING * 2048
SZ_X = 24576
ARENA = OFF_X + SZ_X


ANNOTATE = False


class Ev:
    def __init__(self, sem, step=1):
        self.sem, self.step, self.n = sem, step, 0

    def fire(self):
        self.n += self.step
        return (self.sem, self.n)


class Prog:
    def __init__(self):
        self.q = {k: [] for k in ("pe", "act", "dve", "pool", "sp")}
        self.last = {k: None for k in self.q}

    def op(self, eng, fn, waits=(), ev=None):
        tok = ev.fire() if ev is not None else None
        ws = []
        for w in waits:
            if w is None:
                continue
            if isinstance(w, list):
                ws.extend([u for u in w if u is not None])
            else:
                ws.append(w)
        self.q[eng].append((fn, ws, ev))
        if tok is not None:
            self.last[eng] = tok
        return tok

    def emit(self, eng, handle):
        waited = {}
        for fn, ws, ev in self.q[eng]:
            for sem, val in ws:
                key = id(sem)
                if waited.get(key, 0) >= val:
                    continue
                waited[key] = val
                handle.wait_ge(sem, val)
            ins = fn(handle)
            if ANNOTATE:
                ins.annotate("L%d" % fn.__code__.co_firstlineno)
            if ev is not None:
                ins.then_inc(ev.sem, ev.step)


class Bank:
    def __init__(self, ps, ev_pe):
        self.ps = ps
        self.ev = ev_pe
        self.free = []

    def acquire(self):
        f = self.free
        self.free = []
        return f


def build_program(debug=False):
    nc = bass.Bass("TRN2", target_bir_lowering=False)

    def din(name, shape):
        return nc.dram_tensor(name, shape, F32, kind="ExternalInput").ap()

    x = din("x", [S, D])
    meta = din("meta", [NM, D])
    g1 = din("norm1_g", [1, D])
    w_in = din("w_in", [D, INW])
    lq1 = din("lambda_q1", [1, 64])
    lk1 = din("lambda_k1", [1, 64])
    lq2 = din("lambda_q2", [1, 64])
    lk2 = din("lambda_k2", [1, 64])
    head_g = din("head_g", [1, 128])
    conv_w = din("conv_w", [3, 1024])
    w_out = din("w_out", [D, D])
    g2 = din("norm2_g", [1, D])
    w_gate = din("w_gate", [D, DFF])
    w_up = din("w_up", [D, DFF])
    w_down = din("w_down", [DFF, D])
    gf = din("norm_f_g", [1, D])
    c_ident = din("c_ident", [128, 128])
    c_tri = din("c_tri", [128, 128])
    c_qaug = din("c_qaug", [4, S])
    c_kaug = din("c_kaug", [NH, 4, T])
    out = nc.dram_tensor("out", [S, D], F32, kind="ExternalOutput").ap()
    if debug:
        dbg_aT = nc.dram_tensor("dbg_aT", [128, 16 * T], BF16, kind="ExternalOutput").ap()
        dbg_mix = nc.dram_tensor("dbg_mix", [128, 16 * S], BF16, kind="ExternalOutput").ap()
        dbg_h1 = nc.dram_tensor("dbg_h1", [128, 8 * D], F32, kind="ExternalOutput").ap()

    w_in_v = w_in.rearrange("(k p) n -> p k n", p=128)
    w_out_v = w_out.rearrange("(k p) n -> p k n", p=128)
    w_gate_v = w_gate.rearrange("(k p) n -> p k n", p=128)
    w_up_v = w_up.rearrange("(k p) n -> p k n", p=128)

    P = Prog()
    with contextlib.ExitStack() as es:
        def sb(name, shape, dt):
            return es.enter_context(nc.sbuf_tensor(name, shape, dt))

        nsem = [0]

        def new_ev(step=1):
            nsem[0] += 1
            return Ev(es.enter_context(nc.semaphore(f"s{nsem[0]}")), step)

        arena = sb("arena", [128, ARENA], BF16)
        ident = sb("ident", [128, 128], BF16)
        tri = sb("tri", [128, 128], BF16)
        ones = sb("ones", [128, 128], BF16)
        stat = sb("stat", [128, 4 * 17 + 4 * 16 + 4 * 16], F32)
        lamt = sb("lamt", [128, 4 * 64 + 64 + 16], F32)
        cw = sb("cw", [128, 3, 8], F32)
        hg = sb("hg", [128, 4], F32)

        pss = [es.enter_context(nc.psum_tensor(f"ps{i}", [128, 512], F32)) for i in range(8)]
        banks = [Bank(pss[i], new_ev()) for i in range(8)]

        def tview(bank):
            return bank.ps[:, :].bitcast(BF16).rearrange("p (j n) -> p j n", j=8)

        def av(off, n):
            return arena[:, off:off + n]

        aT = av(OFF_A, SZ_A).rearrange("p (k n) -> p k n", k=16)
        h1 = av(OFF_A, 8 * D * 2).bitcast(F32).rearrange("p (s n) -> p s n", s=8)
        mixT = av(OFF_M, SZ_M).rearrange("p (k n) -> p k n", k=16)
        ring_slots = [av(OFF_R + i * 2048, 2048) for i in range(RING)]
        ring_ready = [new_ev(16) for _ in range(RING)]
        ring_free = [new_ev(1) for _ in range(RING)]
        ring_last = [None] * RING
        ring_n = [0]

        def xv(off_bytes, nbytes, dt=BF16):
            a = av(OFF_X + off_bytes // 2, nbytes // 2)
            return a.bitcast(F32) if dt == F32 else a

        def mv(off_bytes, nbytes, dt=BF16):
            a = av(OFF_M + off_bytes // 2, nbytes // 2)
            return a.bitcast(F32) if dt == F32 else a

        xs = [mv(i * 8192, 8192, F32) for i in range(4)]
        at = [mv(32768, 4096), mv(36864, 4096)]
        g1bc = mv(40960, 8192, F32)
        junkA = mv(49152, 4096)
        QK = 4160
        qA = xv(0, 4128)
        qB = xv(QK, 4128)
        kA = xv(2 * QK, 4128)
        kB = xv(3 * QK, 4128)
        vF = xv(4 * QK, 4128)
        vh = xv(5 * QK, 4352).rearrange("p (b n) -> p b n", b=17)
        o_pt = 5 * QK + 4352
        Pt = [xv(o_pt + i * 1024, 1024) for i in range(4)]
        o_fin = o_pt + 4096
        oc = [xv(o_fin, 2048, F32), xv(o_fin + 2048, 2048, F32)]
        lc = [xv(o_fin + 4096, 2048, F32), xv(o_fin + 6144, 2048, F32)]
        sqb = xv(o_fin + 8192, 1024)
        o_cv = o_fin + 8192 + 1024
        cgS = [xv(o_cv, 2048, F32), xv(o_cv + 2048, 2048, F32)]
        ubuf = xv(o_cv + 4096, 2064, F32)
        ybuf = [xv(o_cv + 6160, 2048, F32), xv(o_cv + 8208, 2048, F32)]
        assert o_cv + 10256 <= 49152
        xr = [xv(0, 2048, F32), xv(2048, 2048, F32)]
        ft = [xv(4096, 4096), xv(8192, 4096), xv(40960, 4096)]
        ost = [xv(4096, 8192, F32), xv(40960, 8192, F32)]
        g2bc = xv(12288, 8192, F32)
        gfbc = xv(20480, 8192, F32)
        sg = [xv(28672, 2048, F32), xv(30720, 2048, F32)]
        actT = [[xv(32768 + (g * 2 + c) * 2048, 2048) for c in range(2)] for g in range(2)]
        junkC = xv(32768, 4096)

        ssA, vvA, lnA, rsA = (stat[:, i * 17:(i + 1) * 17] for i in range(4))
        o2 = 68
        ssC, vvC, lnC, rsC = (stat[:, o2 + i * 16:o2 + (i + 1) * 16] for i in range(4))
        o3 = o2 + 64
        ssF, vvF, lnF, rsF = (stat[:, o3 + i * 16:o3 + (i + 1) * 16] for i in range(4))

        ev_sp = new_ev(16)
        ev_spx = [new_ev(16) for _ in range(4)]
        ev_act = new_ev(1)
        ev_dve = new_ev(1)
        ev_pool = new_ev(16)
        ev_out = new_ev(16)
        ev_rel = new_ev(1)
        ev_poolc = new_ev(1)
        ev_xh = [new_ev(16) for _ in range(8)]
        E_tok = {}

        class Slot:
            pass

        def ring_load(src_ap, shape3=None):
            i = ring_n[0]
            ring_n[0] += 1
            si = i % RING
            slot = ring_slots[si]
            dst = slot if shape3 is None else slot.rearrange("p (k n) -> p k n", k=shape3)
            waits = []
            if i >= RING:
                assert ring_last[si] is not None, "ring slot reused before its last reader was emitted"
                waits = [ring_last[si]]
                ring_last[si] = None
            tok = P.op("pool", lambda e, d=dst, s=src_ap: e.dma_start(out=d, in_=s), waits, ev=ring_ready[si])
            sl = Slot()
            sl.ap, sl.tok, sl.si = dst, tok, si
            return sl

        def ring_release(sl, tok):
            ring_last[sl.si] = tok

        t_c = []
        t_c.append(P.op("pool", lambda e: e.dma_start(out=ident[:, :], in_=c_ident), ev=ev_pool))
        t_c.append(P.op("pool", lambda e: e.dma_start(out=tri[:, :], in_=c_tri), ev=ev_pool))
        t_ones = P.op("dve", lambda e: e.memset(ones[:, :], 1.0), ev=ev_dve)
        epsc = hg[:, 2:3]
        t_eps = P.op("dve", lambda e: e.memset(hg[:, 2:3], EPS), ev=ev_dve)
        t_g1 = P.op("sp", lambda e: e.dma_start(out=g1bc, in_=g1.broadcast_to([128, D])), ev=ev_sp)

        tb_rr = [0]

        def norm_a(src, Pn, col, ss, vv, junk, src_tok):
            t1 = P.op("act", lambda e: e.activation(out=junk[0:Pn, :], in_=src, func=AF.Square, accum_out=ss[0:Pn, col:col + 1]),
                      waits=[src_tok], ev=ev_act)
            return t1

        def norm_b(Pn, col, vv, ln, rs, t1):
            t2b = P.op("act", lambda e: e.activation(out=ln[0:Pn, col:col + 1], in_=vv[0:Pn, col:col + 1], func=AF.Ln, bias=epsc[0:Pn, 0:1], scale=1.0 / D),
                       waits=[t1, t_eps], ev=ev_act)
            t3 = P.op("act", lambda e: e.activation(out=rs[0:Pn, col:col + 1], in_=ln[0:Pn, col:col + 1], func=AF.Exp, scale=-0.5), waits=[t2b], ev=ev_act)
            return t3

        def norm_chain(src, Pn, col, ss, vv, ln, rs, junk, src_tok):
            return norm_b(Pn, col, ss, ln, rs, norm_a(src, Pn, col, ss, vv, junk, src_tok))

        def transposes(a_tile, Pn, dst, c0, tbanks, a_tok):
            evs = []
            pe_tok = None
            for hf in range(2):
                bank = tbanks[tb_rr[0] % len(tbanks)]
                tb_rr[0] += 1
                tv = tview(bank)
                fr = bank.acquire()
                for j in range(8):
                    k = hf * 8 + j
                    last = (j == 7)
                    pe_tok = P.op("pe", lambda e, tv=tv, j=j, k=k: e.transpose(out=tv[:, j, 0:Pn], in_=a_tile[0:Pn, k * 128:(k + 1) * 128],
                                                                             identity=ident[0:Pn, 0:Pn]),
                                  waits=[a_tok, fr, t_c] if j == 0 else [], ev=bank.ev if last else None)
                eng = "act" if hf == 0 else "dve"
                if eng == "act":
                    tk = P.op("act", lambda e, tv=tv, hf=hf: e.copy(out=dst[:, hf * 8:(hf + 1) * 8, c0:c0 + Pn], in_=tv[:, :, 0:Pn]),
                              waits=[pe_tok], ev=ev_act)
                else:
                    tk = P.op("dve", lambda e, tv=tv, hf=hf: e.tensor_copy(out=dst[:, hf * 8:(hf + 1) * 8, c0:c0 + Pn], in_=tv[:, :, 0:Pn]),
                              waits=[pe_tok], ev=ev_dve)
                bank.free.append(tk)
                evs.append(tk)
            return pe_tok, evs

        mm_rr = [0]
        TOK_TILES = [(n * 512, 512) for n in range(4)] + [(S, NM)]
        mmB = banks[0:2]
        LB = banks[2:4]
        SB = banks[4:6]
        OB = banks[6:8]

        def acc16(bank, N, lhs_fn, rhs_fn, first_waits, last_ev):
            fr = bank.acquire()
            tok = None
            for k in range(16):
                l_ap = lhs_fn(k)
                r_ap = rhs_fn(k)
                tok = P.op("pe", lambda e, l_ap=l_ap, r_ap=r_ap, k=k: e.matmul(bank.ps[:, 0:N], lhsT=l_ap, rhs=r_ap, start=(k == 0), stop=(k == 15)),
                           waits=[fr, first_waits] if k == 0 else [], ev=(last_ev if k == 15 else None))
            return tok

        def proj_chunk(col0, tiles, consumer, first_waits):
            sl = ring_load(w_in_v[:, :, col0:col0 + 128], 16)
            for ti, (c0, N) in enumerate(tiles):
                bank = mmB[mm_rr[0] % 2]
                mm_rr[0] += 1
                tok = acc16(bank, N, lambda k: sl.ap[:, k, :], lambda k, c0=c0, N=N: aT[:, k, c0:c0 + N],
                            [sl.tok] + list(first_waits), bank.ev)
                consumer(ti, c0, N, bank, tok)
            ring_release(sl, tok)

        def evac_split(dstA, dstB):
            def cons(ti, c0, N, bank, tok):
                t1 = P.op("act", lambda e: e.copy(out=dstA[0:64, c0:c0 + N], in_=bank.ps[0:64, 0:N]), waits=[tok], ev=ev_act)
                t2 = P.op("dve", lambda e: e.tensor_copy(out=dstB[0:64, c0:c0 + N], in_=bank.ps[64:128, 0:N]), waits=[tok], ev=ev_dve)
                bank.free += [t1, t2]
            return cons

        vrr = [0]

        def evac_v(ti, c0, N, bank, tok):
            vrr[0] += 1
            if vrr[0] % 2:
                t1 = P.op("act", lambda e: e.copy(out=vF[:, c0:c0 + N], in_=bank.ps[:, 0:N]), waits=[tok], ev=ev_act)
            else:
                t1 = P.op("dve", lambda e: e.tensor_copy(out=vF[:, c0:c0 + N], in_=bank.ps[:, 0:N]), waits=[tok], ev=ev_dve)
            bank.free.append(t1)


        def proj_chunk_gen(col0, tiles, consumer, waits_box):
            sl = ring_load(w_in_v[:, :, col0:col0 + 128], 16)
            tok = None
            for ti, (c0, N) in enumerate(tiles):
                bank = mmB[mm_rr[0] % 2]
                mm_rr[0] += 1
                tok = acc16(bank, N, lambda k: sl.ap[:, k, :], lambda k, c0=c0, N=N: aT[:, k, c0:c0 + N],
                            [sl.tok] + list(waits_box[0]), bank.ev)
                consumer(ti, c0, N, bank, tok)
                if ti == len(tiles) - 1:
                    ring_release(sl, tok)
                yield

        a_done = {}
        t_done = {}

        vvtok = {}
        ld_tok = {}

        def stageA1a(i):
            b = i % 4
            Pn = 128 if i < 16 else NM
            src_rows = x[i * 128:(i + 1) * 128, :] if i < 16 else meta
            w = [a_done[i - 4]] if i >= 4 else []
            tld = P.op("sp", lambda e: e.dma_start(out=xs[b][0:Pn, :], in_=src_rows), waits=w, ev=ev_spx[b])
            ld_tok[i] = tld

        def stageA1s(i):
            b = i % 4
            Pn = 128 if i < 16 else NM
            vvtok[i] = norm_a(xs[b][0:Pn, :], Pn, i, ssA, vvA, junkA, ld_tok[i])

        def stageA1b(i):
            b = i % 4
            ab = i % 2
            Pn = 128 if i < 16 else NM
            trs = norm_b(Pn, i, ssA, lnA, rsA, vvtok[i])
            w = [trs, t_g1] + ([t_done[i - 2]] if i >= 2 else [])
            a_done[i] = P.op("dve", lambda e: e.scalar_tensor_tensor(out=at[ab][0:Pn, :], in0=xs[b][0:Pn, :], scalar=rsA[0:Pn, i:i + 1],
                                                                      in1=g1bc[0:Pn, :], op0=ALU.mult, op1=ALU.mult), waits=w, ev=ev_dve)

        def stageA2(i):
            Pn = 128 if i < 16 else NM
            c0 = i * 128 if i < 16 else S
            pt, evs = transposes(at[i % 2], Pn, aT, c0, banks[2:6], a_done[i])
            t_done[i] = pt

        wbox = [[]]
        g_q0 = proj_chunk_gen(0, TOK_TILES[0:4], evac_split(qA, qB), wbox)
        g_k0 = proj_chunk_gen(1024, TOK_TILES, evac_split(kA, kB), wbox)
        g_v0 = proj_chunk_gen(2048, TOK_TILES, evac_v, wbox)
        grp_tok = {}

        def step0(n):
            wbox[0] = grp_tok[n]
            for g in ((g_q0, g_k0, g_v0) if n < 4 else (g_k0, g_v0)):
                next(g)

        for i in range(3):
            stageA1a(i)
        stageA1s(0)
        stageA1b(0)
        for i in range(1, 17):
            if i + 2 < 17:
                stageA1a(i + 2)
            stageA1s(i)
            stageA1b(i)
            stageA2(i - 1)
            if (i - 1) % 4 == 3:
                grp_tok[(i - 1) // 4] = [P.last["act"], P.last["dve"]]
            if i in (5, 9, 13):
                step0(i // 4 - 1)
        stageA2(16)
        step0(3)
        grp_tok[4] = [P.last["act"], P.last["dve"]]
        step0(4)
        for g in (g_q0, g_k0, g_v0):
            for _ in g:
                pass
        lqv = lamt[:, 0:256].rearrange("p (a n) -> p a n", a=4)
        for i, src in enumerate((lq1, lk1, lq2, lk2)):
            tl = P.op("sp", lambda e, i=i, src=src: e.dma_start(out=lqv[:, i, :], in_=src.broadcast_to([128, 64])), ev=ev_sp)
        t_hg = P.op("sp", lambda e: e.dma_start(out=hg[:, 0:1], in_=head_g.rearrange("o d -> d o")), ev=ev_sp)
        for t in range(3):
            t_cw = P.op("sp", lambda e, t=t: e.dma_start(out=cw[:, t, :], in_=conv_w[t:t + 1, :].rearrange("o (j p) -> p (o j)", p=128),
                                                       allow_slow_non_contiguous=True), ev=ev_sp)
        ljunk = lamt[:, 256:320]
        lsc = lamt[:, 320:336]
        P.op("dve", lambda e: e.scalar_tensor_tensor(out=ljunk, in0=lqv[:, 0, :], scalar=1.0, in1=lqv[:, 1, :], op0=ALU.mult,
                                                     op1=ALU.mult, accum_out=lsc[:, 0:1]), waits=[t_cw], ev=ev_dve)
        t_l = P.op("dve", lambda e: e.scalar_tensor_tensor(out=ljunk, in0=lqv[:, 2, :], scalar=1.0, in1=lqv[:, 3, :], op0=ALU.mult,
                                                           op1=ALU.mult, accum_out=lsc[:, 1:2]), ev=ev_dve)
        t_le = P.op("act", lambda e: e.activation(out=lsc[:, 2:4], in_=lsc[:, 0:2], func=AF.Exp), waits=[t_l], ev=ev_act)
        t_l2 = P.op("dve", lambda e: e.tensor_tensor(out=lsc[:, 4:5], in0=lsc[:, 3:4], in1=lsc[:, 2:3], op=ALU.subtract), waits=[t_le], ev=ev_dve)
        t_l3 = P.op("dve", lambda e: e.tensor_scalar(out=lsc[:, 5:6], in0=lsc[:, 4:5], scalar1=-LAMBDA_INIT, scalar2=None, op0=ALU.add), waits=[t_l2], ev=ev_dve)
        nlam = lsc[:, 5:6]
        t_hgs = P.op("dve", lambda e: e.tensor_scalar(out=hg[:, 1:2], in0=hg[:, 0:1], scalar1=1.0 - LAMBDA_INIT, scalar2=None, op0=ALU.mult), ev=ev_dve)
        hgs = hg[:, 1:2]
        tokA = [P.last["act"], P.last["dve"], P.last["pe"]]
        if debug:
            P.op("sp", lambda e: e.dma_start(out=dbg_aT, in_=av(OFF_A, SZ_A)), waits=tokA, ev=ev_out)

        cg_rd = {}
        u_tok = [None]

        def conv_units(j, slots):
            s_cg, s_hi, s_bg = slots
            order = [4, 0, 1, 2, 3]
            for oi, ti in enumerate(order):
                c0, N = TOK_TILES[ti]
                cb = oi % 2
                bank = mmB[mm_rr[0] % 2]
                mm_rr[0] += 1
                tok = acc16(bank, N, lambda k: s_cg.ap[:, k, :], lambda k: aT[:, k, c0:c0 + N], [s_cg.tok], bank.ev)
                if oi == 4:
                    ring_release(s_cg, tok)
                d_ap, s_ap = cgS[cb][:, 0:N], bank.ps[:, 0:N]
                t1 = P.op("act", lambda e, d_ap=d_ap, s_ap=s_ap: e.copy(out=d_ap, in_=s_ap), waits=[tok, cg_rd.get(cb)], ev=ev_act)
                bank.free.append(t1)
                yield
                bank = mmB[mm_rr[0] % 2]
                mm_rr[0] += 1
                tok = acc16(bank, N, lambda k: s_hi.ap[:, k, :], lambda k: aT[:, k, c0:c0 + N], [s_hi.tok], bank.ev)
                if oi == 4:
                    ring_release(s_hi, tok)
                if ti == 4:
                    a_ap, b_ap = cgS[cb][:, NM - 2:NM], bank.ps[:, NM - 2:NM]
                    t2 = P.op("dve", lambda e, a_ap=a_ap, b_ap=b_ap: e.tensor_tensor(out=ubuf[:, 0:2], in0=a_ap, in1=b_ap, op=ALU.mult),
                              waits=[t1, tok, u_tok[0]], ev=ev_dve)
                    bank.free.append(t2)
                    cg_rd[cb] = t2
                    u_tok[0] = t2
                    yield
                    continue
                a_ap, b_ap = cgS[cb][:, :], bank.ps[:, :]
                t2 = P.op("dve", lambda e, a_ap=a_ap, b_ap=b_ap: e.tensor_tensor(out=ubuf[:, 2:514], in0=a_ap, in1=b_ap, op=ALU.mult),
                          waits=[t1, tok, u_tok[0]], ev=ev_dve)
                bank.free.append(t2)
                cg_rd[cb] = t2
                yield
                bank = mmB[mm_rr[0] % 2]
                mm_rr[0] += 1
                tok = acc16(bank, N, lambda k: s_bg.ap[:, k, :], lambda k: aT[:, k, c0:c0 + N], [s_bg.tok], bank.ev)
                if oi == 4:
                    ring_release(s_bg, tok)
                yb = ybuf[cb]
                n = ti
                w0, w1, w2 = cw[:, 0, j:j + 1], cw[:, 1, j:j + 1], cw[:, 2, j:j + 1]
                y1 = P.op("dve", lambda e, yb=yb, w0=w0: e.tensor_scalar(out=yb, in0=ubuf[:, 0:512], scalar1=w0, scalar2=None, op0=ALU.mult),
                          waits=[t2, t_cw], ev=ev_dve)
                y2 = P.op("dve", lambda e, yb=yb, w1=w1: e.scalar_tensor_tensor(out=yb, in0=ubuf[:, 1:513], scalar=w1, in1=yb, op0=ALU.mult, op1=ALU.add),
                          waits=[y1], ev=ev_dve)
                y3 = P.op("dve", lambda e, yb=yb, w2=w2: e.scalar_tensor_tensor(out=yb, in0=ubuf[:, 2:514], scalar=w2, in1=yb, op0=ALU.mult, op1=ALU.add),
                          waits=[y2], ev=ev_dve)
                o_ap, b_ap = mixT[:, 8 + j, n * 512:(n + 1) * 512], bank.ps[:, :]
                t3 = P.op("dve", lambda e, yb=yb, o_ap=o_ap, b_ap=b_ap: e.tensor_tensor(out=o_ap, in0=yb, in1=b_ap, op=ALU.mult),
                          waits=[tok, y3], ev=ev_dve)
                bank.free.append(t3)
                u_tok[0] = P.op("dve", lambda e: e.tensor_copy(out=ubuf[:, 0:2], in_=ubuf[:, 512:514]), waits=[y3], ev=ev_dve)
                yield

        s_rr = [0]
        pt_rr = [0]
        pt_free = [None] * 4
        prev_head_pe = [None]
        fin_tok = [None]
        t_qa = None
        deferred = []

        pull_fn = [None]

        def tick(force=False, tag=None):
            for item in list(deferred):
                item[0] -= 1
                if (force and (tag is None or item[2] == tag)) or item[0] <= 0:
                    deferred.remove(item)
                    if not force and pull_fn[0] is not None and item[2] in ("b", "d"):
                        pull_fn[0]()
                    item[1]()

        for h in range(NH):
            fw = tokA if h == 0 else []
            if h > 0:
                proj_chunk(h * 128, TOK_TILES[0:4], evac_split(qA, qB), fw)
                tick(force=True)
                proj_chunk(2048 + h * 128, TOK_TILES, evac_v, [])
                proj_chunk(1024 + h * 128, TOK_TILES, evac_split(kA, kB), [])
            t_v = [P.last["act"], P.last["dve"]]
            tv_toks = []
            for g0 in range(0, 17, 8):
                bank = mmB[mm_rr[0] % 2]
                mm_rr[0] += 1
                tv = tview(bank)
                fr = bank.acquire()
                nb = min(8, 17 - g0)
                tok = None
                for j in range(nb):
                    blk = g0 + j
                    Pn = 128 if blk < 16 else NM
                    tok = P.op("pe", lambda e, tv=tv, j=j, blk=blk, Pn=Pn: e.transpose(out=tv[0:Pn, j, :], in_=vF[:, blk * 128:blk * 128 + Pn],
                                                                                     identity=ident[:, :]),
                               waits=[t_v, fr] if j == 0 else [], ev=bank.ev if j == nb - 1 else None)
                if nb == 8:
                    tk = P.op("dve", lambda e, tv=tv, g0=g0: e.tensor_copy(out=vh[:, g0:g0 + 8, :], in_=tv[:, :, :]), waits=[tok], ev=ev_dve)
                else:
                    tk = P.op("dve", lambda e, tv=tv: e.tensor_copy(out=vh[0:NM, 16, :], in_=tv[0:NM, 0, :]), waits=[tok], ev=ev_dve)
                bank.free.append(tk)
                tv_toks.append(tk)
            t_qk = [P.last["act"], P.last["dve"]]
            conv_slots = [ring_load(w_in_v[:, :, 4096 + h * 128:4096 + (h + 1) * 128], 16),
                          ring_load(w_in_v[:, :, 5120 + h * 128:5120 + (h + 1) * 128], 16),
                          ring_load(w_in_v[:, :, 3072 + h * 128:3072 + (h + 1) * 128], 16)]
            if h == 0:
                P.op("pool", lambda e: e.dma_start(out=qA[64:68, 0:S], in_=c_qaug), ev=ev_pool)
                t_qa = P.op("pool", lambda e: e.dma_start(out=qB[64:68, 0:S], in_=c_qaug), ev=ev_pool)
            P.op("pool", lambda e, h=h: e.dma_start(out=kA[64:68, 0:T], in_=c_kaug[h]), waits=[prev_head_pe[0]], ev=ev_pool)
            t_ka = P.op("pool", lambda e, h=h: e.dma_start(out=kB[64:68, 0:T], in_=c_kaug[h]), ev=ev_pool)
            cgen = conv_units(h, conv_slots)
            tile_ctr = [0]
            prev_head_pe_tok = P.last["pe"]

            def pull():
                try:
                    next(cgen)
                except StopIteration:
                    pass

            pull_fn[0] = pull
            pull()
            pull()
            for Q in range(4):
                blocks = [("m", S, NM, 16)] + [(j, j * 128, 128, j) for j in range(4 * Q + 4)]
                tiles = []
                for (bid, kc0, nk, vb) in blocks:
                    qlo = 0
                    diag = False
                    if bid != "m" and bid >= 4 * Q:
                        qlo = (bid - 4 * Q) * 128
                        diag = True
                    for m in range(2):
                        tiles.append((bid, kc0, nk, vb, qlo, diag, m))
                nt = len(tiles)
                pend = []
                fro = [OB[0].acquire(), OB[1].acquire()]
                frl = [LB[0].acquire(), LB[1].acquire()]
                last_pv = [None, None]
                last_l = [None, None]

                def issue_S(idx):
                    (bid, kc0, nk, vb, qlo, diag, m) = tiles[idx]
                    kM = kA if m == 0 else kB
                    qM = qA if m == 0 else qB
                    sbk = SB[s_rr[0] % 2]
                    s_rr[0] += 1
                    pi = pt_rr[0] % 4
                    pt_rr[0] += 1
                    fr = sbk.acquire()
                    rhs_ap = qM[0:68, Q * 512 + qlo:(Q + 1) * 512]
                    ts = P.op("pe", lambda e: e.matmul(sbk.ps[0:nk, qlo:512], lhsT=kM[0:68, kc0:kc0 + nk], rhs=rhs_ap, start=True, stop=True),
                              waits=[fr, t_qk, t_ka, t_qa, tv_toks], ev=sbk.ev)
                    te = P.op("act", lambda e: e.activation(out=Pt[pi][0:nk, qlo:512], in_=sbk.ps[0:nk, qlo:512], func=AF.Exp, scale=0.125),
                              waits=[ts, pt_free[pi]], ev=ev_act)
                    sbk.free.append(te)
                    rdy = te
                    if diag:
                        rdy = P.op("dve", lambda e: e.tensor_tensor(out=Pt[pi][0:nk, qlo:qlo + 128], in0=Pt[pi][0:nk, qlo:qlo + 128], in1=tri[:, :],
                                                                    op=ALU.mult), waits=[te], ev=ev_dve)
                    pend.append((idx, pi, rdy))

                def issue_PV(idx, pi, rdy):
                    (bid, kc0, nk, vb, qlo, diag, m) = tiles[idx]
                    first = (idx < 2)
                    lastm = (idx >= nt - 2)
                    vsrc = vh[0:nk, vb, :]
                    t1 = P.op("pe", lambda e: e.matmul(OB[m].ps[:, qlo:512], lhsT=vsrc, rhs=Pt[pi][0:nk, qlo:512], start=first, stop=lastm),
                              waits=[rdy, fro[m] if first else None], ev=OB[m].ev if lastm else None)
                    t2 = P.op("pe", lambda e: e.matmul(LB[m].ps[:, qlo:512], lhsT=ones[0:nk, :], rhs=Pt[pi][0:nk, qlo:512], start=first, stop=lastm),
                              waits=[frl[m] if first else None, t_ones], ev=LB[m].ev)
                    pt_free[pi] = t2
                    if lastm:
                        last_pv[m] = t1
                        last_l[m] = t2

                for idx in range(nt):
                    issue_S(idx)
                    if idx == 1 and Q > 0:
                        pull()
                    if len(pend) > 2:
                        issue_PV(*pend.pop(0))
                        tick()
                        tile_ctr[0] += 1
                        if tile_ctr[0] % 12 == 6:
                            pull()
                while pend:
                    issue_PV(*pend.pop(0))

                t_lnl = []
                for m in range(2):
                    tk = P.op("act", lambda e, m=m: e.activation(out=lc[m], in_=LB[m].ps[:, :], func=AF.Ln), waits=[last_l[m], fin_tok[0]], ev=ev_act)
                    LB[m].free.append(tk)
                    t_lnl.append(tk)
                t_ocs = []
                for m in range(2):
                    tk = P.op("dve", lambda e, m=m: e.tensor_copy(out=oc[m], in_=OB[m].ps[:, :]), waits=[last_pv[m], last_l[m], fin_tok[0]], ev=ev_dve)
                    OB[m].free.append(tk)
                    t_ocs.append(tk)
                st = {}

                def fin1b(t_lnl=t_lnl, t_ocs=t_ocs, st=st):
                    t_rl = []
                    for m in range(2):
                        t_rl.append(P.op("act", lambda e, m=m: e.activation(out=lc[m], in_=lc[m], func=AF.Exp, scale=-1.0), waits=[t_lnl[m]], ev=ev_act))
                    c3 = P.op("dve", lambda e: e.tensor_tensor(out=oc[0], in0=oc[0], in1=lc[0], op=ALU.mult), waits=[t_ocs[0], t_rl[0]], ev=ev_dve)
                    c4 = P.op("dve", lambda e: e.tensor_tensor(out=oc[1], in0=oc[1], in1=lc[1], op=ALU.mult), waits=[t_ocs[1], t_rl[1]], ev=ev_dve)
                    c5 = P.op("dve", lambda e: e.scalar_tensor_tensor(out=oc[0], in0=oc[1], scalar=nlam, in1=oc[0], op0=ALU.mult, op1=ALU.add),
                              waits=[c3, c4, t_l3, t_hgs], ev=ev_dve)
                    st["t_sq"] = P.op("dve", lambda e: e.tensor_tensor(out=sqb, in0=oc[0], in1=oc[0], op=ALU.mult), waits=[c5], ev=ev_dve)
                deferred.append([3, fin1b, "b"])

                def fin2a(st=st):
                    t_sq = st["t_sq"]
                    bank = mmB[mm_rr[0] % 2]
                    mm_rr[0] += 1
                    fr = bank.acquire()
                    t_ms = P.op("pe", lambda e: e.matmul(bank.ps[:, :], lhsT=ones[:, :], rhs=sqb, start=True, stop=True), waits=[t_sq, fr], ev=bank.ev)
                    t_vv = P.op("dve", lambda e: e.tensor_scalar(out=lc[0], in0=bank.ps[:, :], scalar1=1.0 / 128, scalar2=HEPS, op0=ALU.mult, op1=ALU.add),
                                waits=[t_ms], ev=ev_dve)
                    bank.free.append(t_vv)
                    st["t_vv"] = t_vv

                def fin2b(h=h, Q=Q, st=st):
                    t_vv = st["t_vv"]
                    t_ln = P.op("act", lambda e: e.activation(out=lc[0], in_=lc[0], func=AF.Ln), waits=[t_vv], ev=ev_act)
                    t_r = P.op("act", lambda e: e.activation(out=lc[0], in_=lc[0], func=AF.Exp, scale=-0.5), waits=[t_ln], ev=ev_act)
                    o_ap = mixT[:, h, Q * 512:(Q + 1) * 512]
                    fin_tok[0] = P.op("dve", lambda e: e.scalar_tensor_tensor(out=o_ap, in0=oc[0], scalar=hgs, in1=lc[0], op0=ALU.mult, op1=ALU.mult),
                                      waits=[t_r], ev=ev_dve)
                deferred.append([10, fin2a, "c"])
                deferred.append([13, fin2b, "d"])
            prev_head_pe[0] = P.last["pe"]
            pull_fn[0] = None
            for _ in cgen:
                pass
            tick(force=True, tag="b")

        tick(force=True)
        tokB = [P.last["act"], P.last["dve"], P.last["pe"]]
        if debug:
            P.op("sp", lambda e: e.dma_start(out=dbg_mix, in_=av(OFF_M, SZ_M)), waits=tokB, ev=ev_out)

        t_g2 = P.op("sp", lambda e: e.dma_start(out=g2bc, in_=g2.broadcast_to([128, D])), waits=tokB, ev=ev_sp)
        t_gf = P.op("sp", lambda e: e.dma_start(out=gfbc, in_=gf.broadcast_to([128, D])), ev=ev_sp)
        xr_free = [None, None]
        xr_n = [0]
        ft_free = [None, None, None]
        ost_free = [None, None]
        act_free = [[None, None], [None, None]]
        gu_b = banks[0:4]
        dn_b = banks[4:8]
        tb2 = banks[4:6]
        dn_rr = [0]
        out_toks = []
        NG = NFF // 2
        for hf in range(2):
            def norm2_dve(s):
                gs = hf * 8 + s
                trs = norm_chain(h1[:, s, :], 128, gs, ssC, vvC, lnC, rsC, junkC, P.last["dve"])
                b = s % 3
                ta = P.op("dve", lambda e: e.scalar_tensor_tensor(out=ft[b], in0=h1[:, s, :], scalar=rsC[:, gs:gs + 1], in1=g2bc,
                                                                  op0=ALU.mult, op1=ALU.mult), waits=[trs, t_g2, ft_free[b]], ev=ev_dve)
                return ta

            ft_ready = {}
            xh = []
            for s in range(8):
                gs = hf * 8 + s
                w = tokB if hf == 0 else [E_tok[s]]
                xh.append(P.op("sp", lambda e, s=s, gs=gs: e.dma_start(out=h1[:, s, :], in_=x[gs * 128:(gs + 1) * 128, :]), waits=w, ev=ev_xh[s]))
            for dq in range(3):
                frs = [banks[s].acquire() for s in range(8)]
                toks = [None] * 8
                for kg in range(4):
                    sl = ring_load(w_out_v[:, kg * 4:(kg + 1) * 4, dq * 512:(dq + 1) * 512], 4)
                    tok = None
                    for s in range(8):
                        gs = hf * 8 + s
                        for kk in range(4):
                            k = kg * 4 + kk
                            l_ap = mixT[:, k, gs * 128:(gs + 1) * 128]
                            r_ap = sl.ap[:, kk, :]
                            bank = banks[s]
                            last = (k == 15)
                            rel = (s == 7 and kk == 3)
                            ev = bank.ev if last else (ev_rel if rel else None)
                            tok = P.op("pe", lambda e, bank=bank, l_ap=l_ap, r_ap=r_ap, k=k: e.matmul(bank.ps[:, :], lhsT=l_ap, rhs=r_ap, start=(k == 0), stop=(k == 15)),
                                       waits=[sl.tok, tokB, frs[s] if k == 0 else None], ev=ev)
                        if kg == 3:
                            toks[s] = tok
                    ring_release(sl, tok)
                for s in range(8):
                    dst = h1[:, s, dq * 512:(dq + 1) * 512]
                    td = P.op("dve", lambda e, s=s, dst=dst: e.tensor_tensor(out=dst, in0=banks[s].ps[:, :], in1=dst, op=ALU.add), waits=[toks[s], xh[s]], ev=ev_dve)
                    banks[s].free.append(td)
            for dq in range(3, 4):
                slots = [ring_load(w_out_v[:, kg * 4:(kg + 1) * 4, dq * 512:(dq + 1) * 512], 4) for kg in range(4)]
                for s in range(8):
                    gs = hf * 8 + s
                    bank = gu_b[mm_rr[0] % 4]
                    mm_rr[0] += 1
                    tok = acc16(bank, 512, lambda k, gs=gs: mixT[:, k, gs * 128:(gs + 1) * 128], lambda k: slots[k // 4].ap[:, k % 4, :],
                                [sl.tok for sl in slots] + tokB, bank.ev)
                    dst = h1[:, s, dq * 512:(dq + 1) * 512]
                    td = P.op("dve", lambda e, bank=bank, dst=dst: e.tensor_tensor(out=dst, in0=bank.ps[:, :], in1=dst, op=ALU.add), waits=[tok, xh[s]], ev=ev_dve)
                    bank.free.append(td)
                    ft_ready[s] = norm2_dve(s)
                    if s >= 2:
                        pt, evs = transposes(ft[(s - 2) % 3], 128, mixT, (hf * 8 + s - 2) * 128, tb2, ft_ready[s - 2])
                        ft_free[(s - 2) % 3] = pt
                for sl in slots:
                    ring_release(sl, tok)
            for s2 in (6, 7):
                pt, evs = transposes(ft[s2 % 3], 128, mixT, (hf * 8 + s2) * 128, tb2, ft_ready[s2])
                ft_free[s2 % 3] = pt
            tokC = [P.last["act"], P.last["dve"]]

            sg_rr = [0]
            sg_free = [None, None]
            act_rdy = {}

            def GU(g):
                for ci in range(2):
                    c = 2 * g + ci
                    sl_g = ring_load(w_gate_v[:, :, c * 128:(c + 1) * 128], 16)
                    sl_u = ring_load(w_up_v[:, :, c * 128:(c + 1) * 128], 16)
                    for n in range(2):
                        c0 = hf * 1024 + n * 512
                        res = []
                        for wi, sl in enumerate((sl_g, sl_u)):
                            bank = gu_b[mm_rr[0] % 4]
                            mm_rr[0] += 1
                            tok = acc16(bank, 512, lambda k, sl=sl: sl.ap[:, k, :], lambda k: mixT[:, k, c0:c0 + 512], [sl.tok] + tokC, bank.ev)
                            if n == 1:
                                ring_release(sl, tok)
                            res.append((bank, tok))
                            if wi == 0:
                                yield
                        sb_i = sg_rr[0] % 2
                        sg_rr[0] += 1
                        (bg_, tg_), (bu_, tu_) = res
                        t1 = P.op("act", lambda e, bg_=bg_, sb_i=sb_i: e.activation(out=sg[sb_i], in_=bg_.ps[:, :], func=AF.Silu),
                                  waits=[tg_, sg_free[sb_i]], ev=ev_act)
                        bg_.free.append(t1)
                        dst = actT[g % 2][ci][:, n * 512:(n + 1) * 512]
                        t2 = P.op("dve", lambda e, bu_=bu_, sb_i=sb_i, dst=dst: e.tensor_tensor(out=dst, in0=sg[sb_i], in1=bu_.ps[:, :], op=ALU.mult),
                                  waits=[t1, tu_, act_free[g % 2][ci]], ev=ev_dve)
                        bu_.free.append(t2)
                        sg_free[sb_i] = t2
                        act_rdy[(g, ci)] = t2
                        yield

            def DOWN(g, with_E):
                wd = [ring_load(w_down[(2 * g + ci) * 128:(2 * g + ci + 1) * 128, :]) for ci in range(2)]
                tok = None
                for s in range(8):
                    for dq in range(4):
                        bank = dn_b[dn_rr[0] % 4]
                        dn_rr[0] += 1
                        fr = bank.acquire()
                        for ci in range(2):
                            sl = wd[ci]
                            l_ap = actT[g % 2][ci][:, s * 128:(s + 1) * 128]
                            r_ap = sl.ap[:, dq * 512:(dq + 1) * 512]
                            tok = P.op("pe", lambda e, bank=bank, ci=ci, l_ap=l_ap, r_ap=r_ap: e.matmul(bank.ps[:, :], lhsT=l_ap, rhs=r_ap, start=(ci == 0), stop=(ci == 1)),
                                       waits=[sl.tok, act_rdy[(g, ci)], fr if ci == 0 else None], ev=(bank.ev if ci == 1 else None))
                        dst = h1[:, s, dq * 512:(dq + 1) * 512]
                        td = P.op("dve", lambda e, bank=bank, dst=dst: e.tensor_tensor(out=dst, in0=bank.ps[:, :], in1=dst, op=ALU.add), waits=[tok], ev=ev_dve)
                        bank.free.append(td)
                        yield
                    if with_E:
                        E_a(s)
                        if s >= 1:
                            E_b(s - 1)
                if with_E:
                    E_b(7)
                act_free[g % 2][0] = tok
                act_free[g % 2][1] = tok
                ring_release(wd[0], tok)
                ring_release(wd[1], tok)

            e_trs = {}

            def E_a(s):
                gs = hf * 8 + s
                e_trs[s] = norm_chain(h1[:, s, :], 128, gs, ssF, vvF, lnF, rsF, junkE, P.last["dve"])

            def E_b(s):
                gs = hf * 8 + s
                ob = s % 2
                to = P.op("dve", lambda e: e.scalar_tensor_tensor(out=ost[ob], in0=h1[:, s, :], scalar=rsF[:, gs:gs + 1], in1=gfbc,
                                                                  op0=ALU.mult, op1=ALU.mult), waits=[e_trs[s], t_gf, ost_free[ob], P.last["pe"]], ev=ev_dve)
                tp = None
                E_tok[s] = [to]
                tw = P.op("sp", lambda e: e.dma_start(out=out[gs * 128:(gs + 1) * 128, :], in_=ost[ob]), waits=[to, tp], ev=ev_out)
                ost_free[ob] = tw
                out_toks.append(tw)

            junkE = xv(32768, 4096)
            assert (NG - 1) % 2 == 1

            def drain(gen, n):
                for _ in range(n):
                    try:
                        next(gen)
                    except StopIteration:
                        return

            dn = None
            for g in range(NG):
                gu = GU(g)
                sched = [6, 6, 6, 6, 4, 4, 0, 0]
                for u in range(8):
                    next(gu)
                    if dn is not None:
                        drain(dn, sched[u])
                for _ in gu:
                    pass
                if dn is not None:
                    for _ in dn:
                        pass
                dn = DOWN(g, with_E=(g == NG - 1))
            if debug and hf == 0:
                pass
            for _ in dn:
                pass
            ft_free = [ost_free[0], ost_free[0], ost_free[1]]
            act_free = [[None, None], [None, None]]
        if debug:
            P.op("sp", lambda e: e.dma_start(out=dbg_h1, in_=av(OFF_A, 8 * D * 2).bitcast(F32)), waits=[P.last["dve"], P.last["pe"]], ev=ev_out)

        with nc.Block() as block:
            @block.tensor
            def _(e):
                P.emit("pe", e)

            @block.scalar
            def _(e):
                P.emit("act", e)

            @block.vector
            def _(e):
                P.emit("dve", e)

            @block.gpsimd
            def _(e):
                P.emit("pool", e)

            @block.sync
            def _(e):
                P.emit("sp", e)
                e.wait_ge(ev_out.sem, ev_out.n)
    return nc


def _consts():
    ident = np.eye(128, dtype=np.float32)
    tri = (np.arange(128)[:, None] <= np.arange(128)[None, :]).astype(np.float32)
    qpos = (NM + np.arange(S)).astype(np.int64)
    qaug = np.stack([np.ones(S), np.ones(S), (qpos % 128), (qpos // 128) * 128]).astype(np.float32)
    kpos = np.concatenate([NM + np.arange(S), np.arange(NM)]).astype(np.int64)
    base = np.stack([(kpos % 128), (kpos // 128) * 128, -np.ones(T), -np.ones(T)]).astype(np.float32)
    kaug = np.zeros((NH, 4, T), np.float32)
    for h in range(NH):
        c = 8.0 * 2.0 ** (-(h + 1))
        kaug[h] = base * c
    return ident, tri, qaug, kaug


_NC_CACHE = {}


def kernel(x, meta, norm1_g, w_in, lambda_q1, lambda_k1, lambda_q2, lambda_k2, head_g, conv_w, w_out,
           norm2_g, w_gate, w_up, w_down, norm_f_g):
    f = lambda a: np.ascontiguousarray(np.asarray(a, dtype=np.float32))
    x = f(x)
    B = x.shape[0]
    if "nc" not in _NC_CACHE:
        _NC_CACHE["nc"] = build_program()
    nc = _NC_CACHE["nc"]
    ident, tri, qaug, kaug = _consts()
    shared = {
        "meta": f(meta), "norm1_g": f(norm1_g).reshape(1, D), "w_in": f(w_in)[0],
        "lambda_q1": f(lambda_q1).reshape(1, 64), "lambda_k1": f(lambda_k1).reshape(1, 64),
        "lambda_q2": f(lambda_q2).reshape(1, 64), "lambda_k2": f(lambda_k2).reshape(1, 64),
        "head_g": f(head_g).reshape(1, 128), "conv_w": f(conv_w)[0], "w_out": f(w_out)[0],
        "norm2_g": f(norm2_g).reshape(1, D), "w_gate": f(w_gate)[0], "w_up": f(w_up)[0], "w_down": f(w_down)[0],
        "norm_f_g": f(norm_f_g).reshape(1, D),
        "c_ident": ident, "c_tri": tri, "c_qaug": qaug, "c_kaug": kaug,
    }
    in_maps = [dict(shared, x=x[b]) for b in range(B)]
    res = run_bass_kernel_spmd(nc, in_maps, core_ids=list(range(B)))
    return np.stack([r["out"] for r in res.results], axis=0)
```

```python
import contextlib
import numpy as np
import concourse.bass as bass
import concourse.mybir as mybir
from concourse.bass_utils import run_bass_kernel_spmd

F32 = mybir.dt.float32
BF16 = mybir.dt.bfloat16
AF = mybir.ActivationFunctionType
ALU = mybir.AluOpType

D = 2048
S = 2048
NM = 16
T = S + NM
DFF = 5632
NH = 8
INW = 6144
NFF = DFF // 128
LAMBDA_INIT = 0.8 - 0.6 * 1.0
EPS = 1e-6
HEPS = 1e-5

OFF_A = 0
SZ_A = 16 * T
OFF_M = OFF_A + SZ_A
SZ_M = 16 * S
OFF_R = OFF_M + SZ_M
RING = 6
OFF_X = OFF_R + RING * 2048
SZ_X = 24576
ARENA = OFF_X + SZ_X


ANNOTATE = False


class Ev:
    def __init__(self, sem, step=1):
        self.sem, self.step, self.n = sem, step, 0

    def fire(self):
        self.n += self.step
        return (self.sem, self.n)


class Prog:
    def __init__(self):
        self.q = {k: [] for k in ("pe", "act", "dve", "pool", "sp")}
        self.last = {k: None for k in self.q}

    def op(self, eng, fn, waits=(), ev=None):
        tok = ev.fire() if ev is not None else None
        ws = []
        for w in waits:
            if w is None:
                continue
            if isinstance(w, list):
                ws.extend([u for u in w if u is not None])
            else:
                ws.append(w)
        self.q[eng].append((fn, ws, ev))
        if tok is not None:
            self.last[eng] = tok
        return tok

    def emit(self, eng, handle):
        waited = {}
        for fn, ws, ev in self.q[eng]:
            for sem, val in ws:
                key = id(sem)
                if waited.get(key, 0) >= val:
                    continue
                waited[key] = val
                handle.wait_ge(sem, val)
            ins = fn(handle)
            if ANNOTATE:
                ins.annotate("L%d" % fn.__code__.co_firstlineno)
            if ev is not None:
                ins.then_inc(ev.sem, ev.step)


class Bank:
    def __init__(self, ps, ev_pe):
        self.ps = ps
        self.ev = ev_pe
        self.free = []

    def acquire(self):
        f = self.free
        self.free = []
        return f


def build_program(debug=False):
    nc = bass.Bass("TRN2", target_bir_lowering=False)

    def din(name, shape):
        return nc.dram_tensor(name, shape, F32, kind="ExternalInput").ap()

    x = din("x", [S, D])
    meta = din("meta", [NM, D])
    g1 = din("norm1_g", [1, D])
    w_in = din("w_in", [D, INW])
    lq1 = din("lambda_q1", [1, 64])
    lk1 = din("lambda_k1", [1, 64])
    lq2 = din("lambda_q2", [1, 64])
    lk2 = din("lambda_k2", [1, 64])
    head_g = din("head_g", [1, 128])
    conv_w = din("conv_w", [3, 1024])
    w_out = din("w_out", [D, D])
    g2 = din("norm2_g", [1, D])
    w_gate = din("w_gate", [D, DFF])
    w_up = din("w_up", [D, DFF])
    w_down = din("w_down", [DFF, D])
    gf = din("norm_f_g", [1, D])
    c_ident = din("c_ident", [128, 128])
    c_tri = din("c_tri", [128, 128])
    c_qaug = din("c_qaug", [4, S])
    c_kaug = din("c_kaug", [NH, 4, T])
    out = nc.dram_tensor("out", [S, D], F32, kind="ExternalOutput").ap()
    if debug:
        dbg_aT = nc.dram_tensor("dbg_aT", [128, 16 * T], BF16, kind="ExternalOutput").ap()
        dbg_mix = nc.dram_tensor("dbg_mix", [128, 16 * S], BF16, kind="ExternalOutput").ap()
        dbg_h1 = nc.dram_tensor("dbg_h1", [128, 8 * D], F32, kind="ExternalOutput").ap()

    w_in_v = w_in.rearrange("(k p) n -> p k n", p=128)
    w_out_v = w_out.rearrange("(k p) n -> p k n", p=128)
    w_gate_v = w_gate.rearrange("(k p) n -> p k n", p=128)
    w_up_v = w_up.rearrange("(k p) n -> p k n", p=128)

    P = Prog()
    with contextlib.ExitStack() as es:
        def sb(name, shape, dt):
            return es.enter_context(nc.sbuf_tensor(name, shape, dt))

        nsem = [0]

        def new_ev(step=1):
            nsem[0] += 1
            return Ev(es.enter_context(nc.semaphore(f"s{nsem[0]}")), step)

        arena = sb("arena", [128, ARENA], BF16)
        ident = sb("ident", [128, 128], BF16)
        tri = sb("tri", [128, 128], BF16)
        ones = sb("ones", [128, 128], BF16)
        stat = sb("stat", [128, 4 * 17 + 4 * 16 + 4 * 16], F32)
        lamt = sb("lamt", [128, 4 * 64 + 64 + 16], F32)
        cw = sb("cw", [128, 3, 8], F32)
        hg = sb("hg", [128, 4], F32)

        pss = [es.enter_context(nc.psum_tensor(f"ps{i}", [128, 512], F32)) for i in range(8)]
        banks = [Bank(pss[i], new_ev()) for i in range(8)]

        def tview(bank):
            return bank.ps[:, :].bitcast(BF16).rearrange("p (j n) -> p j n", j=8)

        def av(off, n):
            return arena[:, off:off + n]

        aT = av(OFF_A, SZ_A).rearrange("p (k n) -> p k n", k=16)
        h1 = av(OFF_A, 8 * D * 2).bitcast(F32).rearrange("p (s n) -> p s n", s=8)
        mixT = av(OFF_M, SZ_M).rearrange("p (k n) -> p k n", k=16)
        ring_slots = [av(OFF_R + i * 2048, 2048) for i in range(RING)]
        ring_ready = [new_ev(16) for _ in range(RING)]
        ring_free = [new_ev(1) for _ in range(RING)]
        ring_last = [None] * RING
        ring_n = [0]

        def xv(off_bytes, nbytes, dt=BF16):
            a = av(OFF_X + off_bytes // 2, nbytes // 2)
            return a.bitcast(F32) if dt == F32 else a

        def mv(off_bytes, nbytes, dt=BF16):
            a = av(OFF_M + off_bytes // 2, nbytes // 2)
            return a.bitcast(F32) if dt == F32 else a

        xs = [mv(i * 8192, 8192, F32) for i in range(4)]
        at = [mv(32768, 4096), mv(36864, 4096)]
        g1bc = mv(40960, 8192, F32)
        junkA = mv(49152, 4096)
        QK = 4160
        qA = xv(0, 4128)
        qB = xv(QK, 4128)
        kA = xv(2 * QK, 4128)
        kB = xv(3 * QK, 4128)
        vF = xv(4 * QK, 4128)
        vh = xv(5 * QK, 4352).rearrange("p (b n) -> p b n", b=17)
        o_pt = 5 * QK + 4352
        Pt = [xv(o_pt + i * 1024, 1024) for i in range(4)]
        o_fin = o_pt + 4096
        oc = [xv(o_fin, 2048, F32), xv(o_fin + 2048, 2048, F32)]
        lc = [xv(o_fin + 4096, 2048, F32), xv(o_fin + 6144, 2048, F32)]
        sqb = xv(o_fin + 8192, 1024)
        o_cv = o_fin + 8192 + 1024
        cgS = [xv(o_cv, 2048, F32), xv(o_cv + 2048, 2048, F32)]
        ubuf = xv(o_cv + 4096, 2064, F32)
        ybuf = [xv(o_cv + 6160, 2048, F32), xv(o_cv + 8208, 2048, F32)]
        assert o_cv + 10256 <= 49152
        xr = [xv(0, 2048, F32), xv(2048, 2048, F32)]
        ft = [xv(4096, 4096), xv(8192, 4096), xv(40960, 4096)]
        ost = [xv(4096, 8192, F32), xv(40960, 8192, F32)]
        g2bc = xv(12288, 8192, F32)
        gfbc = xv(20480, 8192, F32)
        sg = [xv(28672, 2048, F32), xv(30720, 2048, F32)]
        actT = [[xv(32768 + (g * 2 + c) * 2048, 2048) for c in range(2)] for g in range(2)]
        junkC = xv(32768, 4096)

        ssA, vvA, lnA, rsA = (stat[:, i * 17:(i + 1) * 17] for i in range(4))
        o2 = 68
        ssC, vvC, lnC, rsC = (stat[:, o2 + i * 16:o2 + (i + 1) * 16] for i in range(4))
        o3 = o2 + 64
        ssF, vvF, lnF, rsF = (stat[:, o3 + i * 16:o3 + (i + 1) * 16] for i in range(4))

        ev_sp = new_ev(16)
        ev_spx = [new_ev(16) for _ in range(4)]
        ev_act = new_ev(1)
        ev_dve = new_ev(1)
        ev_pool = new_ev(16)
        ev_out = new_ev(16)
        ev_rel = new_ev(1)
        ev_poolc = new_ev(1)
        ev_xh = [new_ev(16) for _ in range(8)]
        E_tok = {}

        class Slot:
            pass

        def ring_load(src_ap, shape3=None):
            i = ring_n[0]
            ring_n[0] += 1
            si = i % RING
            slot = ring_slots[si]
            dst = slot if shape3 is None else slot.rearrange("p (k n) -> p k n", k=shape3)
            waits = []
            if i >= RING:
                assert ring_last[si] is not None, "ring slot reused before its last reader was emitted"
                waits = [ring_last[si]]
                ring_last[si] = None
            tok = P.op("pool", lambda e, d=dst, s=src_ap: e.dma_start(out=d, in_=s), waits, ev=ring_ready[si])
            sl = Slot()
            sl.ap, sl.tok, sl.si = dst, tok, si
            return sl

        def ring_release(sl, tok):
            ring_last[sl.si] = tok

        t_c = []
        t_c.append(P.op("pool", lambda e: e.dma_start(out=ident[:, :], in_=c_ident), ev=ev_pool))
        t_c.append(P.op("pool", lambda e: e.dma_start(out=tri[:, :], in_=c_tri), ev=ev_pool))
        t_ones = P.op("dve", lambda e: e.memset(ones[:, :], 1.0), ev=ev_dve)
        epsc = hg[:, 2:3]
        t_eps = P.op("dve", lambda e: e.memset(hg[:, 2:3], EPS), ev=ev_dve)
        t_g1 = P.op("sp", lambda e: e.dma_start(out=g1bc, in_=g1.broadcast_to([128, D])), ev=ev_sp)

        tb_rr = [0]

        def norm_a(src, Pn, col, ss, vv, junk, src_tok):
            t1 = P.op("act", lambda e: e.activation(out=junk[0:Pn, :], in_=src, func=AF.Square, accum_out=ss[0:Pn, col:col + 1]),
                      waits=[src_tok], ev=ev_act)
            return t1

        def norm_b(Pn, col, vv, ln, rs, t1):
            t2b = P.op("act", lambda e: e.activation(out=ln[0:Pn, col:col + 1], in_=vv[0:Pn, col:col + 1], func=AF.Ln, bias=epsc[0:Pn, 0:1], scale=1.0 / D),
                       waits=[t1, t_eps], ev=ev_act)
            t3 = P.op("act", lambda e: e.activation(out=rs[0:Pn, col:col + 1], in_=ln[0:Pn, col:col + 1], func=AF.Exp, scale=-0.5), waits=[t2b], ev=ev_act)
            return t3

        def norm_chain(src, Pn, col, ss, vv, ln, rs, junk, src_tok):
            return norm_b(Pn, col, ss, ln, rs, norm_a(src, Pn, col, ss, vv, junk, src_tok))

        def transposes(a_tile, Pn, dst, c0, tbanks, a_tok):
            evs = []
            pe_tok = None
            for hf in range(2):
                bank = tbanks[tb_rr[0] % len(tbanks)]
                tb_rr[0] += 1
                tv = tview(bank)
                fr = bank.acquire()
                for j in range(8):
                    k = hf * 8 + j
                    last = (j == 7)
                    pe_tok = P.op("pe", lambda e, tv=tv, j=j, k=k: e.transpose(out=tv[:, j, 0:Pn], in_=a_tile[0:Pn, k * 128:(k + 1) * 128],
                                                                             identity=ident[0:Pn, 0:Pn]),
                                  waits=[a_tok, fr, t_c] if j == 0 else [], ev=bank.ev if last else None)
                eng = "act" if hf == 0 else "dve"
                if eng == "act":
                    tk = P.op("act", lambda e, tv=tv, hf=hf: e.copy(out=dst[:, hf * 8:(hf + 1) * 8, c0:c0 + Pn], in_=tv[:, :, 0:Pn]),
                              waits=[pe_tok], ev=ev_act)
                else:
                    tk = P.op("dve", lambda e, tv=tv, hf=hf: e.tensor_copy(out=dst[:, hf * 8:(hf + 1) * 8, c0:c0 + Pn], in_=tv[:, :, 0:Pn]),
                              waits=[pe_tok], ev=ev_dve)
                bank.free.append(tk)
                evs.append(tk)
            return pe_tok, evs

        mm_rr = [0]
        TOK_TILES = [(n * 512, 512) for n in range(4)] + [(S, NM)]
        mmB = banks[0:2]
        LB = banks[2:4]
        SB = banks[4:6]
        OB = banks[6:8]

        def acc16(bank, N, lhs_fn, rhs_fn, first_waits, last_ev):
            fr = bank.acquire()
            tok = None
            for k in range(16):
                l_ap = lhs_fn(k)
                r_ap = rhs_fn(k)
                tok = P.op("pe", lambda e, l_ap=l_ap, r_ap=r_ap, k=k: e.matmul(bank.ps[:, 0:N], lhsT=l_ap, rhs=r_ap, start=(k == 0), stop=(k == 15)),
                           waits=[fr, first_waits] if k == 0 else [], ev=(last_ev if k == 15 else None))
            return tok

        def proj_chunk(col0, tiles, consumer, first_waits):
            sl = ring_load(w_in_v[:, :, col0:col0 + 128], 16)
            for ti, (c0, N) in enumerate(tiles):
                bank = mmB[mm_rr[0] % 2]
                mm_rr[0] += 1
                tok = acc16(bank, N, lambda k: sl.ap[:, k, :], lambda k, c0=c0, N=N: aT[:, k, c0:c0 + N],
                            [sl.tok] + list(first_waits), bank.ev)
                consumer(ti, c0, N, bank, tok)
            ring_release(sl, tok)

        def evac_split(dstA, dstB):
            def cons(ti, c0, N, bank, tok):
                t1 = P.op("act", lambda e: e.copy(out=dstA[0:64, c0:c0 + N], in_=bank.ps[0:64, 0:N]), waits=[tok], ev=ev_act)
                t2 = P.op("dve", lambda e: e.tensor_copy(out=dstB[0:64, c0:c0 + N], in_=bank.ps[64:128, 0:N]), waits=[tok], ev=ev_dve)
                bank.free += [t1, t2]
            return cons

        vrr = [0]

        def evac_v(ti, c0, N, bank, tok):
            vrr[0] += 1
            if vrr[0] % 2:
                t1 = P.op("act", lambda e: e.copy(out=vF[:, c0:c0 + N], in_=bank.ps[:, 0:N]), waits=[tok], ev=ev_act)
            else:
                t1 = P.op("dve", lambda e: e.tensor_copy(out=vF[:, c0:c0 + N], in_=bank.ps[:, 0:N]), waits=[tok], ev=ev_dve)
            bank.free.append(t1)


        def proj_chunk_gen(col0, tiles, consumer, waits_box):
            sl = ring_load(w_in_v[:, :, col0:col0 + 128], 16)
            tok = None
            for ti, (c0, N) in enumerate(tiles):
                bank = mmB[mm_rr[0] % 2]
                mm_rr[0] += 1
                tok = acc16(bank, N, lambda k: sl.ap[:, k, :], lambda k, c0=c0, N=N: aT[:, k, c0:c0 + N],
                            [sl.tok] + list(waits_box[0]), bank.ev)
                consumer(ti, c0, N, bank, tok)
                if ti == len(tiles) - 1:
                    ring_release(sl, tok)
                yield

        a_done = {}
        t_done = {}

        vvtok = {}
        ld_tok = {}

        def stageA1a(i):
            b = i % 4
            Pn = 128 if i < 16 else NM
            src_rows = x[i * 128:(i + 1) * 128, :] if i < 16 else meta
            w = [a_done[i - 4]] if i >= 4 else []
            tld = P.op("sp", lambda e: e.dma_start(out=xs[b][0:Pn, :], in_=src_rows), waits=w, ev=ev_spx[b])
            ld_tok[i] = tld

        def stageA1s(i):
            b = i % 4
            Pn = 128 if i < 16 else NM
            vvtok[i] = norm_a(xs[b][0:Pn, :], Pn, i, ssA, vvA, junkA, ld_tok[i])

        def stageA1b(i):
            b = i % 4
            ab = i % 2
            Pn = 128 if i < 16 else NM
            trs = norm_b(Pn, i, ssA, lnA, rsA, vvtok[i])
            w = [trs, t_g1] + ([t_done[i - 2]] if i >= 2 else [])
            a_done[i] = P.op("dve", lambda e: e.scalar_tensor_tensor(out=at[ab][0:Pn, :], in0=xs[b][0:Pn, :], scalar=rsA[0:Pn, i:i + 1],
                                                                      in1=g1bc[0:Pn, :], op0=ALU.mult, op1=ALU.mult), waits=w, ev=ev_dve)

        def stageA2(i):
            Pn = 128 if i < 16 else NM
            c0 = i * 128 if i < 16 else S
            pt, evs = transposes(at[i % 2], Pn, aT, c0, banks[2:6], a_done[i])
            t_done[i] = pt

        wbox = [[]]
        g_q0 = proj_chunk_gen(0, TOK_TILES[0:4], evac_split(qA, qB), wbox)
        g_k0 = proj_chunk_gen(1024, TOK_TILES, evac_split(kA, kB), wbox)
        g_v0 = proj_chunk_gen(2048, TOK_TILES, evac_v, wbox)
        grp_tok = {}

        pend0 = []

        def step_one():
            if pend0:
                g, n = pend0.pop(0)
                wbox[0] = grp_tok[n]
                next(g)

        for i in range(3):
            stageA1a(i)
        stageA1s(0)
        stageA1b(0)
        for i in range(1, 17):
            if i + 2 < 17:
                stageA1a(i + 2)
            stageA1s(i)
            stageA1b(i)
            stageA2(i - 1)
            if (i - 1) % 4 == 3:
                n = (i - 1) // 4
                grp_tok[n] = [P.last["act"], P.last["dve"]]
                pend0.extend([(g_q0, n), (g_k0, n), (g_v0, n)])
            else:
                step_one()
        stageA2(16)
        grp_tok[4] = [P.last["act"], P.last["dve"]]
        pend0.extend([(g_k0, 4), (g_v0, 4)])
        while pend0:
            step_one()
        for g in (g_q0, g_k0, g_v0):
            for _ in g:
                pass
        lqv = lamt[:, 0:256].rearrange("p (a n) -> p a n", a=4)
        for i, src in enumerate((lq1, lk1, lq2, lk2)):
            tl = P.op("sp", lambda e, i=i, src=src: e.dma_start(out=lqv[:, i, :], in_=src.broadcast_to([128, 64])), ev=ev_sp)
        t_hg = P.op("sp", lambda e: e.dma_start(out=hg[:, 0:1], in_=head_g.rearrange("o d -> d o")), ev=ev_sp)
        for t in range(3):
            t_cw = P.op("sp", lambda e, t=t: e.dma_start(out=cw[:, t, :], in_=conv_w[t:t + 1, :].rearrange("o (j p) -> p (o j)", p=128),
                                                       allow_slow_non_contiguous=True), ev=ev_sp)
        ljunk = lamt[:, 256:320]
        lsc = lamt[:, 320:336]
        P.op("dve", lambda e: e.scalar_tensor_tensor(out=ljunk, in0=lqv[:, 0, :], scalar=1.0, in1=lqv[:, 1, :], op0=ALU.mult,
                                                     op1=ALU.mult, accum_out=lsc[:, 0:1]), waits=[t_cw], ev=ev_dve)
        t_l = P.op("dve", lambda e: e.scalar_tensor_tensor(out=ljunk, in0=lqv[:, 2, :], scalar=1.0, in1=lqv[:, 3, :], op0=ALU.mult,
                                                           op1=ALU.mult, accum_out=lsc[:, 1:2]), ev=ev_dve)
        t_le = P.op("act", lambda e: e.activation(out=lsc[:, 2:4], in_=lsc[:, 0:2], func=AF.Exp), waits=[t_l], ev=ev_act)
        t_l2 = P.op("dve", lambda e: e.tensor_tensor(out=lsc[:, 4:5], in0=lsc[:, 3:4], in1=lsc[:, 2:3], op=ALU.subtract), waits=[t_le], ev=ev_dve)
        t_l3 = P.op("dve", lambda e: e.tensor_scalar(out=lsc[:, 5:6], in0=lsc[:, 4:5], scalar1=-LAMBDA_INIT, scalar2=None, op0=ALU.add), waits=[t_l2], ev=ev_dve)
        nlam = lsc[:, 5:6]
        t_hgs = P.op("dve", lambda e: e.tensor_scalar(out=hg[:, 1:2], in0=hg[:, 0:1], scalar1=1.0 - LAMBDA_INIT, scalar2=None, op0=ALU.mult), ev=ev_dve)
        hgs = hg[:, 1:2]
        tokA = [P.last["act"], P.last["dve"], P.last["pe"]]
        if debug:
            P.op("sp", lambda e: e.dma_start(out=dbg_aT, in_=av(OFF_A, SZ_A)), waits=tokA, ev=ev_out)

        cg_rd = {}
        u_tok = [None]

        def conv_units(j, slots):
            s_cg, s_hi, s_bg = slots
            order = [4, 0, 1, 2, 3]
            for oi, ti in enumerate(order):
                c0, N = TOK_TILES[ti]
                cb = oi % 2
                bank = mmB[mm_rr[0] % 2]
                mm_rr[0] += 1
                tok = acc16(bank, N, lambda k: s_cg.ap[:, k, :], lambda k: aT[:, k, c0:c0 + N], [s_cg.tok], bank.ev)
                if oi == 4:
                    ring_release(s_cg, tok)
                d_ap, s_ap = cgS[cb][:, 0:N], bank.ps[:, 0:N]
                t1 = P.op("act", lambda e, d_ap=d_ap, s_ap=s_ap: e.copy(out=d_ap, in_=s_ap), waits=[tok, cg_rd.get(cb)], ev=ev_act)
                bank.free.append(t1)
                yield
                bank = mmB[mm_rr[0] % 2]
                mm_rr[0] += 1
                tok = acc16(bank, N, lambda k: s_hi.ap[:, k, :], lambda k: aT[:, k, c0:c0 + N], [s_hi.tok], bank.ev)
                if oi == 4:
                    ring_release(s_hi, tok)
                if ti == 4:
                    a_ap, b_ap = cgS[cb][:, NM - 2:NM], bank.ps[:, NM - 2:NM]
                    t2 = P.op("dve", lambda e, a_ap=a_ap, b_ap=b_ap: e.tensor_tensor(out=ubuf[:, 0:2], in0=a_ap, in1=b_ap, op=ALU.mult),
                              waits=[t1, tok, u_tok[0]], ev=ev_dve)
                    bank.free.append(t2)
                    cg_rd[cb] = t2
                    u_tok[0] = t2
                    yield
                    continue
                a_ap, b_ap = cgS[cb][:, :], bank.ps[:, :]
                t2 = P.op("dve", lambda e, a_ap=a_ap, b_ap=b_ap: e.tensor_tensor(out=ubuf[:, 2:514], in0=a_ap, in1=b_ap, op=ALU.mult),
                          waits=[t1, tok, u_tok[0]], ev=ev_dve)
                bank.free.append(t2)
                cg_rd[cb] = t2
                yield
                bank = mmB[mm_rr[0] % 2]
                mm_rr[0] += 1
                tok = acc16(bank, N, lambda k: s_bg.ap[:, k, :], lambda k: aT[:, k, c0:c0 + N], [s_bg.tok], bank.ev)
                if oi == 4:
                    ring_release(s_bg, tok)
                yb = ybuf[cb]
                n = ti
                w0, w1, w2 = cw[:, 0, j:j + 1], cw[:, 1, j:j + 1], cw[:, 2, j:j + 1]
                y1 = P.op("dve", lambda e, yb=yb, w0=w0: e.tensor_scalar(out=yb, in0=ubuf[:, 0:512], scalar1=w0, scalar2=None, op0=ALU.mult),
                          waits=[t2, t_cw], ev=ev_dve)
                y2 = P.op("dve", lambda e, yb=yb, w1=w1: e.scalar_tensor_tensor(out=yb, in0=ubuf[:, 1:513], scalar=w1, in1=yb, op0=ALU.mult, op1=ALU.add),
                          waits=[y1], ev=ev_dve)
                y3 = P.op("dve", lambda e, yb=yb, w2=w2: e.scalar_tensor_tensor(out=yb, in0=ubuf[:, 2:514], scalar=w2, in1=yb, op0=ALU.mult, op1=ALU.add),
                          waits=[y2], ev=ev_dve)
                o_ap, b_ap = mixT[:, 8 + j, n * 512:(n + 1) * 512], bank.ps[:, :]
                t3 = P.op("dve", lambda e, yb=yb, o_ap=o_ap, b_ap=b_ap: e.tensor_tensor(out=o_ap, in0=yb, in1=b_ap, op=ALU.mult),
                          waits=[tok, y3], ev=ev_dve)
                bank.free.append(t3)
                u_tok[0] = P.op("dve", lambda e: e.tensor_copy(out=ubuf[:, 0:2], in_=ubuf[:, 512:514]), waits=[y3], ev=ev_dve)
                yield

        s_rr = [0]
        pt_rr = [0]
        pt_free = [None] * 4
        prev_head_pe = [None]
        fin_tok = [None]
        t_qa = None
        deferred = []

        pull_fn = [None]

        def tick(force=False, tag=None):
            for item in list(deferred):
                item[0] -= 1
                if (force and (tag is None or item[2] == tag)) or item[0] <= 0:
                    deferred.remove(item)
                    if not force and pull_fn[0] is not None and item[2] in ("b", "d"):
                        pull_fn[0]()
                    item[1]()

        for h in range(NH):
            fw = tokA if h == 0 else []
            if h > 0:
                proj_chunk(h * 128, TOK_TILES[0:4], evac_split(qA, qB), fw)
                tick(force=True)
                proj_chunk(2048 + h * 128, TOK_TILES, evac_v, [])
                proj_chunk(1024 + h * 128, TOK_TILES, evac_split(kA, kB), [])
            t_v = [P.last["act"], P.last["dve"]]
            tv_toks = []
            for g0 in range(0, 17, 8):
                bank = mmB[mm_rr[0] % 2]
                mm_rr[0] += 1
                tv = tview(bank)
                fr = bank.acquire()
                nb = min(8, 17 - g0)
                tok = None
                for j in range(nb):
                    blk = g0 + j
                    Pn = 128 if blk < 16 else NM
                    tok = P.op("pe", lambda e, tv=tv, j=j, blk=blk, Pn=Pn: e.transpose(out=tv[0:Pn, j, :], in_=vF[:, blk * 128:blk * 128 + Pn],
                                                                                     identity=ident[:, :]),
                               waits=[t_v, fr] if j == 0 else [], ev=bank.ev if j == nb - 1 else None)
                if nb == 8:
                    tk = P.op("dve", lambda e, tv=tv, g0=g0: e.tensor_copy(out=vh[:, g0:g0 + 8, :], in_=tv[:, :, :]), waits=[tok], ev=ev_dve)
                else:
                    tk = P.op("dve", lambda e, tv=tv: e.tensor_copy(out=vh[0:NM, 16, :], in_=tv[0:NM, 0, :]), waits=[tok], ev=ev_dve)
                bank.free.append(tk)
                tv_toks.append(tk)
            t_qk = [P.last["act"], P.last["dve"]]
            conv_slots = [ring_load(w_in_v[:, :, 4096 + h * 128:4096 + (h + 1) * 128], 16),
                          ring_load(w_in_v[:, :, 5120 + h * 128:5120 + (h + 1) * 128], 16),
                          ring_load(w_in_v[:, :, 3072 + h * 128:3072 + (h + 1) * 128], 16)]
            if h == 0:
                P.op("pool", lambda e: e.dma_start(out=qA[64:68, 0:S], in_=c_qaug), ev=ev_pool)
                t_qa = P.op("pool", lambda e: e.dma_start(out=qB[64:68, 0:S], in_=c_qaug), ev=ev_pool)
            P.op("pool", lambda e, h=h: e.dma_start(out=kA[64:68, 0:T], in_=c_kaug[h]), waits=[prev_head_pe[0]], ev=ev_pool)
            t_ka = P.op("pool", lambda e, h=h: e.dma_start(out=kB[64:68, 0:T], in_=c_kaug[h]), ev=ev_pool)
            cgen = conv_units(h, conv_slots)
            tile_ctr = [0]
            prev_head_pe_tok = P.last["pe"]

            def pull():
                try:
                    next(cgen)
                except StopIteration:
                    pass

            pull_fn[0] = pull
            pull()
            pull()
            for Q in range(4):
                blocks = [("m", S, NM, 16)] + [(j, j * 128, 128, j) for j in range(4 * Q + 4)]
                tiles = []
                for (bid, kc0, nk, vb) in blocks:
                    qlo = 0
                    diag = False
                    if bid != "m" and bid >= 4 * Q:
                        qlo = (bid - 4 * Q) * 128
                        diag = True
                    for m in range(2):
                        tiles.append((bid, kc0, nk, vb, qlo, diag, m))
                nt = len(tiles)
                pend = []
                fro = [OB[0].acquire(), OB[1].acquire()]
                frl = [LB[0].acquire(), LB[1].acquire()]
                last_pv = [None, None]
                last_l = [None, None]

                def issue_S(idx):
                    (bid, kc0, nk, vb, qlo, diag, m) = tiles[idx]
                    kM = kA if m == 0 else kB
                    qM = qA if m == 0 else qB
                    sbk = SB[s_rr[0] % 2]
                    s_rr[0] += 1
                    pi = pt_rr[0] % 4
                    pt_rr[0] += 1
                    fr = sbk.acquire()
                    rhs_ap = qM[0:68, Q * 512 + qlo:(Q + 1) * 512]
                    ts = P.op("pe", lambda e: e.matmul(sbk.ps[0:nk, qlo:512], lhsT=kM[0:68, kc0:kc0 + nk], rhs=rhs_ap, start=True, stop=True),
                              waits=[fr, t_qk, t_ka, t_qa, tv_toks], ev=sbk.ev)
                    te = P.op("act", lambda e: e.activation(out=Pt[pi][0:nk, qlo:512], in_=sbk.ps[0:nk, qlo:512], func=AF.Exp, scale=0.125),
                              waits=[ts, pt_free[pi]], ev=ev_act)
                    sbk.free.append(te)
                    rdy = te
                    if diag:
                        rdy = P.op("dve", lambda e: e.tensor_tensor(out=Pt[pi][0:nk, qlo:qlo + 128], in0=Pt[pi][0:nk, qlo:qlo + 128], in1=tri[:, :],
                                                                    op=ALU.mult), waits=[te], ev=ev_dve)
                    pend.append((idx, pi, rdy))

                def issue_PV(idx, pi, rdy):
                    (bid, kc0, nk, vb, qlo, diag, m) = tiles[idx]
                    first = (idx < 2)
                    lastm = (idx >= nt - 2)
                    vsrc = vh[0:nk, vb, :]
                    t1 = P.op("pe", lambda e: e.matmul(OB[m].ps[:, qlo:512], lhsT=vsrc, rhs=Pt[pi][0:nk, qlo:512], start=first, stop=lastm),
                              waits=[rdy, fro[m] if first else None], ev=OB[m].ev if lastm else None)
                    t2 = P.op("pe", lambda e: e.matmul(LB[m].ps[:, qlo:512], lhsT=ones[0:nk, :], rhs=Pt[pi][0:nk, qlo:512], start=first, stop=lastm),
                              waits=[frl[m] if first else None, t_ones], ev=LB[m].ev)
                    pt_free[pi] = t2
                    if lastm:
                        last_pv[m] = t1
                        last_l[m] = t2

                for idx in range(nt):
                    issue_S(idx)
                    if idx == 1 and Q > 0:
                        pull()
                    if len(pend) > 2:
                        issue_PV(*pend.pop(0))
                        tick()
                        tile_ctr[0] += 1
                        if tile_ctr[0] % 12 == 6:
                            pull()
                while pend:
                    issue_PV(*pend.pop(0))

                t_lnl = []
                for m in range(2):
                    tk = P.op("act", lambda e, m=m: e.activation(out=lc[m], in_=LB[m].ps[:, :], func=AF.Ln), waits=[last_l[m], fin_tok[0]], ev=ev_act)
                    LB[m].free.append(tk)
                    t_lnl.append(tk)
                t_ocs = []
                for m in range(2):
                    tk = P.op("dve", lambda e, m=m: e.tensor_copy(out=oc[m], in_=OB[m].ps[:, :]), waits=[last_pv[m], last_l[m], fin_tok[0]], ev=ev_dve)
                    OB[m].free.append(tk)
                    t_ocs.append(tk)
                st = {}

                def fin1b(t_lnl=t_lnl, t_ocs=t_ocs, st=st):
                    t_rl = []
                    for m in range(2):
                        t_rl.append(P.op("act", lambda e, m=m: e.activation(out=lc[m], in_=lc[m], func=AF.Exp, scale=-1.0), waits=[t_lnl[m]], ev=ev_act))
                    c3 = P.op("dve", lambda e: e.tensor_tensor(out=oc[0], in0=oc[0], in1=lc[0], op=ALU.mult), waits=[t_ocs[0], t_rl[0]], ev=ev_dve)
                    c4 = P.op("dve", lambda e: e.tensor_tensor(out=oc[1], in0=oc[1], in1=lc[1], op=ALU.mult), waits=[t_ocs[1], t_rl[1]], ev=ev_dve)
                    c5 = P.op("dve", lambda e: e.scalar_tensor_tensor(out=oc[0], in0=oc[1], scalar=nlam, in1=oc[0], op0=ALU.mult, op1=ALU.add),
                              waits=[c3, c4, t_l3, t_hgs], ev=ev_dve)
                    st["t_sq"] = P.op("dve", lambda e: e.tensor_tensor(out=sqb, in0=oc[0], in1=oc[0], op=ALU.mult), waits=[c5], ev=ev_dve)
                deferred.append([3, fin1b, "b"])

                def fin2a(st=st):
                    t_sq = st["t_sq"]
                    bank = mmB[mm_rr[0] % 2]
                    mm_rr[0] += 1
                    fr = bank.acquire()
                    t_ms = P.op("pe", lambda e: e.matmul(bank.ps[:, :], lhsT=ones[:, :], rhs=sqb, start=True, stop=True), waits=[t_sq, fr], ev=bank.ev)
                    t_vv = P.op("dve", lambda e: e.tensor_scalar(out=lc[0], in0=bank.ps[:, :], scalar1=1.0 / 128, scalar2=HEPS, op0=ALU.mult, op1=ALU.add),
                                waits=[t_ms], ev=ev_dve)
                    bank.free.append(t_vv)
                    st["t_vv"] = t_vv

                def fin2b(h=h, Q=Q, st=st):
                    t_vv = st["t_vv"]
                    t_ln = P.op("act", lambda e: e.activation(out=lc[0], in_=lc[0], func=AF.Ln), waits=[t_vv], ev=ev_act)
                    t_r = P.op("act", lambda e: e.activation(out=lc[0], in_=lc[0], func=AF.Exp, scale=-0.5), waits=[t_ln], ev=ev_act)
                    o_ap = mixT[:, h, Q * 512:(Q + 1) * 512]
                    fin_tok[0] = P.op("dve", lambda e: e.scalar_tensor_tensor(out=o_ap, in0=oc[0], scalar=hgs, in1=lc[0], op0=ALU.mult, op1=ALU.mult),
                                      waits=[t_r], ev=ev_dve)
                deferred.append([10, fin2a, "c"])
                deferred.append([13, fin2b, "d"])
            prev_head_pe[0] = P.last["pe"]
            pull_fn[0] = None
            for _ in cgen:
                pass
            tick(force=True, tag="b")

        tick(force=True)
        tokB = [P.last["act"], P.last["dve"], P.last["pe"]]
        if debug:
            P.op("sp", lambda e: e.dma_start(out=dbg_mix, in_=av(OFF_M, SZ_M)), waits=tokB, ev=ev_out)

        t_g2 = P.op("sp", lambda e: e.dma_start(out=g2bc, in_=g2.broadcast_to([128, D])), waits=tokB, ev=ev_sp)
        t_gf = P.op("sp", lambda e: e.dma_start(out=gfbc, in_=gf.broadcast_to([128, D])), ev=ev_sp)
        xr_free = [None, None]
        xr_n = [0]
        ft_free = [None, None, None]
        ost_free = [None, None]
        act_free = [[None, None], [None, None]]
        gu_b = banks[0:4]
        dn_b = banks[4:8]
        tb2 = banks[4:6]
        dn_rr = [0]
        out_toks = []
        NG = NFF // 2
        for hf in range(2):
            def norm2_dve(s):
                gs = hf * 8 + s
                trs = norm_chain(h1[:, s, :], 128, gs, ssC, vvC, lnC, rsC, junkC, P.last["dve"])
                b = s % 3
                ta = P.op("dve", lambda e: e.scalar_tensor_tensor(out=ft[b], in0=h1[:, s, :], scalar=rsC[:, gs:gs + 1], in1=g2bc,
                                                                  op0=ALU.mult, op1=ALU.mult), waits=[trs, t_g2, ft_free[b]], ev=ev_dve)
                return ta

            ft_ready = {}
            xq = []
            for dq in range(4):
                w = tokB if hf == 0 else [E_tok[s] for s in range(8)]
                src_ap = x[hf * 1024:(hf + 1) * 1024, dq * 512:(dq + 1) * 512].rearrange("(s p) c -> p s c", p=128)
                xq.append(P.op("sp", lambda e, dq=dq, src_ap=src_ap: e.dma_start(out=h1[:, :, dq * 512:(dq + 1) * 512], in_=src_ap), waits=w, ev=ev_xh[dq]))
            for dq in range(3):
                frs = [banks[s].acquire() for s in range(8)]
                toks = [None] * 8
                for kg in range(4):
                    sl = ring_load(w_out_v[:, kg * 4:(kg + 1) * 4, dq * 512:(dq + 1) * 512], 4)
                    tok = None
                    for s in range(8):
                        gs = hf * 8 + s
                        for kk in range(4):
                            k = kg * 4 + kk
                            l_ap = mixT[:, k, gs * 128:(gs + 1) * 128]
                            r_ap = sl.ap[:, kk, :]
                            bank = banks[s]
                            last = (k == 15)
                            rel = (s == 7 and kk == 3)
                            ev = bank.ev if last else (ev_rel if rel else None)
                            tok = P.op("pe", lambda e, bank=bank, l_ap=l_ap, r_ap=r_ap, k=k: e.matmul(bank.ps[:, :], lhsT=l_ap, rhs=r_ap, start=(k == 0), stop=(k == 15)),
                                       waits=[sl.tok, tokB, frs[s] if k == 0 else None], ev=ev)
                        if kg == 3:
                            toks[s] = tok
                    ring_release(sl, tok)
                for s in range(8):
                    dst = h1[:, s, dq * 512:(dq + 1) * 512]
                    td = P.op("dve", lambda e, s=s, dst=dst: e.tensor_tensor(out=dst, in0=banks[s].ps[:, :], in1=dst, op=ALU.add), waits=[toks[s], xq[dq]], ev=ev_dve)
                    banks[s].free.append(td)
            for dq in range(3, 4):
                slots = [ring_load(w_out_v[:, kg * 4:(kg + 1) * 4, dq * 512:(dq + 1) * 512], 4) for kg in range(4)]
                for s in range(8):
                    gs = hf * 8 + s
                    bank = gu_b[mm_rr[0] % 4]
                    mm_rr[0] += 1
                    tok = acc16(bank, 512, lambda k, gs=gs: mixT[:, k, gs * 128:(gs + 1) * 128], lambda k: slots[k // 4].ap[:, k % 4, :],
                                [sl.tok for sl in slots] + tokB, bank.ev)
                    dst = h1[:, s, dq * 512:(dq + 1) * 512]
                    td = P.op("dve", lambda e, bank=bank, dst=dst: e.tensor_tensor(out=dst, in0=bank.ps[:, :], in1=dst, op=ALU.add), waits=[tok, xq[dq]], ev=ev_dve)
                    bank.free.append(td)
                    ft_ready[s] = norm2_dve(s)
                    if s >= 2:
                        pt, evs = transposes(ft[(s - 2) % 3], 128, mixT, (hf * 8 + s - 2) * 128, tb2, ft_ready[s - 2])
                        ft_free[(s - 2) % 3] = pt
                for sl in slots:
                    ring_release(sl, tok)
            for s2 in (6, 7):
                pt, evs = transposes(ft[s2 % 3], 128, mixT, (hf * 8 + s2) * 128, tb2, ft_ready[s2])
                ft_free[s2 % 3] = pt
            tokC = [P.last["act"], P.last["dve"]]

            sg_rr = [0]
            sg_free = [None, None]
            act_rdy = {}

            def GU(g):
                for ci in range(2):
                    c = 2 * g + ci
                    sl_g = ring_load(w_gate_v[:, :, c * 128:(c + 1) * 128], 16)
                    sl_u = ring_load(w_up_v[:, :, c * 128:(c + 1) * 128], 16)
                    for n in range(2):
                        c0 = hf * 1024 + n * 512
                        res = []
                        for wi, sl in enumerate((sl_g, sl_u)):
                            bank = gu_b[mm_rr[0] % 4]
                            mm_rr[0] += 1
                            tok = acc16(bank, 512, lambda k, sl=sl: sl.ap[:, k, :], lambda k: mixT[:, k, c0:c0 + 512], [sl.tok] + tokC, bank.ev)
                            if n == 1:
                                ring_release(sl, tok)
                            res.append((bank, tok))
                            if wi == 0:
                                yield
                        sb_i = sg_rr[0] % 2
                        sg_rr[0] += 1
                        (bg_, tg_), (bu_, tu_) = res
                        t1 = P.op("act", lambda e, bg_=bg_, sb_i=sb_i: e.activation(out=sg[sb_i], in_=bg_.ps[:, :], func=AF.Silu),
                                  waits=[tg_, sg_free[sb_i]], ev=ev_act)
                        bg_.free.append(t1)
                        dst = actT[g % 2][ci][:, n * 512:(n + 1) * 512]
                        t2 = P.op("dve", lambda e, bu_=bu_, sb_i=sb_i, dst=dst: e.tensor_tensor(out=dst, in0=sg[sb_i], in1=bu_.ps[:, :], op=ALU.mult),
                                  waits=[t1, tu_, act_free[g % 2][ci]], ev=ev_dve)
                        bu_.free.append(t2)
                        sg_free[sb_i] = t2
                        act_rdy[(g, ci)] = t2
                        yield

            def DOWN(g, with_E):
                wd = [ring_load(w_down[(2 * g + ci) * 128:(2 * g + ci + 1) * 128, :]) for ci in range(2)]
                tok = None
                for s in range(8):
                    for dq in range(4):
                        bank = dn_b[dn_rr[0] % 4]
                        dn_rr[0] += 1
                        fr = bank.acquire()
                        for ci in range(2):
                            sl = wd[ci]
                            l_ap = actT[g % 2][ci][:, s * 128:(s + 1) * 128]
                            r_ap = sl.ap[:, dq * 512:(dq + 1) * 512]
                            tok = P.op("pe", lambda e, bank=bank, ci=ci, l_ap=l_ap, r_ap=r_ap: e.matmul(bank.ps[:, :], lhsT=l_ap, rhs=r_ap, start=(ci == 0), stop=(ci == 1)),
                                       waits=[sl.tok, act_rdy[(g, ci)], fr if ci == 0 else None], ev=(bank.ev if ci == 1 else None))
                        dst = h1[:, s, dq * 512:(dq + 1) * 512]
                        td = P.op("dve", lambda e, bank=bank, dst=dst: e.tensor_tensor(out=dst, in0=bank.ps[:, :], in1=dst, op=ALU.add), waits=[tok], ev=ev_dve)
                        bank.free.append(td)
                        yield
                    if with_E:
                        E_a(s)
                        if s >= 1:
                            E_b(s - 1)
                if with_E:
                    E_b(7)
                act_free[g % 2][0] = tok
                act_free[g % 2][1] = tok
                ring_release(wd[0], tok)
                ring_release(wd[1], tok)

            e_trs = {}

            def E_a(s):
                gs = hf * 8 + s
                e_trs[s] = norm_chain(h1[:, s, :], 128, gs, ssF, vvF, lnF, rsF, junkE, P.last["dve"])

            def E_b(s):
                gs = hf * 8 + s
                ob = s % 2
                to = P.op("dve", lambda e: e.scalar_tensor_tensor(out=ost[ob], in0=h1[:, s, :], scalar=rsF[:, gs:gs + 1], in1=gfbc,
                                                                  op0=ALU.mult, op1=ALU.mult), waits=[e_trs[s], t_gf, ost_free[ob], P.last["pe"]], ev=ev_dve)
                tp = None
                E_tok[s] = [to]
                tw = P.op("sp", lambda e: e.dma_start(out=out[gs * 128:(gs + 1) * 128, :], in_=ost[ob]), waits=[to, tp], ev=ev_out)
                ost_free[ob] = tw
                out_toks.append(tw)

            junkE = xv(32768, 4096)
            assert (NG - 1) % 2 == 1

            def drain(gen, n):
                for _ in range(n):
                    try:
                        next(gen)
                    except StopIteration:
                        return

            dn = None
            for g in range(NG):
                gu = GU(g)
                sched = [6, 6, 6, 6, 4, 4, 0, 0]
                for u in range(8):
                    next(gu)
                    if dn is not None:
                        drain(dn, sched[u])
                for _ in gu:
                    pass
                if dn is not None:
                    for _ in dn:
                        pass
                dn = DOWN(g, with_E=(g == NG - 1))
            if debug and hf == 0:
                pass
            for _ in dn:
                pass
            ft_free = [ost_free[0], ost_free[0], ost_free[1]]
            act_free = [[None, None], [None, None]]
        if debug:
            P.op("sp", lambda e: e.dma_start(out=dbg_h1, in_=av(OFF_A, 8 * D * 2).bitcast(F32)), waits=[P.last["dve"], P.last["pe"]], ev=ev_out)

        with nc.Block() as block:
            @block.tensor
            def _(e):
                P.emit("pe", e)

            @block.scalar
            def _(e):
                P.emit("act", e)

            @block.vector
            def _(e):
                P.emit("dve", e)

            @block.gpsimd
            def _(e):
                P.emit("pool", e)

            @block.sync
            def _(e):
                P.emit("sp", e)
                e.wait_ge(ev_out.sem, ev_out.n)
    return nc


def _consts():
    ident = np.eye(128, dtype=np.float32)
    tri = (np.arange(128)[:, None] <= np.arange(128)[None, :]).astype(np.float32)
    qpos = (NM + np.arange(S)).astype(np.int64)
    qaug = np.stack([np.ones(S), np.ones(S), (qpos % 128), (qpos // 128) * 128]).astype(np.float32)
    kpos = np.concatenate([NM + np.arange(S), np.arange(NM)]).astype(np.int64)
    base = np.stack([(kpos % 128), (kpos // 128) * 128, -np.ones(T), -np.ones(T)]).astype(np.float32)
    kaug = np.zeros((NH, 4, T), np.float32)
    for h in range(NH):
        c = 8.0 * 2.0 ** (-(h + 1))
        kaug[h] = base * c
    return ident, tri, qaug, kaug


_NC_CACHE = {}


def kernel(x, meta, norm1_g, w_in, lambda_q1, lambda_k1, lambda_q2, lambda_k2, head_g, conv_w, w_out,
           norm2_g, w_gate, w_up, w_down, norm_f_g):
    f = lambda a: np.ascontiguousarray(np.asarray(a, dtype=np.float32))
    x = f(x)
    B = x.shape[0]
    if "nc" not in _NC_CACHE:
        _NC_CACHE["nc"] = build_program()
    nc = _NC_CACHE["nc"]
    ident, tri, qaug, kaug = _consts()
    shared = {
        "meta": f(meta), "norm1_g": f(norm1_g).reshape(1, D), "w_in": f(w_in)[0],
        "lambda_q1": f(lambda_q1).reshape(1, 64), "lambda_k1": f(lambda_k1).reshape(1, 64),
        "lambda_q2": f(lambda_q2).reshape(1, 64), "lambda_k2": f(lambda_k2).reshape(1, 64),
        "head_g": f(head_g).reshape(1, 128), "conv_w": f(conv_w)[0], "w_out": f(w_out)[0],
        "norm2_g": f(norm2_g).reshape(1, D), "w_gate": f(w_gate)[0], "w_up": f(w_up)[0], "w_down": f(w_down)[0],
        "norm_f_g": f(norm_f_g).reshape(1, D),
        "c_ident": ident, "c_tri": tri, "c_qaug": qaug, "c_kaug": kaug,
    }
    in_maps = [dict(shared, x=x[b]) for b in range(B)]
    res = run_bass_kernel_spmd(nc, in_maps, core_ids=list(range(B)))
    return np.stack([r["out"] for r in res.results], axis=0)
```

```python
import contextlib
import numpy as np
import concourse.bass as bass
import concourse.mybir as mybir
from concourse.bass_utils import run_bass_kernel_spmd

F32 = mybir.dt.float32
BF16 = mybir.dt.bfloat16
AF = mybir.ActivationFunctionType
ALU = mybir.AluOpType

D = 2048
S = 2048
NM = 16
T = S + NM
DFF = 5632
NH = 8
INW = 6144
NFF = DFF // 128
LAMBDA_INIT = 0.8 - 0.6 * 1.0
EPS = 1e-6
HEPS = 1e-5

OFF_A = 0
SZ_A = 16 * T
OFF_M = OFF_A + SZ_A
SZ_M = 16 * S
OFF_R = OFF_M + SZ_M
RING = 6
OFF_X = OFF_R + RING * 2048
SZ_X = 24576
ARENA = OFF_X + SZ_X


ANNOTATE = False


class Ev:
    def __init__(self, sem, step=1):
        self.sem, self.step, self.n = sem, step, 0

    def fire(self):
        self.n += self.step
        return (self.sem, self.n)


class Prog:
    def __init__(self):
        self.q = {k: [] for k in ("pe", "act", "dve", "pool", "sp")}
        self.last = {k: None for k in self.q}

    def op(self, eng, fn, waits=(), ev=None):
        tok = ev.fire() if ev is not None else None
        ws = []
        for w in waits:
            if w is None:
                continue
            if isinstance(w, list):
                ws.extend([u for u in w if u is not None])
            else:
                ws.append(w)
        self.q[eng].append((fn, ws, ev))
        if tok is not None:
            self.last[eng] = tok
        return tok

    def emit(self, eng, handle):
        waited = {}
        for fn, ws, ev in self.q[eng]:
            for sem, val in ws:
                key = id(sem)
                if waited.get(key, 0) >= val:
                    continue
                waited[key] = val
                handle.wait_ge(sem, val)
            ins = fn(handle)
            if ANNOTATE:
                ins.annotate("L%d" % fn.__code__.co_firstlineno)
            if ev is not None:
                ins.then_inc(ev.sem, ev.step)


class Bank:
    def __init__(self, ps, ev_pe):
        self.ps = ps
        self.ev = ev_pe
        self.free = []

    def acquire(self):
        f = self.free
        self.free = []
        return f


def build_program(debug=False):
    nc = bass.Bass("TRN2", target_bir_lowering=False)

    def din(name, shape):
        return nc.dram_tensor(name, shape, F32, kind="ExternalInput").ap()

    x = din("x", [S, D])
    meta = din("meta", [NM, D])
    g1 = din("norm1_g", [1, D])
    w_in = din("w_in", [D, INW])
    lq1 = din("lambda_q1", [1, 64])
    lk1 = din("lambda_k1", [1, 64])
    lq2 = din("lambda_q2", [1, 64])
    lk2 = din("lambda_k2", [1, 64])
    head_g = din("head_g", [1, 128])
    conv_w = din("conv_w", [3, 1024])
    w_out = din("w_out", [D, D])
    g2 = din("norm2_g", [1, D])
    w_gate = din("w_gate", [D, DFF])
    w_up = din("w_up", [D, DFF])
    w_down = din("w_down", [DFF, D])
    gf = din("norm_f_g", [1, D])
    c_ident = din("c_ident", [128, 128])
    c_tri = din("c_tri", [128, 128])
    c_qaug = din("c_qaug", [4, S])
    c_kaug = din("c_kaug", [NH, 4, T])
    out = nc.dram_tensor("out", [S, D], F32, kind="ExternalOutput").ap()
    if debug:
        dbg_aT = nc.dram_tensor("dbg_aT", [128, 16 * T], BF16, kind="ExternalOutput").ap()
        dbg_mix = nc.dram_tensor("dbg_mix", [128, 16 * S], BF16, kind="ExternalOutput").ap()
        dbg_h1 = nc.dram_tensor("dbg_h1", [128, 8 * D], F32, kind="ExternalOutput").ap()

    w_in_v = w_in.rearrange("(k p) n -> p k n", p=128)
    w_out_v = w_out.rearrange("(k p) n -> p k n", p=128)
    w_gate_v = w_gate.rearrange("(k p) n -> p k n", p=128)
    w_up_v = w_up.rearrange("(k p) n -> p k n", p=128)

    P = Prog()
    with contextlib.ExitStack() as es:
        def sb(name, shape, dt):
            return es.enter_context(nc.sbuf_tensor(name, shape, dt))

        nsem = [0]

        def new_ev(step=1):
            nsem[0] += 1
            return Ev(es.enter_context(nc.semaphore(f"s{nsem[0]}")), step)

        arena = sb("arena", [128, ARENA], BF16)
        ident = sb("ident", [128, 128], BF16)
        tri = sb("tri", [128, 128], BF16)
        ones = sb("ones", [128, 128], BF16)
        stat = sb("stat", [128, 4 * 17 + 4 * 16 + 4 * 16], F32)
        lamt = sb("lamt", [128, 4 * 64 + 64 + 16], F32)
        cw = sb("cw", [128, 3, 8], F32)
        hg = sb("hg", [128, 4], F32)

        pss = [es.enter_context(nc.psum_tensor(f"ps{i}", [128, 512], F32)) for i in range(8)]
        banks = [Bank(pss[i], new_ev()) for i in range(8)]

        def tview(bank):
            return bank.ps[:, :].bitcast(BF16).rearrange("p (j n) -> p j n", j=8)

        def av(off, n):
            return arena[:, off:off + n]

        aT = av(OFF_A, SZ_A).rearrange("p (k n) -> p k n", k=16)
        h1 = av(OFF_A, 8 * D * 2).bitcast(F32).rearrange("p (s n) -> p s n", s=8)
        mixT = av(OFF_M, SZ_M).rearrange("p (k n) -> p k n", k=16)
        ring_slots = [av(OFF_R + i * 2048, 2048) for i in range(RING)]
        ring_ready = [new_ev(16) for _ in range(RING)]
        ring_free = [new_ev(1) for _ in range(RING)]
        ring_last = [None] * RING
        ring_n = [0]

        def xv(off_bytes, nbytes, dt=BF16):
            a = av(OFF_X + off_bytes // 2, nbytes // 2)
            return a.bitcast(F32) if dt == F32 else a

        def mv(off_bytes, nbytes, dt=BF16):
            a = av(OFF_M + off_bytes // 2, nbytes // 2)
            return a.bitcast(F32) if dt == F32 else a

        xs = [mv(i * 8192, 8192, F32) for i in range(4)]
        at = [mv(32768, 4096), mv(36864, 4096)]
        g1bc = mv(40960, 8192, F32)
        junkA = mv(49152, 4096)
        QK = 4160
        qA = xv(0, 4128)
        qB = xv(QK, 4128)
        kA = xv(2 * QK, 4128)
        kB = xv(3 * QK, 4128)
        vF = xv(4 * QK, 4128)
        vh = xv(5 * QK, 4352).rearrange("p (b n) -> p b n", b=17)
        o_pt = 5 * QK + 4352
        Pt = [xv(o_pt + i * 1024, 1024) for i in range(4)]
        o_fin = o_pt + 4096
        oc = [xv(o_fin, 2048, F32), xv(o_fin + 2048, 2048, F32)]
        lc = [xv(o_fin + 4096, 2048, F32), xv(o_fin + 6144, 2048, F32)]
        sqb = xv(o_fin + 8192, 1024)
        o_cv = o_fin + 8192 + 1024
        cgS = [xv(o_cv, 2048, F32), xv(o_cv + 2048, 2048, F32)]
        ubuf = xv(o_cv + 4096, 2064, F32)
        ybuf = [xv(o_cv + 6160, 2048, F32), xv(o_cv + 8208, 2048, F32)]
        assert o_cv + 10256 <= 49152
        xr = [xv(0, 2048, F32), xv(2048, 2048, F32)]
        ft = [xv(4096, 4096), xv(8192, 4096), xv(40960, 4096)]
        ost = [xv(4096, 8192, F32), xv(40960, 8192, F32)]
        g2bc = xv(12288, 8192, F32)
        gfbc = xv(20480, 8192, F32)
        sg = [xv(28672, 2048, F32), xv(30720, 2048, F32)]
        actT = [[xv(32768 + (g * 2 + c) * 2048, 2048) for c in range(2)] for g in range(2)]
        junkC = xv(32768, 4096)

        ssA, vvA, lnA, rsA = (stat[:, i * 17:(i + 1) * 17] for i in range(4))
        o2 = 68
        ssC, vvC, lnC, rsC = (stat[:, o2 + i * 16:o2 + (i + 1) * 16] for i in range(4))
        o3 = o2 + 64
        ssF, vvF, lnF, rsF = (stat[:, o3 + i * 16:o3 + (i + 1) * 16] for i in range(4))

        ev_sp = new_ev(16)
        ev_spx = [new_ev(16) for _ in range(4)]
        ev_act = new_ev(1)
        ev_dve = new_ev(1)
        ev_pool = new_ev(16)
        ev_out = new_ev(16)
        ev_rel = new_ev(1)
        ev_poolc = new_ev(1)
        ev_xh = [new_ev(16) for _ in range(8)]
        E_tok = {}

        class Slot:
            pass

        def ring_load(src_ap, shape3=None):
            i = ring_n[0]
            ring_n[0] += 1
            si = i % RING
            slot = ring_slots[si]
            dst = slot if shape3 is None else slot.rearrange("p (k n) -> p k n", k=shape3)
            waits = []
            if i >= RING:
                assert ring_last[si] is not None, "ring slot reused before its last reader was emitted"
                waits = [ring_last[si]]
                ring_last[si] = None
            tok = P.op("pool", lambda e, d=dst, s=src_ap: e.dma_start(out=d, in_=s), waits, ev=ring_ready[si])
            sl = Slot()
            sl.ap, sl.tok, sl.si = dst, tok, si
            return sl

        def ring_release(sl, tok):
            ring_last[sl.si] = tok

        t_c = []
        t_c.append(P.op("pool", lambda e: e.dma_start(out=ident[:, :], in_=c_ident), ev=ev_pool))
        t_c.append(P.op("pool", lambda e: e.dma_start(out=tri[:, :], in_=c_tri), ev=ev_pool))
        t_ones = P.op("dve", lambda e: e.memset(ones[:, :], 1.0), ev=ev_dve)
        epsc = hg[:, 2:3]
        t_eps = P.op("dve", lambda e: e.memset(hg[:, 2:3], EPS), ev=ev_dve)
        t_g1 = P.op("sp", lambda e: e.dma_start(out=g1bc, in_=g1.broadcast_to([128, D])), ev=ev_sp)

        tb_rr = [0]

        def norm_a(src, Pn, col, ss, vv, junk, src_tok):
            t1 = P.op("act", lambda e: e.activation(out=junk[0:Pn, :], in_=src, func=AF.Square, accum_out=ss[0:Pn, col:col + 1]),
                      waits=[src_tok], ev=ev_act)
            return t1

        def norm_b(Pn, col, vv, ln, rs, t1):
            t2b = P.op("act", lambda e: e.activation(out=ln[0:Pn, col:col + 1], in_=vv[0:Pn, col:col + 1], func=AF.Ln, bias=epsc[0:Pn, 0:1], scale=1.0 / D),
                       waits=[t1, t_eps], ev=ev_act)
            t3 = P.op("act", lambda e: e.activation(out=rs[0:Pn, col:col + 1], in_=ln[0:Pn, col:col + 1], func=AF.Exp, scale=-0.5), waits=[t2b], ev=ev_act)
            return t3

        def norm_chain(src, Pn, col, ss, vv, ln, rs, junk, src_tok):
            return norm_b(Pn, col, ss, ln, rs, norm_a(src, Pn, col, ss, vv, junk, src_tok))

        def transposes(a_tile, Pn, dst, c0, tbanks, a_tok):
            evs = []
            pe_tok = None
            for hf in range(2):
                bank = tbanks[tb_rr[0] % len(tbanks)]
                tb_rr[0] += 1
                tv = tview(bank)
                fr = bank.acquire()
                for j in range(8):
                    k = hf * 8 + j
                    last = (j == 7)
                    pe_tok = P.op("pe", lambda e, tv=tv, j=j, k=k: e.transpose(out=tv[:, j, 0:Pn], in_=a_tile[0:Pn, k * 128:(k + 1) * 128],
                                                                             identity=ident[0:Pn, 0:Pn]),
                                  waits=[a_tok, fr, t_c] if j == 0 else [], ev=bank.ev if last else None)
                eng = "act" if hf == 0 else "dve"
                if eng == "act":
                    tk = P.op("act", lambda e, tv=tv, hf=hf: e.copy(out=dst[:, hf * 8:(hf + 1) * 8, c0:c0 + Pn], in_=tv[:, :, 0:Pn]),
                              waits=[pe_tok], ev=ev_act)
                else:
                    tk = P.op("dve", lambda e, tv=tv, hf=hf: e.tensor_copy(out=dst[:, hf * 8:(hf + 1) * 8, c0:c0 + Pn], in_=tv[:, :, 0:Pn]),
                              waits=[pe_tok], ev=ev_dve)
                bank.free.append(tk)
                evs.append(tk)
            return pe_tok, evs

        mm_rr = [0]
        TOK_TILES = [(n * 512, 512) for n in range(4)] + [(S, NM)]
        mmB = banks[0:2]
        LB = banks[2:4]
        SB = banks[4:6]
        OB = banks[6:8]

        def acc16(bank, N, lhs_fn, rhs_fn, first_waits, last_ev):
            fr = bank.acquire()
            tok = None
            for k in range(16):
                l_ap = lhs_fn(k)
                r_ap = rhs_fn(k)
                tok = P.op("pe", lambda e, l_ap=l_ap, r_ap=r_ap, k=k: e.matmul(bank.ps[:, 0:N], lhsT=l_ap, rhs=r_ap, start=(k == 0), stop=(k == 15)),
                           waits=[fr, first_waits] if k == 0 else [], ev=(last_ev if k == 15 else None))
            return tok

        def proj_chunk(col0, tiles, consumer, first_waits):
            sl = ring_load(w_in_v[:, :, col0:col0 + 128], 16)
            for ti, (c0, N) in enumerate(tiles):
                bank = mmB[mm_rr[0] % 2]
                mm_rr[0] += 1
                tok = acc16(bank, N, lambda k: sl.ap[:, k, :], lambda k, c0=c0, N=N: aT[:, k, c0:c0 + N],
                            [sl.tok] + list(first_waits), bank.ev)
                consumer(ti, c0, N, bank, tok)
            ring_release(sl, tok)

        def evac_split(dstA, dstB):
            def cons(ti, c0, N, bank, tok):
                t1 = P.op("act", lambda e: e.copy(out=dstA[0:64, c0:c0 + N], in_=bank.ps[0:64, 0:N]), waits=[tok], ev=ev_act)
                t2 = P.op("dve", lambda e: e.tensor_copy(out=dstB[0:64, c0:c0 + N], in_=bank.ps[64:128, 0:N]), waits=[tok], ev=ev_dve)
                bank.free += [t1, t2]
            return cons

        vrr = [0]

        def evac_v(ti, c0, N, bank, tok):
            vrr[0] += 1
            if vrr[0] % 2:
                t1 = P.op("act", lambda e: e.copy(out=vF[:, c0:c0 + N], in_=bank.ps[:, 0:N]), waits=[tok], ev=ev_act)
            else:
                t1 = P.op("dve", lambda e: e.tensor_copy(out=vF[:, c0:c0 + N], in_=bank.ps[:, 0:N]), waits=[tok], ev=ev_dve)
            bank.free.append(t1)


        def proj_chunk_gen(col0, tiles, consumer, waits_box):
            sl = ring_load(w_in_v[:, :, col0:col0 + 128], 16)
            tok = None
            for ti, (c0, N) in enumerate(tiles):
                bank = mmB[mm_rr[0] % 2]
                mm_rr[0] += 1
                tok = acc16(bank, N, lambda k: sl.ap[:, k, :], lambda k, c0=c0, N=N: aT[:, k, c0:c0 + N],
                            [sl.tok] + list(waits_box[0]), bank.ev)
                consumer(ti, c0, N, bank, tok)
                if ti == len(tiles) - 1:
                    ring_release(sl, tok)
                yield

        a_done = {}
        t_done = {}

        vvtok = {}
        ld_tok = {}

        def stageA1a(i):
            b = i % 4
            Pn = 128 if i < 16 else NM
            src_rows = x[i * 128:(i + 1) * 128, :] if i < 16 else meta
            w = [a_done[i - 4]] if i >= 4 else []
            tld = P.op("sp", lambda e: e.dma_start(out=xs[b][0:Pn, :], in_=src_rows), waits=w, ev=ev_spx[b])
            ld_tok[i] = tld

        def stageA1s(i):
            b = i % 4
            Pn = 128 if i < 16 else NM
            vvtok[i] = norm_a(xs[b][0:Pn, :], Pn, i, ssA, vvA, junkA, ld_tok[i])

        def stageA1b(i):
            b = i % 4
            ab = i % 2
            Pn = 128 if i < 16 else NM
            trs = norm_b(Pn, i, ssA, lnA, rsA, vvtok[i])
            w = [trs, t_g1] + ([t_done[i - 2]] if i >= 2 else [])
            a_done[i] = P.op("dve", lambda e: e.scalar_tensor_tensor(out=at[ab][0:Pn, :], in0=xs[b][0:Pn, :], scalar=rsA[0:Pn, i:i + 1],
                                                                      in1=g1bc[0:Pn, :], op0=ALU.mult, op1=ALU.mult), waits=w, ev=ev_dve)

        def stageA2(i):
            Pn = 128 if i < 16 else NM
            c0 = i * 128 if i < 16 else S
            pt, evs = transposes(at[i % 2], Pn, aT, c0, banks[2:6], a_done[i])
            t_done[i] = pt

        wbox = [[]]
        g_q0 = proj_chunk_gen(0, TOK_TILES[0:4], evac_split(qA, qB), wbox)
        g_k0 = proj_chunk_gen(1024, TOK_TILES, evac_split(kA, kB), wbox)
        g_v0 = proj_chunk_gen(2048, TOK_TILES, evac_v, wbox)
        grp_tok = {}

        pend0 = []

        def step_one():
            if pend0:
                g, n = pend0.pop(0)
                wbox[0] = grp_tok[n]
                next(g)

        for i in range(3):
            stageA1a(i)
        stageA1s(0)
        stageA1b(0)
        for i in range(1, 17):
            if i + 2 < 17:
                stageA1a(i + 2)
            stageA1s(i)
            stageA1b(i)
            stageA2(i - 1)
            if (i - 1) % 4 == 3:
                n = (i - 1) // 4
                grp_tok[n] = [P.last["act"], P.last["dve"]]
                pend0.extend([(g_q0, n), (g_k0, n), (g_v0, n)])
            else:
                step_one()
        stageA2(16)
        grp_tok[4] = [P.last["act"], P.last["dve"]]
        pend0.extend([(g_k0, 4), (g_v0, 4)])
        while pend0:
            step_one()
        for g in (g_q0, g_k0, g_v0):
            for _ in g:
                pass
        lqv = lamt[:, 0:256].rearrange("p (a n) -> p a n", a=4)
        for i, src in enumerate((lq1, lk1, lq2, lk2)):
            tl = P.op("sp", lambda e, i=i, src=src: e.dma_start(out=lqv[:, i, :], in_=src.broadcast_to([128, 64])), ev=ev_sp)
        t_hg = P.op("sp", lambda e: e.dma_start(out=hg[:, 0:1], in_=head_g.rearrange("o d -> d o")), ev=ev_sp)
        for t in range(3):
            t_cw = P.op("sp", lambda e, t=t: e.dma_start(out=cw[:, t, :], in_=conv_w[t:t + 1, :].rearrange("o (j p) -> p (o j)", p=128),
                                                       allow_slow_non_contiguous=True), ev=ev_sp)
        ljunk = lamt[:, 256:320]
        lsc = lamt[:, 320:336]
        P.op("dve", lambda e: e.scalar_tensor_tensor(out=ljunk, in0=lqv[:, 0, :], scalar=1.0, in1=lqv[:, 1, :], op0=ALU.mult,
                                                     op1=ALU.mult, accum_out=lsc[:, 0:1]), waits=[t_cw], ev=ev_dve)
        t_l = P.op("dve", lambda e: e.scalar_tensor_tensor(out=ljunk, in0=lqv[:, 2, :], scalar=1.0, in1=lqv[:, 3, :], op0=ALU.mult,
                                                           op1=ALU.mult, accum_out=lsc[:, 1:2]), ev=ev_dve)
        t_le = P.op("act", lambda e: e.activation(out=lsc[:, 2:4], in_=lsc[:, 0:2], func=AF.Exp), waits=[t_l], ev=ev_act)
        t_l2 = P.op("dve", lambda e: e.tensor_tensor(out=lsc[:, 4:5], in0=lsc[:, 3:4], in1=lsc[:, 2:3], op=ALU.subtract), waits=[t_le], ev=ev_dve)
        t_l3 = P.op("dve", lambda e: e.tensor_scalar(out=lsc[:, 5:6], in0=lsc[:, 4:5], scalar1=-LAMBDA_INIT, scalar2=None, op0=ALU.add), waits=[t_l2], ev=ev_dve)
        nlam = lsc[:, 5:6]
        t_hgs = P.op("dve", lambda e: e.tensor_scalar(out=hg[:, 1:2], in0=hg[:, 0:1], scalar1=1.0 - LAMBDA_INIT, scalar2=None, op0=ALU.mult), ev=ev_dve)
        hgs = hg[:, 1:2]
        tokA = [P.last["act"], P.last["dve"], P.last["pe"]]
        if debug:
            P.op("sp", lambda e: e.dma_start(out=dbg_aT, in_=av(OFF_A, SZ_A)), waits=tokA, ev=ev_out)

        cg_rd = {}
        u_tok = [None]

        def conv_units(j, slots):
            s_cg, s_hi, s_bg = slots
            order = [4, 0, 1, 2, 3]
            for oi, ti in enumerate(order):
                c0, N = TOK_TILES[ti]
                cb = oi % 2
                bank = mmB[mm_rr[0] % 2]
                mm_rr[0] += 1
                tok = acc16(bank, N, lambda k: s_cg.ap[:, k, :], lambda k: aT[:, k, c0:c0 + N], [s_cg.tok], bank.ev)
                if oi == 4:
                    ring_release(s_cg, tok)
                d_ap, s_ap = cgS[cb][:, 0:N], bank.ps[:, 0:N]
                t1 = P.op("act", lambda e, d_ap=d_ap, s_ap=s_ap: e.copy(out=d_ap, in_=s_ap), waits=[tok, cg_rd.get(cb)], ev=ev_act)
                bank.free.append(t1)
                yield
                bank = mmB[mm_rr[0] % 2]
                mm_rr[0] += 1
                tok = acc16(bank, N, lambda k: s_hi.ap[:, k, :], lambda k: aT[:, k, c0:c0 + N], [s_hi.tok], bank.ev)
                if oi == 4:
                    ring_release(s_hi, tok)
                if ti == 4:
                    a_ap, b_ap = cgS[cb][:, NM - 2:NM], bank.ps[:, NM - 2:NM]
                    t2 = P.op("dve", lambda e, a_ap=a_ap, b_ap=b_ap: e.tensor_tensor(out=ubuf[:, 0:2], in0=a_ap, in1=b_ap, op=ALU.mult),
                              waits=[t1, tok, u_tok[0]], ev=ev_dve)
                    bank.free.append(t2)
                    cg_rd[cb] = t2
                    u_tok[0] = t2
                    yield
                    continue
                a_ap, b_ap = cgS[cb][:, :], bank.ps[:, :]
                t2 = P.op("dve", lambda e, a_ap=a_ap, b_ap=b_ap: e.tensor_tensor(out=ubuf[:, 2:514], in0=a_ap, in1=b_ap, op=ALU.mult),
                          waits=[t1, tok, u_tok[0]], ev=ev_dve)
                bank.free.append(t2)
                cg_rd[cb] = t2
                yield
                bank = mmB[mm_rr[0] % 2]
                mm_rr[0] += 1
                tok = acc16(bank, N, lambda k: s_bg.ap[:, k, :], lambda k: aT[:, k, c0:c0 + N], [s_bg.tok], bank.ev)
                if oi == 4:
                    ring_release(s_bg, tok)
                yb = ybuf[cb]
                n = ti
                w0, w1, w2 = cw[:, 0, j:j + 1], cw[:, 1, j:j + 1], cw[:, 2, j:j + 1]
                y1 = P.op("dve", lambda e, yb=yb, w0=w0: e.tensor_scalar(out=yb, in0=ubuf[:, 0:512], scalar1=w0, scalar2=None, op0=ALU.mult),
                          waits=[t2, t_cw], ev=ev_dve)
                y2 = P.op("dve", lambda e, yb=yb, w1=w1: e.scalar_tensor_tensor(out=yb, in0=ubuf[:, 1:513], scalar=w1, in1=yb, op0=ALU.mult, op1=ALU.add),
                          waits=[y1], ev=ev_dve)
                y3 = P.op("dve", lambda e, yb=yb, w2=w2: e.scalar_tensor_tensor(out=yb, in0=ubuf[:, 2:514], scalar=w2, in1=yb, op0=ALU.mult, op1=ALU.add),
                          waits=[y2], ev=ev_dve)
                o_ap, b_ap = mixT[:, 8 + j, n * 512:(n + 1) * 512], bank.ps[:, :]
                t3 = P.op("dve", lambda e, yb=yb, o_ap=o_ap, b_ap=b_ap: e.tensor_tensor(out=o_ap, in0=yb, in1=b_ap, op=ALU.mult),
                          waits=[tok, y3], ev=ev_dve)
                bank.free.append(t3)
                u_tok[0] = P.op("dve", lambda e: e.tensor_copy(out=ubuf[:, 0:2], in_=ubuf[:, 512:514]), waits=[y3], ev=ev_dve)
                yield

        s_rr = [0]
        pt_rr = [0]
        pt_free = [None] * 4
        prev_head_pe = [None]
        fin_tok = [None]
        t_qa = None
        deferred = []

        pull_fn = [None]

        def tick(force=False, tag=None):
            for item in list(deferred):
                item[0] -= 1
                if (force and (tag is None or item[2] == tag)) or item[0] <= 0:
                    deferred.remove(item)
                    if not force and pull_fn[0] is not None and item[2] in ("b", "d"):
                        pull_fn[0]()
                    item[1]()

        for h in range(NH):
            fw = tokA if h == 0 else []
            if h > 0:
                proj_chunk(h * 128, TOK_TILES[0:4], evac_split(qA, qB), fw)
                tick(force=True)
                proj_chunk(2048 + h * 128, TOK_TILES, evac_v, [])
                proj_chunk(1024 + h * 128, TOK_TILES, evac_split(kA, kB), [])
            t_v = [P.last["act"], P.last["dve"]]
            tv_toks = []
            for g0 in range(0, 17, 8):
                bank = mmB[mm_rr[0] % 2]
                mm_rr[0] += 1
                tv = tview(bank)
                fr = bank.acquire()
                nb = min(8, 17 - g0)
                tok = None
                for j in range(nb):
                    blk = g0 + j
                    Pn = 128 if blk < 16 else NM
                    tok = P.op("pe", lambda e, tv=tv, j=j, blk=blk, Pn=Pn: e.transpose(out=tv[0:Pn, j, :], in_=vF[:, blk * 128:blk * 128 + Pn],
                                                                                     identity=ident[:, :]),
                               waits=[t_v, fr] if j == 0 else [], ev=bank.ev if j == nb - 1 else None)
                if nb == 8:
                    tk = P.op("dve", lambda e, tv=tv, g0=g0: e.tensor_copy(out=vh[:, g0:g0 + 8, :], in_=tv[:, :, :]), waits=[tok], ev=ev_dve)
                else:
                    tk = P.op("dve", lambda e, tv=tv: e.tensor_copy(out=vh[0:NM, 16, :], in_=tv[0:NM, 0, :]), waits=[tok], ev=ev_dve)
                bank.free.append(tk)
                tv_toks.append(tk)
            t_qk = [P.last["act"], P.last["dve"]]
            conv_slots = [ring_load(w_in_v[:, :, 4096 + h * 128:4096 + (h + 1) * 128], 16),
                          ring_load(w_in_v[:, :, 5120 + h * 128:5120 + (h + 1) * 128], 16),
                          ring_load(w_in_v[:, :, 3072 + h * 128:3072 + (h + 1) * 128], 16)]
            if h == 0:
                P.op("pool", lambda e: e.dma_start(out=qA[64:68, 0:S], in_=c_qaug), ev=ev_pool)
                t_qa = P.op("pool", lambda e: e.dma_start(out=qB[64:68, 0:S], in_=c_qaug), ev=ev_pool)
            P.op("pool", lambda e, h=h: e.dma_start(out=kA[64:68, 0:T], in_=c_kaug[h]), waits=[prev_head_pe[0]], ev=ev_pool)
            t_ka = P.op("pool", lambda e, h=h: e.dma_start(out=kB[64:68, 0:T], in_=c_kaug[h]), ev=ev_pool)
            cgen = conv_units(h, conv_slots)
            tile_ctr = [0]
            prev_head_pe_tok = P.last["pe"]

            def pull():
                try:
                    next(cgen)
                except StopIteration:
                    pass

            pull_fn[0] = pull
            pull()
            pull()
            for Q in range(4):
                blocks = [("m", S, NM, 16)] + [(j, j * 128, 128, j) for j in range(4 * Q + 4)]
                tiles = []
                for (bid, kc0, nk, vb) in blocks:
                    qlo = 0
                    diag = False
                    if bid != "m" and bid >= 4 * Q:
                        qlo = (bid - 4 * Q) * 128
                        diag = True
                    for m in range(2):
                        tiles.append((bid, kc0, nk, vb, qlo, diag, m))
                nt = len(tiles)
                pend = []
                fro = [OB[0].acquire(), OB[1].acquire()]
                frl = [LB[0].acquire(), LB[1].acquire()]
                last_pv = [None, None]
                last_l = [None, None]

                def issue_S(idx):
                    (bid, kc0, nk, vb, qlo, diag, m) = tiles[idx]
                    kM = kA if m == 0 else kB
                    qM = qA if m == 0 else qB
                    sbk = SB[s_rr[0] % 2]
                    s_rr[0] += 1
                    pi = pt_rr[0] % 4
                    pt_rr[0] += 1
                    fr = sbk.acquire()
                    rhs_ap = qM[0:68, Q * 512 + qlo:(Q + 1) * 512]
                    ts = P.op("pe", lambda e: e.matmul(sbk.ps[0:nk, qlo:512], lhsT=kM[0:68, kc0:kc0 + nk], rhs=rhs_ap, start=True, stop=True),
                              waits=[fr, t_qk, t_ka, t_qa, tv_toks], ev=sbk.ev)
                    te = P.op("act", lambda e: e.activation(out=Pt[pi][0:nk, qlo:512], in_=sbk.ps[0:nk, qlo:512], func=AF.Exp, scale=0.125),
                              waits=[ts, pt_free[pi]], ev=ev_act)
                    sbk.free.append(te)
                    rdy = te
                    if diag:
                        rdy = P.op("dve", lambda e: e.tensor_tensor(out=Pt[pi][0:nk, qlo:qlo + 128], in0=Pt[pi][0:nk, qlo:qlo + 128], in1=tri[:, :],
                                                                    op=ALU.mult), waits=[te], ev=ev_dve)
                    pend.append((idx, pi, rdy))

                def issue_PV(idx, pi, rdy):
                    (bid, kc0, nk, vb, qlo, diag, m) = tiles[idx]
                    first = (idx < 2)
                    lastm = (idx >= nt - 2)
                    vsrc = vh[0:nk, vb, :]
                    t1 = P.op("pe", lambda e: e.matmul(OB[m].ps[:, qlo:512], lhsT=vsrc, rhs=Pt[pi][0:nk, qlo:512], start=first, stop=lastm),
                              waits=[rdy, fro[m] if first else None], ev=OB[m].ev if lastm else None)
                    t2 = P.op("pe", lambda e: e.matmul(LB[m].ps[:, qlo:512], lhsT=ones[0:nk, :], rhs=Pt[pi][0:nk, qlo:512], start=first, stop=lastm),
                              waits=[frl[m] if first else None, t_ones], ev=LB[m].ev)
                    pt_free[pi] = t2
                    if lastm:
                        last_pv[m] = t1
                        last_l[m] = t2

                for idx in range(nt):
                    issue_S(idx)
                    if idx == 1 and Q > 0:
                        pull()
                    if len(pend) > 2:
                        issue_PV(*pend.pop(0))
                        tick()
                        tile_ctr[0] += 1
                        if tile_ctr[0] % 12 == 6:
                            pull()
                while pend:
                    issue_PV(*pend.pop(0))

                t_lnl = []
                for m in range(2):
                    tk = P.op("act", lambda e, m=m: e.activation(out=lc[m], in_=LB[m].ps[:, :], func=AF.Ln), waits=[last_l[m], fin_tok[0]], ev=ev_act)
                    LB[m].free.append(tk)
                    t_lnl.append(tk)
                t_ocs = []
                for m in range(2):
                    tk = P.op("dve", lambda e, m=m: e.tensor_copy(out=oc[m], in_=OB[m].ps[:, :]), waits=[last_pv[m], last_l[m], fin_tok[0]], ev=ev_dve)
                    OB[m].free.append(tk)
                    t_ocs.append(tk)
                st = {}

                def fin1b(t_lnl=t_lnl, t_ocs=t_ocs, st=st):
                    t_rl = []
                    for m in range(2):
                        t_rl.append(P.op("act", lambda e, m=m: e.activation(out=lc[m], in_=lc[m], func=AF.Exp, scale=-1.0), waits=[t_lnl[m]], ev=ev_act))
                    c3 = P.op("dve", lambda e: e.tensor_tensor(out=oc[0], in0=oc[0], in1=lc[0], op=ALU.mult), waits=[t_ocs[0], t_rl[0]], ev=ev_dve)
                    c4 = P.op("dve", lambda e: e.tensor_tensor(out=oc[1], in0=oc[1], in1=lc[1], op=ALU.mult), waits=[t_ocs[1], t_rl[1]], ev=ev_dve)
                    c5 = P.op("dve", lambda e: e.scalar_tensor_tensor(out=oc[0], in0=oc[1], scalar=nlam, in1=oc[0], op0=ALU.mult, op1=ALU.add),
                              waits=[c3, c4, t_l3, t_hgs], ev=ev_dve)
                    st["t_sq"] = P.op("dve", lambda e: e.tensor_tensor(out=sqb, in0=oc[0], in1=oc[0], op=ALU.mult), waits=[c5], ev=ev_dve)
                deferred.append([3, fin1b, "b"])

                def fin2a(st=st):
                    t_sq = st["t_sq"]
                    bank = mmB[mm_rr[0] % 2]
                    mm_rr[0] += 1
                    fr = bank.acquire()
                    t_ms = P.op("pe", lambda e: e.matmul(bank.ps[:, :], lhsT=ones[:, :], rhs=sqb, start=True, stop=True), waits=[t_sq, fr], ev=bank.ev)
                    t_vv = P.op("dve", lambda e: e.tensor_scalar(out=lc[0], in0=bank.ps[:, :], scalar1=1.0 / 128, scalar2=HEPS, op0=ALU.mult, op1=ALU.add),
                                waits=[t_ms], ev=ev_dve)
                    bank.free.append(t_vv)
                    st["t_vv"] = t_vv

                def fin2b(h=h, Q=Q, st=st):
                    t_vv = st["t_vv"]
                    t_ln = P.op("act", lambda e: e.activation(out=lc[0], in_=lc[0], func=AF.Ln), waits=[t_vv], ev=ev_act)
                    t_r = P.op("act", lambda e: e.activation(out=lc[0], in_=lc[0], func=AF.Exp, scale=-0.5), waits=[t_ln], ev=ev_act)
                    o_ap = mixT[:, h, Q * 512:(Q + 1) * 512]
                    fin_tok[0] = P.op("dve", lambda e: e.scalar_tensor_tensor(out=o_ap, in0=oc[0], scalar=hgs, in1=lc[0], op0=ALU.mult, op1=ALU.mult),
                                      waits=[t_r], ev=ev_dve)
                deferred.append([10, fin2a, "c"])
                deferred.append([13, fin2b, "d"])
            prev_head_pe[0] = P.last["pe"]
            pull_fn[0] = None
            for _ in cgen:
                pass
            tick(force=True, tag="b")

        tick(force=True)
        tokB = [P.last["act"], P.last["dve"], P.last["pe"]]
        if debug:
            P.op("sp", lambda e: e.dma_start(out=dbg_mix, in_=av(OFF_M, SZ_M)), waits=tokB, ev=ev_out)

        t_g2 = P.op("sp", lambda e: e.dma_start(out=g2bc, in_=g2.broadcast_to([128, D])), waits=tokB, ev=ev_sp)
        t_gf = P.op("sp", lambda e: e.dma_start(out=gfbc, in_=gf.broadcast_to([128, D])), ev=ev_sp)
        xr_free = [None, None]
        xr_n = [0]
        ft_free = [None, None, None]
        ost_free = [None, None]
        act_free = [[None, None], [None, None]]
        gu_b = banks[0:4]
        dn_b = banks[4:8]
        tb2 = banks[4:6]
        dn_rr = [0]
        out_toks = []
        NG = NFF // 2
        for hf in range(2):
            def norm2_dve(s):
                gs = hf * 8 + s
                trs = norm_chain(h1[:, s, :], 128, gs, ssC, vvC, lnC, rsC, junkC, P.last["dve"])
                b = s % 3
                ta = P.op("dve", lambda e: e.scalar_tensor_tensor(out=ft[b], in0=h1[:, s, :], scalar=rsC[:, gs:gs + 1], in1=g2bc,
                                                                  op0=ALU.mult, op1=ALU.mult), waits=[trs, t_g2, ft_free[b]], ev=ev_dve)
                return ta

            ft_ready = {}
            xq = []
            for dq in range(4):
                w = tokB if hf == 0 else [E_tok[s] for s in range(8)]
                src_ap = x[hf * 1024:(hf + 1) * 1024, dq * 512:(dq + 1) * 512].rearrange("(s p) c -> p s c", p=128)
                xq.append(P.op("sp", lambda e, dq=dq, src_ap=src_ap: e.dma_start(out=h1[:, :, dq * 512:(dq + 1) * 512], in_=src_ap), waits=w, ev=ev_xh[dq]))
            for dq in range(3):
                frs = [banks[s].acquire() for s in range(8)]
                toks = [None] * 8
                for kg in range(4):
                    sl = ring_load(w_out_v[:, kg * 4:(kg + 1) * 4, dq * 512:(dq + 1) * 512], 4)
                    tok = None
                    for s in range(8):
                        gs = hf * 8 + s
                        for kk in range(4):
                            k = kg * 4 + kk
                            l_ap = mixT[:, k, gs * 128:(gs + 1) * 128]
                            r_ap = sl.ap[:, kk, :]
                            bank = banks[s]
                            last = (k == 15)
                            rel = (s == 7 and kk == 3)
                            ev = bank.ev if last else (ev_rel if rel else None)
                            tok = P.op("pe", lambda e, bank=bank, l_ap=l_ap, r_ap=r_ap, k=k: e.matmul(bank.ps[:, :], lhsT=l_ap, rhs=r_ap, start=(k == 0), stop=(k == 15)),
                                       waits=[sl.tok, tokB, frs[s] if k == 0 else None], ev=ev)
                        if kg == 3:
                            toks[s] = tok
                    ring_release(sl, tok)
                for s in range(8):
                    dst = h1[:, s, dq * 512:(dq + 1) * 512]
                    td = P.op("dve", lambda e, s=s, dst=dst: e.tensor_tensor(out=dst, in0=banks[s].ps[:, :], in1=dst, op=ALU.add), waits=[toks[s], xq[dq]], ev=ev_dve)
                    banks[s].free.append(td)
            for dq in range(3, 4):
                slots = [ring_load(w_out_v[:, kg * 4:(kg + 1) * 4, dq * 512:(dq + 1) * 512], 4) for kg in range(4)]
                for s in range(8):
                    gs = hf * 8 + s
                    bank = gu_b[mm_rr[0] % 4]
                    mm_rr[0] += 1
                    tok = acc16(bank, 512, lambda k, gs=gs: mixT[:, k, gs * 128:(gs + 1) * 128], lambda k: slots[k // 4].ap[:, k % 4, :],
                                [sl.tok for sl in slots] + tokB, bank.ev)
                    dst = h1[:, s, dq * 512:(dq + 1) * 512]
                    td = P.op("dve", lambda e, bank=bank, dst=dst: e.tensor_tensor(out=dst, in0=bank.ps[:, :], in1=dst, op=ALU.add), waits=[tok, xq[dq]], ev=ev_dve)
                    bank.free.append(td)
                    ft_ready[s] = norm2_dve(s)
                    if s >= 2:
                        pt, evs = transposes(ft[(s - 2) % 3], 128, mixT, (hf * 8 + s - 2) * 128, tb2, ft_ready[s - 2])
                        ft_free[(s - 2) % 3] = pt
                for sl in slots:
                    ring_release(sl, tok)
            for s2 in (6, 7):
                pt, evs = transposes(ft[s2 % 3], 128, mixT, (hf * 8 + s2) * 128, tb2, ft_ready[s2])
                ft_free[s2 % 3] = pt
            tokC = [P.last["act"], P.last["dve"]]

            sg_rr = [0]
            sg_free = [None, None]
            act_rdy = {}

            def GU(g):
                for ci in range(2):
                    c = 2 * g + ci
                    sl_g = ring_load(w_gate_v[:, :, c * 128:(c + 1) * 128], 16)
                    sl_u = ring_load(w_up_v[:, :, c * 128:(c + 1) * 128], 16)
                    for n in range(2):
                        c0 = hf * 1024 + n * 512
                        res = []
                        for wi, sl in enumerate((sl_g, sl_u)):
                            bank = gu_b[mm_rr[0] % 4]
                            mm_rr[0] += 1
                            tok = acc16(bank, 512, lambda k, sl=sl: sl.ap[:, k, :], lambda k: mixT[:, k, c0:c0 + 512], [sl.tok] + tokC, bank.ev)
                            if n == 1:
                                ring_release(sl, tok)
                            res.append((bank, tok))
                            if wi == 0:
                                yield
                        sb_i = sg_rr[0] % 2
                        sg_rr[0] += 1
                        (bg_, tg_), (bu_, tu_) = res
                        t1 = P.op("act", lambda e, bg_=bg_, sb_i=sb_i: e.activation(out=sg[sb_i], in_=bg_.ps[:, :], func=AF.Silu),
                                  waits=[tg_, sg_free[sb_i]], ev=ev_act)
                        bg_.free.append(t1)
                        dst = actT[g % 2][ci][:, n * 512:(n + 1) * 512]
                        t2 = P.op("dve", lambda e, bu_=bu_, sb_i=sb_i, dst=dst: e.tensor_tensor(out=dst, in0=sg[sb_i], in1=bu_.ps[:, :], op=ALU.mult),
                                  waits=[t1, tu_, act_free[g % 2][ci]], ev=ev_dve)
                        bu_.free.append(t2)
                        sg_free[sb_i] = t2
                        act_rdy[(g, ci)] = t2
                        yield

            def DOWN(gl, with_E):
                chunks = [(g, ci) for g in gl for ci in range(2)]
                wd = [ring_load(w_down[(2 * g + ci) * 128:(2 * g + ci + 1) * 128, :]) for (g, ci) in chunks]
                nch = len(chunks)
                tok = None
                for s in range(8):
                    for dq in range(4):
                        bank = dn_b[dn_rr[0] % 4]
                        dn_rr[0] += 1
                        fr = bank.acquire()
                        for j, (g, ci) in enumerate(chunks):
                            sl = wd[j]
                            l_ap = actT[g % 2][ci][:, s * 128:(s + 1) * 128]
                            r_ap = sl.ap[:, dq * 512:(dq + 1) * 512]
                            tok = P.op("pe", lambda e, bank=bank, j=j, l_ap=l_ap, r_ap=r_ap: e.matmul(bank.ps[:, :], lhsT=l_ap, rhs=r_ap, start=(j == 0), stop=(j == nch - 1)),
                                       waits=[sl.tok, act_rdy[(g, ci)], fr if j == 0 else None], ev=(bank.ev if j == nch - 1 else None))
                        dst = h1[:, s, dq * 512:(dq + 1) * 512]
                        td = P.op("dve", lambda e, bank=bank, dst=dst: e.tensor_tensor(out=dst, in0=bank.ps[:, :], in1=dst, op=ALU.add), waits=[tok], ev=ev_dve)
                        bank.free.append(td)
                        yield
                    if with_E:
                        E_a(s)
                        if s >= 1:
                            E_b(s - 1)
                if with_E:
                    E_b(7)
                for g in gl:
                    act_free[g % 2][0] = tok
                    act_free[g % 2][1] = tok
                for sl in wd:
                    ring_release(sl, tok)

            e_trs = {}

            def E_a(s):
                gs = hf * 8 + s
                e_trs[s] = norm_chain(h1[:, s, :], 128, gs, ssF, vvF, lnF, rsF, junkE, P.last["dve"])

            def E_b(s):
                gs = hf * 8 + s
                ob = s % 2
                to = P.op("dve", lambda e: e.scalar_tensor_tensor(out=ost[ob], in0=h1[:, s, :], scalar=rsF[:, gs:gs + 1], in1=gfbc,
                                                                  op0=ALU.mult, op1=ALU.mult), waits=[e_trs[s], t_gf, ost_free[ob], P.last["pe"]], ev=ev_dve)
                tp = None
                E_tok[s] = [to]
                tw = P.op("sp", lambda e: e.dma_start(out=out[gs * 128:(gs + 1) * 128, :], in_=ost[ob]), waits=[to, tp], ev=ev_out)
                ost_free[ob] = tw
                out_toks.append(tw)

            junkE = xv(28672, 4096)

            def drain(gen, n):
                for _ in range(n):
                    try:
                        next(gen)
                    except StopIteration:
                        return

            dn = None
            for g in range(NG):
                gu = GU(g)
                sched = [6, 6, 6, 6, 4, 4, 0, 0]
                for u in range(8):
                    next(gu)
                    if dn is not None:
                        drain(dn, sched[u])
                for _ in gu:
                    pass
                if dn is not None:
                    for _ in dn:
                        pass
                dn = DOWN([g], False) if g < NG - 2 else None
            for _ in DOWN([NG - 2, NG - 1], True):
                pass
            ft_free = [ost_free[0], ost_free[0], ost_free[1]]
            act_free = [[None, None], [None, None]]
        if debug:
            P.op("sp", lambda e: e.dma_start(out=dbg_h1, in_=av(OFF_A, 8 * D * 2).bitcast(F32)), waits=[P.last["dve"], P.last["pe"]], ev=ev_out)

        with nc.Block() as block:
            @block.tensor
            def _(e):
                P.emit("pe", e)

            @block.scalar
            def _(e):
                P.emit("act", e)

            @block.vector
            def _(e):
                P.emit("dve", e)

            @block.gpsimd
            def _(e):
                P.emit("pool", e)

            @block.sync
            def _(e):
                P.emit("sp", e)
                e.wait_ge(ev_out.sem, ev_out.n)
    return nc


def _consts():
    ident = np.eye(128, dtype=np.float32)
    tri = (np.arange(128)[:, None] <= np.arange(128)[None, :]).astype(np.float32)
    qpos = (NM + np.arange(S)).astype(np.int64)
    qaug = np.stack([np.ones(S), np.ones(S), (qpos % 128), (qpos // 128) * 128]).astype(np.float32)
    kpos = np.concatenate([NM + np.arange(S), np.arange(NM)]).astype(np.int64)
    base = np.stack([(kpos % 128), (kpos // 128) * 128, -np.ones(T), -np.ones(T)]).astype(np.float32)
    kaug = np.zeros((NH, 4, T), np.float32)
    for h in range(NH):
        c = 8.0 * 2.0 ** (-(h + 1))
        kaug[h] = base * c
    return ident, tri, qaug, kaug


_NC_CACHE = {}


def kernel(x, meta, norm1_g, w_in, lambda_q1, lambda_k1, lambda_q2, lambda_k2, head_g, conv_w, w_out,
           norm2_g, w_gate, w_up, w_down, norm_f_g):
    f = lambda a: np.ascontiguousarray(np.asarray(a, dtype=np.float32))
    x = f(x)
    B = x.shape[0]
    if "nc" not in _NC_CACHE:
        _NC_CACHE["nc"] = build_program()
    nc = _NC_CACHE["nc"]
    ident, tri, qaug, kaug = _consts()
    shared = {
        "meta": f(meta), "norm1_g": f(norm1_g).reshape(1, D), "w_in": f(w_in)[0],
        "lambda_q1": f(lambda_q1).reshape(1, 64), "lambda_k1": f(lambda_k1).reshape(1, 64),
        "lambda_q2": f(lambda_q2).reshape(1, 64), "lambda_k2": f(lambda_k2).reshape(1, 64),
        "head_g": f(head_g).reshape(1, 128), "conv_w": f(conv_w)[0], "w_out": f(w_out)[0],
        "norm2_g": f(norm2_g).reshape(1, D), "w_gate": f(w_gate)[0], "w_up": f(w_up)[0], "w_down": f(w_down)[0],
        "norm_f_g": f(norm_f_g).reshape(1, D),
        "c_ident": ident, "c_tri": tri, "c_qaug": qaug, "c_kaug": kaug,
    }
    in_maps = [dict(shared, x=x[b]) for b in range(B)]
    res = run_bass_kernel_spmd(nc, in_maps, core_ids=list(range(B)))
    return np.stack([r["out"] for r in res.results], axis=0)
```

```python
import contextlib
import numpy as np
import concourse.bass as bass
import concourse.mybir as mybir
from concourse.bass_utils import run_bass_kernel_spmd

F32 = mybir.dt.float32
BF16 = mybir.dt.bfloat16
AF = mybir.ActivationFunctionType
ALU = mybir.AluOpType

D = 2048
S = 2048
NM = 16
T = S + NM
DFF = 5632
NH = 8
INW = 6144
NFF = DFF // 128
LAMBDA_INIT = 0.8 - 0.6 * 1.0
EPS = 1e-6
HEPS = 1e-5

OFF_A = 0
SZ_A = 16 * T
OFF_M = OFF_A + SZ_A
SZ_M = 16 * S
OFF_R = OFF_M + SZ_M
RING = 6
OFF_X = OFF_R + RING * 2048
SZ_X = 24576
ARENA = OFF_X + SZ_X


ANNOTATE = False


class Ev:
    def __init__(self, sem, step=1):
        self.sem, self.step, self.n = sem, step, 0

    def fire(self):
        self.n += self.step
        return (self.sem, self.n)


class Prog:
    def __init__(self):
        self.q = {k: [] for k in ("pe", "act", "dve", "pool", "sp")}
        self.last = {k: None for k in self.q}

    def op(self, eng, fn, waits=(), ev=None):
        tok = ev.fire() if ev is not None else None
        ws = []
        for w in waits:
            if w is None:
                continue
            if isinstance(w, list):
                ws.extend([u for u in w if u is not None])
            else:
                ws.append(w)
        self.q[eng].append((fn, ws, ev))
        if tok is not None:
            self.last[eng] = tok
        return tok

    def emit(self, eng, handle):
        waited = {}
        for fn, ws, ev in self.q[eng]:
            for sem, val in ws:
                key = id(sem)
                if waited.get(key, 0) >= val:
                    continue
                waited[key] = val
                handle.wait_ge(sem, val)
            ins = fn(handle)
            if ANNOTATE:
                ins.annotate("L%d" % fn.__code__.co_firstlineno)
            if ev is not None:
                ins.then_inc(ev.sem, ev.step)


class Bank:
    def __init__(self, ps, ev_pe):
        self.ps = ps
        self.ev = ev_pe
        self.free = []

    def acquire(self):
        f = self.free
        self.free = []
        return f


def build_program(debug=False):
    nc = bass.Bass("TRN2", target_bir_lowering=False)

    def din(name, shape):
        return nc.dram_tensor(name, shape, F32, kind="ExternalInput").ap()

    x = din("x", [S, D])
    meta = din("meta", [NM, D])
    g1 = din("norm1_g", [1, D])
    w_in = din("w_in", [D, INW])
    lq1 = din("lambda_q1", [1, 64])
    lk1 = din("lambda_k1", [1, 64])
    lq2 = din("lambda_q2", [1, 64])
    lk2 = din("lambda_k2", [1, 64])
    head_g = din("head_g", [1, 128])
    conv_w = din("conv_w", [3, 1024])
    w_out = din("w_out", [D, D])
    g2 = din("norm2_g", [1, D])
    w_gate = din("w_gate", [D, DFF])
    w_up = din("w_up", [D, DFF])
    w_down = din("w_down", [DFF, D])
    gf = din("norm_f_g", [1, D])
    c_ident = din("c_ident", [128, 128])
    c_tri = din("c_tri", [128, 128])
    c_qaug = din("c_qaug", [4, S])
    c_kaug = din("c_kaug", [NH, 4, T])
    out = nc.dram_tensor("out", [S, D], F32, kind="ExternalOutput").ap()
    if debug:
        dbg_aT = nc.dram_tensor("dbg_aT", [128, 16 * T], BF16, kind="ExternalOutput").ap()
        dbg_mix = nc.dram_tensor("dbg_mix", [128, 16 * S], BF16, kind="ExternalOutput").ap()
        dbg_h1 = nc.dram_tensor("dbg_h1", [128, 8 * D], F32, kind="ExternalOutput").ap()

    w_in_v = w_in.rearrange("(k p) n -> p k n", p=128)
    w_out_v = w_out.rearrange("(k p) n -> p k n", p=128)
    w_gate_v = w_gate.rearrange("(k p) n -> p k n", p=128)
    w_up_v = w_up.rearrange("(k p) n -> p k n", p=128)

    P = Prog()
    with contextlib.ExitStack() as es:
        def sb(name, shape, dt):
            return es.enter_context(nc.sbuf_tensor(name, shape, dt))

        nsem = [0]

        def new_ev(step=1):
            nsem[0] += 1
            return Ev(es.enter_context(nc.semaphore(f"s{nsem[0]}")), step)

        arena = sb("arena", [128, ARENA], BF16)
        ident = sb("ident", [128, 128], BF16)
        tri = sb("tri", [128, 128], BF16)
        ones = sb("ones", [128, 128], BF16)
        stat = sb("stat", [128, 4 * 17 + 4 * 16 + 4 * 16], F32)
        lamt = sb("lamt", [128, 4 * 64 + 64 + 16], F32)
        cw = sb("cw", [128, 3, 8], F32)
        hg = sb("hg", [128, 4], F32)

        pss = [es.enter_context(nc.psum_tensor(f"ps{i}", [128, 512], F32)) for i in range(8)]
        banks = [Bank(pss[i], new_ev()) for i in range(8)]

        def tview(bank):
            return bank.ps[:, :].bitcast(BF16).rearrange("p (j n) -> p j n", j=8)

        def av(off, n):
            return arena[:, off:off + n]

        aT = av(OFF_A, SZ_A).rearrange("p (k n) -> p k n", k=16)
        h1 = av(OFF_A, 8 * D * 2).bitcast(F32).rearrange("p (s n) -> p s n", s=8)
        mixT = av(OFF_M, SZ_M).rearrange("p (k n) -> p k n", k=16)
        ring_slots = [av(OFF_R + i * 2048, 2048) for i in range(RING)]
        ring_ready = [new_ev(16) for _ in range(RING)]
        ring_free = [new_ev(1) for _ in range(RING)]
        ring_last = [None] * RING
        ring_n = [0]

        def xv(off_bytes, nbytes, dt=BF16):
            a = av(OFF_X + off_bytes // 2, nbytes // 2)
            return a.bitcast(F32) if dt == F32 else a

        def mv(off_bytes, nbytes, dt=BF16):
            a = av(OFF_M + off_bytes // 2, nbytes // 2)
            return a.bitcast(F32) if dt == F32 else a

        xs = [mv(i * 8192, 8192, F32) for i in range(4)]
        at = [mv(32768, 4096), mv(36864, 4096)]
        g1bc = mv(40960, 8192, F32)
        junkA = mv(49152, 4096)
        QK = 4160
        qA = xv(0, 4128)
        qB = xv(QK, 4128)
        kA = xv(2 * QK, 4128)
        kB = xv(3 * QK, 4128)
        vF = xv(4 * QK, 4128)
        vh = xv(5 * QK, 4352).rearrange("p (b n) -> p b n", b=17)
        o_pt = 5 * QK + 4352
        Pt = [xv(o_pt + i * 1024, 1024) for i in range(4)]
        o_fin = o_pt + 4096
        oc = [xv(o_fin, 2048, F32), xv(o_fin + 2048, 2048, F32)]
        lc = [xv(o_fin + 4096, 2048, F32), xv(o_fin + 6144, 2048, F32)]
        sqb = xv(o_fin + 8192, 1024)
        o_cv = o_fin + 8192 + 1024
        cgS = [xv(o_cv, 2048, F32), xv(o_cv + 2048, 2048, F32)]
        ubuf = xv(o_cv + 4096, 2064, F32)
        ybuf = [xv(o_cv + 6160, 2048, F32), xv(o_cv + 8208, 2048, F32)]
        assert o_cv + 10256 <= 49152
        xr = [xv(0, 2048, F32), xv(2048, 2048, F32)]
        ft = [xv(4096, 4096), xv(8192, 4096), xv(40960, 4096)]
        ost = [xv(4096, 8192, F32), xv(40960, 8192, F32)]
        g2bc = xv(12288, 8192, F32)
        gfbc = xv(20480, 8192, F32)
        sg = [xv(28672, 2048, F32), xv(30720, 2048, F32)]
        actT = [[xv(32768 + (g * 2 + c) * 2048, 2048) for c in range(2)] for g in range(2)]
        junkC = xv(32768, 4096)

        ssA, vvA, lnA, rsA = (stat[:, i * 17:(i + 1) * 17] for i in range(4))
        o2 = 68
        ssC, vvC, lnC, rsC = (stat[:, o2 + i * 16:o2 + (i + 1) * 16] for i in range(4))
        o3 = o2 + 64
        ssF, vvF, lnF, rsF = (stat[:, o3 + i * 16:o3 + (i + 1) * 16] for i in range(4))

        ev_sp = new_ev(16)
        ev_spx = [new_ev(16) for _ in range(4)]
        ev_act = new_ev(1)
        ev_dve = new_ev(1)
        ev_pool = new_ev(16)
        ev_out = new_ev(16)
        ev_outb = [new_ev(16), new_ev(16)]
        ev_rel = new_ev(1)
        ev_poolc = new_ev(1)
        ev_xh = [new_ev(16) for _ in range(8)]
        E_tok = {}

        class Slot:
            pass

        def ring_load(src_ap, shape3=None):
            i = ring_n[0]
            ring_n[0] += 1
            si = i % RING
            slot = ring_slots[si]
            dst = slot if shape3 is None else slot.rearrange("p (k n) -> p k n", k=shape3)
            waits = []
            if i >= RING:
                assert ring_last[si] is not None, "ring slot reused before its last reader was emitted"
                waits = [ring_last[si]]
                ring_last[si] = None
            tok = P.op("pool", lambda e, d=dst, s=src_ap: e.dma_start(out=d, in_=s), waits, ev=ring_ready[si])
            sl = Slot()
            sl.ap, sl.tok, sl.si = dst, tok, si
            return sl

        def ring_release(sl, tok):
            ring_last[sl.si] = tok

        t_c = []
        t_c.append(P.op("pool", lambda e: e.dma_start(out=ident[:, :], in_=c_ident), ev=ev_pool))
        t_c.append(P.op("pool", lambda e: e.dma_start(out=tri[:, :], in_=c_tri), ev=ev_pool))
        t_ones = P.op("dve", lambda e: e.memset(ones[:, :], 1.0), ev=ev_dve)
        epsc = hg[:, 2:3]
        t_eps = P.op("dve", lambda e: e.memset(hg[:, 2:3], EPS), ev=ev_dve)
        t_g1 = P.op("sp", lambda e: e.dma_start(out=g1bc, in_=g1.broadcast_to([128, D])), ev=ev_sp)

        tb_rr = [0]

        def norm_a(src, Pn, col, ss, vv, junk, src_tok):
            t1 = P.op("act", lambda e: e.activation(out=junk[0:Pn, :], in_=src, func=AF.Square, accum_out=ss[0:Pn, col:col + 1]),
                      waits=[src_tok], ev=ev_act)
            return t1

        def norm_b(Pn, col, vv, ln, rs, t1):
            t2b = P.op("act", lambda e: e.activation(out=ln[0:Pn, col:col + 1], in_=vv[0:Pn, col:col + 1], func=AF.Ln, bias=epsc[0:Pn, 0:1], scale=1.0 / D),
                       waits=[t1, t_eps], ev=ev_act)
            t3 = P.op("act", lambda e: e.activation(out=rs[0:Pn, col:col + 1], in_=ln[0:Pn, col:col + 1], func=AF.Exp, scale=-0.5), waits=[t2b], ev=ev_act)
            return t3

        def norm_chain(src, Pn, col, ss, vv, ln, rs, junk, src_tok):
            return norm_b(Pn, col, ss, ln, rs, norm_a(src, Pn, col, ss, vv, junk, src_tok))

        def transposes(a_tile, Pn, dst, c0, tbanks, a_tok):
            evs = []
            pe_tok = None
            for hf in range(2):
                bank = tbanks[tb_rr[0] % len(tbanks)]
                tb_rr[0] += 1
                tv = tview(bank)
                fr = bank.acquire()
                for j in range(8):
                    k = hf * 8 + j
                    last = (j == 7)
                    pe_tok = P.op("pe", lambda e, tv=tv, j=j, k=k: e.transpose(out=tv[:, j, 0:Pn], in_=a_tile[0:Pn, k * 128:(k + 1) * 128],
                                                                             identity=ident[0:Pn, 0:Pn]),
                                  waits=[a_tok, fr, t_c] if j == 0 else [], ev=bank.ev if last else None)
                eng = "act" if hf == 0 else "dve"
                if eng == "act":
                    tk = P.op("act", lambda e, tv=tv, hf=hf: e.copy(out=dst[:, hf * 8:(hf + 1) * 8, c0:c0 + Pn], in_=tv[:, :, 0:Pn]),
                              waits=[pe_tok], ev=ev_act)
                else:
                    tk = P.op("dve", lambda e, tv=tv, hf=hf: e.tensor_copy(out=dst[:, hf * 8:(hf + 1) * 8, c0:c0 + Pn], in_=tv[:, :, 0:Pn]),
                              waits=[pe_tok], ev=ev_dve)
                bank.free.append(tk)
                evs.append(tk)
            return pe_tok, evs

        mm_rr = [0]
        TOK_TILES = [(n * 512, 512) for n in range(4)] + [(S, NM)]
        mmB = banks[0:2]
        LB = banks[2:4]
        SB = banks[4:6]
        OB = banks[6:8]

        def acc16(bank, N, lhs_fn, rhs_fn, first_waits, last_ev):
            fr = bank.acquire()
            tok = None
            for k in range(16):
                l_ap = lhs_fn(k)
                r_ap = rhs_fn(k)
                tok = P.op("pe", lambda e, l_ap=l_ap, r_ap=r_ap, k=k: e.matmul(bank.ps[:, 0:N], lhsT=l_ap, rhs=r_ap, start=(k == 0), stop=(k == 15)),
                           waits=[fr, first_waits] if k == 0 else [], ev=(last_ev if k == 15 else None))
            return tok

        def proj_chunk(col0, tiles, consumer, first_waits):
            sl = ring_load(w_in_v[:, :, col0:col0 + 128], 16)
            for ti, (c0, N) in enumerate(tiles):
                bank = mmB[mm_rr[0] % 2]
                mm_rr[0] += 1
                tok = acc16(bank, N, lambda k: sl.ap[:, k, :], lambda k, c0=c0, N=N: aT[:, k, c0:c0 + N],
                            [sl.tok] + list(first_waits), bank.ev)
                consumer(ti, c0, N, bank, tok)
            ring_release(sl, tok)

        def evac_split(dstA, dstB):
            def cons(ti, c0, N, bank, tok):
                t1 = P.op("act", lambda e: e.copy(out=dstA[0:64, c0:c0 + N], in_=bank.ps[0:64, 0:N]), waits=[tok], ev=ev_act)
                t2 = P.op("dve", lambda e: e.tensor_copy(out=dstB[0:64, c0:c0 + N], in_=bank.ps[64:128, 0:N]), waits=[tok], ev=ev_dve)
                bank.free += [t1, t2]
            return cons

        vrr = [0]

        def evac_v(ti, c0, N, bank, tok):
            vrr[0] += 1
            if vrr[0] % 2:
                t1 = P.op("act", lambda e: e.copy(out=vF[:, c0:c0 + N], in_=bank.ps[:, 0:N]), waits=[tok], ev=ev_act)
            else:
                t1 = P.op("dve", lambda e: e.tensor_copy(out=vF[:, c0:c0 + N], in_=bank.ps[:, 0:N]), waits=[tok], ev=ev_dve)
            bank.free.append(t1)


        def proj_chunk_gen(col0, tiles, consumer, waits_box):
            sl = ring_load(w_in_v[:, :, col0:col0 + 128], 16)
            tok = None
            for ti, (c0, N) in enumerate(tiles):
                bank = mmB[mm_rr[0] % 2]
                mm_rr[0] += 1
                tok = acc16(bank, N, lambda k: sl.ap[:, k, :], lambda k, c0=c0, N=N: aT[:, k, c0:c0 + N],
                            [sl.tok] + list(waits_box[0]), bank.ev)
                consumer(ti, c0, N, bank, tok)
                if ti == len(tiles) - 1:
                    ring_release(sl, tok)
                yield

        a_done = {}
        t_done = {}

        vvtok = {}
        ld_tok = {}

        def stageA1a(i):
            b = i % 4
            Pn = 128 if i < 16 else NM
            src_rows = x[i * 128:(i + 1) * 128, :] if i < 16 else meta
            w = [a_done[i - 4]] if i >= 4 else []
            tld = P.op("sp", lambda e: e.dma_start(out=xs[b][0:Pn, :], in_=src_rows), waits=w, ev=ev_spx[b])
            ld_tok[i] = tld

        def stageA1s(i):
            b = i % 4
            Pn = 128 if i < 16 else NM
            vvtok[i] = norm_a(xs[b][0:Pn, :], Pn, i, ssA, vvA, junkA, ld_tok[i])

        trsA = {}

        def stageA1n(i):
            Pn = 128 if i < 16 else NM
            trsA[i] = norm_b(Pn, i, ssA, lnA, rsA, vvtok[i])

        def stageA1b(i):
            b = i % 4
            ab = i % 2
            Pn = 128 if i < 16 else NM
            w = [trsA[i], t_g1] + ([t_done[i - 2]] if i >= 2 else [])
            a_done[i] = P.op("dve", lambda e: e.scalar_tensor_tensor(out=at[ab][0:Pn, :], in0=xs[b][0:Pn, :], scalar=rsA[0:Pn, i:i + 1],
                                                                      in1=g1bc[0:Pn, :], op0=ALU.mult, op1=ALU.mult), waits=w, ev=ev_dve)

        def stageA2(i):
            Pn = 128 if i < 16 else NM
            c0 = i * 128 if i < 16 else S
            pt, evs = transposes(at[i % 2], Pn, aT, c0, banks[2:6], a_done[i])
            t_done[i] = pt

        wbox = [[]]
        g_q0 = proj_chunk_gen(0, TOK_TILES[0:4], evac_split(qA, qB), wbox)
        g_k0 = proj_chunk_gen(1024, TOK_TILES, evac_split(kA, kB), wbox)
        g_v0 = proj_chunk_gen(2048, TOK_TILES, evac_v, wbox)
        grp_tok = {}

        pend0 = []

        def step_one():
            if pend0:
                g, n = pend0.pop(0)
                wbox[0] = grp_tok[n]
                next(g)

        for i in range(3):
            stageA1a(i)
        stageA1s(0)
        stageA1n(0)
        stageA1b(0)
        for i in range(1, 17):
            if i + 2 < 17:
                stageA1a(i + 2)
            stageA1s(i)
            stageA1n(i)
            stageA1b(i)
            stageA2(i - 1)
            if (i - 1) % 4 == 3:
                n = (i - 1) // 4
                grp_tok[n] = [P.last["act"], P.last["dve"]]
                pend0.extend([(g_q0, n), (g_k0, n), (g_v0, n)])
            else:
                step_one()
        stageA2(16)
        grp_tok[4] = [P.last["act"], P.last["dve"]]
        pend0.extend([(g_k0, 4), (g_v0, 4)])
        while pend0:
            step_one()
        for g in (g_q0, g_k0, g_v0):
            for _ in g:
                pass
        lqv = lamt[:, 0:256].rearrange("p (a n) -> p a n", a=4)
        for i, src in enumerate((lq1, lk1, lq2, lk2)):
            tl = P.op("sp", lambda e, i=i, src=src: e.dma_start(out=lqv[:, i, :], in_=src.broadcast_to([128, 64])), ev=ev_sp)
        t_hg = P.op("sp", lambda e: e.dma_start(out=hg[:, 0:1], in_=head_g.rearrange("o d -> d o")), ev=ev_sp)
        for t in range(3):
            t_cw = P.op("sp", lambda e, t=t: e.dma_start(out=cw[:, t, :], in_=conv_w[t:t + 1, :].rearrange("o (j p) -> p (o j)", p=128),
                                                       allow_slow_non_contiguous=True), ev=ev_sp)
        ljunk = lamt[:, 256:320]
        lsc = lamt[:, 320:336]
        P.op("dve", lambda e: e.scalar_tensor_tensor(out=ljunk, in0=lqv[:, 0, :], scalar=1.0, in1=lqv[:, 1, :], op0=ALU.mult,
                                                     op1=ALU.mult, accum_out=lsc[:, 0:1]), waits=[t_cw], ev=ev_dve)
        t_l = P.op("dve", lambda e: e.scalar_tensor_tensor(out=ljunk, in0=lqv[:, 2, :], scalar=1.0, in1=lqv[:, 3, :], op0=ALU.mult,
                                                           op1=ALU.mult, accum_out=lsc[:, 1:2]), ev=ev_dve)
        t_le = P.op("act", lambda e: e.activation(out=lsc[:, 2:4], in_=lsc[:, 0:2], func=AF.Exp), waits=[t_l], ev=ev_act)
        t_l2 = P.op("dve", lambda e: e.tensor_tensor(out=lsc[:, 4:5], in0=lsc[:, 3:4], in1=lsc[:, 2:3], op=ALU.subtract), waits=[t_le], ev=ev_dve)
        t_l3 = P.op("dve", lambda e: e.tensor_scalar(out=lsc[:, 5:6], in0=lsc[:, 4:5], scalar1=-LAMBDA_INIT, scalar2=None, op0=ALU.add), waits=[t_l2], ev=ev_dve)
        nlam = lsc[:, 5:6]
        t_hgs = P.op("dve", lambda e: e.tensor_scalar(out=hg[:, 1:2], in0=hg[:, 0:1], scalar1=1.0 - LAMBDA_INIT, scalar2=None, op0=ALU.mult), ev=ev_dve)
        hgs = hg[:, 1:2]
        tokA = [P.last["act"], P.last["dve"], P.last["pe"]]
        if debug:
            P.op("sp", lambda e: e.dma_start(out=dbg_aT, in_=av(OFF_A, SZ_A)), waits=tokA, ev=ev_out)

        cg_rd = {}
        u_tok = [None]
        aT_dead = [None]

        def conv_units(j, slots):
            s_cg, s_hi, s_bg = slots
            order = [4, 0, 1, 2, 3]
            for oi, ti in enumerate(order):
                c0, N = TOK_TILES[ti]
                cb = oi % 2
                bank = mmB[mm_rr[0] % 2]
                mm_rr[0] += 1
                tok = acc16(bank, N, lambda k: s_cg.ap[:, k, :], lambda k: aT[:, k, c0:c0 + N], [s_cg.tok], bank.ev)
                if oi == 4:
                    ring_release(s_cg, tok)
                d_ap, s_ap = cgS[cb][:, 0:N], bank.ps[:, 0:N]
                t1 = P.op("act", lambda e, d_ap=d_ap, s_ap=s_ap: e.copy(out=d_ap, in_=s_ap), waits=[tok, cg_rd.get(cb)], ev=ev_act)
                bank.free.append(t1)
                yield
                bank = mmB[mm_rr[0] % 2]
                mm_rr[0] += 1
                tok = acc16(bank, N, lambda k: s_hi.ap[:, k, :], lambda k: aT[:, k, c0:c0 + N], [s_hi.tok], bank.ev)
                if oi == 4:
                    ring_release(s_hi, tok)
                if ti == 4:
                    a_ap, b_ap = cgS[cb][:, NM - 2:NM], bank.ps[:, NM - 2:NM]
                    t2 = P.op("dve", lambda e, a_ap=a_ap, b_ap=b_ap: e.tensor_tensor(out=ubuf[:, 0:2], in0=a_ap, in1=b_ap, op=ALU.mult),
                              waits=[t1, tok, u_tok[0]], ev=ev_dve)
                    bank.free.append(t2)
                    cg_rd[cb] = t2
                    u_tok[0] = t2
                    yield
                    continue
                a_ap, b_ap = cgS[cb][:, :], bank.ps[:, :]
                t2 = P.op("dve", lambda e, a_ap=a_ap, b_ap=b_ap: e.tensor_tensor(out=ubuf[:, 2:514], in0=a_ap, in1=b_ap, op=ALU.mult),
                          waits=[t1, tok, u_tok[0]], ev=ev_dve)
                bank.free.append(t2)
                cg_rd[cb] = t2
                yield
                bank = mmB[mm_rr[0] % 2]
                mm_rr[0] += 1
                tok = acc16(bank, N, lambda k: s_bg.ap[:, k, :], lambda k: aT[:, k, c0:c0 + N], [s_bg.tok], bank.ev)
                if oi == 4:
                    ring_release(s_bg, tok)
                    aT_dead[0] = tok
                yb = ybuf[cb]
                n = ti
                w0, w1, w2 = cw[:, 0, j:j + 1], cw[:, 1, j:j + 1], cw[:, 2, j:j + 1]
                y1 = P.op("dve", lambda e, yb=yb, w0=w0: e.tensor_scalar(out=yb, in0=ubuf[:, 0:512], scalar1=w0, scalar2=None, op0=ALU.mult),
                          waits=[t2, t_cw], ev=ev_dve)
                y2 = P.op("dve", lambda e, yb=yb, w1=w1: e.scalar_tensor_tensor(out=yb, in0=ubuf[:, 1:513], scalar=w1, in1=yb, op0=ALU.mult, op1=ALU.add),
                          waits=[y1], ev=ev_dve)
                y3 = P.op("dve", lambda e, yb=yb, w2=w2: e.scalar_tensor_tensor(out=yb, in0=ubuf[:, 2:514], scalar=w2, in1=yb, op0=ALU.mult, op1=ALU.add),
                          waits=[y2], ev=ev_dve)
                o_ap, b_ap = mixT[:, 8 + j, n * 512:(n + 1) * 512], bank.ps[:, :]
                t3 = P.op("dve", lambda e, yb=yb, o_ap=o_ap, b_ap=b_ap: e.tensor_tensor(out=o_ap, in0=yb, in1=b_ap, op=ALU.mult),
                          waits=[tok, y3], ev=ev_dve)
                bank.free.append(t3)
                u_tok[0] = P.op("dve", lambda e: e.tensor_copy(out=ubuf[:, 0:2], in_=ubuf[:, 512:514]), waits=[y3], ev=ev_dve)
                yield

        s_rr = [0]
        pt_rr = [0]
        pt_free = [None] * 4
        prev_head_pe = [None]
        fin_tok = [None]
        t_qa = None
        deferred = []

        pull_fn = [None]

        def tick(force=False, tag=None):
            for item in list(deferred):
                item[0] -= 1
                if (force and (tag is None or item[2] == tag)) or item[0] <= 0:
                    deferred.remove(item)
                    if not force and pull_fn[0] is not None and item[2] in ("b", "d"):
                        pull_fn[0]()
                    item[1]()

        for h in range(NH):
            fw = tokA if h == 0 else []
            if h > 0:
                proj_chunk(h * 128, TOK_TILES[0:4], evac_split(qA, qB), fw)
                tick(force=True)
                proj_chunk(2048 + h * 128, TOK_TILES, evac_v, [])
                proj_chunk(1024 + h * 128, TOK_TILES, evac_split(kA, kB), [])
            t_v = [P.last["act"], P.last["dve"]]
            tv_toks = []
            for g0 in range(0, 17, 8):
                bank = mmB[mm_rr[0] % 2]
                mm_rr[0] += 1
                tv = tview(bank)
                fr = bank.acquire()
                nb = min(8, 17 - g0)
                tok = None
                for j in range(nb):
                    blk = g0 + j
                    Pn = 128 if blk < 16 else NM
                    tok = P.op("pe", lambda e, tv=tv, j=j, blk=blk, Pn=Pn: e.transpose(out=tv[0:Pn, j, :], in_=vF[:, blk * 128:blk * 128 + Pn],
                                                                                     identity=ident[:, :]),
                               waits=[t_v, fr] if j == 0 else [], ev=bank.ev if j == nb - 1 else None)
                if nb == 8:
                    tk = P.op("dve", lambda e, tv=tv, g0=g0: e.tensor_copy(out=vh[:, g0:g0 + 8, :], in_=tv[:, :, :]), waits=[tok], ev=ev_dve)
                else:
                    tk = P.op("dve", lambda e, tv=tv: e.tensor_copy(out=vh[0:NM, 16, :], in_=tv[0:NM, 0, :]), waits=[tok], ev=ev_dve)
                bank.free.append(tk)
                tv_toks.append(tk)
            t_qk = [P.last["act"], P.last["dve"]]
            conv_slots = [ring_load(w_in_v[:, :, 4096 + h * 128:4096 + (h + 1) * 128], 16),
                          ring_load(w_in_v[:, :, 5120 + h * 128:5120 + (h + 1) * 128], 16),
                          ring_load(w_in_v[:, :, 3072 + h * 128:3072 + (h + 1) * 128], 16)]
            if h == 0:
                P.op("pool", lambda e: e.dma_start(out=qA[64:68, 0:S], in_=c_qaug), ev=ev_pool)
                t_qa = P.op("pool", lambda e: e.dma_start(out=qB[64:68, 0:S], in_=c_qaug), ev=ev_pool)
            P.op("pool", lambda e, h=h: e.dma_start(out=kA[64:68, 0:T], in_=c_kaug[h]), waits=[prev_head_pe[0]], ev=ev_pool)
            t_ka = P.op("pool", lambda e, h=h: e.dma_start(out=kB[64:68, 0:T], in_=c_kaug[h]), ev=ev_pool)
            cgen = conv_units(h, conv_slots)
            tile_ctr = [0]
            prev_head_pe_tok = P.last["pe"]

            def pull():
                try:
                    next(cgen)
                except StopIteration:
                    pass

            pull_fn[0] = pull
            pull()
            pull()
            for Q in range(4):
                blocks = [("m", S, NM, 16)] + [(j, j * 128, 128, j) for j in range(4 * Q + 4)]
                tiles = []
                for (bid, kc0, nk, vb) in blocks:
                    qlo = 0
                    diag = False
                    if bid != "m" and bid >= 4 * Q:
                        qlo = (bid - 4 * Q) * 128
                        diag = True
                    for m in range(2):
                        tiles.append((bid, kc0, nk, vb, qlo, diag, m))
                nt = len(tiles)
                pend = []
                fro = [OB[0].acquire(), OB[1].acquire()]
                frl = [LB[0].acquire(), LB[1].acquire()]
                last_pv = [None, None]
                last_l = [None, None]

                def issue_S(idx):
                    (bid, kc0, nk, vb, qlo, diag, m) = tiles[idx]
                    kM = kA if m == 0 else kB
                    qM = qA if m == 0 else qB
                    sbk = SB[s_rr[0] % 2]
                    s_rr[0] += 1
                    pi = pt_rr[0] % 4
                    pt_rr[0] += 1
                    fr = sbk.acquire()
                    rhs_ap = qM[0:68, Q * 512 + qlo:(Q + 1) * 512]
                    ts = P.op("pe", lambda e: e.matmul(sbk.ps[0:nk, qlo:512], lhsT=kM[0:68, kc0:kc0 + nk], rhs=rhs_ap, start=True, stop=True),
                              waits=[fr, t_qk, t_ka, t_qa, tv_toks], ev=sbk.ev)
                    te = P.op("act", lambda e: e.activation(out=Pt[pi][0:nk, qlo:512], in_=sbk.ps[0:nk, qlo:512], func=AF.Exp, scale=0.125),
                              waits=[ts, pt_free[pi]], ev=ev_act)
                    sbk.free.append(te)
                    rdy = te
                    if diag:
                        rdy = P.op("dve", lambda e: e.tensor_tensor(out=Pt[pi][0:nk, qlo:qlo + 128], in0=Pt[pi][0:nk, qlo:qlo + 128], in1=tri[:, :],
                                                                    op=ALU.mult), waits=[te], ev=ev_dve)
                    pend.append((idx, pi, rdy))

                def issue_PV(idx, pi, rdy):
                    (bid, kc0, nk, vb, qlo, diag, m) = tiles[idx]
                    first = (idx < 2)
                    lastm = (idx >= nt - 2)
                    vsrc = vh[0:nk, vb, :]
                    t1 = P.op("pe", lambda e: e.matmul(OB[m].ps[:, qlo:512], lhsT=vsrc, rhs=Pt[pi][0:nk, qlo:512], start=first, stop=lastm),
                              waits=[rdy, fro[m] if first else None], ev=OB[m].ev if lastm else None)
                    t2 = P.op("pe", lambda e: e.matmul(LB[m].ps[:, qlo:512], lhsT=ones[0:nk, :], rhs=Pt[pi][0:nk, qlo:512], start=first, stop=lastm),
                              waits=[frl[m] if first else None, t_ones], ev=LB[m].ev)
                    pt_free[pi] = t2
                    if lastm:
                        last_pv[m] = t1
                        last_l[m] = t2

                for idx in range(nt):
                    issue_S(idx)
                    if idx == 1 and Q > 0:
                        pull()
                    if len(pend) > 2:
                        issue_PV(*pend.pop(0))
                        tick()
                        tile_ctr[0] += 1
                        if tile_ctr[0] % 12 == 6:
                            pull()
                while pend:
                    issue_PV(*pend.pop(0))

                t_lnl = []
                for m in range(2):
                    tk = P.op("act", lambda e, m=m: e.activation(out=lc[m], in_=LB[m].ps[:, :], func=AF.Ln), waits=[last_l[m], fin_tok[0]], ev=ev_act)
                    LB[m].free.append(tk)
                    t_lnl.append(tk)
                t_ocs = []
                for m in range(2):
                    tk = P.op("dve", lambda e, m=m: e.tensor_copy(out=oc[m], in_=OB[m].ps[:, :]), waits=[last_pv[m], last_l[m], fin_tok[0]], ev=ev_dve)
                    OB[m].free.append(tk)
                    t_ocs.append(tk)
                st = {}

                def fin1b(t_lnl=t_lnl, t_ocs=t_ocs, st=st):
                    t_rl = []
                    for m in range(2):
                        t_rl.append(P.op("act", lambda e, m=m: e.activation(out=lc[m], in_=lc[m], func=AF.Exp, scale=-1.0), waits=[t_lnl[m]], ev=ev_act))
                    c3 = P.op("dve", lambda e: e.tensor_tensor(out=oc[0], in0=oc[0], in1=lc[0], op=ALU.mult), waits=[t_ocs[0], t_rl[0]], ev=ev_dve)
                    c4 = P.op("dve", lambda e: e.tensor_tensor(out=oc[1], in0=oc[1], in1=lc[1], op=ALU.mult), waits=[t_ocs[1], t_rl[1]], ev=ev_dve)
                    c5 = P.op("dve", lambda e: e.scalar_tensor_tensor(out=oc[0], in0=oc[1], scalar=nlam, in1=oc[0], op0=ALU.mult, op1=ALU.add),
                              waits=[c3, c4, t_l3, t_hgs], ev=ev_dve)
                    st["t_sq"] = P.op("dve", lambda e: e.tensor_tensor(out=sqb, in0=oc[0], in1=oc[0], op=ALU.mult), waits=[c5], ev=ev_dve)
                deferred.append([3, fin1b, "b"])

                def fin2a(st=st):
                    t_sq = st["t_sq"]
                    bank = mmB[mm_rr[0] % 2]
                    mm_rr[0] += 1
                    fr = bank.acquire()
                    t_ms = P.op("pe", lambda e: e.matmul(bank.ps[:, :], lhsT=ones[:, :], rhs=sqb, start=True, stop=True), waits=[t_sq, fr], ev=bank.ev)
                    t_vv = P.op("dve", lambda e: e.tensor_scalar(out=lc[0], in0=bank.ps[:, :], scalar1=1.0 / 128, scalar2=HEPS, op0=ALU.mult, op1=ALU.add),
                                waits=[t_ms], ev=ev_dve)
                    bank.free.append(t_vv)
                    st["t_vv"] = t_vv

                def fin2b(h=h, Q=Q, st=st):
                    t_vv = st["t_vv"]
                    t_ln = P.op("act", lambda e: e.activation(out=lc[0], in_=lc[0], func=AF.Ln), waits=[t_vv], ev=ev_act)
                    t_r = P.op("act", lambda e: e.activation(out=lc[0], in_=lc[0], func=AF.Exp, scale=-0.5), waits=[t_ln], ev=ev_act)
                    o_ap = mixT[:, h, Q * 512:(Q + 1) * 512]
                    fin_tok[0] = P.op("dve", lambda e: e.scalar_tensor_tensor(out=o_ap, in0=oc[0], scalar=hgs, in1=lc[0], op0=ALU.mult, op1=ALU.mult),
                                      waits=[t_r], ev=ev_dve)
                deferred.append([10, fin2a, "c"])
                deferred.append([13, fin2b, "d"])
            prev_head_pe[0] = P.last["pe"]
            pull_fn[0] = None
            for _ in cgen:
                pass
            tick(force=True, tag="b")

        tick(force=True)
        tokB = [P.last["act"], P.last["dve"], P.last["pe"]]
        if debug:
            P.op("sp", lambda e: e.dma_start(out=dbg_mix, in_=av(OFF_M, SZ_M)), waits=tokB, ev=ev_out)

        t_g = {}
        xr_free = [None, None]
        xr_n = [0]
        ft_free = [None, None, None]
        ost_free = [None, None]
        act_free = [[None, None], [None, None]]
        gu_b = banks[0:4]
        dn_b = banks[4:8]
        tb2 = banks[4:6]
        dn_rr = [0]
        out_toks = []
        NG = NFF // 2
        for hf in range(2):
            n2_trs = {}

            def norm2_chain(s):
                gs = hf * 8 + s
                n2_trs[s] = norm_chain(h1[:, s, :], 128, gs, ssC, vvC, lnC, rsC, junkC, P.last["dve"])

            def norm2_f(s):
                gs = hf * 8 + s
                b = s % 3
                ta = P.op("dve", lambda e: e.scalar_tensor_tensor(out=ft[b], in0=h1[:, s, :], scalar=rsC[:, gs:gs + 1], in1=g2bc,
                                                                  op0=ALU.mult, op1=ALU.mult), waits=[n2_trs[s], t_g["gf"], ft_free[b]], ev=ev_dve)
                return ta

            ft_ready = {}
            xq = []
            for dq in range(4):
                w = [aT_dead[0]] if hf == 0 else [E_tok[s] for s in range(8)]
                src_ap = x[hf * 1024:(hf + 1) * 1024, dq * 512:(dq + 1) * 512].rearrange("(s p) c -> p s c", p=128)
                xq.append(P.op("sp", lambda e, dq=dq, src_ap=src_ap: e.dma_start(out=h1[:, :, dq * 512:(dq + 1) * 512], in_=src_ap), waits=w, ev=ev_xh[dq]))
                if hf == 0 and dq == 0:
                    t_g["g2"] = P.op("sp", lambda e: e.dma_start(out=g2bc, in_=g2.broadcast_to([128, D])), waits=tokB, ev=ev_sp)
                    t_g["gf"] = P.op("sp", lambda e: e.dma_start(out=gfbc, in_=gf.broadcast_to([128, D])), ev=ev_sp)
            for dq in range(3):
                frs = [banks[s].acquire() for s in range(8)]
                toks = [None] * 8
                for kg in range(4):
                    sl = ring_load(w_out_v[:, kg * 4:(kg + 1) * 4, dq * 512:(dq + 1) * 512], 4)
                    tok = None
                    for s in range(8):
                        gs = hf * 8 + s
                        for kk in range(4):
                            k = kg * 4 + kk
                            l_ap = mixT[:, k, gs * 128:(gs + 1) * 128]
                            r_ap = sl.ap[:, kk, :]
                            bank = banks[s]
                            last = (k == 15)
                            rel = (s == 7 and kk == 3)
                            ev = bank.ev if last else (ev_rel if rel else None)
                            tok = P.op("pe", lambda e, bank=bank, l_ap=l_ap, r_ap=r_ap, k=k: e.matmul(bank.ps[:, :], lhsT=l_ap, rhs=r_ap, start=(k == 0), stop=(k == 15)),
                                       waits=[sl.tok, tokB, frs[s] if k == 0 else None], ev=ev)
                        if kg == 3:
                            toks[s] = tok
                    ring_release(sl, tok)
                for s in range(8):
                    dst = h1[:, s, dq * 512:(dq + 1) * 512]
                    td = P.op("dve", lambda e, s=s, dst=dst: e.tensor_tensor(out=dst, in0=banks[s].ps[:, :], in1=dst, op=ALU.add), waits=[toks[s], xq[dq]], ev=ev_dve)
                    banks[s].free.append(td)
            for dq in range(3, 4):
                slots = [ring_load(w_out_v[:, kg * 4:(kg + 1) * 4, dq * 512:(dq + 1) * 512], 4) for kg in range(4)]
                for s in range(8):
                    gs = hf * 8 + s
                    bank = gu_b[mm_rr[0] % 4]
                    mm_rr[0] += 1
                    tok = acc16(bank, 512, lambda k, gs=gs: mixT[:, k, gs * 128:(gs + 1) * 128], lambda k: slots[k // 4].ap[:, k % 4, :],
                                [sl.tok for sl in slots] + tokB, bank.ev)
                    dst = h1[:, s, dq * 512:(dq + 1) * 512]
                    td = P.op("dve", lambda e, bank=bank, dst=dst: e.tensor_tensor(out=dst, in0=bank.ps[:, :], in1=dst, op=ALU.add), waits=[tok, xq[dq]], ev=ev_dve)
                    bank.free.append(td)
                    norm2_chain(s)
                    if s >= 2:
                        pt, evs = transposes(ft[(s - 2) % 3], 128, mixT, (hf * 8 + s - 2) * 128, tb2, ft_ready[s - 2])
                        ft_free[(s - 2) % 3] = pt
                    ft_ready[s] = norm2_f(s)
                for sl in slots:
                    ring_release(sl, tok)
            for s2 in (6, 7):
                pt, evs = transposes(ft[s2 % 3], 128, mixT, (hf * 8 + s2) * 128, tb2, ft_ready[s2])
                ft_free[s2 % 3] = pt
            tokC = [P.last["act"], P.last["dve"]]

            sg_rr = [0]
            sg_free = [None, None]
            act_rdy = {}

            def GU(g):
                for ci in range(2):
                    c = 2 * g + ci
                    sl_g = ring_load(w_gate_v[:, :, c * 128:(c + 1) * 128], 16)
                    sl_u = ring_load(w_up_v[:, :, c * 128:(c + 1) * 128], 16)
                    for n in range(2):
                        c0 = hf * 1024 + n * 512
                        res = []
                        for wi, sl in enumerate((sl_g, sl_u)):
                            bank = gu_b[mm_rr[0] % 4]
                            mm_rr[0] += 1
                            tok = acc16(bank, 512, lambda k, sl=sl: sl.ap[:, k, :], lambda k: mixT[:, k, c0:c0 + 512], [sl.tok] + tokC, bank.ev)
                            if n == 1:
                                ring_release(sl, tok)
                            res.append((bank, tok))
                            if wi == 0:
                                yield
                        sb_i = sg_rr[0] % 2
                        sg_rr[0] += 1
                        (bg_, tg_), (bu_, tu_) = res
                        t1 = P.op("act", lambda e, bg_=bg_, sb_i=sb_i: e.activation(out=sg[sb_i], in_=bg_.ps[:, :], func=AF.Silu),
                                  waits=[tg_, sg_free[sb_i]], ev=ev_act)
                        bg_.free.append(t1)
                        dst = actT[g % 2][ci][:, n * 512:(n + 1) * 512]
                        t2 = P.op("dve", lambda e, bu_=bu_, sb_i=sb_i, dst=dst: e.tensor_tensor(out=dst, in0=sg[sb_i], in1=bu_.ps[:, :], op=ALU.mult),
                                  waits=[t1, tu_, act_free[g % 2][ci]], ev=ev_dve)
                        bu_.free.append(t2)
                        sg_free[sb_i] = t2
                        act_rdy[(g, ci)] = t2
                        yield

            def DOWN(gl, with_E):
                chunks = [(g, ci) for g in gl for ci in range(2)]
                wd = [ring_load(w_down[(2 * g + ci) * 128:(2 * g + ci + 1) * 128, :]) for (g, ci) in chunks]
                nch = len(chunks)
                tok = None
                for s in range(8):
                    for dq in range(4):
                        bank = dn_b[dn_rr[0] % 4]
                        dn_rr[0] += 1
                        fr = bank.acquire()
                        for j, (g, ci) in enumerate(chunks):
                            sl = wd[j]
                            l_ap = actT[g % 2][ci][:, s * 128:(s + 1) * 128]
                            r_ap = sl.ap[:, dq * 512:(dq + 1) * 512]
                            tok = P.op("pe", lambda e, bank=bank, j=j, l_ap=l_ap, r_ap=r_ap: e.matmul(bank.ps[:, :], lhsT=l_ap, rhs=r_ap, start=(j == 0), stop=(j == nch - 1)),
                                       waits=[sl.tok, act_rdy[(g, ci)], fr if j == 0 else None], ev=(bank.ev if j == nch - 1 else None))
                        dst = h1[:, s, dq * 512:(dq + 1) * 512]
                        td = P.op("dve", lambda e, bank=bank, dst=dst: e.tensor_tensor(out=dst, in0=bank.ps[:, :], in1=dst, op=ALU.add), waits=[tok], ev=ev_dve)
                        bank.free.append(td)
                        yield
                    if with_E:
                        E_a(s)
                        if s >= 1:
                            E_b(s - 1)
                if with_E:
                    E_b(7)
                for g in gl:
                    act_free[g % 2][0] = tok
                    act_free[g % 2][1] = tok
                for sl in wd:
                    ring_release(sl, tok)

            e_trs = {}

            def E_a(s):
                gs = hf * 8 + s
                e_trs[s] = norm_chain(h1[:, s, :], 128, gs, ssF, vvF, lnF, rsF, junkE, P.last["dve"])

            def E_b(s):
                gs = hf * 8 + s
                ob = s % 2
                to = P.op("dve", lambda e: e.scalar_tensor_tensor(out=ost[ob], in0=h1[:, s, :], scalar=rsF[:, gs:gs + 1], in1=gfbc,
                                                                  op0=ALU.mult, op1=ALU.mult), waits=[e_trs[s], t_g["gf"], ost_free[ob], P.last["pe"]], ev=ev_dve)
                tp = None
                E_tok[s] = [to]
                tw = P.op("sp", lambda e: e.dma_start(out=out[gs * 128:(gs + 1) * 128, :], in_=ost[ob]), waits=[to, tp], ev=ev_outb[ob])
                ost_free[ob] = tw
                out_toks.append(tw)

            junkE = xv(28672, 4096)

            def drain(gen, n):
                for _ in range(n):
                    try:
                        next(gen)
                    except StopIteration:
                        return

            dn = None
            for g in range(NG):
                gu = GU(g)
                sched = [6, 6, 6, 6, 4, 4, 0, 0]
                for u in range(8):
                    next(gu)
                    if dn is not None:
                        drain(dn, sched[u])
                for _ in gu:
                    pass
                if dn is not None:
                    for _ in dn:
                        pass
                dn = DOWN([g], False) if g < NG - 2 else None
            for _ in DOWN([NG - 2, NG - 1], True):
                pass
            ft_free = [ost_free[0], ost_free[0], ost_free[1]]
            act_free = [[None, None], [None, None]]
        if debug:
            P.op("sp", lambda e: e.dma_start(out=dbg_h1, in_=av(OFF_A, 8 * D * 2).bitcast(F32)), waits=[P.last["dve"], P.last["pe"]], ev=ev_out)

        with nc.Block() as block:
            @block.tensor
            def _(e):
                P.emit("pe", e)

            @block.scalar
            def _(e):
                P.emit("act", e)

            @block.vector
            def _(e):
                P.emit("dve", e)

            @block.gpsimd
            def _(e):
                P.emit("pool", e)

            @block.sync
            def _(e):
                P.emit("sp", e)
                for evo in (ev_outb[0], ev_outb[1], ev_out):
                    if evo.n > 0:
                        e.wait_ge(evo.sem, evo.n)
    return nc


def _consts():
    ident = np.eye(128, dtype=np.float32)
    tri = (np.arange(128)[:, None] <= np.arange(128)[None, :]).astype(np.float32)
    qpos = (NM + np.arange(S)).astype(np.int64)
    qaug = np.stack([np.ones(S), np.ones(S), (qpos % 128), (qpos // 128) * 128]).astype(np.float32)
    kpos = np.concatenate([NM + np.arange(S), np.arange(NM)]).astype(np.int64)
    base = np.stack([(kpos % 128), (kpos // 128) * 128, -np.ones(T), -np.ones(T)]).astype(np.float32)
    kaug = np.zeros((NH, 4, T), np.float32)
    for h in range(NH):
        c = 8.0 * 2.0 ** (-(h + 1))
        kaug[h] = base * c
    return ident, tri, qaug, kaug


_NC_CACHE = {}


def kernel(x, meta, norm1_g, w_in, lambda_q1, lambda_k1, lambda_q2, lambda_k2, head_g, conv_w, w_out,
           norm2_g, w_gate, w_up, w_down, norm_f_g):
    f = lambda a: np.ascontiguousarray(np.asarray(a, dtype=np.float32))
    x = f(x)
    B = x.shape[0]
    if "nc" not in _NC_CACHE:
        _NC_CACHE["nc"] = build_program()
    nc = _NC_CACHE["nc"]
    ident, tri, qaug, kaug = _consts()
    shared = {
        "meta": f(meta), "norm1_g": f(norm1_g).reshape(1, D), "w_in": f(w_in)[0],
        "lambda_q1": f(lambda_q1).reshape(1, 64), "lambda_k1": f(lambda_k1).reshape(1, 64),
        "lambda_q2": f(lambda_q2).reshape(1, 64), "lambda_k2": f(lambda_k2).reshape(1, 64),
        "head_g": f(head_g).reshape(1, 128), "conv_w": f(conv_w)[0], "w_out": f(w_out)[0],
        "norm2_g": f(norm2_g).reshape(1, D), "w_gate": f(w_gate)[0], "w_up": f(w_up)[0], "w_down": f(w_down)[0],
        "norm_f_g": f(norm_f_g).reshape(1, D),
        "c_ident": ident, "c_tri": tri, "c_qaug": qaug, "c_kaug": kaug,
    }
    in_maps = [dict(shared, x=x[b]) for b in range(B)]
    res = run_bass_kernel_spmd(nc, in_maps, core_ids=list(range(B)))
    return np.stack([r["out"] for r in res.results], axis=0)
```

```python
import contextlib
import numpy as np
import concourse.bass as bass
import concourse.mybir as mybir
from concourse.bass_utils import run_bass_kernel_spmd

F32 = mybir.dt.float32
BF16 = mybir.dt.bfloat16
AF = mybir.ActivationFunctionType
ALU = mybir.AluOpType

D = 2048
S = 2048
NM = 16
T = S + NM
DFF = 5632
NH = 8
INW = 6144
NFF = DFF // 128
LAMBDA_INIT = 0.8 - 0.6 * 1.0
EPS = 1e-6
HEPS = 1e-5

OFF_A = 0
SZ_A = 16 * T
OFF_M = OFF_A + SZ_A
SZ_M = 16 * S
OFF_R = OFF_M + SZ_M
RING = 6
OFF_X = OFF_R + RING * 2048
SZ_X = 24576
ARENA = OFF_X + SZ_X


ANNOTATE = False


class Ev:
    def __init__(self, sem, step=1):
        self.sem, self.step, self.n = sem, step, 0

    def fire(self):
        self.n += self.step
        return (self.sem, self.n)


class Prog:
    def __init__(self):
        self.q = {k: [] for k in ("pe", "act", "dve", "pool", "sp")}
        self.last = {k: None for k in self.q}

    def op(self, eng, fn, waits=(), ev=None):
        tok = ev.fire() if ev is not None else None
        ws = []
        for w in waits:
            if w is None:
                continue
            if isinstance(w, list):
                ws.extend([u for u in w if u is not None])
            else:
                ws.append(w)
        self.q[eng].append((fn, ws, ev))
        if tok is not None:
            self.last[eng] = tok
        return tok

    def emit(self, eng, handle):
        waited = {}
        for fn, ws, ev in self.q[eng]:
            for sem, val in ws:
                key = id(sem)
                if waited.get(key, 0) >= val:
                    continue
                waited[key] = val
                handle.wait_ge(sem, val)
            ins = fn(handle)
            if ANNOTATE:
                ins.annotate("L%d" % fn.__code__.co_firstlineno)
            if ev is not None:
                ins.then_inc(ev.sem, ev.step)


class Bank:
    def __init__(self, ps, ev_pe):
        self.ps = ps
        self.ev = ev_pe
        self.free = []

    def acquire(self):
        f = self.free
        self.free = []
        return f


def build_program(debug=False):
    nc = bass.Bass("TRN2", target_bir_lowering=False)

    def din(name, shape):
        return nc.dram_tensor(name, shape, F32, kind="ExternalInput").ap()

    x = din("x", [S, D])
    meta = din("meta", [NM, D])
    g1 = din("norm1_g", [1, D])
    w_in = din("w_in", [D, INW])
    lq1 = din("lambda_q1", [1, 64])
    lk1 = din("lambda_k1", [1, 64])
    lq2 = din("lambda_q2", [1, 64])
    lk2 = din("lambda_k2", [1, 64])
    head_g = din("head_g", [1, 128])
    conv_w = din("conv_w", [3, 1024])
    w_out = din("w_out", [D, D])
    g2 = din("norm2_g", [1, D])
    w_gate = din("w_gate", [D, DFF])
    w_up = din("w_up", [D, DFF])
    w_down = din("w_down", [DFF, D])
    gf = din("norm_f_g", [1, D])
    c_ident = din("c_ident", [128, 128])
    c_tri = din("c_tri", [128, 128])
    c_qaug = din("c_qaug", [4, S])
    c_kaug = din("c_kaug", [NH, 4, T])
    out = nc.dram_tensor("out", [S, D], F32, kind="ExternalOutput").ap()
    if debug:
        dbg_aT = nc.dram_tensor("dbg_aT", [128, 16 * T], BF16, kind="ExternalOutput").ap()
        dbg_mix = nc.dram_tensor("dbg_mix", [128, 16 * S], BF16, kind="ExternalOutput").ap()
        dbg_h1 = nc.dram_tensor("dbg_h1", [128, 8 * D], F32, kind="ExternalOutput").ap()

    w_in_v = w_in.rearrange("(k p) n -> p k n", p=128)
    w_out_v = w_out.rearrange("(k p) n -> p k n", p=128)
    w_gate_v = w_gate.rearrange("(k p) n -> p k n", p=128)
    w_up_v = w_up.rearrange("(k p) n -> p k n", p=128)

    P = Prog()
    with contextlib.ExitStack() as es:
        def sb(name, shape, dt):
            return es.enter_context(nc.sbuf_tensor(name, shape, dt))

        nsem = [0]

        def new_ev(step=1):
            nsem[0] += 1
            return Ev(es.enter_context(nc.semaphore(f"s{nsem[0]}")), step)

        arena = sb("arena", [128, ARENA], BF16)
        ident = sb("ident", [128, 128], BF16)
        tri = sb("tri", [128, 128], BF16)
        ones = sb("ones", [128, 128], BF16)
        stat = sb("stat", [128, 4 * 17 + 4 * 16 + 4 * 16], F32)
        lamt = sb("lamt", [128, 4 * 64 + 64 + 16], F32)
        cw = sb("cw", [128, 3, 8], F32)
        hg = sb("hg", [128, 4], F32)

        pss = [es.enter_context(nc.psum_tensor(f"ps{i}", [128, 512], F32)) for i in range(8)]
        banks = [Bank(pss[i], new_ev()) for i in range(8)]

        def tview(bank):
            return bank.ps[:, :].bitcast(BF16).rearrange("p (j n) -> p j n", j=8)

        def av(off, n):
            return arena[:, off:off + n]

        aT = av(OFF_A, SZ_A).rearrange("p (k n) -> p k n", k=16)
        h1 = av(OFF_A, 8 * D * 2).bitcast(F32).rearrange("p (s n) -> p s n", s=8)
        mixT = av(OFF_M, SZ_M).rearrange("p (k n) -> p k n", k=16)
        ring_slots = [av(OFF_R + i * 2048, 2048) for i in range(RING)]
        ring_ready = [new_ev(16) for _ in range(RING)]
        ring_free = [new_ev(1) for _ in range(RING)]
        ring_last = [None] * RING
        ring_n = [0]

        def xv(off_bytes, nbytes, dt=BF16):
            a = av(OFF_X + off_bytes // 2, nbytes // 2)
            return a.bitcast(F32) if dt == F32 else a

        def mv(off_bytes, nbytes, dt=BF16):
            a = av(OFF_M + off_bytes // 2, nbytes // 2)
            return a.bitcast(F32) if dt == F32 else a

        xs = [mv(i * 8192, 8192, F32) for i in range(4)]
        at = [mv(32768, 4096), mv(36864, 4096)]
        g1bc = mv(40960, 8192, F32)
        junkA = mv(49152, 4096)
        QK = 4160
        qA = xv(0, 4128)
        qB = xv(QK, 4128)
        kA = xv(2 * QK, 4128)
        kB = xv(3 * QK, 4128)
        vF = xv(4 * QK, 4128)
        vh = xv(5 * QK, 4352).rearrange("p (b n) -> p b n", b=17)
        o_pt = 5 * QK + 4352
        Pt = [xv(o_pt + i * 1024, 1024) for i in range(4)]
        o_fin = o_pt + 4096
        oc = [xv(o_fin, 2048, F32), xv(o_fin + 2048, 2048, F32)]
        lc = [xv(o_fin + 4096, 2048, F32), xv(o_fin + 6144, 2048, F32)]
        sqb = xv(o_fin + 8192, 1024)
        o_cv = o_fin + 8192 + 1024
        cgS = [xv(o_cv, 2048, F32), xv(o_cv + 2048, 2048, F32)]
        ubuf = xv(o_cv + 4096, 2064, F32)
        ybuf = [xv(o_cv + 6160, 2048, F32), xv(o_cv + 8208, 2048, F32)]
        assert o_cv + 10256 <= 49152
        xr = [xv(0, 2048, F32), xv(2048, 2048, F32)]
        ft = [xv(4096, 4096), xv(8192, 4096), xv(40960, 4096)]
        ost = [xv(4096, 8192, F32), xv(40960, 8192, F32)]
        g2bc = xv(12288, 8192, F32)
        gfbc = xv(20480, 8192, F32)
        sg = [xv(28672, 2048, F32), xv(30720, 2048, F32)]
        actT = [[xv(32768 + (g * 2 + c) * 2048, 2048) for c in range(2)] for g in range(2)]
        junkC = xv(32768, 4096)

        ssA, vvA, lnA, rsA = (stat[:, i * 17:(i + 1) * 17] for i in range(4))
        o2 = 68
        ssC, vvC, lnC, rsC = (stat[:, o2 + i * 16:o2 + (i + 1) * 16] for i in range(4))
        o3 = o2 + 64
        ssF, vvF, lnF, rsF = (stat[:, o3 + i * 16:o3 + (i + 1) * 16] for i in range(4))

        ev_sp = new_ev(16)
        ev_spx = [new_ev(16) for _ in range(4)]
        ev_act = new_ev(1)
        ev_dve = new_ev(1)
        ev_pool = new_ev(16)
        ev_c = new_ev(16)
        ev_qa = new_ev(16)
        ev_ka = new_ev(16)
        ev_out = new_ev(16)
        ev_outb = [new_ev(16), new_ev(16)]
        ev_rel = new_ev(1)
        ev_poolc = new_ev(1)
        ev_xh = [new_ev(16) for _ in range(8)]
        E_tok = {}

        class Slot:
            pass

        def ring_load(src_ap, shape3=None):
            i = ring_n[0]
            ring_n[0] += 1
            si = i % RING
            slot = ring_slots[si]
            dst = slot if shape3 is None else slot.rearrange("p (k n) -> p k n", k=shape3)
            waits = []
            if i >= RING:
                assert ring_last[si] is not None, "ring slot reused before its last reader was emitted"
                waits = [ring_last[si]]
                ring_last[si] = None
            tok = P.op("pool", lambda e, d=dst, s=src_ap: e.dma_start(out=d, in_=s), waits, ev=ring_ready[si])
            sl = Slot()
            sl.ap, sl.tok, sl.si = dst, tok, si
            return sl

        def ring_release(sl, tok):
            ring_last[sl.si] = tok

        t_c = []
        P.op("pool", lambda e: e.dma_start(out=ident[:, :], in_=c_ident), ev=ev_c)
        t_c.append(P.op("pool", lambda e: e.dma_start(out=tri[:, :], in_=c_tri), ev=ev_c))
        t_ones = P.op("dve", lambda e: e.memset(ones[:, :], 1.0), ev=ev_dve)
        epsc = hg[:, 2:3]
        t_eps = P.op("dve", lambda e: e.memset(hg[:, 2:3], EPS), ev=ev_dve)
        t_g1 = P.op("sp", lambda e: e.dma_start(out=g1bc, in_=g1.broadcast_to([128, D])), ev=ev_sp)

        tb_rr = [0]

        def norm_a(src, Pn, col, ss, vv, junk, src_tok):
            t1 = P.op("act", lambda e: e.activation(out=junk[0:Pn, :], in_=src, func=AF.Square, accum_out=ss[0:Pn, col:col + 1]),
                      waits=[src_tok], ev=ev_act)
            return t1

        def norm_b(Pn, col, vv, ln, rs, t1):
            t2b = P.op("act", lambda e: e.activation(out=ln[0:Pn, col:col + 1], in_=vv[0:Pn, col:col + 1], func=AF.Ln, bias=epsc[0:Pn, 0:1], scale=1.0 / D),
                       waits=[t1, t_eps], ev=ev_act)
            t3 = P.op("act", lambda e: e.activation(out=rs[0:Pn, col:col + 1], in_=ln[0:Pn, col:col + 1], func=AF.Exp, scale=-0.5), waits=[t2b], ev=ev_act)
            return t3

        def norm_chain(src, Pn, col, ss, vv, ln, rs, junk, src_tok):
            return norm_b(Pn, col, ss, ln, rs, norm_a(src, Pn, col, ss, vv, junk, src_tok))

        def transposes(a_tile, Pn, dst, c0, tbanks, a_tok):
            evs = []
            pe_tok = None
            for hf in range(2):
                bank = tbanks[tb_rr[0] % len(tbanks)]
                tb_rr[0] += 1
                tv = tview(bank)
                fr = bank.acquire()
                for j in range(8):
                    k = hf * 8 + j
                    last = (j == 7)
                    pe_tok = P.op("pe", lambda e, tv=tv, j=j, k=k: e.transpose(out=tv[:, j, 0:Pn], in_=a_tile[0:Pn, k * 128:(k + 1) * 128],
                                                                             identity=ident[0:Pn, 0:Pn]),
                                  waits=[a_tok, fr, t_c] if j == 0 else [], ev=bank.ev if last else None)
                eng = "act" if hf == 0 else "dve"
                if eng == "act":
                    tk = P.op("act", lambda e, tv=tv, hf=hf: e.copy(out=dst[:, hf * 8:(hf + 1) * 8, c0:c0 + Pn], in_=tv[:, :, 0:Pn]),
                              waits=[pe_tok], ev=ev_act)
                else:
                    tk = P.op("dve", lambda e, tv=tv, hf=hf: e.tensor_copy(out=dst[:, hf * 8:(hf + 1) * 8, c0:c0 + Pn], in_=tv[:, :, 0:Pn]),
                              waits=[pe_tok], ev=ev_dve)
                bank.free.append(tk)
                evs.append(tk)
            return pe_tok, evs

        mm_rr = [0]
        TOK_TILES = [(n * 512, 512) for n in range(4)] + [(S, NM)]
        mmB = banks[0:2]
        LB = banks[2:4]
        SB = banks[4:6]
        OB = banks[6:8]

        def acc16(bank, N, lhs_fn, rhs_fn, first_waits, last_ev):
            fr = bank.acquire()
            tok = None
            for k in range(16):
                l_ap = lhs_fn(k)
                r_ap = rhs_fn(k)
                tok = P.op("pe", lambda e, l_ap=l_ap, r_ap=r_ap, k=k: e.matmul(bank.ps[:, 0:N], lhsT=l_ap, rhs=r_ap, start=(k == 0), stop=(k == 15)),
                           waits=[fr, first_waits] if k == 0 else [], ev=(last_ev if k == 15 else None))
            return tok

        def proj_chunk(col0, tiles, consumer, first_waits):
            sl = ring_load(w_in_v[:, :, col0:col0 + 128], 16)
            for ti, (c0, N) in enumerate(tiles):
                bank = mmB[mm_rr[0] % 2]
                mm_rr[0] += 1
                tok = acc16(bank, N, lambda k: sl.ap[:, k, :], lambda k, c0=c0, N=N: aT[:, k, c0:c0 + N],
                            [sl.tok] + list(first_waits), bank.ev)
                consumer(ti, c0, N, bank, tok)
            ring_release(sl, tok)

        def evac_split(dstA, dstB):
            def cons(ti, c0, N, bank, tok):
                t1 = P.op("act", lambda e: e.copy(out=dstA[0:64, c0:c0 + N], in_=bank.ps[0:64, 0:N]), waits=[tok], ev=ev_act)
                t2 = P.op("dve", lambda e: e.tensor_copy(out=dstB[0:64, c0:c0 + N], in_=bank.ps[64:128, 0:N]), waits=[tok], ev=ev_dve)
                bank.free += [t1, t2]
            return cons

        vrr = [0]

        def evac_v(ti, c0, N, bank, tok):
            vrr[0] += 1
            if vrr[0] % 2:
                t1 = P.op("act", lambda e: e.copy(out=vF[:, c0:c0 + N], in_=bank.ps[:, 0:N]), waits=[tok], ev=ev_act)
            else:
                t1 = P.op("dve", lambda e: e.tensor_copy(out=vF[:, c0:c0 + N], in_=bank.ps[:, 0:N]), waits=[tok], ev=ev_dve)
            bank.free.append(t1)


        def proj_chunk_gen(col0, tiles, consumer, waits_box):
            sl = ring_load(w_in_v[:, :, col0:col0 + 128], 16)
            tok = None
            for ti, (c0, N) in enumerate(tiles):
                bank = mmB[mm_rr[0] % 2]
                mm_rr[0] += 1
                tok = acc16(bank, N, lambda k: sl.ap[:, k, :], lambda k, c0=c0, N=N: aT[:, k, c0:c0 + N],
                            [sl.tok] + list(waits_box[0]), bank.ev)
                consumer(ti, c0, N, bank, tok)
                if ti == len(tiles) - 1:
                    ring_release(sl, tok)
                yield

        a_done = {}
        t_done = {}

        vvtok = {}
        ld_tok = {}

        def stageA1a(i):
            b = i % 4
            Pn = 128 if i < 16 else NM
            src_rows = x[i * 128:(i + 1) * 128, :] if i < 16 else meta
            w = [a_done[i - 4]] if i >= 4 else []
            tld = P.op("sp", lambda e: e.dma_start(out=xs[b][0:Pn, :], in_=src_rows), waits=w, ev=ev_spx[b])
            ld_tok[i] = tld

        def stageA1s(i):
            b = i % 4
            Pn = 128 if i < 16 else NM
            vvtok[i] = norm_a(xs[b][0:Pn, :], Pn, i, ssA, vvA, junkA, ld_tok[i])

        trsA = {}

        def stageA1n(i):
            Pn = 128 if i < 16 else NM
            trsA[i] = norm_b(Pn, i, ssA, lnA, rsA, vvtok[i])

        def stageA1b(i):
            b = i % 4
            ab = i % 2
            Pn = 128 if i < 16 else NM
            w = [trsA[i], t_g1] + ([t_done[i - 2]] if i >= 2 else [])
            a_done[i] = P.op("dve", lambda e: e.scalar_tensor_tensor(out=at[ab][0:Pn, :], in0=xs[b][0:Pn, :], scalar=rsA[0:Pn, i:i + 1],
                                                                      in1=g1bc[0:Pn, :], op0=ALU.mult, op1=ALU.mult), waits=w, ev=ev_dve)

        def stageA2(i):
            Pn = 128 if i < 16 else NM
            c0 = i * 128 if i < 16 else S
            pt, evs = transposes(at[i % 2], Pn, aT, c0, banks[2:6], a_done[i])
            t_done[i] = pt

        wbox = [[]]
        g_q0 = proj_chunk_gen(0, TOK_TILES[0:4], evac_split(qA, qB), wbox)
        g_k0 = proj_chunk_gen(1024, TOK_TILES, evac_split(kA, kB), wbox)
        g_v0 = proj_chunk_gen(2048, TOK_TILES, evac_v, wbox)
        grp_tok = {}

        pend0 = []

        def step_one():
            if pend0:
                g, n = pend0.pop(0)
                wbox[0] = grp_tok[n]
                next(g)

        for i in range(3):
            stageA1a(i)
        stageA1s(0)
        stageA1n(0)
        stageA1b(0)
        for i in range(1, 17):
            if i + 2 < 17:
                stageA1a(i + 2)
            stageA1s(i)
            stageA1n(i)
            stageA1b(i)
            stageA2(i - 1)
            if (i - 1) % 4 == 3:
                n = (i - 1) // 4
                grp_tok[n] = [P.last["act"], P.last["dve"]]
                pend0.extend([(g_q0, n), (g_k0, n), (g_v0, n)])
            else:
                step_one()
        stageA2(16)
        grp_tok[4] = [P.last["act"], P.last["dve"]]
        pend0.extend([(g_k0, 4), (g_v0, 4)])
        while pend0:
            step_one()
        for g in (g_q0, g_k0, g_v0):
            for _ in g:
                pass
        lqv = lamt[:, 0:256].rearrange("p (a n) -> p a n", a=4)
        for i, src in enumerate((lq1, lk1, lq2, lk2)):
            tl = P.op("sp", lambda e, i=i, src=src: e.dma_start(out=lqv[:, i, :], in_=src.broadcast_to([128, 64])), ev=ev_sp)
        t_hg = P.op("sp", lambda e: e.dma_start(out=hg[:, 0:1], in_=head_g.rearrange("o d -> d o")), ev=ev_sp)
        for t in range(3):
            t_cw = P.op("sp", lambda e, t=t: e.dma_start(out=cw[:, t, :], in_=conv_w[t:t + 1, :].rearrange("o (j p) -> p (o j)", p=128),
                                                       allow_slow_non_contiguous=True), ev=ev_sp)
        ljunk = lamt[:, 256:320]
        lsc = lamt[:, 320:336]
        P.op("dve", lambda e: e.scalar_tensor_tensor(out=ljunk, in0=lqv[:, 0, :], scalar=1.0, in1=lqv[:, 1, :], op0=ALU.mult,
                                                     op1=ALU.mult, accum_out=lsc[:, 0:1]), waits=[t_cw], ev=ev_dve)
        t_l = P.op("dve", lambda e: e.scalar_tensor_tensor(out=ljunk, in0=lqv[:, 2, :], scalar=1.0, in1=lqv[:, 3, :], op0=ALU.mult,
                                                           op1=ALU.mult, accum_out=lsc[:, 1:2]), ev=ev_dve)
        t_le = P.op("act", lambda e: e.activation(out=lsc[:, 2:4], in_=lsc[:, 0:2], func=AF.Exp), waits=[t_l], ev=ev_act)
        t_l2 = P.op("dve", lambda e: e.tensor_tensor(out=lsc[:, 4:5], in0=lsc[:, 3:4], in1=lsc[:, 2:3], op=ALU.subtract), waits=[t_le], ev=ev_dve)
        t_l3 = P.op("dve", lambda e: e.tensor_scalar(out=lsc[:, 5:6], in0=lsc[:, 4:5], scalar1=-LAMBDA_INIT, scalar2=None, op0=ALU.add), waits=[t_l2], ev=ev_dve)
        nlam = lsc[:, 5:6]
        t_hgs = P.op("dve", lambda e: e.tensor_scalar(out=hg[:, 1:2], in0=hg[:, 0:1], scalar1=1.0 - LAMBDA_INIT, scalar2=None, op0=ALU.mult), ev=ev_dve)
        hgs = hg[:, 1:2]
        tokA = [P.last["act"], P.last["dve"], P.last["pe"]]
        if debug:
            P.op("sp", lambda e: e.dma_start(out=dbg_aT, in_=av(OFF_A, SZ_A)), waits=tokA, ev=ev_out)

        cg_rd = {}
        u_tok = [None]
        aT_dead = [None]

        def conv_units(j, slots):
            s_cg, s_hi, s_bg = slots
            order = [4, 0, 1, 2, 3]
            for oi, ti in enumerate(order):
                c0, N = TOK_TILES[ti]
                cb = oi % 2
                bank = mmB[mm_rr[0] % 2]
                mm_rr[0] += 1
                tok = acc16(bank, N, lambda k: s_cg.ap[:, k, :], lambda k: aT[:, k, c0:c0 + N], [s_cg.tok], bank.ev)
                if oi == 4:
                    ring_release(s_cg, tok)
                d_ap, s_ap = cgS[cb][:, 0:N], bank.ps[:, 0:N]
                t1 = P.op("act", lambda e, d_ap=d_ap, s_ap=s_ap: e.copy(out=d_ap, in_=s_ap), waits=[tok, cg_rd.get(cb)], ev=ev_act)
                bank.free.append(t1)
                yield
                bank = mmB[mm_rr[0] % 2]
                mm_rr[0] += 1
                tok = acc16(bank, N, lambda k: s_hi.ap[:, k, :], lambda k: aT[:, k, c0:c0 + N], [s_hi.tok], bank.ev)
                if oi == 4:
                    ring_release(s_hi, tok)
                if ti == 4:
                    a_ap, b_ap = cgS[cb][:, NM - 2:NM], bank.ps[:, NM - 2:NM]
                    t2 = P.op("dve", lambda e, a_ap=a_ap, b_ap=b_ap: e.tensor_tensor(out=ubuf[:, 0:2], in0=a_ap, in1=b_ap, op=ALU.mult),
                              waits=[t1, tok, u_tok[0]], ev=ev_dve)
                    bank.free.append(t2)
                    cg_rd[cb] = t2
                    u_tok[0] = t2
                    yield
                    continue
                a_ap, b_ap = cgS[cb][:, :], bank.ps[:, :]
                t2 = P.op("dve", lambda e, a_ap=a_ap, b_ap=b_ap: e.tensor_tensor(out=ubuf[:, 2:514], in0=a_ap, in1=b_ap, op=ALU.mult),
                          waits=[t1, tok, u_tok[0]], ev=ev_dve)
                bank.free.append(t2)
                cg_rd[cb] = t2
                yield
                bank = mmB[mm_rr[0] % 2]
                mm_rr[0] += 1
                tok = acc16(bank, N, lambda k: s_bg.ap[:, k, :], lambda k: aT[:, k, c0:c0 + N], [s_bg.tok], bank.ev)
                if oi == 4:
                    ring_release(s_bg, tok)
                    aT_dead[0] = tok
                yb = ybuf[cb]
                n = ti
                w0, w1, w2 = cw[:, 0, j:j + 1], cw[:, 1, j:j + 1], cw[:, 2, j:j + 1]
                y1 = P.op("dve", lambda e, yb=yb, w0=w0: e.tensor_scalar(out=yb, in0=ubuf[:, 0:512], scalar1=w0, scalar2=None, op0=ALU.mult),
                          waits=[t2, t_cw], ev=ev_dve)
                y2 = P.op("dve", lambda e, yb=yb, w1=w1: e.scalar_tensor_tensor(out=yb, in0=ubuf[:, 1:513], scalar=w1, in1=yb, op0=ALU.mult, op1=ALU.add),
                          waits=[y1], ev=ev_dve)
                y3 = P.op("dve", lambda e, yb=yb, w2=w2: e.scalar_tensor_tensor(out=yb, in0=ubuf[:, 2:514], scalar=w2, in1=yb, op0=ALU.mult, op1=ALU.add),
                          waits=[y2], ev=ev_dve)
                o_ap, b_ap = mixT[:, 8 + j, n * 512:(n + 1) * 512], bank.ps[:, :]
                t3 = P.op("dve", lambda e, yb=yb, o_ap=o_ap, b_ap=b_ap: e.tensor_tensor(out=o_ap, in0=yb, in1=b_ap, op=ALU.mult),
                          waits=[tok, y3], ev=ev_dve)
                bank.free.append(t3)
                u_tok[0] = P.op("dve", lambda e: e.tensor_copy(out=ubuf[:, 0:2], in_=ubuf[:, 512:514]), waits=[y3], ev=ev_dve)
                yield

        s_rr = [0]
        pt_rr = [0]
        pt_free = [None] * 4
        prev_head_pe = [None]
        fin_tok = [None]
        t_qa = None
        deferred = []

        pull_fn = [None]

        def tick(force=False, tag=None):
            for item in list(deferred):
                item[0] -= 1
                if (force and (tag is None or item[2] == tag)) or item[0] <= 0:
                    deferred.remove(item)
                    if not force and pull_fn[0] is not None and item[2] in ("b", "d"):
                        pull_fn[0]()
                    item[1]()

        for h in range(NH):
            fw = tokA if h == 0 else []
            if h > 0:
                proj_chunk(h * 128, TOK_TILES[0:4], evac_split(qA, qB), fw)
                tick(force=True)
                proj_chunk(2048 + h * 128, TOK_TILES, evac_v, [])
                proj_chunk(1024 + h * 128, TOK_TILES, evac_split(kA, kB), [])
            t_v = [P.last["act"], P.last["dve"]]
            tv_toks = []
            for g0 in range(0, 17, 8):
                bank = mmB[mm_rr[0] % 2]
                mm_rr[0] += 1
                tv = tview(bank)
                fr = bank.acquire()
                nb = min(8, 17 - g0)
                tok = None
                for j in range(nb):
                    blk = g0 + j
                    Pn = 128 if blk < 16 else NM
                    tok = P.op("pe", lambda e, tv=tv, j=j, blk=blk, Pn=Pn: e.transpose(out=tv[0:Pn, j, :], in_=vF[:, blk * 128:blk * 128 + Pn],
                                                                                     identity=ident[:, :]),
                               waits=[t_v, fr] if j == 0 else [], ev=bank.ev if j == nb - 1 else None)
                if nb == 8:
                    tk = P.op("dve", lambda e, tv=tv, g0=g0: e.tensor_copy(out=vh[:, g0:g0 + 8, :], in_=tv[:, :, :]), waits=[tok], ev=ev_dve)
                else:
                    tk = P.op("dve", lambda e, tv=tv: e.tensor_copy(out=vh[0:NM, 16, :], in_=tv[0:NM, 0, :]), waits=[tok], ev=ev_dve)
                bank.free.append(tk)
                tv_toks.append(tk)
            t_qk = [P.last["act"], P.last["dve"]]
            conv_slots = [ring_load(w_in_v[:, :, 4096 + h * 128:4096 + (h + 1) * 128], 16),
                          ring_load(w_in_v[:, :, 5120 + h * 128:5120 + (h + 1) * 128], 16),
                          ring_load(w_in_v[:, :, 3072 + h * 128:3072 + (h + 1) * 128], 16)]
            if h == 0:
                P.op("pool", lambda e: e.dma_start(out=qA[64:68, 0:S], in_=c_qaug), ev=ev_qa)
                t_qa = P.op("pool", lambda e: e.dma_start(out=qB[64:68, 0:S], in_=c_qaug), ev=ev_qa)
            P.op("pool", lambda e, h=h: e.dma_start(out=kA[64:68, 0:T], in_=c_kaug[h]), waits=[prev_head_pe[0]], ev=ev_ka)
            t_ka = P.op("pool", lambda e, h=h: e.dma_start(out=kB[64:68, 0:T], in_=c_kaug[h]), ev=ev_ka)
            cgen = conv_units(h, conv_slots)
            tile_ctr = [0]
            prev_head_pe_tok = P.last["pe"]

            def pull():
                try:
                    next(cgen)
                except StopIteration:
                    pass

            pull_fn[0] = pull
            pull()
            pull()
            for Q in range(4):
                blocks = [("m", S, NM, 16)] + [(j, j * 128, 128, j) for j in range(4 * Q + 4)]
                tiles = []
                for (bid, kc0, nk, vb) in blocks:
                    qlo = 0
                    diag = False
                    if bid != "m" and bid >= 4 * Q:
                        qlo = (bid - 4 * Q) * 128
                        diag = True
                    for m in range(2):
                        tiles.append((bid, kc0, nk, vb, qlo, diag, m))
                nt = len(tiles)
                pend = []
                fro = [OB[0].acquire(), OB[1].acquire()]
                frl = [LB[0].acquire(), LB[1].acquire()]
                last_pv = [None, None]
                last_l = [None, None]

                def issue_S(idx):
                    (bid, kc0, nk, vb, qlo, diag, m) = tiles[idx]
                    kM = kA if m == 0 else kB
                    qM = qA if m == 0 else qB
                    sbk = SB[s_rr[0] % 2]
                    s_rr[0] += 1
                    pi = pt_rr[0] % 4
                    pt_rr[0] += 1
                    fr = sbk.acquire()
                    rhs_ap = qM[0:68, Q * 512 + qlo:(Q + 1) * 512]
                    ts = P.op("pe", lambda e: e.matmul(sbk.ps[0:nk, qlo:512], lhsT=kM[0:68, kc0:kc0 + nk], rhs=rhs_ap, start=True, stop=True),
                              waits=[fr, t_qk, t_ka, t_qa, tv_toks], ev=sbk.ev)
                    te = P.op("act", lambda e: e.activation(out=Pt[pi][0:nk, qlo:512], in_=sbk.ps[0:nk, qlo:512], func=AF.Exp, scale=0.125),
                              waits=[ts, pt_free[pi]], ev=ev_act)
                    sbk.free.append(te)
                    rdy = te
                    if diag:
                        rdy = P.op("dve", lambda e: e.tensor_tensor(out=Pt[pi][0:nk, qlo:qlo + 128], in0=Pt[pi][0:nk, qlo:qlo + 128], in1=tri[:, :],
                                                                    op=ALU.mult), waits=[te], ev=ev_dve)
                    pend.append((idx, pi, rdy))

                def issue_PV(idx, pi, rdy):
                    (bid, kc0, nk, vb, qlo, diag, m) = tiles[idx]
                    first = (idx < 2)
                    lastm = (idx >= nt - 2)
                    vsrc = vh[0:nk, vb, :]
                    t1 = P.op("pe", lambda e: e.matmul(OB[m].ps[:, qlo:512], lhsT=vsrc, rhs=Pt[pi][0:nk, qlo:512], start=first, stop=lastm),
                              waits=[rdy, fro[m] if first else None], ev=OB[m].ev if lastm else None)
                    t2 = P.op("pe", lambda e: e.matmul(LB[m].ps[:, qlo:512], lhsT=ones[0:nk, :], rhs=Pt[pi][0:nk, qlo:512], start=first, stop=lastm),
                              waits=[frl[m] if first else None, t_ones], ev=LB[m].ev)
                    pt_free[pi] = t2
                    if lastm:
                        last_pv[m] = t1
                        last_l[m] = t2

                for idx in range(nt):
                    issue_S(idx)
                    if idx == 1 and Q > 0:
                        pull()
                    if len(pend) > 2:
                        issue_PV(*pend.pop(0))
                        tick()
                        tile_ctr[0] += 1
                        if tile_ctr[0] % 12 == 6:
                            pull()
                while pend:
                    issue_PV(*pend.pop(0))

                t_lnl = []
                for m in range(2):
                    tk = P.op("act", lambda e, m=m: e.activation(out=lc[m], in_=LB[m].ps[:, :], func=AF.Ln), waits=[last_l[m], fin_tok[0]], ev=ev_act)
                    LB[m].free.append(tk)
                    t_lnl.append(tk)
                t_ocs = []
                for m in range(2):
                    tk = P.op("dve", lambda e, m=m: e.tensor_copy(out=oc[m], in_=OB[m].ps[:, :]), waits=[last_pv[m], last_l[m], fin_tok[0]], ev=ev_dve)
                    OB[m].free.append(tk)
                    t_ocs.append(tk)
                st = {}

                def fin1b(t_lnl=t_lnl, t_ocs=t_ocs, st=st):
                    t_rl = []
                    for m in range(2):
                        t_rl.append(P.op("act", lambda e, m=m: e.activation(out=lc[m], in_=lc[m], func=AF.Exp, scale=-1.0), waits=[t_lnl[m]], ev=ev_act))
                    c3 = P.op("dve", lambda e: e.tensor_tensor(out=oc[0], in0=oc[0], in1=lc[0], op=ALU.mult), waits=[t_ocs[0], t_rl[0]], ev=ev_dve)
                    c4 = P.op("dve", lambda e: e.tensor_tensor(out=oc[1], in0=oc[1], in1=lc[1], op=ALU.mult), waits=[t_ocs[1], t_rl[1]], ev=ev_dve)
                    c5 = P.op("dve", lambda e: e.scalar_tensor_tensor(out=oc[0], in0=oc[1], scalar=nlam, in1=oc[0], op0=ALU.mult, op1=ALU.add),
                              waits=[c3, c4, t_l3, t_hgs], ev=ev_dve)
                    st["t_sq"] = P.op("dve", lambda e: e.tensor_tensor(out=sqb, in0=oc[0], in1=oc[0], op=ALU.mult), waits=[c5], ev=ev_dve)
                deferred.append([3, fin1b, "b"])

                def fin2a(st=st):
                    t_sq = st["t_sq"]
                    bank = mmB[mm_rr[0] % 2]
                    mm_rr[0] += 1
                    fr = bank.acquire()
                    t_ms = P.op("pe", lambda e: e.matmul(bank.ps[:, :], lhsT=ones[:, :], rhs=sqb, start=True, stop=True), waits=[t_sq, fr], ev=bank.ev)
                    t_vv = P.op("dve", lambda e: e.tensor_scalar(out=lc[0], in0=bank.ps[:, :], scalar1=1.0 / 128, scalar2=HEPS, op0=ALU.mult, op1=ALU.add),
                                waits=[t_ms], ev=ev_dve)
                    bank.free.append(t_vv)
                    st["t_vv"] = t_vv

                def fin2b(h=h, Q=Q, st=st):
                    t_vv = st["t_vv"]
                    t_ln = P.op("act", lambda e: e.activation(out=lc[0], in_=lc[0], func=AF.Ln), waits=[t_vv], ev=ev_act)
                    t_r = P.op("act", lambda e: e.activation(out=lc[0], in_=lc[0], func=AF.Exp, scale=-0.5), waits=[t_ln], ev=ev_act)
                    o_ap = mixT[:, h, Q * 512:(Q + 1) * 512]
                    fin_tok[0] = P.op("dve", lambda e: e.scalar_tensor_tensor(out=o_ap, in0=oc[0], scalar=hgs, in1=lc[0], op0=ALU.mult, op1=ALU.mult),
                                      waits=[t_r], ev=ev_dve)
                deferred.append([10, fin2a, "c"])
                deferred.append([13, fin2b, "d"])
            prev_head_pe[0] = P.last["pe"]
            pull_fn[0] = None
            for _ in cgen:
                pass
            tick(force=True, tag="b")

        tick(force=True)
        tokB = [P.last["act"], P.last["dve"], P.last["pe"]]
        if debug:
            P.op("sp", lambda e: e.dma_start(out=dbg_mix, in_=av(OFF_M, SZ_M)), waits=tokB, ev=ev_out)

        t_g = {}
        xr_free = [None, None]
        xr_n = [0]
        ft_free = [None, None, None]
        ost_free = [None, None]
        act_free = [[None, None], [None, None]]
        gu_b = banks[0:4]
        dn_b = banks[4:8]
        tb2 = banks[4:6]
        dn_rr = [0]
        out_toks = []
        NG = NFF // 2
        for hf in range(2):
            n2_trs = {}

            def norm2_chain(s):
                gs = hf * 8 + s
                n2_trs[s] = norm_chain(h1[:, s, :], 128, gs, ssC, vvC, lnC, rsC, junkC, P.last["dve"])

            def norm2_f(s):
                gs = hf * 8 + s
                b = s % 3
                ta = P.op("dve", lambda e: e.scalar_tensor_tensor(out=ft[b], in0=h1[:, s, :], scalar=rsC[:, gs:gs + 1], in1=g2bc,
                                                                  op0=ALU.mult, op1=ALU.mult), waits=[n2_trs[s], t_g["gf"], ft_free[b]], ev=ev_dve)
                return ta

            ft_ready = {}
            xq = []
            for dq in range(4):
                w = [aT_dead[0]] if hf == 0 else [E_tok[s] for s in range(8)]
                src_ap = x[hf * 1024:(hf + 1) * 1024, dq * 512:(dq + 1) * 512].rearrange("(s p) c -> p s c", p=128)
                xq.append(P.op("sp", lambda e, dq=dq, src_ap=src_ap: e.dma_start(out=h1[:, :, dq * 512:(dq + 1) * 512], in_=src_ap), waits=w, ev=ev_xh[dq]))
                if hf == 0 and dq == 0:
                    t_g["g2"] = P.op("sp", lambda e: e.dma_start(out=g2bc, in_=g2.broadcast_to([128, D])), waits=tokB, ev=ev_sp)
                    t_g["gf"] = P.op("sp", lambda e: e.dma_start(out=gfbc, in_=gf.broadcast_to([128, D])), ev=ev_sp)
            for dq in range(3):
                frs = [banks[s].acquire() for s in range(8)]
                toks = [None] * 8
                for kg in range(4):
                    sl = ring_load(w_out_v[:, kg * 4:(kg + 1) * 4, dq * 512:(dq + 1) * 512], 4)
                    tok = None
                    for s in range(8):
                        gs = hf * 8 + s
                        for kk in range(4):
                            k = kg * 4 + kk
                            l_ap = mixT[:, k, gs * 128:(gs + 1) * 128]
                            r_ap = sl.ap[:, kk, :]
                            bank = banks[s]
                            last = (k == 15)
                            rel = (s == 7 and kk == 3)
                            ev = bank.ev if last else (ev_rel if rel else None)
                            tok = P.op("pe", lambda e, bank=bank, l_ap=l_ap, r_ap=r_ap, k=k: e.matmul(bank.ps[:, :], lhsT=l_ap, rhs=r_ap, start=(k == 0), stop=(k == 15)),
                                       waits=[sl.tok, tokB, frs[s] if k == 0 else None], ev=ev)
                        if kg == 3:
                            toks[s] = tok
                    ring_release(sl, tok)
                for s in range(8):
                    dst = h1[:, s, dq * 512:(dq + 1) * 512]
                    td = P.op("dve", lambda e, s=s, dst=dst: e.tensor_tensor(out=dst, in0=banks[s].ps[:, :], in1=dst, op=ALU.add), waits=[toks[s], xq[dq]], ev=ev_dve)
                    banks[s].free.append(td)
            for dq in range(3, 4):
                slots = [ring_load(w_out_v[:, kg * 4:(kg + 1) * 4, dq * 512:(dq + 1) * 512], 4) for kg in range(4)]
                for s in range(8):
                    gs = hf * 8 + s
                    bank = gu_b[mm_rr[0] % 4]
                    mm_rr[0] += 1
                    tok = acc16(bank, 512, lambda k, gs=gs: mixT[:, k, gs * 128:(gs + 1) * 128], lambda k: slots[k // 4].ap[:, k % 4, :],
                                [sl.tok for sl in slots] + tokB, bank.ev)
                    dst = h1[:, s, dq * 512:(dq + 1) * 512]
                    td = P.op("dve", lambda e, bank=bank, dst=dst: e.tensor_tensor(out=dst, in0=bank.ps[:, :], in1=dst, op=ALU.add), waits=[tok, xq[dq]], ev=ev_dve)
                    bank.free.append(td)
                    norm2_chain(s)
                    if s >= 2:
                        pt, evs = transposes(ft[(s - 2) % 3], 128, mixT, (hf * 8 + s - 2) * 128, tb2, ft_ready[s - 2])
                        ft_free[(s - 2) % 3] = pt
                    ft_ready[s] = norm2_f(s)
                for sl in slots:
                    ring_release(sl, tok)
            for s2 in (6, 7):
                pt, evs = transposes(ft[s2 % 3], 128, mixT, (hf * 8 + s2) * 128, tb2, ft_ready[s2])
                ft_free[s2 % 3] = pt
            tokC = [P.last["act"], P.last["dve"]]

            sg_rr = [0]
            sg_free = [None, None]
            act_rdy = {}

            def GU(g):
                for ci in range(2):
                    c = 2 * g + ci
                    sl_g = ring_load(w_gate_v[:, :, c * 128:(c + 1) * 128], 16)
                    sl_u = ring_load(w_up_v[:, :, c * 128:(c + 1) * 128], 16)
                    for n in range(2):
                        c0 = hf * 1024 + n * 512
                        res = []
                        for wi, sl in enumerate((sl_g, sl_u)):
                            bank = gu_b[mm_rr[0] % 4]
                            mm_rr[0] += 1
                            tok = acc16(bank, 512, lambda k, sl=sl: sl.ap[:, k, :], lambda k: mixT[:, k, c0:c0 + 512], [sl.tok] + tokC, bank.ev)
                            if n == 1:
                                ring_release(sl, tok)
                            res.append((bank, tok))
                            if wi == 0:
                                yield
                        sb_i = sg_rr[0] % 2
                        sg_rr[0] += 1
                        (bg_, tg_), (bu_, tu_) = res
                        t1 = P.op("act", lambda e, bg_=bg_, sb_i=sb_i: e.activation(out=sg[sb_i], in_=bg_.ps[:, :], func=AF.Silu),
                                  waits=[tg_, sg_free[sb_i]], ev=ev_act)
                        bg_.free.append(t1)
                        dst = actT[g % 2][ci][:, n * 512:(n + 1) * 512]
                        t2 = P.op("dve", lambda e, bu_=bu_, sb_i=sb_i, dst=dst: e.tensor_tensor(out=dst, in0=sg[sb_i], in1=bu_.ps[:, :], op=ALU.mult),
                                  waits=[t1, tu_, act_free[g % 2][ci]], ev=ev_dve)
                        bu_.free.append(t2)
                        sg_free[sb_i] = t2
                        act_rdy[(g, ci)] = t2
                        yield

            def DOWN(gl, with_E):
                chunks = [(g, ci) for g in gl for ci in range(2)]
                wd = [ring_load(w_down[(2 * g + ci) * 128:(2 * g + ci + 1) * 128, :]) for (g, ci) in chunks]
                nch = len(chunks)
                tok = None
                for s in range(8):
                    for dq in range(4):
                        bank = dn_b[dn_rr[0] % 4]
                        dn_rr[0] += 1
                        fr = bank.acquire()
                        for j, (g, ci) in enumerate(chunks):
                            sl = wd[j]
                            l_ap = actT[g % 2][ci][:, s * 128:(s + 1) * 128]
                            r_ap = sl.ap[:, dq * 512:(dq + 1) * 512]
                            tok = P.op("pe", lambda e, bank=bank, j=j, l_ap=l_ap, r_ap=r_ap: e.matmul(bank.ps[:, :], lhsT=l_ap, rhs=r_ap, start=(j == 0), stop=(j == nch - 1)),
                                       waits=[sl.tok, act_rdy[(g, ci)], fr if j == 0 else None], ev=(bank.ev if j == nch - 1 else None))
                        dst = h1[:, s, dq * 512:(dq + 1) * 512]
                        td = P.op("dve", lambda e, bank=bank, dst=dst: e.tensor_tensor(out=dst, in0=bank.ps[:, :], in1=dst, op=ALU.add), waits=[tok], ev=ev_dve)
                        bank.free.append(td)
                        yield
                    if with_E:
                        E_a(s)
                        if s >= 1:
                            E_b(s - 1)
                if with_E:
                    E_b(7)
                for g in gl:
                    act_free[g % 2][0] = tok
                    act_free[g % 2][1] = tok
                for sl in wd:
                    ring_release(sl, tok)

            e_trs = {}

            def E_a(s):
                gs = hf * 8 + s
                e_trs[s] = norm_chain(h1[:, s, :], 128, gs, ssF, vvF, lnF, rsF, junkE, P.last["dve"])

            def E_b(s):
                gs = hf * 8 + s
                ob = s % 2
                to = P.op("dve", lambda e: e.scalar_tensor_tensor(out=ost[ob], in0=h1[:, s, :], scalar=rsF[:, gs:gs + 1], in1=gfbc,
                                                                  op0=ALU.mult, op1=ALU.mult), waits=[e_trs[s], t_g["gf"], ost_free[ob], P.last["pe"]], ev=ev_dve)
                tp = None
                E_tok[s] = [to]
                tw = P.op("sp", lambda e: e.dma_start(out=out[gs * 128:(gs + 1) * 128, :], in_=ost[ob]), waits=[to, tp], ev=ev_outb[ob])
                ost_free[ob] = tw
                out_toks.append(tw)

            junkE = xv(28672, 4096)

            def drain(gen, n):
                for _ in range(n):
                    try:
                        next(gen)
                    except StopIteration:
                        return

            dn = None
            for g in range(NG):
                gu = GU(g)
                sched = [6, 6, 6, 6, 4, 4, 0, 0]
                for u in range(8):
                    next(gu)
                    if dn is not None:
                        drain(dn, sched[u])
                for _ in gu:
                    pass
                if dn is not None:
                    for _ in dn:
                        pass
                dn = DOWN([g], False) if g < NG - 2 else None
            for _ in DOWN([NG - 2, NG - 1], True):
                pass
            ft_free = [ost_free[0], ost_free[0], ost_free[1]]
            act_free = [[None, None], [None, None]]
        if debug:
            P.op("sp", lambda e: e.dma_start(out=dbg_h1, in_=av(OFF_A, 8 * D * 2).bitcast(F32)), waits=[P.last["dve"], P.last["pe"]], ev=ev_out)

        with nc.Block() as block:
            @block.tensor
            def _(e):
                P.emit("pe", e)

            @block.scalar
            def _(e):
                P.emit("act", e)

            @block.vector
            def _(e):
                P.emit("dve", e)

            @block.gpsimd
            def _(e):
                P.emit("pool", e)

            @block.sync
            def _(e):
                P.emit("sp", e)
                for evo in (ev_outb[0], ev_outb[1], ev_out):
                    if evo.n > 0:
                        e.wait_ge(evo.sem, evo.n)
    return nc


def _consts():
    ident = np.eye(128, dtype=np.float32)
    tri = (np.arange(128)[:, None] <= np.arange(128)[None, :]).astype(np.float32)
    qpos = (NM + np.arange(S)).astype(np.int64)
    qaug = np.stack([np.ones(S), np.ones(S), (qpos % 128), (qpos // 128) * 128]).astype(np.float32)
    kpos = np.concatenate([NM + np.arange(S), np.arange(NM)]).astype(np.int64)
    base = np.stack([(kpos % 128), (kpos // 128) * 128, -np.ones(T), -np.ones(T)]).astype(np.float32)
    kaug = np.zeros((NH, 4, T), np.float32)
    for h in range(NH):
        c = 8.0 * 2.0 ** (-(h + 1))
        kaug[h] = base * c
    return ident, tri, qaug, kaug


_NC_CACHE = {}


def kernel(x, meta, norm1_g, w_in, lambda_q1, lambda_k1, lambda_q2, lambda_k2, head_g, conv_w, w_out,
           norm2_g, w_gate, w_up, w_down, norm_f_g):
    f = lambda a: np.ascontiguousarray(np.asarray(a, dtype=np.float32))
    x = f(x)
    B = x.shape[0]
    if "nc" not in _NC_CACHE:
        _NC_CACHE["nc"] = build_program()
    nc = _NC_CACHE["nc"]
    ident, tri, qaug, kaug = _consts()
    shared = {
        "meta": f(meta), "norm1_g": f(norm1_g).reshape(1, D), "w_in": f(w_in)[0],
        "lambda_q1": f(lambda_q1).reshape(1, 64), "lambda_k1": f(lambda_k1).reshape(1, 64),
        "lambda_q2": f(lambda_q2).reshape(1, 64), "lambda_k2": f(lambda_k2).reshape(1, 64),
        "head_g": f(head_g).reshape(1, 128), "conv_w": f(conv_w)[0], "w_out": f(w_out)[0],
        "norm2_g": f(norm2_g).reshape(1, D), "w_gate": f(w_gate)[0], "w_up": f(w_up)[0], "w_down": f(w_down)[0],
        "norm_f_g": f(norm_f_g).reshape(1, D),
        "c_ident": ident, "c_tri": tri, "c_qaug": qaug, "c_kaug": kaug,
    }
    in_maps = [dict(shared, x=x[b]) for b in range(B)]
    res = run_bass_kernel_spmd(nc, in_maps, core_ids=list(range(B)))
    return np.stack([r["out"] for r in res.results], axis=0)
```

```python
import contextlib
import numpy as np
import concourse.bass as bass
import concourse.mybir as mybir
from concourse.bass_utils import run_bass_kernel_spmd

F32 = mybir.dt.float32
BF16 = mybir.dt.bfloat16
AF = mybir.ActivationFunctionType
ALU = mybir.AluOpType

D = 2048
S = 2048
NM = 16
T = S + NM
DFF = 5632
NH = 8
INW = 6144
NFF = DFF // 128
LAMBDA_INIT = 0.8 - 0.6 * 1.0
EPS = 1e-6
HEPS = 1e-5

OFF_A = 0
SZ_A = 16 * T
OFF_M = OFF_A + SZ_A
SZ_M = 16 * S
OFF_R = OFF_M + SZ_M
RING = 6
OFF_X = OFF_R + RING * 2048
SZ_X = 24576
ARENA = OFF_X + SZ_X


ANNOTATE = False


class Ev:
    def __init__(self, sem, step=1):
        self.sem, self.step, self.n = sem, step, 0

    def fire(self):
        self.n += self.step
        return (self.sem, self.n)


class Prog:
    def __init__(self):
        self.q = {k: [] for k in ("pe", "act", "dve", "pool", "sp")}
        self.last = {k: None for k in self.q}

    def op(self, eng, fn, waits=(), ev=None):
        tok = ev.fire() if ev is not None else None
        ws = []
        for w in waits:
            if w is None:
                continue
            if isinstance(w, list):
                ws.extend([u for u in w if u is not None])
            else:
                ws.append(w)
        self.q[eng].append((fn, ws, ev))
        if tok is not None:
            self.last[eng] = tok
        return tok

    def emit(self, eng, handle):
        waited = {}
        for fn, ws, ev in self.q[eng]:
            for sem, val in ws:
                key = id(sem)
                if waited.get(key, 0) >= val:
                    continue
                waited[key] = val
                handle.wait_ge(sem, val)
            ins = fn(handle)
            if ANNOTATE:
                ins.annotate("L%d" % fn.__code__.co_firstlineno)
            if ev is not None:
                ins.then_inc(ev.sem, ev.step)


class Bank:
    def __init__(self, ps, ev_pe):
        self.ps = ps
        self.ev = ev_pe
        self.free = []

    def acquire(self):
        f = self.free
        self.free = []
        return f


def build_program(debug=False):
    nc = bass.Bass("TRN2", target_bir_lowering=False)

    def din(name, shape):
        return nc.dram_tensor(name, shape, F32, kind="ExternalInput").ap()

    x = din("x", [S, D])
    meta = din("meta", [NM, D])
    g1 = din("norm1_g", [1, D])
    w_in = din("w_in", [D, INW])
    lq1 = din("lambda_q1", [1, 64])
    lk1 = din("lambda_k1", [1, 64])
    lq2 = din("lambda_q2", [1, 64])
    lk2 = din("lambda_k2", [1, 64])
    head_g = din("head_g", [1, 128])
    conv_w = din("conv_w", [3, 1024])
    w_out = din("w_out", [D, D])
    g2 = din("norm2_g", [1, D])
    w_gate = din("w_gate", [D, DFF])
    w_up = din("w_up", [D, DFF])
    w_down = din("w_down", [DFF, D])
    gf = din("norm_f_g", [1, D])
    c_ident = din("c_ident", [128, 128])
    c_tri = din("c_tri", [128, 128])
    c_qaug = din("c_qaug", [4, S])
    c_kaug = din("c_kaug", [NH, 4, T])
    out = nc.dram_tensor("out", [S, D], F32, kind="ExternalOutput").ap()
    if debug:
        dbg_aT = nc.dram_tensor("dbg_aT", [128, 16 * T], BF16, kind="ExternalOutput").ap()
        dbg_mix = nc.dram_tensor("dbg_mix", [128, 16 * S], BF16, kind="ExternalOutput").ap()
        dbg_h1 = nc.dram_tensor("dbg_h1", [128, 8 * D], F32, kind="ExternalOutput").ap()

    w_in_v = w_in.rearrange("(k p) n -> p k n", p=128)
    w_out_v = w_out.rearrange("(k p) n -> p k n", p=128)
    w_gate_v = w_gate.rearrange("(k p) n -> p k n", p=128)
    w_up_v = w_up.rearrange("(k p) n -> p k n", p=128)

    P = Prog()
    with contextlib.ExitStack() as es:
        def sb(name, shape, dt):
            return es.enter_context(nc.sbuf_tensor(name, shape, dt))

        nsem = [0]

        def new_ev(step=1):
            nsem[0] += 1
            return Ev(es.enter_context(nc.semaphore(f"s{nsem[0]}")), step)

        arena = sb("arena", [128, ARENA], BF16)
        ident = sb("ident", [128, 128], BF16)
        tri = sb("tri", [128, 128], BF16)
        ones = sb("ones", [128, 128], BF16)
        stat = sb("stat", [128, 4 * 17 + 4 * 16 + 4 * 16], F32)
        lamt = sb("lamt", [128, 4 * 64 + 64 + 16], F32)
        cw = sb("cw", [128, 3, 8], F32)
        hg = sb("hg", [128, 4], F32)

        pss = [es.enter_context(nc.psum_tensor(f"ps{i}", [128, 512], F32)) for i in range(8)]
        banks = [Bank(pss[i], new_ev()) for i in range(8)]

        def tview(bank):
            return bank.ps[:, :].bitcast(BF16).rearrange("p (j n) -> p j n", j=8)

        def av(off, n):
            return arena[:, off:off + n]

        aT = av(OFF_A, SZ_A).rearrange("p (k n) -> p k n", k=16)
        h1 = av(OFF_A, 8 * D * 2).bitcast(F32).rearrange("p (s n) -> p s n", s=8)
        mixT = av(OFF_M, SZ_M).rearrange("p (k n) -> p k n", k=16)
        ring_slots = [av(OFF_R + i * 2048, 2048) for i in range(RING)]
        ring_ready = [new_ev(16) for _ in range(RING)]
        ring_free = [new_ev(1) for _ in range(RING)]
        ring_last = [None] * RING
        ring_n = [0]

        def xv(off_bytes, nbytes, dt=BF16):
            a = av(OFF_X + off_bytes // 2, nbytes // 2)
            return a.bitcast(F32) if dt == F32 else a

        def mv(off_bytes, nbytes, dt=BF16):
            a = av(OFF_M + off_bytes // 2, nbytes // 2)
            return a.bitcast(F32) if dt == F32 else a

        xs = [mv(i * 8192, 8192, F32) for i in range(4)]
        at = [mv(32768, 4096), mv(36864, 4096)]
        g1bc = mv(40960, 8192, F32)
        junkA = mv(49152, 4096)
        QK = 4160
        qA = xv(0, 4128)
        qB = xv(QK, 4128)
        kA = xv(2 * QK, 4128)
        kB = xv(3 * QK, 4128)
        vF = xv(4 * QK, 4128)
        vh = xv(5 * QK, 4352).rearrange("p (b n) -> p b n", b=17)
        o_pt = 5 * QK + 4352
        Pt = [xv(o_pt + i * 1024, 1024) for i in range(4)]
        o_fin = o_pt + 4096
        oc = [xv(o_fin, 2048, F32), xv(o_fin + 2048, 2048, F32)]
        lc = [xv(o_fin + 4096, 2048, F32), xv(o_fin + 6144, 2048, F32)]
        sqb = xv(o_fin + 8192, 1024)
        o_cv = o_fin + 8192 + 1024
        cgS = [xv(o_cv, 2048, F32), xv(o_cv + 2048, 2048, F32)]
        ubuf = xv(o_cv + 4096, 2064, F32)
        ybuf = [xv(o_cv + 6160, 2048, F32), xv(o_cv + 8208, 2048, F32)]
        assert o_cv + 10256 <= 49152
        xr = [xv(0, 2048, F32), xv(2048, 2048, F32)]
        ft = [xv(4096, 4096), xv(8192, 4096), xv(40960, 4096)]
        ost = [xv(4096, 8192, F32), xv(40960, 8192, F32)]
        g2bc = xv(12288, 8192, F32)
        gfbc = xv(20480, 8192, F32)
        sg = [xv(28672, 2048, F32), xv(30720, 2048, F32)]
        actT = [[xv(32768 + (g * 2 + c) * 2048, 2048) for c in range(2)] for g in range(2)]
        junkC = xv(32768, 4096)

        ssA, vvA, lnA, rsA = (stat[:, i * 17:(i + 1) * 17] for i in range(4))
        o2 = 68
        ssC, vvC, lnC, rsC = (stat[:, o2 + i * 16:o2 + (i + 1) * 16] for i in range(4))
        o3 = o2 + 64
        ssF, vvF, lnF, rsF = (stat[:, o3 + i * 16:o3 + (i + 1) * 16] for i in range(4))

        ev_sp = new_ev(16)
        ev_spx = [new_ev(16) for _ in range(4)]
        ev_act = new_ev(1)
        ev_dve = new_ev(1)
        ev_pool = new_ev(16)
        ev_c = new_ev(16)
        ev_qa = new_ev(16)
        ev_ka = new_ev(16)
        ev_out = new_ev(16)
        ev_outb = [new_ev(16), new_ev(16)]
        ev_rel = new_ev(1)
        ev_poolc = new_ev(1)
        ev_xh = [new_ev(16) for _ in range(8)]
        E_tok = {}

        class Slot:
            pass

        def ring_load(src_ap, shape3=None):
            i = ring_n[0]
            ring_n[0] += 1
            si = i % RING
            slot = ring_slots[si]
            dst = slot if shape3 is None else slot.rearrange("p (k n) -> p k n", k=shape3)
            waits = []
            if i >= RING:
                assert ring_last[si] is not None, "ring slot reused before its last reader was emitted"
                waits = [ring_last[si]]
                ring_last[si] = None
            tok = P.op("pool", lambda e, d=dst, s=src_ap: e.dma_start(out=d, in_=s), waits, ev=ring_ready[si])
            sl = Slot()
            sl.ap, sl.tok, sl.si = dst, tok, si
            return sl

        def ring_release(sl, tok):
            ring_last[sl.si] = tok

        t_c = []
        P.op("pool", lambda e: e.dma_start(out=ident[:, :], in_=c_ident), ev=ev_c)
        t_c.append(P.op("pool", lambda e: e.dma_start(out=tri[:, :], in_=c_tri), ev=ev_c))
        t_ones = P.op("dve", lambda e: e.memset(ones[:, :], 1.0), ev=ev_dve)
        epsc = hg[:, 2:3]
        t_eps = P.op("dve", lambda e: e.memset(hg[:, 2:3], EPS), ev=ev_dve)
        t_g1 = P.op("sp", lambda e: e.dma_start(out=g1bc, in_=g1.broadcast_to([128, D])), ev=ev_sp)

        tb_rr = [0]

        def norm_a(src, Pn, col, ss, vv, junk, src_tok):
            t1 = P.op("act", lambda e: e.activation(out=junk[0:Pn, :], in_=src, func=AF.Square, accum_out=ss[0:Pn, col:col + 1]),
                      waits=[src_tok], ev=ev_act)
            return t1

        def norm_b(Pn, col, vv, ln, rs, t1):
            t2b = P.op("act", lambda e: e.activation(out=ln[0:Pn, col:col + 1], in_=vv[0:Pn, col:col + 1], func=AF.Ln, bias=epsc[0:Pn, 0:1], scale=1.0 / D),
                       waits=[t1, t_eps], ev=ev_act)
            t3 = P.op("act", lambda e: e.activation(out=rs[0:Pn, col:col + 1], in_=ln[0:Pn, col:col + 1], func=AF.Exp, scale=-0.5), waits=[t2b], ev=ev_act)
            return t3

        def norm_chain(src, Pn, col, ss, vv, ln, rs, junk, src_tok):
            return norm_b(Pn, col, ss, ln, rs, norm_a(src, Pn, col, ss, vv, junk, src_tok))

        def transposes(a_tile, Pn, dst, c0, tbanks, a_tok):
            evs = []
            pe_tok = None
            for hf in range(2):
                bank = tbanks[tb_rr[0] % len(tbanks)]
                tb_rr[0] += 1
                tv = tview(bank)
                fr = bank.acquire()
                for j in range(8):
                    k = hf * 8 + j
                    last = (j == 7)
                    pe_tok = P.op("pe", lambda e, tv=tv, j=j, k=k: e.transpose(out=tv[:, j, 0:Pn], in_=a_tile[0:Pn, k * 128:(k + 1) * 128],
                                                                             identity=ident[0:Pn, 0:Pn]),
                                  waits=[a_tok, fr, t_c] if j == 0 else [], ev=bank.ev if last else None)
                eng = "act" if hf == 0 else "dve"
                if eng == "act":
                    tk = P.op("act", lambda e, tv=tv, hf=hf: e.copy(out=dst[:, hf * 8:(hf + 1) * 8, c0:c0 + Pn], in_=tv[:, :, 0:Pn]),
                              waits=[pe_tok], ev=ev_act)
                else:
                    tk = P.op("dve", lambda e, tv=tv, hf=hf: e.tensor_copy(out=dst[:, hf * 8:(hf + 1) * 8, c0:c0 + Pn], in_=tv[:, :, 0:Pn]),
                              waits=[pe_tok], ev=ev_dve)
                bank.free.append(tk)
                evs.append(tk)
            return pe_tok, evs

        mm_rr = [0]
        TOK_TILES = [(n * 512, 512) for n in range(4)] + [(S, NM)]
        mmB = banks[0:2]
        LB = banks[2:4]
        SB = banks[4:6]
        OB = banks[6:8]

        def acc16(bank, N, lhs_fn, rhs_fn, first_waits, last_ev):
            fr = bank.acquire()
            tok = None
            for k in range(16):
                l_ap = lhs_fn(k)
                r_ap = rhs_fn(k)
                tok = P.op("pe", lambda e, l_ap=l_ap, r_ap=r_ap, k=k: e.matmul(bank.ps[:, 0:N], lhsT=l_ap, rhs=r_ap, start=(k == 0), stop=(k == 15)),
                           waits=[fr, first_waits] if k == 0 else [], ev=(last_ev if k == 15 else None))
            return tok

        def proj_chunk(col0, tiles, consumer, first_waits):
            sl = ring_load(w_in_v[:, :, col0:col0 + 128], 16)
            for ti, (c0, N) in enumerate(tiles):
                bank = mmB[mm_rr[0] % 2]
                mm_rr[0] += 1
                tok = acc16(bank, N, lambda k: sl.ap[:, k, :], lambda k, c0=c0, N=N: aT[:, k, c0:c0 + N],
                            [sl.tok] + list(first_waits), bank.ev)
                consumer(ti, c0, N, bank, tok)
            ring_release(sl, tok)

        def evac_split(dstA, dstB):
            def cons(ti, c0, N, bank, tok):
                t1 = P.op("act", lambda e: e.copy(out=dstA[0:64, c0:c0 + N], in_=bank.ps[0:64, 0:N]), waits=[tok], ev=ev_act)
                t2 = P.op("dve", lambda e: e.tensor_copy(out=dstB[0:64, c0:c0 + N], in_=bank.ps[64:128, 0:N]), waits=[tok], ev=ev_dve)
                bank.free += [t1, t2]
            return cons

        vrr = [0]

        def evac_v(ti, c0, N, bank, tok):
            vrr[0] += 1
            if vrr[0] % 2:
                t1 = P.op("act", lambda e: e.copy(out=vF[:, c0:c0 + N], in_=bank.ps[:, 0:N]), waits=[tok], ev=ev_act)
            else:
                t1 = P.op("dve", lambda e: e.tensor_copy(out=vF[:, c0:c0 + N], in_=bank.ps[:, 0:N]), waits=[tok], ev=ev_dve)
            bank.free.append(t1)


        def proj_chunk_gen(col0, tiles, consumer, waits_box):
            sl = ring_load(w_in_v[:, :, col0:col0 + 128], 16)
            tok = None
            for ti, (c0, N) in enumerate(tiles):
                bank = mmB[mm_rr[0] % 2]
                mm_rr[0] += 1
                tok = acc16(bank, N, lambda k: sl.ap[:, k, :], lambda k, c0=c0, N=N: aT[:, k, c0:c0 + N],
                            [sl.tok] + list(waits_box[0]), bank.ev)
                consumer(ti, c0, N, bank, tok)
                if ti == len(tiles) - 1:
                    ring_release(sl, tok)
                yield

        a_done = {}
        t_done = {}

        vvtok = {}
        ld_tok = {}

        def stageA1a(i):
            b = i % 4
            Pn = 128 if i < 16 else NM
            src_rows = x[i * 128:(i + 1) * 128, :] if i < 16 else meta
            w = [a_done[i - 4]] if i >= 4 else []
            tld = P.op("sp", lambda e: e.dma_start(out=xs[b][0:Pn, :], in_=src_rows), waits=w, ev=ev_spx[b])
            ld_tok[i] = tld

        def stageA1s(i):
            b = i % 4
            Pn = 128 if i < 16 else NM
            vvtok[i] = norm_a(xs[b][0:Pn, :], Pn, i, ssA, vvA, junkA, ld_tok[i])

        trsA = {}

        def stageA1n(i):
            Pn = 128 if i < 16 else NM
            trsA[i] = norm_b(Pn, i, ssA, lnA, rsA, vvtok[i])

        def stageA1b(i):
            b = i % 4
            ab = i % 2
            Pn = 128 if i < 16 else NM
            w = [trsA[i], t_g1] + ([t_done[i - 2]] if i >= 2 else [])
            a_done[i] = P.op("dve", lambda e: e.scalar_tensor_tensor(out=at[ab][0:Pn, :], in0=xs[b][0:Pn, :], scalar=rsA[0:Pn, i:i + 1],
                                                                      in1=g1bc[0:Pn, :], op0=ALU.mult, op1=ALU.mult), waits=w, ev=ev_dve)

        def stageA2(i):
            Pn = 128 if i < 16 else NM
            c0 = i * 128 if i < 16 else S
            pt, evs = transposes(at[i % 2], Pn, aT, c0, banks[2:6], a_done[i])
            t_done[i] = pt

        wbox = [[]]
        g_q0 = proj_chunk_gen(0, TOK_TILES[0:4], evac_split(qA, qB), wbox)
        g_k0 = proj_chunk_gen(1024, TOK_TILES, evac_split(kA, kB), wbox)
        g_v0 = proj_chunk_gen(2048, TOK_TILES, evac_v, wbox)
        grp_tok = {}

        pend0 = []

        def step_one():
            if pend0:
                g, n = pend0.pop(0)
                wbox[0] = grp_tok[n]
                next(g)

        for i in range(3):
            stageA1a(i)
        stageA1s(0)
        stageA1n(0)
        stageA1b(0)
        for i in range(1, 17):
            if i + 2 < 17:
                stageA1a(i + 2)
            stageA1s(i)
            stageA1n(i)
            stageA1b(i)
            stageA2(i - 1)
            if (i - 1) % 4 == 3:
                n = (i - 1) // 4
                grp_tok[n] = [P.last["act"], P.last["dve"]]
                pend0.extend([(g_q0, n), (g_k0, n), (g_v0, n)])
            else:
                step_one()
        stageA2(16)
        grp_tok[4] = [P.last["act"], P.last["dve"]]
        pend0.extend([(g_k0, 4), (g_v0, 4)])
        while pend0:
            step_one()
        for g in (g_q0, g_k0, g_v0):
            for _ in g:
                pass
        lqv = lamt[:, 0:256].rearrange("p (a n) -> p a n", a=4)
        for i, src in enumerate((lq1, lk1, lq2, lk2)):
            tl = P.op("sp", lambda e, i=i, src=src: e.dma_start(out=lqv[:, i, :], in_=src.broadcast_to([128, 64])), ev=ev_sp)
        t_hg = P.op("sp", lambda e: e.dma_start(out=hg[:, 0:1], in_=head_g.rearrange("o d -> d o")), ev=ev_sp)
        for t in range(3):
            t_cw = P.op("sp", lambda e, t=t: e.dma_start(out=cw[:, t, :], in_=conv_w[t:t + 1, :].rearrange("o (j p) -> p (o j)", p=128),
                                                       allow_slow_non_contiguous=True), ev=ev_sp)
        ljunk = lamt[:, 256:320]
        lsc = lamt[:, 320:336]
        P.op("dve", lambda e: e.scalar_tensor_tensor(out=ljunk, in0=lqv[:, 0, :], scalar=1.0, in1=lqv[:, 1, :], op0=ALU.mult,
                                                     op1=ALU.mult, accum_out=lsc[:, 0:1]), waits=[t_cw], ev=ev_dve)
        t_l = P.op("dve", lambda e: e.scalar_tensor_tensor(out=ljunk, in0=lqv[:, 2, :], scalar=1.0, in1=lqv[:, 3, :], op0=ALU.mult,
                                                           op1=ALU.mult, accum_out=lsc[:, 1:2]), ev=ev_dve)
        t_le = P.op("act", lambda e: e.activation(out=lsc[:, 2:4], in_=lsc[:, 0:2], func=AF.Exp), waits=[t_l], ev=ev_act)
        t_l2 = P.op("dve", lambda e: e.tensor_tensor(out=lsc[:, 4:5], in0=lsc[:, 3:4], in1=lsc[:, 2:3], op=ALU.subtract), waits=[t_le], ev=ev_dve)
        t_l3 = P.op("dve", lambda e: e.tensor_scalar(out=lsc[:, 5:6], in0=lsc[:, 4:5], scalar1=-LAMBDA_INIT, scalar2=None, op0=ALU.add), waits=[t_l2], ev=ev_dve)
        nlam = lsc[:, 5:6]
        t_hgs = P.op("dve", lambda e: e.tensor_scalar(out=hg[:, 1:2], in0=hg[:, 0:1], scalar1=1.0 - LAMBDA_INIT, scalar2=None, op0=ALU.mult), ev=ev_dve)
        hgs = hg[:, 1:2]
        tokA = [P.last["act"], P.last["dve"], P.last["pe"]]
        if debug:
            P.op("sp", lambda e: e.dma_start(out=dbg_aT, in_=av(OFF_A, SZ_A)), waits=tokA, ev=ev_out)

        cg_rd = {}
        u_tok = [None]
        aT_dead = [None]

        def conv_units(j, slots):
            s_cg, s_hi, s_bg = slots
            order = [4, 0, 1, 2, 3]
            for oi, ti in enumerate(order):
                c0, N = TOK_TILES[ti]
                cb = oi % 2
                bank = mmB[mm_rr[0] % 2]
                mm_rr[0] += 1
                tok = acc16(bank, N, lambda k: s_cg.ap[:, k, :], lambda k: aT[:, k, c0:c0 + N], [s_cg.tok], bank.ev)
                if oi == 4:
                    ring_release(s_cg, tok)
                d_ap, s_ap = cgS[cb][:, 0:N], bank.ps[:, 0:N]
                t1 = P.op("act", lambda e, d_ap=d_ap, s_ap=s_ap: e.copy(out=d_ap, in_=s_ap), waits=[tok, cg_rd.get(cb)], ev=ev_act)
                bank.free.append(t1)
                yield
                bank = mmB[mm_rr[0] % 2]
                mm_rr[0] += 1
                tok = acc16(bank, N, lambda k: s_hi.ap[:, k, :], lambda k: aT[:, k, c0:c0 + N], [s_hi.tok], bank.ev)
                if oi == 4:
                    ring_release(s_hi, tok)
                if ti == 4:
                    a_ap, b_ap = cgS[cb][:, NM - 2:NM], bank.ps[:, NM - 2:NM]
                    t2 = P.op("dve", lambda e, a_ap=a_ap, b_ap=b_ap: e.tensor_tensor(out=ubuf[:, 0:2], in0=a_ap, in1=b_ap, op=ALU.mult),
                              waits=[t1, tok, u_tok[0]], ev=ev_dve)
                    bank.free.append(t2)
                    cg_rd[cb] = t2
                    u_tok[0] = t2
                    yield
                    continue
                a_ap, b_ap = cgS[cb][:, :], bank.ps[:, :]
                t2 = P.op("dve", lambda e, a_ap=a_ap, b_ap=b_ap: e.tensor_tensor(out=ubuf[:, 2:514], in0=a_ap, in1=b_ap, op=ALU.mult),
                          waits=[t1, tok, u_tok[0]], ev=ev_dve)
                bank.free.append(t2)
                cg_rd[cb] = t2
                yield
                bank = mmB[mm_rr[0] % 2]
                mm_rr[0] += 1
                tok = acc16(bank, N, lambda k: s_bg.ap[:, k, :], lambda k: aT[:, k, c0:c0 + N], [s_bg.tok], bank.ev)
                if oi == 4:
                    ring_release(s_bg, tok)
                    aT_dead[0] = tok
                yb = ybuf[cb]
                n = ti
                w0, w1, w2 = cw[:, 0, j:j + 1], cw[:, 1, j:j + 1], cw[:, 2, j:j + 1]
                y1 = P.op("dve", lambda e, yb=yb, w0=w0: e.tensor_scalar(out=yb, in0=ubuf[:, 0:512], scalar1=w0, scalar2=None, op0=ALU.mult),
                          waits=[t2, t_cw], ev=ev_dve)
                y2 = P.op("dve", lambda e, yb=yb, w1=w1: e.scalar_tensor_tensor(out=yb, in0=ubuf[:, 1:513], scalar=w1, in1=yb, op0=ALU.mult, op1=ALU.add),
                          waits=[y1], ev=ev_dve)
                y3 = P.op("dve", lambda e, yb=yb, w2=w2: e.scalar_tensor_tensor(out=yb, in0=ubuf[:, 2:514], scalar=w2, in1=yb, op0=ALU.mult, op1=ALU.add),
                          waits=[y2], ev=ev_dve)
                o_ap, b_ap = mixT[:, 8 + j, n * 512:(n + 1) * 512], bank.ps[:, :]
                t3 = P.op("dve", lambda e, yb=yb, o_ap=o_ap, b_ap=b_ap: e.tensor_tensor(out=o_ap, in0=yb, in1=b_ap, op=ALU.mult),
                          waits=[tok, y3], ev=ev_dve)
                bank.free.append(t3)
                u_tok[0] = P.op("dve", lambda e: e.tensor_copy(out=ubuf[:, 0:2], in_=ubuf[:, 512:514]), waits=[y3], ev=ev_dve)
                yield

        s_rr = [0]
        pt_rr = [0]
        pt_free = [None] * 4
        prev_head_pe = [None]
        fin_tok = [None]
        t_qa = None
        deferred = []

        pull_fn = [None]

        def tick(force=False, tag=None):
            for item in list(deferred):
                item[0] -= 1
                if (force and (tag is None or item[2] == tag)) or item[0] <= 0:
                    deferred.remove(item)
                    if not force and pull_fn[0] is not None and item[2] in ("b", "d"):
                        pull_fn[0]()
                    item[1]()

        for h in range(NH):
            fw = tokA if h == 0 else []
            if h > 0:
                proj_chunk(h * 128, TOK_TILES[0:4], evac_split(qA, qB), fw)
                tick(force=True)
                proj_chunk(2048 + h * 128, TOK_TILES, evac_v, [])
                proj_chunk(1024 + h * 128, TOK_TILES, evac_split(kA, kB), [])
            t_v = [P.last["act"], P.last["dve"]]
            tv_toks = []
            for g0 in range(0, 17, 8):
                bank = mmB[mm_rr[0] % 2]
                mm_rr[0] += 1
                tv = tview(bank)
                fr = bank.acquire()
                nb = min(8, 17 - g0)
                tok = None
                for j in range(nb):
                    blk = g0 + j
                    Pn = 128 if blk < 16 else NM
                    tok = P.op("pe", lambda e, tv=tv, j=j, blk=blk, Pn=Pn: e.transpose(out=tv[0:Pn, j, :], in_=vF[:, blk * 128:blk * 128 + Pn],
                                                                                     identity=ident[:, :]),
                               waits=[t_v, fr] if j == 0 else [], ev=bank.ev if j == nb - 1 else None)
                if nb == 8:
                    tk = P.op("dve", lambda e, tv=tv, g0=g0: e.tensor_copy(out=vh[:, g0:g0 + 8, :], in_=tv[:, :, :]), waits=[tok], ev=ev_dve)
                else:
                    tk = P.op("dve", lambda e, tv=tv: e.tensor_copy(out=vh[0:NM, 16, :], in_=tv[0:NM, 0, :]), waits=[tok], ev=ev_dve)
                bank.free.append(tk)
                tv_toks.append(tk)
            t_qk = [P.last["act"], P.last["dve"]]
            conv_slots = [ring_load(w_in_v[:, :, 4096 + h * 128:4096 + (h + 1) * 128], 16),
                          ring_load(w_in_v[:, :, 5120 + h * 128:5120 + (h + 1) * 128], 16),
                          ring_load(w_in_v[:, :, 3072 + h * 128:3072 + (h + 1) * 128], 16)]
            if h == 0:
                P.op("pool", lambda e: e.dma_start(out=qA[64:68, 0:S], in_=c_qaug), ev=ev_qa)
                t_qa = P.op("pool", lambda e: e.dma_start(out=qB[64:68, 0:S], in_=c_qaug), ev=ev_qa)
            P.op("pool", lambda e, h=h: e.dma_start(out=kA[64:68, 0:T], in_=c_kaug[h]), waits=[prev_head_pe[0]], ev=ev_ka)
            t_ka = P.op("pool", lambda e, h=h: e.dma_start(out=kB[64:68, 0:T], in_=c_kaug[h]), ev=ev_ka)
            cgen = conv_units(h, conv_slots)
            tile_ctr = [0]
            prev_head_pe_tok = P.last["pe"]

            def pull():
                try:
                    next(cgen)
                except StopIteration:
                    pass

            pull_fn[0] = pull
            pull()
            pull()
            for Q in range(4):
                blocks = [("m", S, NM, 16)] + [(j, j * 128, 128, j) for j in range(4 * Q + 4)]
                tiles = []
                for (bid, kc0, nk, vb) in blocks:
                    qlo = 0
                    diag = False
                    if bid != "m" and bid >= 4 * Q:
                        qlo = (bid - 4 * Q) * 128
                        diag = True
                    for m in range(2):
                        tiles.append((bid, kc0, nk, vb, qlo, diag, m))
                nt = len(tiles)
                pend = []
                fro = [OB[0].acquire(), OB[1].acquire()]
                frl = [LB[0].acquire(), LB[1].acquire()]
                last_pv = [None, None]
                last_l = [None, None]

                def issue_S(idx):
                    (bid, kc0, nk, vb, qlo, diag, m) = tiles[idx]
                    kM = kA if m == 0 else kB
                    qM = qA if m == 0 else qB
                    sbk = SB[s_rr[0] % 2]
                    s_rr[0] += 1
                    pi = pt_rr[0] % 4
                    pt_rr[0] += 1
                    fr = sbk.acquire()
                    rhs_ap = qM[0:68, Q * 512 + qlo:(Q + 1) * 512]
                    ts = P.op("pe", lambda e: e.matmul(sbk.ps[0:nk, qlo:512], lhsT=kM[0:68, kc0:kc0 + nk], rhs=rhs_ap, start=True, stop=True),
                              waits=[fr, t_qk, t_ka, t_qa, tv_toks], ev=sbk.ev)
                    te = P.op("act", lambda e: e.activation(out=Pt[pi][0:nk, qlo:512], in_=sbk.ps[0:nk, qlo:512], func=AF.Exp, scale=0.125),
                              waits=[ts, pt_free[pi]], ev=ev_act)
                    sbk.free.append(te)
                    rdy = te
                    if diag:
                        rdy = P.op("dve", lambda e: e.tensor_tensor(out=Pt[pi][0:nk, qlo:qlo + 128], in0=Pt[pi][0:nk, qlo:qlo + 128], in1=tri[:, :],
                                                                    op=ALU.mult), waits=[te], ev=ev_dve)
                    pend.append((idx, pi, rdy))

                def issue_PV(idx, pi, rdy):
                    (bid, kc0, nk, vb, qlo, diag, m) = tiles[idx]
                    first = (idx < 2)
                    lastm = (idx >= nt - 2)
                    vsrc = vh[0:nk, vb, :]
                    t1 = P.op("pe", lambda e: e.matmul(OB[m].ps[:, qlo:512], lhsT=vsrc, rhs=Pt[pi][0:nk, qlo:512], start=first, stop=lastm),
                              waits=[rdy, fro[m] if first else None], ev=OB[m].ev if lastm else None)
                    t2 = P.op("pe", lambda e: e.matmul(LB[m].ps[:, qlo:512], lhsT=ones[0:nk, :], rhs=Pt[pi][0:nk, qlo:512], start=first, stop=lastm),
                              waits=[frl[m] if first else None, t_ones], ev=LB[m].ev)
                    pt_free[pi] = t2
                    if lastm:
                        last_pv[m] = t1
                        last_l[m] = t2

                for idx in range(nt):
                    issue_S(idx)
                    if idx == 1 and Q > 0:
                        pull()
                    if len(pend) > 2:
                        issue_PV(*pend.pop(0))
                        tick()
                        tile_ctr[0] += 1
                        if tile_ctr[0] % 12 == 6:
                            pull()
                while pend:
                    issue_PV(*pend.pop(0))

                t_lnl = []
                for m in range(2):
                    tk = P.op("act", lambda e, m=m: e.activation(out=lc[m], in_=LB[m].ps[:, :], func=AF.Ln), waits=[last_l[m], fin_tok[0]], ev=ev_act)
                    LB[m].free.append(tk)
                    t_lnl.append(tk)
                t_ocs = []
                for m in range(2):
                    tk = P.op("dve", lambda e, m=m: e.tensor_copy(out=oc[m], in_=OB[m].ps[:, :]), waits=[last_pv[m], last_l[m], fin_tok[0]], ev=ev_dve)
                    OB[m].free.append(tk)
                    t_ocs.append(tk)
                st = {}

                def fin1b(t_lnl=t_lnl, t_ocs=t_ocs, st=st):
                    t_rl = []
                    for m in range(2):
                        t_rl.append(P.op("act", lambda e, m=m: e.activation(out=lc[m], in_=lc[m], func=AF.Exp, scale=-1.0), waits=[t_lnl[m]], ev=ev_act))
                    c3 = P.op("dve", lambda e: e.tensor_tensor(out=oc[0], in0=oc[0], in1=lc[0], op=ALU.mult), waits=[t_ocs[0], t_rl[0]], ev=ev_dve)
                    c4 = P.op("dve", lambda e: e.tensor_tensor(out=oc[1], in0=oc[1], in1=lc[1], op=ALU.mult), waits=[t_ocs[1], t_rl[1]], ev=ev_dve)
                    c5 = P.op("dve", lambda e: e.scalar_tensor_tensor(out=oc[0], in0=oc[1], scalar=nlam, in1=oc[0], op0=ALU.mult, op1=ALU.add),
                              waits=[c3, c4, t_l3, t_hgs], ev=ev_dve)
                    st["t_sq"] = P.op("dve", lambda e: e.tensor_tensor(out=sqb, in0=oc[0], in1=oc[0], op=ALU.mult), waits=[c5], ev=ev_dve)
                deferred.append([3, fin1b, "b"])

                def fin2a(st=st):
                    t_sq = st["t_sq"]
                    bank = mmB[mm_rr[0] % 2]
                    mm_rr[0] += 1
                    fr = bank.acquire()
                    t_ms = P.op("pe", lambda e: e.matmul(bank.ps[:, :], lhsT=ones[:, :], rhs=sqb, start=True, stop=True), waits=[t_sq, fr], ev=bank.ev)
                    t_vv = P.op("dve", lambda e: e.tensor_scalar(out=lc[0], in0=bank.ps[:, :], scalar1=1.0 / 128, scalar2=HEPS, op0=ALU.mult, op1=ALU.add),
                                waits=[t_ms], ev=ev_dve)
                    bank.free.append(t_vv)
                    st["t_vv"] = t_vv

                def fin2b(h=h, Q=Q, st=st):
                    t_vv = st["t_vv"]
                    t_ln = P.op("act", lambda e: e.activation(out=lc[0], in_=lc[0], func=AF.Ln), waits=[t_vv], ev=ev_act)
                    t_r = P.op("act", lambda e: e.activation(out=lc[0], in_=lc[0], func=AF.Exp, scale=-0.5), waits=[t_ln], ev=ev_act)
                    o_ap = mixT[:, h, Q * 512:(Q + 1) * 512]
                    fin_tok[0] = P.op("dve", lambda e: e.scalar_tensor_tensor(out=o_ap, in0=oc[0], scalar=hgs, in1=lc[0], op0=ALU.mult, op1=ALU.mult),
                                      waits=[t_r], ev=ev_dve)
                deferred.append([10, fin2a, "c"])
                deferred.append([13, fin2b, "d"])
            prev_head_pe[0] = P.last["pe"]
            pull_fn[0] = None
            for _ in cgen:
                pass
            tick(force=True, tag="b")

        tick(force=True)
        tokB = [P.last["act"], P.last["dve"], P.last["pe"]]
        if debug:
            P.op("sp", lambda e: e.dma_start(out=dbg_mix, in_=av(OFF_M, SZ_M)), waits=tokB, ev=ev_out)

        t_g = {}
        xr_free = [None, None]
        xr_n = [0]
        ft_free = [None, None, None]
        ost_free = [None, None]
        act_free = [[None, None], [None, None]]
        gu_b = banks[0:4]
        dn_b = banks[4:8]
        tb2 = banks[4:6]
        dn_rr = [0]
        out_toks = []
        NG = NFF // 2
        for hf in range(2):
            n2_trs = {}

            def norm2_chain(s):
                gs = hf * 8 + s
                n2_trs[s] = norm_chain(h1[:, s, :], 128, gs, ssC, vvC, lnC, rsC, junkC, P.last["dve"])

            def norm2_f(s):
                gs = hf * 8 + s
                b = s % 3
                ta = P.op("dve", lambda e: e.scalar_tensor_tensor(out=ft[b], in0=h1[:, s, :], scalar=rsC[:, gs:gs + 1], in1=g2bc,
                                                                  op0=ALU.mult, op1=ALU.mult), waits=[n2_trs[s], t_g["gf"], ft_free[b]], ev=ev_dve)
                return ta

            ft_ready = {}
            xq = []
            for dq in range(4):
                w = [aT_dead[0]] if hf == 0 else [E_tok[s] for s in range(8)]
                src_ap = x[hf * 1024:(hf + 1) * 1024, dq * 512:(dq + 1) * 512].rearrange("(s p) c -> p s c", p=128)
                xq.append(P.op("sp", lambda e, dq=dq, src_ap=src_ap: e.dma_start(out=h1[:, :, dq * 512:(dq + 1) * 512], in_=src_ap), waits=w, ev=ev_xh[dq]))
                if hf == 0 and dq == 0:
                    t_g["g2"] = P.op("sp", lambda e: e.dma_start(out=g2bc, in_=g2.broadcast_to([128, D])), waits=tokB, ev=ev_sp)
                    t_g["gf"] = P.op("sp", lambda e: e.dma_start(out=gfbc, in_=gf.broadcast_to([128, D])), ev=ev_sp)
            for dq in range(3):
                frs = [banks[s].acquire() for s in range(8)]
                toks = [None] * 8
                for kg in range(4):
                    sl = ring_load(w_out_v[:, kg * 4:(kg + 1) * 4, dq * 512:(dq + 1) * 512], 4)
                    tok = None
                    for s in range(8):
                        gs = hf * 8 + s
                        for kk in range(4):
                            k = kg * 4 + kk
                            l_ap = mixT[:, k, gs * 128:(gs + 1) * 128]
                            r_ap = sl.ap[:, kk, :]
                            bank = banks[s]
                            last = (k == 15)
                            rel = (s == 7 and kk == 3)
                            ev = bank.ev if last else (ev_rel if rel else None)
                            tok = P.op("pe", lambda e, bank=bank, l_ap=l_ap, r_ap=r_ap, k=k: e.matmul(bank.ps[:, :], lhsT=l_ap, rhs=r_ap, start=(k == 0), stop=(k == 15)),
                                       waits=[sl.tok, tokB, frs[s] if k == 0 else None], ev=ev)
                        if kg == 3:
                            toks[s] = tok
                    ring_release(sl, tok)
                for s in range(8):
                    dst = h1[:, s, dq * 512:(dq + 1) * 512]
                    td = P.op("dve", lambda e, s=s, dst=dst: e.tensor_tensor(out=dst, in0=banks[s].ps[:, :], in1=dst, op=ALU.add), waits=[toks[s], xq[dq]], ev=ev_dve)
                    banks[s].free.append(td)
            for dq in range(3, 4):
                slots = [ring_load(w_out_v[:, kg * 4:(kg + 1) * 4, dq * 512:(dq + 1) * 512], 4) for kg in range(4)]
                for s in range(8):
                    gs = hf * 8 + s
                    bank = gu_b[mm_rr[0] % 4]
                    mm_rr[0] += 1
                    tok = acc16(bank, 512, lambda k, gs=gs: mixT[:, k, gs * 128:(gs + 1) * 128], lambda k: slots[k // 4].ap[:, k % 4, :],
                                [sl.tok for sl in slots] + tokB, bank.ev)
                    dst = h1[:, s, dq * 512:(dq + 1) * 512]
                    td = P.op("dve", lambda e, bank=bank, dst=dst: e.tensor_tensor(out=dst, in0=bank.ps[:, :], in1=dst, op=ALU.add), waits=[tok, xq[dq]], ev=ev_dve)
                    bank.free.append(td)
                    norm2_chain(s)
                    if s >= 2:
                        pt, evs = transposes(ft[(s - 2) % 3], 128, mixT, (hf * 8 + s - 2) * 128, tb2, ft_ready[s - 2])
                        ft_free[(s - 2) % 3] = pt
                        if s == 5:
                            tokC_n0 = [P.last["act"], P.last["dve"]]
                    ft_ready[s] = norm2_f(s)
                for sl in slots:
                    ring_release(sl, tok)

            sg_rr = [0]
            sg_free = [None, None]
            act_rdy = {}

            def GU(g):
                for ci in range(2):
                    c = 2 * g + ci
                    sl_g = ring_load(w_gate_v[:, :, c * 128:(c + 1) * 128], 16)
                    sl_u = ring_load(w_up_v[:, :, c * 128:(c + 1) * 128], 16)
                    for n in range(2):
                        c0 = hf * 1024 + n * 512
                        res = []
                        for wi, sl in enumerate((sl_g, sl_u)):
                            bank = gu_b[mm_rr[0] % 4]
                            mm_rr[0] += 1
                            tok = acc16(bank, 512, lambda k, sl=sl: sl.ap[:, k, :], lambda k: mixT[:, k, c0:c0 + 512], [sl.tok] + tokC, bank.ev)
                            if n == 1:
                                ring_release(sl, tok)
                            res.append((bank, tok))
                            if wi == 0:
                                yield
                        sb_i = sg_rr[0] % 2
                        sg_rr[0] += 1
                        (bg_, tg_), (bu_, tu_) = res
                        t1 = P.op("act", lambda e, bg_=bg_, sb_i=sb_i: e.activation(out=sg[sb_i], in_=bg_.ps[:, :], func=AF.Silu),
                                  waits=[tg_, sg_free[sb_i]], ev=ev_act)
                        bg_.free.append(t1)
                        dst = actT[g % 2][ci][:, n * 512:(n + 1) * 512]
                        t2 = P.op("dve", lambda e, bu_=bu_, sb_i=sb_i, dst=dst: e.tensor_tensor(out=dst, in0=sg[sb_i], in1=bu_.ps[:, :], op=ALU.mult),
                                  waits=[t1, tu_, act_free[g % 2][ci]], ev=ev_dve)
                        bu_.free.append(t2)
                        sg_free[sb_i] = t2
                        act_rdy[(g, ci)] = t2
                        yield

            def DOWN(gl, with_E):
                chunks = [(g, ci) for g in gl for ci in range(2)]
                wd = [ring_load(w_down[(2 * g + ci) * 128:(2 * g + ci + 1) * 128, :]) for (g, ci) in chunks]
                nch = len(chunks)
                tok = None
                for s in range(8):
                    for dq in range(4):
                        bank = dn_b[dn_rr[0] % 4]
                        dn_rr[0] += 1
                        fr = bank.acquire()
                        for j, (g, ci) in enumerate(chunks):
                            sl = wd[j]
                            l_ap = actT[g % 2][ci][:, s * 128:(s + 1) * 128]
                            r_ap = sl.ap[:, dq * 512:(dq + 1) * 512]
                            tok = P.op("pe", lambda e, bank=bank, j=j, l_ap=l_ap, r_ap=r_ap: e.matmul(bank.ps[:, :], lhsT=l_ap, rhs=r_ap, start=(j == 0), stop=(j == nch - 1)),
                                       waits=[sl.tok, act_rdy[(g, ci)], fr if j == 0 else None], ev=(bank.ev if j == nch - 1 else None))
                        dst = h1[:, s, dq * 512:(dq + 1) * 512]
                        td = P.op("dve", lambda e, bank=bank, dst=dst: e.tensor_tensor(out=dst, in0=bank.ps[:, :], in1=dst, op=ALU.add), waits=[tok], ev=ev_dve)
                        bank.free.append(td)
                        yield
                    if with_E:
                        E_a(s)
                        if s >= 1:
                            E_b(s - 1)
                if with_E:
                    E_b(7)
                for g in gl:
                    act_free[g % 2][0] = tok
                    act_free[g % 2][1] = tok
                for sl in wd:
                    ring_release(sl, tok)

            e_trs = {}

            def E_a(s):
                gs = hf * 8 + s
                e_trs[s] = norm_chain(h1[:, s, :], 128, gs, ssF, vvF, lnF, rsF, junkE, P.last["dve"])

            def E_b(s):
                gs = hf * 8 + s
                ob = s % 2
                to = P.op("dve", lambda e: e.scalar_tensor_tensor(out=ost[ob], in0=h1[:, s, :], scalar=rsF[:, gs:gs + 1], in1=gfbc,
                                                                  op0=ALU.mult, op1=ALU.mult), waits=[e_trs[s], t_g["gf"], ost_free[ob], P.last["pe"]], ev=ev_dve)
                tp = None
                E_tok[s] = [to]
                tw = P.op("sp", lambda e: e.dma_start(out=out[gs * 128:(gs + 1) * 128, :], in_=ost[ob]), waits=[to, tp], ev=ev_outb[ob])
                ost_free[ob] = tw
                out_toks.append(tw)

            junkE = xv(28672, 4096)

            def drain(gen, n):
                for _ in range(n):
                    try:
                        next(gen)
                    except StopIteration:
                        return

            tokC = tokC_n0
            gu0 = GU(0)
            next(gu0)
            next(gu0)
            for s2 in (6, 7):
                pt, evs = transposes(ft[s2 % 3], 128, mixT, (hf * 8 + s2) * 128, tb2, ft_ready[s2])
                ft_free[s2 % 3] = pt
            tokC = [P.last["act"], P.last["dve"]]
            dn = None
            for g in range(NG):
                gu = gu0 if g == 0 else GU(g)
                sched = [6, 6, 6, 6, 4, 4, 0, 0]
                for u in range(8):
                    if g == 0 and u < 2:
                        continue
                    next(gu)
                    if dn is not None:
                        drain(dn, sched[u])
                for _ in gu:
                    pass
                if dn is not None:
                    for _ in dn:
                        pass
                dn = DOWN([g], False) if g < NG - 2 else None
            for _ in DOWN([NG - 2, NG - 1], True):
                pass
            ft_free = [ost_free[0], ost_free[0], ost_free[1]]
            act_free = [[None, None], [None, None]]
        if debug:
            P.op("sp", lambda e: e.dma_start(out=dbg_h1, in_=av(OFF_A, 8 * D * 2).bitcast(F32)), waits=[P.last["dve"], P.last["pe"]], ev=ev_out)

        with nc.Block() as block:
            @block.tensor
            def _(e):
                P.emit("pe", e)

            @block.scalar
            def _(e):
                P.emit("act", e)

            @block.vector
            def _(e):
                P.emit("dve", e)

            @block.gpsimd
            def _(e):
                P.emit("pool", e)

            @block.sync
            def _(e):
                P.emit("sp", e)
                for evo in (ev_outb[0], ev_outb[1], ev_out):
                    if evo.n > 0:
                        e.wait_ge(evo.sem, evo.n)
    return nc


def _consts():
    ident = np.eye(128, dtype=np.float32)
    tri = (np.arange(128)[:, None] <= np.arange(128)[None, :]).astype(np.float32)
    qpos = (NM + np.arange(S)).astype(np.int64)
    qaug = np.stack([np.ones(S), np.ones(S), (qpos % 128), (qpos // 128) * 128]).astype(np.float32)
    kpos = np.concatenate([NM + np.arange(S), np.arange(NM)]).astype(np.int64)
    base = np.stack([(kpos % 128), (kpos // 128) * 128, -np.ones(T), -np.ones(T)]).astype(np.float32)
    kaug = np.zeros((NH, 4, T), np.float32)
    for h in range(NH):
        c = 8.0 * 2.0 ** (-(h + 1))
        kaug[h] = base * c
    return ident, tri, qaug, kaug


_NC_CACHE = {}


def kernel(x, meta, norm1_g, w_in, lambda_q1, lambda_k1, lambda_q2, lambda_k2, head_g, conv_w, w_out,
           norm2_g, w_gate, w_up, w_down, norm_f_g):
    f = lambda a: np.ascontiguousarray(np.asarray(a, dtype=np.float32))
    x = f(x)
    B = x.shape[0]
    if "nc" not in _NC_CACHE:
        _NC_CACHE["nc"] = build_program()
    nc = _NC_CACHE["nc"]
    ident, tri, qaug, kaug = _consts()
    shared = {
        "meta": f(meta), "norm1_g": f(norm1_g).reshape(1, D), "w_in": f(w_in)[0],
        "lambda_q1": f(lambda_q1).reshape(1, 64), "lambda_k1": f(lambda_k1).reshape(1, 64),
        "lambda_q2": f(lambda_q2).reshape(1, 64), "lambda_k2": f(lambda_k2).reshape(1, 64),
        "head_g": f(head_g).reshape(1, 128), "conv_w": f(conv_w)[0], "w_out": f(w_out)[0],
        "norm2_g": f(norm2_g).reshape(1, D), "w_gate": f(w_gate)[0], "w_up": f(w_up)[0], "w_down": f(w_down)[0],
        "norm_f_g": f(norm_f_g).reshape(1, D),
        "c_ident": ident, "c_tri": tri, "c_qaug": qaug, "c_kaug": kaug,
    }
    in_maps = [dict(shared, x=x[b]) for b in range(B)]
    res = run_bass_kernel_spmd(nc, in_maps, core_ids=list(range(B)))
    return np.stack([r["out"] for r in res.results], axis=0)
```

```python
import contextlib
import numpy as np
import concourse.bass as bass
import concourse.mybir as mybir
from concourse.bass_utils import run_bass_kernel_spmd

F32 = mybir.dt.float32
BF16 = mybir.dt.bfloat16
AF = mybir.ActivationFunctionType
ALU = mybir.AluOpType

D = 2048
S = 2048
NM = 16
T = S + NM
DFF = 5632
NH = 8
INW = 6144
NFF = DFF // 128
LAMBDA_INIT = 0.8 - 0.6 * 1.0
EPS = 1e-6
HEPS = 1e-5

OFF_A = 0
SZ_A = 16 * T
OFF_M = OFF_A + SZ_A
SZ_M = 16 * S
OFF_R = OFF_M + SZ_M
RING = 6
OFF_X = OFF_R + RING * 2048
SZ_X = 24576
ARENA = OFF_X + SZ_X


ANNOTATE = False


class Ev:
    def __init__(self, sem, step=1):
        self.sem, self.step, self.n = sem, step, 0

    def fire(self):
        self.n += self.step
        return (self.sem, self.n)


class Prog:
    def __init__(self):
        self.q = {k: [] for k in ("pe", "act", "dve", "pool", "sp")}
        self.last = {k: None for k in self.q}

    def op(self, eng, fn, waits=(), ev=None):
        tok = ev.fire() if ev is not None else None
        ws = []
        for w in waits:
            if w is None:
                continue
            if isinstance(w, list):
                ws.extend([u for u in w if u is not None])
            else:
                ws.append(w)
        self.q[eng].append((fn, ws, ev))
        if tok is not None:
            self.last[eng] = tok
        return tok

    def emit(self, eng, handle):
        waited = {}
        for fn, ws, ev in self.q[eng]:
            for sem, val in ws:
                key = id(sem)
                if waited.get(key, 0) >= val:
                    continue
                waited[key] = val
                handle.wait_ge(sem, val)
            ins = fn(handle)
            if ANNOTATE:
                ins.annotate("L%d" % fn.__code__.co_firstlineno)
            if ev is not None:
                ins.then_inc(ev.sem, ev.step)


class Bank:
    def __init__(self, ps, ev_pe):
        self.ps = ps
        self.ev = ev_pe
        self.free = []

    def acquire(self):
        f = self.free
        self.free = []
        return f


def build_program(debug=False):
    nc = bass.Bass("TRN2", target_bir_lowering=False)

    def din(name, shape):
        return nc.dram_tensor(name, shape, F32, kind="ExternalInput").ap()

    x = din("x", [S, D])
    meta = din("meta", [NM, D])
    g1 = din("norm1_g", [1, D])
    w_in = din("w_in", [D, INW])
    lq1 = din("lambda_q1", [1, 64])
    lk1 = din("lambda_k1", [1, 64])
    lq2 = din("lambda_q2", [1, 64])
    lk2 = din("lambda_k2", [1, 64])
    head_g = din("head_g", [1, 128])
    conv_w = din("conv_w", [3, 1024])
    w_out = din("w_out", [D, D])
    g2 = din("norm2_g", [1, D])
    w_gate = din("w_gate", [D, DFF])
    w_up = din("w_up", [D, DFF])
    w_down = din("w_down", [DFF, D])
    gf = din("norm_f_g", [1, D])
    c_ident = din("c_ident", [128, 128])
    c_tri = din("c_tri", [128, 128])
    c_qaug = din("c_qaug", [4, S])
    c_kaug = din("c_kaug", [NH, 4, T])
    out = nc.dram_tensor("out", [S, D], F32, kind="ExternalOutput").ap()
    if debug:
        dbg_aT = nc.dram_tensor("dbg_aT", [128, 16 * T], BF16, kind="ExternalOutput").ap()
        dbg_mix = nc.dram_tensor("dbg_mix", [128, 16 * S], BF16, kind="ExternalOutput").ap()
        dbg_h1 = nc.dram_tensor("dbg_h1", [128, 8 * D], F32, kind="ExternalOutput").ap()

    w_in_v = w_in.rearrange("(k p) n -> p k n", p=128)
    w_out_v = w_out.rearrange("(k p) n -> p k n", p=128)
    w_gate_v = w_gate.rearrange("(k p) n -> p k n", p=128)
    w_up_v = w_up.rearrange("(k p) n -> p k n", p=128)

    P = Prog()
    with contextlib.ExitStack() as es:
        def sb(name, shape, dt):
            return es.enter_context(nc.sbuf_tensor(name, shape, dt))

        nsem = [0]

        def new_ev(step=1):
            nsem[0] += 1
            return Ev(es.enter_context(nc.semaphore(f"s{nsem[0]}")), step)

        arena = sb("arena", [128, ARENA], BF16)
        ident = sb("ident", [128, 128], BF16)
        tri = sb("tri", [128, 128], BF16)
        ones = sb("ones", [128, 128], BF16)
        stat = sb("stat", [128, 4 * 17 + 4 * 16 + 4 * 16], F32)
        lamt = sb("lamt", [128, 4 * 64 + 64 + 16], F32)
        cw = sb("cw", [128, 3, 8], F32)
        hg = sb("hg", [128, 4], F32)

        pss = [es.enter_context(nc.psum_tensor(f"ps{i}", [128, 512], F32)) for i in range(8)]
        banks = [Bank(pss[i], new_ev()) for i in range(8)]

        def tview(bank):
            return bank.ps[:, :].bitcast(BF16).rearrange("p (j n) -> p j n", j=8)

        def av(off, n):
            return arena[:, off:off + n]

        aT = av(OFF_A, SZ_A).rearrange("p (k n) -> p k n", k=16)
        h1 = av(OFF_A, 8 * D * 2).bitcast(F32).rearrange("p (s n) -> p s n", s=8)
        mixT = av(OFF_M, SZ_M).rearrange("p (k n) -> p k n", k=16)
        ring_slots = [av(OFF_R + i * 2048, 2048) for i in range(RING)]
        ring_ready = [new_ev(16) for _ in range(RING)]
        ring_free = [new_ev(1) for _ in range(RING)]
        ring_last = [None] * RING
        ring_n = [0]

        def xv(off_bytes, nbytes, dt=BF16):
            a = av(OFF_X + off_bytes // 2, nbytes // 2)
            return a.bitcast(F32) if dt == F32 else a

        def mv(off_bytes, nbytes, dt=BF16):
            a = av(OFF_M + off_bytes // 2, nbytes // 2)
            return a.bitcast(F32) if dt == F32 else a

        xs = [mv(i * 8192, 8192, F32) for i in range(4)]
        at = [mv(32768, 4096), mv(36864, 4096)]
        g1bc = mv(40960, 8192, F32)
        junkA = mv(49152, 4096)
        QK = 4160
        qA = xv(0, 4128)
        qB = xv(QK, 4128)
        kA = xv(2 * QK, 4128)
        kB = xv(3 * QK, 4128)
        vF = xv(4 * QK, 4128)
        vh = xv(5 * QK, 4352).rearrange("p (b n) -> p b n", b=17)
        o_pt = 5 * QK + 4352
        Pt = [xv(o_pt + i * 1024, 1024) for i in range(4)]
        o_fin = o_pt + 4096
        oc = [xv(o_fin, 2048, F32), xv(o_fin + 2048, 2048, F32)]
        lc = [xv(o_fin + 4096, 2048, F32), xv(o_fin + 6144, 2048, F32)]
        sqb = xv(o_fin + 8192, 1024)
        o_cv = o_fin + 8192 + 1024
        cgS = [xv(o_cv, 2048, F32), xv(o_cv + 2048, 2048, F32)]
        ubuf = xv(o_cv + 4096, 2064, F32)
        ybuf = [xv(o_cv + 6160, 2048, F32), xv(o_cv + 8208, 2048, F32)]
        assert o_cv + 10256 <= 49152
        xr = [xv(0, 2048, F32), xv(2048, 2048, F32)]
        ft = [xv(4096, 4096), xv(8192, 4096), xv(40960, 4096)]
        ost = [xv(4096, 8192, F32), xv(40960, 8192, F32)]
        g2bc = xv(12288, 8192, F32)
        gfbc = xv(20480, 8192, F32)
        sg = [xv(28672, 2048, F32), xv(30720, 2048, F32)]
        actT = [[xv(32768 + (g * 2 + c) * 2048, 2048) for c in range(2)] for g in range(2)]
        junkC = xv(32768, 4096)

        ssA, vvA, lnA, rsA = (stat[:, i * 17:(i + 1) * 17] for i in range(4))
        o2 = 68
        ssC, vvC, lnC, rsC = (stat[:, o2 + i * 16:o2 + (i + 1) * 16] for i in range(4))
        o3 = o2 + 64
        ssF, vvF, lnF, rsF = (stat[:, o3 + i * 16:o3 + (i + 1) * 16] for i in range(4))

        ev_sp = new_ev(16)
        ev_spx = [new_ev(16) for _ in range(4)]
        ev_act = new_ev(1)
        ev_dve = new_ev(1)
        ev_pool = new_ev(16)
        ev_c = new_ev(16)
        ev_qa = new_ev(16)
        ev_ka = new_ev(16)
        ev_out = new_ev(16)
        ev_outb = [new_ev(16), new_ev(16)]
        ev_rel = new_ev(1)
        ev_poolc = new_ev(1)
        ev_xh = [new_ev(16) for _ in range(8)]
        E_tok = {}

        class Slot:
            pass

        def ring_load(src_ap, shape3=None):
            i = ring_n[0]
            ring_n[0] += 1
            si = i % RING
            slot = ring_slots[si]
            dst = slot if shape3 is None else slot.rearrange("p (k n) -> p k n", k=shape3)
            waits = []
            if i >= RING:
                assert ring_last[si] is not None, "ring slot reused before its last reader was emitted"
                waits = [ring_last[si]]
                ring_last[si] = None
            tok = P.op("pool", lambda e, d=dst, s=src_ap: e.dma_start(out=d, in_=s), waits, ev=ring_ready[si])
            sl = Slot()
            sl.ap, sl.tok, sl.si = dst, tok, si
            return sl

        def ring_release(sl, tok):
            ring_last[sl.si] = tok

        t_c = []
        P.op("pool", lambda e: e.dma_start(out=ident[:, :], in_=c_ident), ev=ev_c)
        t_c.append(P.op("pool", lambda e: e.dma_start(out=tri[:, :], in_=c_tri), ev=ev_c))
        t_ones = P.op("dve", lambda e: e.memset(ones[:, :], 1.0), ev=ev_dve)
        epsc = hg[:, 2:3]
        t_eps = P.op("dve", lambda e: e.memset(hg[:, 2:3], EPS), ev=ev_dve)
        t_g1 = P.op("sp", lambda e: e.dma_start(out=g1bc, in_=g1.broadcast_to([128, D])), ev=ev_sp)

        tb_rr = [0]

        def norm_a(src, Pn, col, ss, vv, junk, src_tok):
            t1 = P.op("act", lambda e: e.activation(out=junk[0:Pn, :], in_=src, func=AF.Square, accum_out=ss[0:Pn, col:col + 1]),
                      waits=[src_tok], ev=ev_act)
            return t1

        def norm_b(Pn, col, vv, ln, rs, t1):
            t2b = P.op("act", lambda e: e.activation(out=ln[0:Pn, col:col + 1], in_=vv[0:Pn, col:col + 1], func=AF.Ln, bias=epsc[0:Pn, 0:1], scale=1.0 / D),
                       waits=[t1, t_eps], ev=ev_act)
            t3 = P.op("act", lambda e: e.activation(out=rs[0:Pn, col:col + 1], in_=ln[0:Pn, col:col + 1], func=AF.Exp, scale=-0.5), waits=[t2b], ev=ev_act)
            return t3

        def norm_chain(src, Pn, col, ss, vv, ln, rs, junk, src_tok):
            return norm_b(Pn, col, ss, ln, rs, norm_a(src, Pn, col, ss, vv, junk, src_tok))

        def transposes(a_tile, Pn, dst, c0, tbanks, a_tok):
            evs = []
            pe_tok = None
            for hf in range(2):
                bank = tbanks[tb_rr[0] % len(tbanks)]
                tb_rr[0] += 1
                tv = tview(bank)
                fr = bank.acquire()
                for j in range(8):
                    k = hf * 8 + j
                    last = (j == 7)
                    pe_tok = P.op("pe", lambda e, tv=tv, j=j, k=k: e.transpose(out=tv[:, j, 0:Pn], in_=a_tile[0:Pn, k * 128:(k + 1) * 128],
                                                                             identity=ident[0:Pn, 0:Pn]),
                                  waits=[a_tok, fr, t_c] if j == 0 else [], ev=bank.ev if last else None)
                eng = "act" if hf == 0 else "dve"
                if eng == "act":
                    tk = P.op("act", lambda e, tv=tv, hf=hf: e.copy(out=dst[:, hf * 8:(hf + 1) * 8, c0:c0 + Pn], in_=tv[:, :, 0:Pn]),
                              waits=[pe_tok], ev=ev_act)
                else:
                    tk = P.op("dve", lambda e, tv=tv, hf=hf: e.tensor_copy(out=dst[:, hf * 8:(hf + 1) * 8, c0:c0 + Pn], in_=tv[:, :, 0:Pn]),
                              waits=[pe_tok], ev=ev_dve)
                bank.free.append(tk)
                evs.append(tk)
            return pe_tok, evs

        mm_rr = [0]
        TOK_TILES = [(n * 512, 512) for n in range(4)] + [(S, NM)]
        mmB = banks[0:2]
        LB = banks[2:4]
        SB = banks[4:6]
        OB = banks[6:8]

        def acc16(bank, N, lhs_fn, rhs_fn, first_waits, last_ev):
            fr = bank.acquire()
            tok = None
            for k in range(16):
                l_ap = lhs_fn(k)
                r_ap = rhs_fn(k)
                tok = P.op("pe", lambda e, l_ap=l_ap, r_ap=r_ap, k=k: e.matmul(bank.ps[:, 0:N], lhsT=l_ap, rhs=r_ap, start=(k == 0), stop=(k == 15)),
                           waits=[fr, first_waits] if k == 0 else [], ev=(last_ev if k == 15 else None))
            return tok

        def proj_chunk(col0, tiles, consumer, first_waits):
            sl = ring_load(w_in_v[:, :, col0:col0 + 128], 16)
            for ti, (c0, N) in enumerate(tiles):
                bank = mmB[mm_rr[0] % 2]
                mm_rr[0] += 1
                tok = acc16(bank, N, lambda k: sl.ap[:, k, :], lambda k, c0=c0, N=N: aT[:, k, c0:c0 + N],
                            [sl.tok] + list(first_waits), bank.ev)
                consumer(ti, c0, N, bank, tok)
            ring_release(sl, tok)

        def evac_split(dstA, dstB):
            def cons(ti, c0, N, bank, tok):
                t1 = P.op("act", lambda e: e.copy(out=dstA[0:64, c0:c0 + N], in_=bank.ps[0:64, 0:N]), waits=[tok], ev=ev_act)
                t2 = P.op("dve", lambda e: e.tensor_copy(out=dstB[0:64, c0:c0 + N], in_=bank.ps[64:128, 0:N]), waits=[tok], ev=ev_dve)
                bank.free += [t1, t2]
            return cons

        vrr = [0]

        def evac_v(ti, c0, N, bank, tok):
            vrr[0] += 1
            if vrr[0] % 2:
                t1 = P.op("act", lambda e: e.copy(out=vF[:, c0:c0 + N], in_=bank.ps[:, 0:N]), waits=[tok], ev=ev_act)
            else:
                t1 = P.op("dve", lambda e: e.tensor_copy(out=vF[:, c0:c0 + N], in_=bank.ps[:, 0:N]), waits=[tok], ev=ev_dve)
            bank.free.append(t1)


        def proj_chunk_gen(col0, tiles, consumer, waits_box):
            sl = ring_load(w_in_v[:, :, col0:col0 + 128], 16)
            tok = None
            for ti, (c0, N) in enumerate(tiles):
                bank = mmB[mm_rr[0] % 2]
                mm_rr[0] += 1
                tok = acc16(bank, N, lambda k: sl.ap[:, k, :], lambda k, c0=c0, N=N: aT[:, k, c0:c0 + N],
                            [sl.tok] + list(waits_box[0]), bank.ev)
                consumer(ti, c0, N, bank, tok)
                if ti == len(tiles) - 1:
                    ring_release(sl, tok)
                yield

        a_done = {}
        t_done = {}

        vvtok = {}
        ld_tok = {}

        def stageA1a(i):
            b = i % 4
            Pn = 128 if i < 16 else NM
            src_rows = x[i * 128:(i + 1) * 128, :] if i < 16 else meta
            w = [a_done[i - 4]] if i >= 4 else []
            tld = P.op("sp", lambda e: e.dma_start(out=xs[b][0:Pn, :], in_=src_rows), waits=w, ev=ev_spx[b])
            ld_tok[i] = tld

        def stageA1s(i):
            b = i % 4
            Pn = 128 if i < 16 else NM
            vvtok[i] = norm_a(xs[b][0:Pn, :], Pn, i, ssA, vvA, junkA, ld_tok[i])

        trsA = {}

        def stageA1n(i):
            Pn = 128 if i < 16 else NM
            trsA[i] = norm_b(Pn, i, ssA, lnA, rsA, vvtok[i])

        def stageA1b(i):
            b = i % 4
            ab = i % 2
            Pn = 128 if i < 16 else NM
            w = [trsA[i], t_g1] + ([t_done[i - 2]] if i >= 2 else [])
            a_done[i] = P.op("dve", lambda e: e.scalar_tensor_tensor(out=at[ab][0:Pn, :], in0=xs[b][0:Pn, :], scalar=rsA[0:Pn, i:i + 1],
                                                                      in1=g1bc[0:Pn, :], op0=ALU.mult, op1=ALU.mult), waits=w, ev=ev_dve)

        def stageA2(i):
            Pn = 128 if i < 16 else NM
            c0 = i * 128 if i < 16 else S
            pt, evs = transposes(at[i % 2], Pn, aT, c0, banks[2:6], a_done[i])
            t_done[i] = pt

        wbox = [[]]
        g_q0 = proj_chunk_gen(0, TOK_TILES[0:4], evac_split(qA, qB), wbox)
        g_k0 = proj_chunk_gen(1024, TOK_TILES, evac_split(kA, kB), wbox)
        g_v0 = proj_chunk_gen(2048, TOK_TILES, evac_v, wbox)
        grp_tok = {}

        pend0 = []

        def step_one():
            if pend0:
                g, n = pend0.pop(0)
                wbox[0] = grp_tok[n]
                next(g)

        for i in range(3):
            stageA1a(i)
        stageA1s(0)
        stageA1n(0)
        stageA1b(0)
        for i in range(1, 17):
            if i + 2 < 17:
                stageA1a(i + 2)
            stageA1s(i)
            stageA1n(i)
            stageA1b(i)
            stageA2(i - 1)
            if (i - 1) % 4 == 3:
                n = (i - 1) // 4
                grp_tok[n] = [P.last["act"], P.last["dve"]]
                pend0.extend([(g_q0, n), (g_k0, n), (g_v0, n)])
            else:
                step_one()
        stageA2(16)
        grp_tok[4] = [P.last["act"], P.last["dve"]]
        pend0.extend([(g_k0, 4), (g_v0, 4)])
        while pend0:
            step_one()
        for g in (g_q0, g_k0, g_v0):
            for _ in g:
                pass
        lqv = lamt[:, 0:256].rearrange("p (a n) -> p a n", a=4)
        for i, src in enumerate((lq1, lk1, lq2, lk2)):
            tl = P.op("sp", lambda e, i=i, src=src: e.dma_start(out=lqv[:, i, :], in_=src.broadcast_to([128, 64])), ev=ev_sp)
        t_hg = P.op("sp", lambda e: e.dma_start(out=hg[:, 0:1], in_=head_g.rearrange("o d -> d o")), ev=ev_sp)
        for t in range(3):
            t_cw = P.op("sp", lambda e, t=t: e.dma_start(out=cw[:, t, :], in_=conv_w[t:t + 1, :].rearrange("o (j p) -> p (o j)", p=128),
                                                       allow_slow_non_contiguous=True), ev=ev_sp)
        ljunk = lamt[:, 256:320]
        lsc = lamt[:, 320:336]
        P.op("dve", lambda e: e.scalar_tensor_tensor(out=ljunk, in0=lqv[:, 0, :], scalar=1.0, in1=lqv[:, 1, :], op0=ALU.mult,
                                                     op1=ALU.mult, accum_out=lsc[:, 0:1]), waits=[t_cw], ev=ev_dve)
        t_l = P.op("dve", lambda e: e.scalar_tensor_tensor(out=ljunk, in0=lqv[:, 2, :], scalar=1.0, in1=lqv[:, 3, :], op0=ALU.mult,
                                                           op1=ALU.mult, accum_out=lsc[:, 1:2]), ev=ev_dve)
        t_le = P.op("act", lambda e: e.activation(out=lsc[:, 2:4], in_=lsc[:, 0:2], func=AF.Exp), waits=[t_l], ev=ev_act)
        t_l2 = P.op("dve", lambda e: e.tensor_tensor(out=lsc[:, 4:5], in0=lsc[:, 3:4], in1=lsc[:, 2:3], op=ALU.subtract), waits=[t_le], ev=ev_dve)
        t_l3 = P.op("dve", lambda e: e.tensor_scalar(out=lsc[:, 5:6], in0=lsc[:, 4:5], scalar1=-LAMBDA_INIT, scalar2=None, op0=ALU.add), waits=[t_l2], ev=ev_dve)
        nlam = lsc[:, 5:6]
        t_hgs = P.op("dve", lambda e: e.tensor_scalar(out=hg[:, 1:2], in0=hg[:, 0:1], scalar1=1.0 - LAMBDA_INIT, scalar2=None, op0=ALU.mult), ev=ev_dve)
        hgs = hg[:, 1:2]
        tokA = [P.last["act"], P.last["dve"], P.last["pe"]]
        if debug:
            P.op("sp", lambda e: e.dma_start(out=dbg_aT, in_=av(OFF_A, SZ_A)), waits=tokA, ev=ev_out)

        cg_rd = {}
        u_tok = [None]
        aT_dead = [None]

        def conv_units(j, slots):
            s_cg, s_hi, s_bg = slots
            order = [4, 0, 1, 2, 3]
            for oi, ti in enumerate(order):
                c0, N = TOK_TILES[ti]
                cb = oi % 2
                bank = mmB[mm_rr[0] % 2]
                mm_rr[0] += 1
                tok = acc16(bank, N, lambda k: s_cg.ap[:, k, :], lambda k: aT[:, k, c0:c0 + N], [s_cg.tok], bank.ev)
                if oi == 4:
                    ring_release(s_cg, tok)
                d_ap, s_ap = cgS[cb][:, 0:N], bank.ps[:, 0:N]
                t1 = P.op("act", lambda e, d_ap=d_ap, s_ap=s_ap: e.copy(out=d_ap, in_=s_ap), waits=[tok, cg_rd.get(cb)], ev=ev_act)
                bank.free.append(t1)
                yield
                bank = mmB[mm_rr[0] % 2]
                mm_rr[0] += 1
                tok = acc16(bank, N, lambda k: s_hi.ap[:, k, :], lambda k: aT[:, k, c0:c0 + N], [s_hi.tok], bank.ev)
                if oi == 4:
                    ring_release(s_hi, tok)
                if ti == 4:
                    a_ap, b_ap = cgS[cb][:, NM - 2:NM], bank.ps[:, NM - 2:NM]
                    t2 = P.op("dve", lambda e, a_ap=a_ap, b_ap=b_ap: e.tensor_tensor(out=ubuf[:, 0:2], in0=a_ap, in1=b_ap, op=ALU.mult),
                              waits=[t1, tok, u_tok[0]], ev=ev_dve)
                    bank.free.append(t2)
                    cg_rd[cb] = t2
                    u_tok[0] = t2
                    yield
                    continue
                a_ap, b_ap = cgS[cb][:, :], bank.ps[:, :]
                t2 = P.op("dve", lambda e, a_ap=a_ap, b_ap=b_ap: e.tensor_tensor(out=ubuf[:, 2:514], in0=a_ap, in1=b_ap, op=ALU.mult),
                          waits=[t1, tok, u_tok[0]], ev=ev_dve)
                bank.free.append(t2)
                cg_rd[cb] = t2
                yield
                bank = mmB[mm_rr[0] % 2]
                mm_rr[0] += 1
                tok = acc16(bank, N, lambda k: s_bg.ap[:, k, :], lambda k: aT[:, k, c0:c0 + N], [s_bg.tok], bank.ev)
                if oi == 4:
                    ring_release(s_bg, tok)
                    aT_dead[0] = tok
                yb = ybuf[cb]
                n = ti
                w0, w1, w2 = cw[:, 0, j:j + 1], cw[:, 1, j:j + 1], cw[:, 2, j:j + 1]
                y1 = P.op("dve", lambda e, yb=yb, w0=w0: e.tensor_scalar(out=yb, in0=ubuf[:, 0:512], scalar1=w0, scalar2=None, op0=ALU.mult),
                          waits=[t2, t_cw], ev=ev_dve)
                y2 = P.op("dve", lambda e, yb=yb, w1=w1: e.scalar_tensor_tensor(out=yb, in0=ubuf[:, 1:513], scalar=w1, in1=yb, op0=ALU.mult, op1=ALU.add),
                          waits=[y1], ev=ev_dve)
                y3 = P.op("dve", lambda e, yb=yb, w2=w2: e.scalar_tensor_tensor(out=yb, in0=ubuf[:, 2:514], scalar=w2, in1=yb, op0=ALU.mult, op1=ALU.add),
                          waits=[y2], ev=ev_dve)
                o_ap, b_ap = mixT[:, 8 + j, n * 512:(n + 1) * 512], bank.ps[:, :]
                t3 = P.op("dve", lambda e, yb=yb, o_ap=o_ap, b_ap=b_ap: e.tensor_tensor(out=o_ap, in0=yb, in1=b_ap, op=ALU.mult),
                          waits=[tok, y3], ev=ev_dve)
                bank.free.append(t3)
                u_tok[0] = P.op("dve", lambda e: e.tensor_copy(out=ubuf[:, 0:2], in_=ubuf[:, 512:514]), waits=[y3], ev=ev_dve)
                yield

        s_rr = [0]
        pt_rr = [0]
        pt_free = [None] * 4
        prev_head_pe = [None]
        fin_tok = [None]
        t_qa = None
        deferred = []

        pull_fn = [None]

        def tick(force=False, tag=None):
            for item in list(deferred):
                item[0] -= 1
                if (force and (tag is None or item[2] == tag)) or item[0] <= 0:
                    deferred.remove(item)
                    if not force and pull_fn[0] is not None and item[2] in ("b", "d"):
                        pull_fn[0]()
                    item[1]()

        for h in range(NH):
            fw = tokA if h == 0 else []
            if h > 0:
                proj_chunk(h * 128, TOK_TILES[0:4], evac_split(qA, qB), fw)
                tick(force=True)
                proj_chunk(2048 + h * 128, TOK_TILES, evac_v, [])
                proj_chunk(1024 + h * 128, TOK_TILES, evac_split(kA, kB), [])
            t_v = [P.last["act"], P.last["dve"]]
            tv_toks = []
            for g0 in range(0, 17, 8):
                bank = mmB[mm_rr[0] % 2]
                mm_rr[0] += 1
                tv = tview(bank)
                fr = bank.acquire()
                nb = min(8, 17 - g0)
                tok = None
                for j in range(nb):
                    blk = g0 + j
                    Pn = 128 if blk < 16 else NM
                    tok = P.op("pe", lambda e, tv=tv, j=j, blk=blk, Pn=Pn: e.transpose(out=tv[0:Pn, j, :], in_=vF[:, blk * 128:blk * 128 + Pn],
                                                                                     identity=ident[:, :]),
                               waits=[t_v, fr] if j == 0 else [], ev=bank.ev if j == nb - 1 else None)
                if nb == 8:
                    tk = P.op("dve", lambda e, tv=tv, g0=g0: e.tensor_copy(out=vh[:, g0:g0 + 8, :], in_=tv[:, :, :]), waits=[tok], ev=ev_dve)
                else:
                    tk = P.op("dve", lambda e, tv=tv: e.tensor_copy(out=vh[0:NM, 16, :], in_=tv[0:NM, 0, :]), waits=[tok], ev=ev_dve)
                bank.free.append(tk)
                tv_toks.append(tk)
            t_qk = [P.last["act"], P.last["dve"]]
            conv_slots = [ring_load(w_in_v[:, :, 4096 + h * 128:4096 + (h + 1) * 128], 16),
                          ring_load(w_in_v[:, :, 5120 + h * 128:5120 + (h + 1) * 128], 16),
                          ring_load(w_in_v[:, :, 3072 + h * 128:3072 + (h + 1) * 128], 16)]
            if h == 0:
                P.op("pool", lambda e: e.dma_start(out=qA[64:68, 0:S], in_=c_qaug), ev=ev_qa)
                t_qa = P.op("pool", lambda e: e.dma_start(out=qB[64:68, 0:S], in_=c_qaug), ev=ev_qa)
            P.op("pool", lambda e, h=h: e.dma_start(out=kA[64:68, 0:T], in_=c_kaug[h]), waits=[prev_head_pe[0]], ev=ev_ka)
            t_ka = P.op("pool", lambda e, h=h: e.dma_start(out=kB[64:68, 0:T], in_=c_kaug[h]), ev=ev_ka)
            cgen = conv_units(h, conv_slots)
            tile_ctr = [0]
            prev_head_pe_tok = P.last["pe"]

            def pull():
                try:
                    next(cgen)
                except StopIteration:
                    pass

            pull_fn[0] = pull
            pull()
            pull()
            for Q in range(4):
                blocks = [("m", S, NM, 16)] + [(j, j * 128, 128, j) for j in range(4 * Q + 4)]
                tiles = []
                for (bid, kc0, nk, vb) in blocks:
                    qlo = 0
                    diag = False
                    if bid != "m" and bid >= 4 * Q:
                        qlo = (bid - 4 * Q) * 128
                        diag = True
                    for m in range(2):
                        tiles.append((bid, kc0, nk, vb, qlo, diag, m))
                nt = len(tiles)
                pend = []
                fro = [OB[0].acquire(), OB[1].acquire()]
                frl = [LB[0].acquire(), LB[1].acquire()]
                last_pv = [None, None]
                last_l = [None, None]

                def issue_S(idx):
                    (bid, kc0, nk, vb, qlo, diag, m) = tiles[idx]
                    kM = kA if m == 0 else kB
                    qM = qA if m == 0 else qB
                    sbk = SB[s_rr[0] % 2]
                    s_rr[0] += 1
                    pi = pt_rr[0] % 4
                    pt_rr[0] += 1
                    fr = sbk.acquire()
                    rhs_ap = qM[0:68, Q * 512 + qlo:(Q + 1) * 512]
                    ts = P.op("pe", lambda e: e.matmul(sbk.ps[0:nk, qlo:512], lhsT=kM[0:68, kc0:kc0 + nk], rhs=rhs_ap, start=True, stop=True),
                              waits=[fr, t_qk, t_ka, t_qa, tv_toks], ev=sbk.ev)
                    te = P.op("act", lambda e: e.activation(out=Pt[pi][0:nk, qlo:512], in_=sbk.ps[0:nk, qlo:512], func=AF.Exp, scale=0.125),
                              waits=[ts, pt_free[pi]], ev=ev_act)
                    sbk.free.append(te)
                    rdy = te
                    if diag:
                        rdy = P.op("dve", lambda e: e.tensor_tensor(out=Pt[pi][0:nk, qlo:qlo + 128], in0=Pt[pi][0:nk, qlo:qlo + 128], in1=tri[:, :],
                                                                    op=ALU.mult), waits=[te], ev=ev_dve)
                    pend.append((idx, pi, rdy))

                def issue_PV(idx, pi, rdy):
                    (bid, kc0, nk, vb, qlo, diag, m) = tiles[idx]
                    first = (idx < 2)
                    lastm = (idx >= nt - 2)
                    vsrc = vh[0:nk, vb, :]
                    t1 = P.op("pe", lambda e: e.matmul(OB[m].ps[:, qlo:512], lhsT=vsrc, rhs=Pt[pi][0:nk, qlo:512], start=first, stop=lastm),
                              waits=[rdy, fro[m] if first else None], ev=OB[m].ev if lastm else None)
                    t2 = P.op("pe", lambda e: e.matmul(LB[m].ps[:, qlo:512], lhsT=ones[0:nk, :], rhs=Pt[pi][0:nk, qlo:512], start=first, stop=lastm),
                              waits=[frl[m] if first else None, t_ones], ev=LB[m].ev)
                    pt_free[pi] = t2
                    if lastm:
                        last_pv[m] = t1
                        last_l[m] = t2

                for idx in range(nt):
                    issue_S(idx)
                    if idx == 1 and Q > 0:
                        pull()
                    if len(pend) > 2:
                        issue_PV(*pend.pop(0))
                        tick()
                        tile_ctr[0] += 1
                        if tile_ctr[0] % 12 == 6:
                            pull()
                while pend:
                    issue_PV(*pend.pop(0))

                t_lnl = []
                for m in range(2):
                    tk = P.op("act", lambda e, m=m: e.activation(out=lc[m], in_=LB[m].ps[:, :], func=AF.Ln), waits=[last_l[m], fin_tok[0]], ev=ev_act)
                    LB[m].free.append(tk)
                    t_lnl.append(tk)
                t_ocs = []
                for m in range(2):
                    tk = P.op("dve", lambda e, m=m: e.tensor_copy(out=oc[m], in_=OB[m].ps[:, :]), waits=[last_pv[m], last_l[m], fin_tok[0]], ev=ev_dve)
                    OB[m].free.append(tk)
                    t_ocs.append(tk)
                st = {}

                def fin1b(t_lnl=t_lnl, t_ocs=t_ocs, st=st):
                    t_rl = []
                    for m in range(2):
                        t_rl.append(P.op("act", lambda e, m=m: e.activation(out=lc[m], in_=lc[m], func=AF.Exp, scale=-1.0), waits=[t_lnl[m]], ev=ev_act))
                    c3 = P.op("dve", lambda e: e.tensor_tensor(out=oc[0], in0=oc[0], in1=lc[0], op=ALU.mult), waits=[t_ocs[0], t_rl[0]], ev=ev_dve)
                    c4 = P.op("dve", lambda e: e.tensor_tensor(out=oc[1], in0=oc[1], in1=lc[1], op=ALU.mult), waits=[t_ocs[1], t_rl[1]], ev=ev_dve)
                    c5 = P.op("dve", lambda e: e.scalar_tensor_tensor(out=oc[0], in0=oc[1], scalar=nlam, in1=oc[0], op0=ALU.mult, op1=ALU.add),
                              waits=[c3, c4, t_l3, t_hgs], ev=ev_dve)
                    st["t_sq"] = P.op("dve", lambda e: e.tensor_tensor(out=sqb, in0=oc[0], in1=oc[0], op=ALU.mult), waits=[c5], ev=ev_dve)
                deferred.append([3, fin1b, "b"])

                def fin2a(st=st):
                    t_sq = st["t_sq"]
                    bank = mmB[mm_rr[0] % 2]
                    mm_rr[0] += 1
                    fr = bank.acquire()
                    t_ms = P.op("pe", lambda e: e.matmul(bank.ps[:, :], lhsT=ones[:, :], rhs=sqb, start=True, stop=True), waits=[t_sq, fr], ev=bank.ev)
                    t_vv = P.op("dve", lambda e: e.tensor_scalar(out=lc[0], in0=bank.ps[:, :], scalar1=1.0 / 128, scalar2=HEPS, op0=ALU.mult, op1=ALU.add),
                                waits=[t_ms], ev=ev_dve)
                    bank.free.append(t_vv)
                    st["t_vv"] = t_vv

                def fin2b(h=h, Q=Q, st=st):
                    t_vv = st["t_vv"]
                    t_ln = P.op("act", lambda e: e.activation(out=lc[0], in_=lc[0], func=AF.Ln), waits=[t_vv], ev=ev_act)
                    t_r = P.op("act", lambda e: e.activation(out=lc[0], in_=lc[0], func=AF.Exp, scale=-0.5), waits=[t_ln], ev=ev_act)
                    o_ap = mixT[:, h, Q * 512:(Q + 1) * 512]
                    fin_tok[0] = P.op("dve", lambda e: e.scalar_tensor_tensor(out=o_ap, in0=oc[0], scalar=hgs, in1=lc[0], op0=ALU.mult, op1=ALU.mult),
                                      waits=[t_r], ev=ev_dve)
                deferred.append([10, fin2a, "c"])
                deferred.append([13, fin2b, "d"])
            prev_head_pe[0] = P.last["pe"]
            pull_fn[0] = None
            for _ in cgen:
                pass
            tick(force=True, tag="b")

        tick(force=True)
        tokB = [P.last["act"], P.last["dve"], P.last["pe"]]
        if debug:
            P.op("sp", lambda e: e.dma_start(out=dbg_mix, in_=av(OFF_M, SZ_M)), waits=tokB, ev=ev_out)

        t_g = {}
        xr_free = [None, None]
        xr_n = [0]
        ft_free = [None, None, None]
        ost_free = [None, None]
        act_free = [[None, None], [None, None]]
        gu_b = banks[0:4]
        dn_b = banks[4:8]
        tb2 = banks[4:6]
        dn_rr = [0]
        out_toks = []
        NG = NFF // 2
        for hf in range(2):
            n2_trs = {}

            def norm2_chain(s):
                gs = hf * 8 + s
                n2_trs[s] = norm_chain(h1[:, s, :], 128, gs, ssC, vvC, lnC, rsC, junkC, P.last["dve"])

            def norm2_f(s):
                gs = hf * 8 + s
                b = s % 3
                ta = P.op("dve", lambda e: e.scalar_tensor_tensor(out=ft[b], in0=h1[:, s, :], scalar=rsC[:, gs:gs + 1], in1=g2bc,
                                                                  op0=ALU.mult, op1=ALU.mult), waits=[n2_trs[s], t_g["gf"], ft_free[b]], ev=ev_dve)
                return ta

            ft_ready = {}
            xq = []
            for dq in range(4):
                w = [aT_dead[0]] if hf == 0 else [E_tok[s] for s in range(8)]
                src_ap = x[hf * 1024:(hf + 1) * 1024, dq * 512:(dq + 1) * 512].rearrange("(s p) c -> p s c", p=128)
                xq.append(P.op("sp", lambda e, dq=dq, src_ap=src_ap: e.dma_start(out=h1[:, :, dq * 512:(dq + 1) * 512], in_=src_ap), waits=w, ev=ev_xh[dq]))
                if hf == 0 and dq == 0:
                    t_g["g2"] = P.op("sp", lambda e: e.dma_start(out=g2bc, in_=g2.broadcast_to([128, D])), waits=tokB, ev=ev_sp)
                    t_g["gf"] = P.op("sp", lambda e: e.dma_start(out=gfbc, in_=gf.broadcast_to([128, D])), ev=ev_sp)
            for dq in range(3):
                frs = [banks[s].acquire() for s in range(8)]
                toks = [None] * 8
                for kg in range(4):
                    sl = ring_load(w_out_v[:, kg * 4:(kg + 1) * 4, dq * 512:(dq + 1) * 512], 4)
                    tok = None
                    for s in range(8):
                        gs = hf * 8 + s
                        for kk in range(4):
                            k = kg * 4 + kk
                            l_ap = mixT[:, k, gs * 128:(gs + 1) * 128]
                            r_ap = sl.ap[:, kk, :]
                            bank = banks[s]
                            last = (k == 15)
                            rel = (s == 7 and kk == 3)
                            ev = bank.ev if last else (ev_rel if rel else None)
                            tok = P.op("pe", lambda e, bank=bank, l_ap=l_ap, r_ap=r_ap, k=k: e.matmul(bank.ps[:, :], lhsT=l_ap, rhs=r_ap, start=(k == 0), stop=(k == 15)),
                                       waits=[sl.tok, tokB, frs[s] if k == 0 else None], ev=ev)
                        if kg == 3:
                            toks[s] = tok
                    ring_release(sl, tok)
                for s in range(8):
                    dst = h1[:, s, dq * 512:(dq + 1) * 512]
                    td = P.op("dve", lambda e, s=s, dst=dst: e.tensor_tensor(out=dst, in0=banks[s].ps[:, :], in1=dst, op=ALU.add), waits=[toks[s], xq[dq]], ev=ev_dve)
                    banks[s].free.append(td)
            for dq in range(3, 4):
                slots = [ring_load(w_out_v[:, kg * 4:(kg + 1) * 4, dq * 512:(dq + 1) * 512], 4) for kg in range(4)]
                for s in range(8):
                    gs = hf * 8 + s
                    bank = gu_b[mm_rr[0] % 4]
                    mm_rr[0] += 1
                    tok = acc16(bank, 512, lambda k, gs=gs: mixT[:, k, gs * 128:(gs + 1) * 128], lambda k: slots[k // 4].ap[:, k % 4, :],
                                [sl.tok for sl in slots] + tokB, bank.ev)
                    dst = h1[:, s, dq * 512:(dq + 1) * 512]
                    td = P.op("dve", lambda e, bank=bank, dst=dst: e.tensor_tensor(out=dst, in0=bank.ps[:, :], in1=dst, op=ALU.add), waits=[tok, xq[dq]], ev=ev_dve)
                    bank.free.append(td)
                    norm2_chain(s)
                    if s >= 2:
                        pt, evs = transposes(ft[(s - 2) % 3], 128, mixT, (hf * 8 + s - 2) * 128, tb2, ft_ready[s - 2])
                        ft_free[(s - 2) % 3] = pt
                        if s == 5:
                            tokC_n0 = [P.last["act"], P.last["dve"]]
                    if s >= 1:
                        ft_ready[s - 1] = norm2_f(s - 1)
                ft_ready[7] = norm2_f(7)
                for sl in slots:
                    ring_release(sl, tok)

            sg_rr = [0]
            sg_free = [None, None]
            act_rdy = {}

            def GU(g):
                for ci in range(2):
                    c = 2 * g + ci
                    sl_g = ring_load(w_gate_v[:, :, c * 128:(c + 1) * 128], 16)
                    sl_u = ring_load(w_up_v[:, :, c * 128:(c + 1) * 128], 16)
                    for n in range(2):
                        c0 = hf * 1024 + n * 512
                        res = []
                        for wi, sl in enumerate((sl_g, sl_u)):
                            bank = gu_b[mm_rr[0] % 4]
                            mm_rr[0] += 1
                            tok = acc16(bank, 512, lambda k, sl=sl: sl.ap[:, k, :], lambda k: mixT[:, k, c0:c0 + 512], [sl.tok] + tokC, bank.ev)
                            if n == 1:
                                ring_release(sl, tok)
                            res.append((bank, tok))
                            if wi == 0:
                                yield
                        sb_i = sg_rr[0] % 2
                        sg_rr[0] += 1
                        (bg_, tg_), (bu_, tu_) = res
                        t1 = P.op("act", lambda e, bg_=bg_, sb_i=sb_i: e.activation(out=sg[sb_i], in_=bg_.ps[:, :], func=AF.Silu),
                                  waits=[tg_, sg_free[sb_i]], ev=ev_act)
                        bg_.free.append(t1)
                        dst = actT[g % 2][ci][:, n * 512:(n + 1) * 512]
                        t2 = P.op("dve", lambda e, bu_=bu_, sb_i=sb_i, dst=dst: e.tensor_tensor(out=dst, in0=sg[sb_i], in1=bu_.ps[:, :], op=ALU.mult),
                                  waits=[t1, tu_, act_free[g % 2][ci]], ev=ev_dve)
                        bu_.free.append(t2)
                        sg_free[sb_i] = t2
                        act_rdy[(g, ci)] = t2
                        yield

            def DOWN(gl, with_E):
                chunks = [(g, ci) for g in gl for ci in range(2)]
                wd = [ring_load(w_down[(2 * g + ci) * 128:(2 * g + ci + 1) * 128, :]) for (g, ci) in chunks]
                nch = len(chunks)
                tok = None
                for s in range(8):
                    for dq in range(4):
                        bank = dn_b[dn_rr[0] % 4]
                        dn_rr[0] += 1
                        fr = bank.acquire()
                        for j, (g, ci) in enumerate(chunks):
                            sl = wd[j]
                            l_ap = actT[g % 2][ci][:, s * 128:(s + 1) * 128]
                            r_ap = sl.ap[:, dq * 512:(dq + 1) * 512]
                            tok = P.op("pe", lambda e, bank=bank, j=j, l_ap=l_ap, r_ap=r_ap: e.matmul(bank.ps[:, :], lhsT=l_ap, rhs=r_ap, start=(j == 0), stop=(j == nch - 1)),
                                       waits=[sl.tok, act_rdy[(g, ci)], fr if j == 0 else None], ev=(bank.ev if j == nch - 1 else None))
                        dst = h1[:, s, dq * 512:(dq + 1) * 512]
                        td = P.op("dve", lambda e, bank=bank, dst=dst: e.tensor_tensor(out=dst, in0=bank.ps[:, :], in1=dst, op=ALU.add), waits=[tok], ev=ev_dve)
                        bank.free.append(td)
                        yield
                    if with_E:
                        E_a(s)
                        if s >= 1:
                            E_b(s - 1)
                if with_E:
                    E_b(7)
                for g in gl:
                    act_free[g % 2][0] = tok
                    act_free[g % 2][1] = tok
                for sl in wd:
                    ring_release(sl, tok)

            e_trs = {}

            def E_a(s):
                gs = hf * 8 + s
                e_trs[s] = norm_chain(h1[:, s, :], 128, gs, ssF, vvF, lnF, rsF, junkE, P.last["dve"])

            def E_b(s):
                gs = hf * 8 + s
                ob = s % 2
                to = P.op("dve", lambda e: e.scalar_tensor_tensor(out=ost[ob], in0=h1[:, s, :], scalar=rsF[:, gs:gs + 1], in1=gfbc,
                                                                  op0=ALU.mult, op1=ALU.mult), waits=[e_trs[s], t_g["gf"], ost_free[ob], P.last["pe"]], ev=ev_dve)
                tp = None
                E_tok[s] = [to]
                tw = P.op("sp", lambda e: e.dma_start(out=out[gs * 128:(gs + 1) * 128, :], in_=ost[ob]), waits=[to, tp], ev=ev_outb[ob])
                ost_free[ob] = tw
                out_toks.append(tw)

            junkE = xv(28672, 4096)

            def drain(gen, n):
                for _ in range(n):
                    try:
                        next(gen)
                    except StopIteration:
                        return

            tokC = tokC_n0
            gu0 = GU(0)
            next(gu0)
            next(gu0)
            for s2 in (6, 7):
                pt, evs = transposes(ft[s2 % 3], 128, mixT, (hf * 8 + s2) * 128, tb2, ft_ready[s2])
                ft_free[s2 % 3] = pt
            tokC = [P.last["act"], P.last["dve"]]
            dn = None
            for g in range(NG):
                gu = gu0 if g == 0 else GU(g)
                sched = [6, 6, 6, 6, 4, 4, 0, 0]
                for u in range(8):
                    if g == 0 and u < 2:
                        continue
                    next(gu)
                    if dn is not None:
                        drain(dn, sched[u])
                for _ in gu:
                    pass
                if dn is not None:
                    for _ in dn:
                        pass
                dn = DOWN([g], False) if g < NG - 2 else None
            for _ in DOWN([NG - 2, NG - 1], True):
                pass
            ft_free = [ost_free[0], ost_free[0], ost_free[1]]
            act_free = [[None, None], [None, None]]
        if debug:
            P.op("sp", lambda e: e.dma_start(out=dbg_h1, in_=av(OFF_A, 8 * D * 2).bitcast(F32)), waits=[P.last["dve"], P.last["pe"]], ev=ev_out)

        with nc.Block() as block:
            @block.tensor
            def _(e):
                P.emit("pe", e)

            @block.scalar
            def _(e):
                P.emit("act", e)

            @block.vector
            def _(e):
                P.emit("dve", e)

            @block.gpsimd
            def _(e):
                P.emit("pool", e)

            @block.sync
            def _(e):
                P.emit("sp", e)
                for evo in (ev_outb[0], ev_outb[1], ev_out):
                    if evo.n > 0:
                        e.wait_ge(evo.sem, evo.n)
    return nc


def _consts():
    ident = np.eye(128, dtype=np.float32)
    tri = (np.arange(128)[:, None] <= np.arange(128)[None, :]).astype(np.float32)
    qpos = (NM + np.arange(S)).astype(np.int64)
    qaug = np.stack([np.ones(S), np.ones(S), (qpos % 128), (qpos // 128) * 128]).astype(np.float32)
    kpos = np.concatenate([NM + np.arange(S), np.arange(NM)]).astype(np.int64)
    base = np.stack([(kpos % 128), (kpos // 128) * 128, -np.ones(T), -np.ones(T)]).astype(np.float32)
    kaug = np.zeros((NH, 4, T), np.float32)
    for h in range(NH):
        c = 8.0 * 2.0 ** (-(h + 1))
        kaug[h] = base * c
    return ident, tri, qaug, kaug


_NC_CACHE = {}


def kernel(x, meta, norm1_g, w_in, lambda_q1, lambda_k1, lambda_q2, lambda_k2, head_g, conv_w, w_out,
           norm2_g, w_gate, w_up, w_down, norm_f_g):
    f = lambda a: np.ascontiguousarray(np.asarray(a, dtype=np.float32))
    x = f(x)
    B = x.shape[0]
    if "nc" not in _NC_CACHE:
        _NC_CACHE["nc"] = build_program()
    nc = _NC_CACHE["nc"]
    ident, tri, qaug, kaug = _consts()
    shared = {
        "meta": f(meta), "norm1_g": f(norm1_g).reshape(1, D), "w_in": f(w_in)[0],
        "lambda_q1": f(lambda_q1).reshape(1, 64), "lambda_k1": f(lambda_k1).reshape(1, 64),
        "lambda_q2": f(lambda_q2).reshape(1, 64), "lambda_k2": f(lambda_k2).reshape(1, 64),
        "head_g": f(head_g).reshape(1, 128), "conv_w": f(conv_w)[0], "w_out": f(w_out)[0],
        "norm2_g": f(norm2_g).reshape(1, D), "w_gate": f(w_gate)[0], "w_up": f(w_up)[0], "w_down": f(w_down)[0],
        "norm_f_g": f(norm_f_g).reshape(1, D),
        "c_ident": ident, "c_tri": tri, "c_qaug": qaug, "c_kaug": kaug,
    }
    in_maps = [dict(shared, x=x[b]) for b in range(B)]
    res = run_bass_kernel_spmd(nc, in_maps, core_ids=list(range(B)))
    return np.stack([r["out"] for r in res.results], axis=0)
```

```python
import contextlib
import numpy as np
import concourse.bass as bass
import concourse.mybir as mybir
from concourse.bass_utils import run_bass_kernel_spmd

F32 = mybir.dt.float32
BF16 = mybir.dt.bfloat16
AF = mybir.ActivationFunctionType
ALU = mybir.AluOpType

D = 2048
S = 2048
NM = 16
T = S + NM
DFF = 5632
NH = 8
INW = 6144
NFF = DFF // 128
LAMBDA_INIT = 0.8 - 0.6 * 1.0
EPS = 1e-6
HEPS = 1e-5

OFF_A = 0
SZ_A = 16 * T
OFF_M = OFF_A + SZ_A
SZ_M = 16 * S
OFF_R = OFF_M + SZ_M
RING = 6
OFF_X = OFF_R + RING * 2048
SZ_X = 24576
ARENA = OFF_X + SZ_X


ANNOTATE = False


class Ev:
    def __init__(self, sem, step=1):
        self.sem, self.step, self.n = sem, step, 0

    def fire(self):
        self.n += self.step
        return (self.sem, self.n)


class Prog:
    def __init__(self):
        self.q = {k: [] for k in ("pe", "act", "dve", "pool", "sp")}
        self.last = {k: None for k in self.q}

    def op(self, eng, fn, waits=(), ev=None):
        tok = ev.fire() if ev is not None else None
        ws = []
        for w in waits:
            if w is None:
                continue
            if isinstance(w, list):
                ws.extend([u for u in w if u is not None])
            else:
                ws.append(w)
        self.q[eng].append((fn, ws, ev))
        if tok is not None:
            self.last[eng] = tok
        return tok

    def emit(self, eng, handle):
        waited = {}
        for fn, ws, ev in self.q[eng]:
            for sem, val in ws:
                key = id(sem)
                if waited.get(key, 0) >= val:
                    continue
                waited[key] = val
                handle.wait_ge(sem, val)
            ins = fn(handle)
            if ANNOTATE:
                ins.annotate("L%d" % fn.__code__.co_firstlineno)
            if ev is not None:
                ins.then_inc(ev.sem, ev.step)


class Bank:
    def __init__(self, ps, ev_pe):
        self.ps = ps
        self.ev = ev_pe
        self.free = []

    def acquire(self):
        f = self.free
        self.free = []
        return f


def build_program(debug=False):
    nc = bass.Bass("TRN2", target_bir_lowering=False)

    def din(name, shape):
        return nc.dram_tensor(name, shape, F32, kind="ExternalInput").ap()

    x = din("x", [S, D])
    meta = din("meta", [NM, D])
    g1 = din("norm1_g", [1, D])
    w_in = din("w_in", [D, INW])
    lq1 = din("lambda_q1", [1, 64])
    lk1 = din("lambda_k1", [1, 64])
    lq2 = din("lambda_q2", [1, 64])
    lk2 = din("lambda_k2", [1, 64])
    head_g = din("head_g", [1, 128])
    conv_w = din("conv_w", [3, 1024])
    w_out = din("w_out", [D, D])
    g2 = din("norm2_g", [1, D])
    w_gate = din("w_gate", [D, DFF])
    w_up = din("w_up", [D, DFF])
    w_down = din("w_down", [DFF, D])
    gf = din("norm_f_g", [1, D])
    c_ident = din("c_ident", [128, 128])
    c_tri = din("c_tri", [128, 128])
    c_qaug = din("c_qaug", [4, S])
    c_kaug = din("c_kaug", [NH, 4, T])
    out = nc.dram_tensor("out", [S, D], F32, kind="ExternalOutput").ap()
    if debug:
        dbg_aT = nc.dram_tensor("dbg_aT", [128, 16 * T], BF16, kind="ExternalOutput").ap()
        dbg_mix = nc.dram_tensor("dbg_mix", [128, 16 * S], BF16, kind="ExternalOutput").ap()
        dbg_h1 = nc.dram_tensor("dbg_h1", [128, 8 * D], F32, kind="ExternalOutput").ap()

    w_in_v = w_in.rearrange("(k p) n -> p k n", p=128)
    w_out_v = w_out.rearrange("(k p) n -> p k n", p=128)
    w_gate_v = w_gate.rearrange("(k p) n -> p k n", p=128)
    w_up_v = w_up.rearrange("(k p) n -> p k n", p=128)

    P = Prog()
    with contextlib.ExitStack() as es:
        def sb(name, shape, dt):
            return es.enter_context(nc.sbuf_tensor(name, shape, dt))

        nsem = [0]

        def new_ev(step=1):
            nsem[0] += 1
            return Ev(es.enter_context(nc.semaphore(f"s{nsem[0]}")), step)

        arena = sb("arena", [128, ARENA], BF16)
        ident = sb("ident", [128, 128], BF16)
        tri = sb("tri", [128, 128], BF16)
        ones = sb("ones", [128, 128], BF16)
        stat = sb("stat", [128, 4 * 17 + 4 * 16 + 4 * 16], F32)
        lamt = sb("lamt", [128, 4 * 64 + 64 + 16], F32)
        cw = sb("cw", [128, 3, 8], F32)
        hg = sb("hg", [128, 4], F32)

        pss = [es.enter_context(nc.psum_tensor(f"ps{i}", [128, 512], F32)) for i in range(8)]
        banks = [Bank(pss[i], new_ev()) for i in range(8)]

        def tview(bank):
            return bank.ps[:, :].bitcast(BF16).rearrange("p (j n) -> p j n", j=8)

        def av(off, n):
            return arena[:, off:off + n]

        aT = av(OFF_A, SZ_A).rearrange("p (k n) -> p k n", k=16)
        h1 = av(OFF_A, 8 * D * 2).bitcast(F32).rearrange("p (s n) -> p s n", s=8)
        mixT = av(OFF_M, SZ_M).rearrange("p (k n) -> p k n", k=16)
        ring_slots = [av(OFF_R + i * 2048, 2048) for i in range(RING)]
        ring_ready = [new_ev(16) for _ in range(RING)]
        ring_free = [new_ev(1) for _ in range(RING)]
        ring_last = [None] * RING
        ring_n = [0]

        def xv(off_bytes, nbytes, dt=BF16):
            a = av(OFF_X + off_bytes // 2, nbytes // 2)
            return a.bitcast(F32) if dt == F32 else a

        def mv(off_bytes, nbytes, dt=BF16):
            a = av(OFF_M + off_bytes // 2, nbytes // 2)
            return a.bitcast(F32) if dt == F32 else a

        xs = [mv(i * 8192, 8192, F32) for i in range(4)]
        at = [mv(32768, 4096), mv(36864, 4096)]
        g1bc = mv(40960, 8192, F32)
        junkA = mv(49152, 4096)
        QK = 4160
        qA = xv(0, 4128)
        qB = xv(QK, 4128)
        kA = xv(2 * QK, 4128)
        kB = xv(3 * QK, 4128)
        vF = xv(4 * QK, 4128)
        vh = xv(5 * QK, 4352).rearrange("p (b n) -> p b n", b=17)
        o_pt = 5 * QK + 4352
        Pt = [xv(o_pt + i * 1024, 1024) for i in range(4)]
        o_fin = o_pt + 4096
        oc = [xv(o_fin, 2048, F32), xv(o_fin + 2048, 2048, F32)]
        lc = [xv(o_fin + 4096, 2048, F32), xv(o_fin + 6144, 2048, F32)]
        sqb = xv(o_fin + 8192, 1024)
        o_cv = o_fin + 8192 + 1024
        cgS = [xv(o_cv, 2048, F32), xv(o_cv + 2048, 2048, F32)]
        ubuf = xv(o_cv + 4096, 2064, F32)
        ybuf = [xv(o_cv + 6160, 2048, F32), xv(o_cv + 8208, 2048, F32)]
        assert o_cv + 10256 <= 49152
        xr = [xv(0, 2048, F32), xv(2048, 2048, F32)]
        ft = [xv(4096, 4096), xv(8192, 4096), xv(40960, 4096)]
        ost = [xv(4096, 8192, F32), xv(40960, 8192, F32)]
        g2bc = xv(12288, 8192, F32)
        gfbc = xv(20480, 8192, F32)
        sg = [xv(28672, 2048, F32), xv(30720, 2048, F32)]
        actT = [[xv(32768 + (g * 2 + c) * 2048, 2048) for c in range(2)] for g in range(2)]
        junkC = xv(32768, 4096)

        ssA, vvA, lnA, rsA = (stat[:, i * 17:(i + 1) * 17] for i in range(4))
        o2 = 68
        ssC, vvC, lnC, rsC = (stat[:, o2 + i * 16:o2 + (i + 1) * 16] for i in range(4))
        o3 = o2 + 64
        ssF, vvF, lnF, rsF = (stat[:, o3 + i * 16:o3 + (i + 1) * 16] for i in range(4))

        ev_sp = new_ev(16)
        ev_spx = [new_ev(16) for _ in range(4)]
        ev_act = new_ev(1)
        ev_dve = new_ev(1)
        ev_pool = new_ev(16)
        ev_c = new_ev(16)
        ev_qa = new_ev(16)
        ev_ka = new_ev(16)
        ev_out = new_ev(16)
        ev_outb = [new_ev(16), new_ev(16)]
        ev_rel = new_ev(1)
        ev_poolc = new_ev(1)
        ev_xh = [new_ev(16) for _ in range(8)]
        E_tok = {}

        class Slot:
            pass

        def ring_load(src_ap, shape3=None):
            i = ring_n[0]
            ring_n[0] += 1
            si = i % RING
            slot = ring_slots[si]
            dst = slot if shape3 is None else slot.rearrange("p (k n) -> p k n", k=shape3)
            waits = []
            if i >= RING:
                assert ring_last[si] is not None, "ring slot reused before its last reader was emitted"
                waits = [ring_last[si]]
                ring_last[si] = None
            tok = P.op("pool", lambda e, d=dst, s=src_ap: e.dma_start(out=d, in_=s), waits, ev=ring_ready[si])
            sl = Slot()
            sl.ap, sl.tok, sl.si = dst, tok, si
            return sl

        def ring_release(sl, tok):
            ring_last[sl.si] = tok

        t_c = []
        P.op("pool", lambda e: e.dma_start(out=ident[:, :], in_=c_ident), ev=ev_c)
        t_c.append(P.op("pool", lambda e: e.dma_start(out=tri[:, :], in_=c_tri), ev=ev_c))
        t_ones = P.op("dve", lambda e: e.memset(ones[:, :], 1.0), ev=ev_dve)
        epsc = hg[:, 2:3]
        t_eps = P.op("dve", lambda e: e.memset(hg[:, 2:3], EPS), ev=ev_dve)
        t_g1 = P.op("sp", lambda e: e.dma_start(out=g1bc, in_=g1.broadcast_to([128, D])), ev=ev_sp)

        tb_rr = [0]

        def norm_a(src, Pn, col, ss, vv, junk, src_tok):
            t1 = P.op("act", lambda e: e.activation(out=junk[0:Pn, :], in_=src, func=AF.Square, accum_out=ss[0:Pn, col:col + 1]),
                      waits=[src_tok], ev=ev_act)
            return t1

        def norm_b(Pn, col, vv, ln, rs, t1):
            t2b = P.op("act", lambda e: e.activation(out=ln[0:Pn, col:col + 1], in_=vv[0:Pn, col:col + 1], func=AF.Ln, bias=epsc[0:Pn, 0:1], scale=1.0 / D),
                       waits=[t1, t_eps], ev=ev_act)
            t3 = P.op("act", lambda e: e.activation(out=rs[0:Pn, col:col + 1], in_=ln[0:Pn, col:col + 1], func=AF.Exp, scale=-0.5), waits=[t2b], ev=ev_act)
            return t3

        def norm_chain(src, Pn, col, ss, vv, ln, rs, junk, src_tok):
            return norm_b(Pn, col, ss, ln, rs, norm_a(src, Pn, col, ss, vv, junk, src_tok))

        def transposes(a_tile, Pn, dst, c0, tbanks, a_tok):
            evs = []
            pe_tok = None
            for hf in range(2):
                bank = tbanks[tb_rr[0] % len(tbanks)]
                tb_rr[0] += 1
                tv = tview(bank)
                fr = bank.acquire()
                for j in range(8):
                    k = hf * 8 + j
                    last = (j == 7)
                    pe_tok = P.op("pe", lambda e, tv=tv, j=j, k=k: e.transpose(out=tv[:, j, 0:Pn], in_=a_tile[0:Pn, k * 128:(k + 1) * 128],
                                                                             identity=ident[0:Pn, 0:Pn]),
                                  waits=[a_tok, fr, t_c] if j == 0 else [], ev=bank.ev if last else None)
                eng = "act" if hf == 0 else "dve"
                if eng == "act":
                    tk = P.op("act", lambda e, tv=tv, hf=hf: e.copy(out=dst[:, hf * 8:(hf + 1) * 8, c0:c0 + Pn], in_=tv[:, :, 0:Pn]),
                              waits=[pe_tok], ev=ev_act)
                else:
                    tk = P.op("dve", lambda e, tv=tv, hf=hf: e.tensor_copy(out=dst[:, hf * 8:(hf + 1) * 8, c0:c0 + Pn], in_=tv[:, :, 0:Pn]),
                              waits=[pe_tok], ev=ev_dve)
                bank.free.append(tk)
                evs.append(tk)
            return pe_tok, evs

        mm_rr = [0]
        TOK_TILES = [(n * 512, 512) for n in range(4)] + [(S, NM)]
        mmB = banks[0:2]
        LB = banks[2:4]
        SB = banks[4:6]
        OB = banks[6:8]

        def acc16(bank, N, lhs_fn, rhs_fn, first_waits, last_ev):
            fr = bank.acquire()
            tok = None
            for k in range(16):
                l_ap = lhs_fn(k)
                r_ap = rhs_fn(k)
                tok = P.op("pe", lambda e, l_ap=l_ap, r_ap=r_ap, k=k: e.matmul(bank.ps[:, 0:N], lhsT=l_ap, rhs=r_ap, start=(k == 0), stop=(k == 15)),
                           waits=[fr, first_waits] if k == 0 else [], ev=(last_ev if k == 15 else None))
            return tok

        def proj_chunk(col0, tiles, consumer, first_waits):
            sl = ring_load(w_in_v[:, :, col0:col0 + 128], 16)
            for ti, (c0, N) in enumerate(tiles):
                bank = mmB[mm_rr[0] % 2]
                mm_rr[0] += 1
                tok = acc16(bank, N, lambda k: sl.ap[:, k, :], lambda k, c0=c0, N=N: aT[:, k, c0:c0 + N],
                            [sl.tok] + list(first_waits), bank.ev)
                consumer(ti, c0, N, bank, tok)
            ring_release(sl, tok)

        def evac_split(dstA, dstB):
            def cons(ti, c0, N, bank, tok):
                t1 = P.op("act", lambda e: e.copy(out=dstA[0:64, c0:c0 + N], in_=bank.ps[0:64, 0:N]), waits=[tok], ev=ev_act)
                t2 = P.op("dve", lambda e: e.tensor_copy(out=dstB[0:64, c0:c0 + N], in_=bank.ps[64:128, 0:N]), waits=[tok], ev=ev_dve)
                bank.free += [t1, t2]
            return cons

        vrr = [0]

        def evac_v(ti, c0, N, bank, tok):
            vrr[0] += 1
            if vrr[0] % 2:
                t1 = P.op("act", lambda e: e.copy(out=vF[:, c0:c0 + N], in_=bank.ps[:, 0:N]), waits=[tok], ev=ev_act)
            else:
                t1 = P.op("dve", lambda e: e.tensor_copy(out=vF[:, c0:c0 + N], in_=bank.ps[:, 0:N]), waits=[tok], ev=ev_dve)
            bank.free.append(t1)


        def proj_chunk_gen(col0, tiles, consumer, waits_box):
            sl = ring_load(w_in_v[:, :, col0:col0 + 128], 16)
            tok = None
            for ti, (c0, N) in enumerate(tiles):
                bank = mmB[mm_rr[0] % 2]
                mm_rr[0] += 1
                tok = acc16(bank, N, lambda k: sl.ap[:, k, :], lambda k, c0=c0, N=N: aT[:, k, c0:c0 + N],
                            [sl.tok] + list(waits_box[0]), bank.ev)
                consumer(ti, c0, N, bank, tok)
                if ti == len(tiles) - 1:
                    ring_release(sl, tok)
                yield

        a_done = {}
        t_done = {}

        vvtok = {}
        ld_tok = {}

        def stageA1a(i):
            b = i % 4
            Pn = 128 if i < 16 else NM
            src_rows = x[i * 128:(i + 1) * 128, :] if i < 16 else meta
            w = [a_done[i - 4]] if i >= 4 else []
            tld = P.op("sp", lambda e: e.dma_start(out=xs[b][0:Pn, :], in_=src_rows), waits=w, ev=ev_spx[b])
            ld_tok[i] = tld

        def stageA1s(i):
            b = i % 4
            Pn = 128 if i < 16 else NM
            vvtok[i] = norm_a(xs[b][0:Pn, :], Pn, i, ssA, vvA, junkA, ld_tok[i])

        trsA = {}

        def stageA1n(i):
            Pn = 128 if i < 16 else NM
            trsA[i] = norm_b(Pn, i, ssA, lnA, rsA, vvtok[i])

        def stageA1b(i):
            b = i % 4
            ab = i % 2
            Pn = 128 if i < 16 else NM
            w = [trsA[i], t_g1] + ([t_done[i - 2]] if i >= 2 else [])
            a_done[i] = P.op("dve", lambda e: e.scalar_tensor_tensor(out=at[ab][0:Pn, :], in0=xs[b][0:Pn, :], scalar=rsA[0:Pn, i:i + 1],
                                                                      in1=g1bc[0:Pn, :], op0=ALU.mult, op1=ALU.mult), waits=w, ev=ev_dve)

        def stageA2(i):
            Pn = 128 if i < 16 else NM
            c0 = i * 128 if i < 16 else S
            pt, evs = transposes(at[i % 2], Pn, aT, c0, banks[2:6], a_done[i])
            t_done[i] = pt

        wbox = [[]]
        g_q0 = proj_chunk_gen(0, TOK_TILES[0:4], evac_split(qA, qB), wbox)
        g_k0 = proj_chunk_gen(1024, TOK_TILES, evac_split(kA, kB), wbox)
        g_v0 = proj_chunk_gen(2048, TOK_TILES, evac_v, wbox)
        grp_tok = {}

        pend0 = []

        def step_one():
            if pend0:
                g, n = pend0.pop(0)
                wbox[0] = grp_tok[n]
                next(g)

        for i in range(3):
            stageA1a(i)
        stageA1s(0)
        stageA1n(0)
        stageA1b(0)
        for i in range(1, 17):
            if i + 2 < 17:
                stageA1a(i + 2)
            stageA1s(i)
            stageA1n(i)
            stageA1b(i)
            stageA2(i - 1)
            if (i - 1) % 4 == 3:
                n = (i - 1) // 4
                grp_tok[n] = [P.last["act"], P.last["dve"]]
                pend0.extend([(g_q0, n), (g_k0, n), (g_v0, n)])
            else:
                step_one()
        stageA2(16)
        grp_tok[4] = [P.last["act"], P.last["dve"]]
        pend0.extend([(g_k0, 4), (g_v0, 4)])
        while pend0:
            step_one()
        for g in (g_q0, g_k0, g_v0):
            for _ in g:
                pass
        lqv = lamt[:, 0:256].rearrange("p (a n) -> p a n", a=4)
        for i, src in enumerate((lq1, lk1, lq2, lk2)):
            tl = P.op("sp", lambda e, i=i, src=src: e.dma_start(out=lqv[:, i, :], in_=src.broadcast_to([128, 64])), ev=ev_sp)
        t_hg = P.op("sp", lambda e: e.dma_start(out=hg[:, 0:1], in_=head_g.rearrange("o d -> d o")), ev=ev_sp)
        for t in range(3):
            t_cw = P.op("sp", lambda e, t=t: e.dma_start(out=cw[:, t, :], in_=conv_w[t:t + 1, :].rearrange("o (j p) -> p (o j)", p=128),
                                                       allow_slow_non_contiguous=True), ev=ev_sp)
        ljunk = lamt[:, 256:320]
        lsc = lamt[:, 320:336]
        P.op("dve", lambda e: e.scalar_tensor_tensor(out=ljunk, in0=lqv[:, 0, :], scalar=1.0, in1=lqv[:, 1, :], op0=ALU.mult,
                                                     op1=ALU.mult, accum_out=lsc[:, 0:1]), waits=[t_cw], ev=ev_dve)
        t_l = P.op("dve", lambda e: e.scalar_tensor_tensor(out=ljunk, in0=lqv[:, 2, :], scalar=1.0, in1=lqv[:, 3, :], op0=ALU.mult,
                                                           op1=ALU.mult, accum_out=lsc[:, 1:2]), ev=ev_dve)
        t_le = P.op("act", lambda e: e.activation(out=lsc[:, 2:4], in_=lsc[:, 0:2], func=AF.Exp), waits=[t_l], ev=ev_act)
        t_l2 = P.op("dve", lambda e: e.tensor_tensor(out=lsc[:, 4:5], in0=lsc[:, 3:4], in1=lsc[:, 2:3], op=ALU.subtract), waits=[t_le], ev=ev_dve)
        t_l3 = P.op("dve", lambda e: e.tensor_scalar(out=lsc[:, 5:6], in0=lsc[:, 4:5], scalar1=-LAMBDA_INIT, scalar2=None, op0=ALU.add), waits=[t_l2], ev=ev_dve)
        nlam = lsc[:, 5:6]
        t_hgs = P.op("dve", lambda e: e.tensor_scalar(out=hg[:, 1:2], in0=hg[:, 0:1], scalar1=1.0 - LAMBDA_INIT, scalar2=None, op0=ALU.mult), ev=ev_dve)
        hgs = hg[:, 1:2]
        tokA = [P.last["act"], P.last["dve"], P.last["pe"]]
        if debug:
            P.op("sp", lambda e: e.dma_start(out=dbg_aT, in_=av(OFF_A, SZ_A)), waits=tokA, ev=ev_out)

        cg_rd = {}
        u_tok = [None]
        wr_done = {}
        tok_h0 = [None]
        aT_dead = [None]

        def conv_units(j, slots):
            s_cg, s_hi, s_bg = slots
            order = [4, 0, 1, 2, 3]
            for oi, ti in enumerate(order):
                c0, N = TOK_TILES[ti]
                cb = oi % 2
                bank = mmB[mm_rr[0] % 2]
                mm_rr[0] += 1
                tok = acc16(bank, N, lambda k: s_cg.ap[:, k, :], lambda k: aT[:, k, c0:c0 + N], [s_cg.tok], bank.ev)
                if oi == 4:
                    ring_release(s_cg, tok)
                d_ap, s_ap = cgS[cb][:, 0:N], bank.ps[:, 0:N]
                t1 = P.op("act", lambda e, d_ap=d_ap, s_ap=s_ap: e.copy(out=d_ap, in_=s_ap), waits=[tok, cg_rd.get(cb)], ev=ev_act)
                bank.free.append(t1)
                yield
                bank = mmB[mm_rr[0] % 2]
                mm_rr[0] += 1
                tok = acc16(bank, N, lambda k: s_hi.ap[:, k, :], lambda k: aT[:, k, c0:c0 + N], [s_hi.tok], bank.ev)
                if oi == 4:
                    ring_release(s_hi, tok)
                if ti == 4:
                    a_ap, b_ap = cgS[cb][:, NM - 2:NM], bank.ps[:, NM - 2:NM]
                    t2 = P.op("dve", lambda e, a_ap=a_ap, b_ap=b_ap: e.tensor_tensor(out=ubuf[:, 0:2], in0=a_ap, in1=b_ap, op=ALU.mult),
                              waits=[t1, tok, u_tok[0]], ev=ev_dve)
                    bank.free.append(t2)
                    cg_rd[cb] = t2
                    u_tok[0] = t2
                    yield
                    continue
                a_ap, b_ap = cgS[cb][:, :], bank.ps[:, :]
                t2 = P.op("dve", lambda e, a_ap=a_ap, b_ap=b_ap: e.tensor_tensor(out=ubuf[:, 2:514], in0=a_ap, in1=b_ap, op=ALU.mult),
                          waits=[t1, tok, u_tok[0]], ev=ev_dve)
                bank.free.append(t2)
                cg_rd[cb] = t2
                yield
                bank = mmB[mm_rr[0] % 2]
                mm_rr[0] += 1
                tok = acc16(bank, N, lambda k: s_bg.ap[:, k, :], lambda k: aT[:, k, c0:c0 + N], [s_bg.tok], bank.ev)
                if oi == 4:
                    ring_release(s_bg, tok)
                    aT_dead[0] = tok
                yb = ybuf[cb]
                n = ti
                w0, w1, w2 = cw[:, 0, j:j + 1], cw[:, 1, j:j + 1], cw[:, 2, j:j + 1]
                y1 = P.op("dve", lambda e, yb=yb, w0=w0: e.tensor_scalar(out=yb, in0=ubuf[:, 0:512], scalar1=w0, scalar2=None, op0=ALU.mult),
                          waits=[t2, t_cw], ev=ev_dve)
                y2 = P.op("dve", lambda e, yb=yb, w1=w1: e.scalar_tensor_tensor(out=yb, in0=ubuf[:, 1:513], scalar=w1, in1=yb, op0=ALU.mult, op1=ALU.add),
                          waits=[y1], ev=ev_dve)
                y3 = P.op("dve", lambda e, yb=yb, w2=w2: e.scalar_tensor_tensor(out=yb, in0=ubuf[:, 2:514], scalar=w2, in1=yb, op0=ALU.mult, op1=ALU.add),
                          waits=[y2], ev=ev_dve)
                o_ap, b_ap = mixT[:, 8 + j, n * 512:(n + 1) * 512], bank.ps[:, :]
                t3 = P.op("dve", lambda e, yb=yb, o_ap=o_ap, b_ap=b_ap: e.tensor_tensor(out=o_ap, in0=yb, in1=b_ap, op=ALU.mult),
                          waits=[tok, y3], ev=ev_dve)
                bank.free.append(t3)
                wr_done[("conv", j, n)] = True
                u_tok[0] = P.op("dve", lambda e: e.tensor_copy(out=ubuf[:, 0:2], in_=ubuf[:, 512:514]), waits=[y3], ev=ev_dve)
                yield

        s_rr = [0]
        pt_rr = [0]
        pt_free = [None] * 4
        prev_head_pe = [None]
        fin_tok = [None]
        t_qa = None
        deferred = []

        pull_fn = [None]

        def tick(force=False, tag=None):
            for item in list(deferred):
                item[0] -= 1
                if (force and (tag is None or item[2] == tag)) or item[0] <= 0:
                    deferred.remove(item)
                    if not force and pull_fn[0] is not None and item[2] in ("b", "d"):
                        pull_fn[0]()
                    item[1]()

        for h in range(NH):
            fw = tokA if h == 0 else []
            if h > 0:
                proj_chunk(h * 128, TOK_TILES[0:4], evac_split(qA, qB), fw)
                tick(force=True)
                proj_chunk(2048 + h * 128, TOK_TILES, evac_v, [])
                proj_chunk(1024 + h * 128, TOK_TILES, evac_split(kA, kB), [])
            t_v = [P.last["act"], P.last["dve"]]
            tv_toks = []
            for g0 in range(0, 17, 8):
                bank = mmB[mm_rr[0] % 2]
                mm_rr[0] += 1
                tv = tview(bank)
                fr = bank.acquire()
                nb = min(8, 17 - g0)
                tok = None
                for j in range(nb):
                    blk = g0 + j
                    Pn = 128 if blk < 16 else NM
                    tok = P.op("pe", lambda e, tv=tv, j=j, blk=blk, Pn=Pn: e.transpose(out=tv[0:Pn, j, :], in_=vF[:, blk * 128:blk * 128 + Pn],
                                                                                     identity=ident[:, :]),
                               waits=[t_v, fr] if j == 0 else [], ev=bank.ev if j == nb - 1 else None)
                if nb == 8:
                    tk = P.op("dve", lambda e, tv=tv, g0=g0: e.tensor_copy(out=vh[:, g0:g0 + 8, :], in_=tv[:, :, :]), waits=[tok], ev=ev_dve)
                else:
                    tk = P.op("dve", lambda e, tv=tv: e.tensor_copy(out=vh[0:NM, 16, :], in_=tv[0:NM, 0, :]), waits=[tok], ev=ev_dve)
                bank.free.append(tk)
                tv_toks.append(tk)
            t_qk = [P.last["act"], P.last["dve"]]
            conv_slots = [ring_load(w_in_v[:, :, 4096 + h * 128:4096 + (h + 1) * 128], 16),
                          ring_load(w_in_v[:, :, 5120 + h * 128:5120 + (h + 1) * 128], 16),
                          ring_load(w_in_v[:, :, 3072 + h * 128:3072 + (h + 1) * 128], 16)]
            if h == 0:
                P.op("pool", lambda e: e.dma_start(out=qA[64:68, 0:S], in_=c_qaug), ev=ev_qa)
                t_qa = P.op("pool", lambda e: e.dma_start(out=qB[64:68, 0:S], in_=c_qaug), ev=ev_qa)
            P.op("pool", lambda e, h=h: e.dma_start(out=kA[64:68, 0:T], in_=c_kaug[h]), waits=[prev_head_pe[0]], ev=ev_ka)
            t_ka = P.op("pool", lambda e, h=h: e.dma_start(out=kB[64:68, 0:T], in_=c_kaug[h]), ev=ev_ka)
            cgen = conv_units(h, conv_slots)
            tile_ctr = [0]
            prev_head_pe_tok = P.last["pe"]

            def pull():
                try:
                    next(cgen)
                except StopIteration:
                    pass

            pull_fn[0] = pull
            pull()
            pull()
            for Q in range(4):
                if h == NH - 1 and Q == 3 and wr_done.get(("fin", h, 1)) and wr_done.get(("conv", h, 1)):
                    tok_h0[0] = [P.last["dve"]]
                blocks = [("m", S, NM, 16)] + [(j, j * 128, 128, j) for j in range(4 * Q + 4)]
                tiles = []
                for (bid, kc0, nk, vb) in blocks:
                    qlo = 0
                    diag = False
                    if bid != "m" and bid >= 4 * Q:
                        qlo = (bid - 4 * Q) * 128
                        diag = True
                    for m in range(2):
                        tiles.append((bid, kc0, nk, vb, qlo, diag, m))
                nt = len(tiles)
                pend = []
                fro = [OB[0].acquire(), OB[1].acquire()]
                frl = [LB[0].acquire(), LB[1].acquire()]
                last_pv = [None, None]
                last_l = [None, None]

                def issue_S(idx):
                    (bid, kc0, nk, vb, qlo, diag, m) = tiles[idx]
                    kM = kA if m == 0 else kB
                    qM = qA if m == 0 else qB
                    sbk = SB[s_rr[0] % 2]
                    s_rr[0] += 1
                    pi = pt_rr[0] % 4
                    pt_rr[0] += 1
                    fr = sbk.acquire()
                    rhs_ap = qM[0:68, Q * 512 + qlo:(Q + 1) * 512]
                    ts = P.op("pe", lambda e: e.matmul(sbk.ps[0:nk, qlo:512], lhsT=kM[0:68, kc0:kc0 + nk], rhs=rhs_ap, start=True, stop=True),
                              waits=[fr, t_qk, t_ka, t_qa, tv_toks], ev=sbk.ev)
                    te = P.op("act", lambda e: e.activation(out=Pt[pi][0:nk, qlo:512], in_=sbk.ps[0:nk, qlo:512], func=AF.Exp, scale=0.125),
                              waits=[ts, pt_free[pi]], ev=ev_act)
                    sbk.free.append(te)
                    rdy = te
                    if diag:
                        rdy = P.op("dve", lambda e: e.tensor_tensor(out=Pt[pi][0:nk, qlo:qlo + 128], in0=Pt[pi][0:nk, qlo:qlo + 128], in1=tri[:, :],
                                                                    op=ALU.mult), waits=[te], ev=ev_dve)
                    pend.append((idx, pi, rdy))

                def issue_PV(idx, pi, rdy):
                    (bid, kc0, nk, vb, qlo, diag, m) = tiles[idx]
                    first = (idx < 2)
                    lastm = (idx >= nt - 2)
                    vsrc = vh[0:nk, vb, :]
                    t1 = P.op("pe", lambda e: e.matmul(OB[m].ps[:, qlo:512], lhsT=vsrc, rhs=Pt[pi][0:nk, qlo:512], start=first, stop=lastm),
                              waits=[rdy, fro[m] if first else None], ev=OB[m].ev if lastm else None)
                    t2 = P.op("pe", lambda e: e.matmul(LB[m].ps[:, qlo:512], lhsT=ones[0:nk, :], rhs=Pt[pi][0:nk, qlo:512], start=first, stop=lastm),
                              waits=[frl[m] if first else None, t_ones], ev=LB[m].ev)
                    pt_free[pi] = t2
                    if lastm:
                        last_pv[m] = t1
                        last_l[m] = t2

                for idx in range(nt):
                    issue_S(idx)
                    if idx == 1 and Q > 0:
                        pull()
                    if len(pend) > 2:
                        issue_PV(*pend.pop(0))
                        tick()
                        tile_ctr[0] += 1
                        if tile_ctr[0] % 12 == 6:
                            pull()
                while pend:
                    issue_PV(*pend.pop(0))

                t_lnl = []
                for m in range(2):
                    tk = P.op("act", lambda e, m=m: e.activation(out=lc[m], in_=LB[m].ps[:, :], func=AF.Ln), waits=[last_l[m], fin_tok[0]], ev=ev_act)
                    LB[m].free.append(tk)
                    t_lnl.append(tk)
                t_ocs = []
                for m in range(2):
                    tk = P.op("dve", lambda e, m=m: e.tensor_copy(out=oc[m], in_=OB[m].ps[:, :]), waits=[last_pv[m], last_l[m], fin_tok[0]], ev=ev_dve)
                    OB[m].free.append(tk)
                    t_ocs.append(tk)
                st = {}

                def fin1b(t_lnl=t_lnl, t_ocs=t_ocs, st=st):
                    t_rl = []
                    for m in range(2):
                        t_rl.append(P.op("act", lambda e, m=m: e.activation(out=lc[m], in_=lc[m], func=AF.Exp, scale=-1.0), waits=[t_lnl[m]], ev=ev_act))
                    c3 = P.op("dve", lambda e: e.tensor_tensor(out=oc[0], in0=oc[0], in1=lc[0], op=ALU.mult), waits=[t_ocs[0], t_rl[0]], ev=ev_dve)
                    c4 = P.op("dve", lambda e: e.tensor_tensor(out=oc[1], in0=oc[1], in1=lc[1], op=ALU.mult), waits=[t_ocs[1], t_rl[1]], ev=ev_dve)
                    c5 = P.op("dve", lambda e: e.scalar_tensor_tensor(out=oc[0], in0=oc[1], scalar=nlam, in1=oc[0], op0=ALU.mult, op1=ALU.add),
                              waits=[c3, c4, t_l3, t_hgs], ev=ev_dve)
                    st["t_sq"] = P.op("dve", lambda e: e.tensor_tensor(out=sqb, in0=oc[0], in1=oc[0], op=ALU.mult), waits=[c5], ev=ev_dve)
                deferred.append([3, fin1b, "b"])

                def fin2a(st=st):
                    t_sq = st["t_sq"]
                    bank = mmB[mm_rr[0] % 2]
                    mm_rr[0] += 1
                    fr = bank.acquire()
                    t_ms = P.op("pe", lambda e: e.matmul(bank.ps[:, :], lhsT=ones[:, :], rhs=sqb, start=True, stop=True), waits=[t_sq, fr], ev=bank.ev)
                    t_vv = P.op("dve", lambda e: e.tensor_scalar(out=lc[0], in0=bank.ps[:, :], scalar1=1.0 / 128, scalar2=HEPS, op0=ALU.mult, op1=ALU.add),
                                waits=[t_ms], ev=ev_dve)
                    bank.free.append(t_vv)
                    st["t_vv"] = t_vv

                def fin2b(h=h, Q=Q, st=st):
                    t_vv = st["t_vv"]
                    t_ln = P.op("act", lambda e: e.activation(out=lc[0], in_=lc[0], func=AF.Ln), waits=[t_vv], ev=ev_act)
                    t_r = P.op("act", lambda e: e.activation(out=lc[0], in_=lc[0], func=AF.Exp, scale=-0.5), waits=[t_ln], ev=ev_act)
                    o_ap = mixT[:, h, Q * 512:(Q + 1) * 512]
                    fin_tok[0] = P.op("dve", lambda e: e.scalar_tensor_tensor(out=o_ap, in0=oc[0], scalar=hgs, in1=lc[0], op0=ALU.mult, op1=ALU.mult),
                                      waits=[t_r], ev=ev_dve)
                    wr_done[("fin", h, Q)] = True
                deferred.append([10, fin2a, "c"])
                deferred.append([13, fin2b, "d"])
            prev_head_pe[0] = P.last["pe"]
            pull_fn[0] = None
            for _ in cgen:
                pass
            tick(force=True, tag="b")

        tick(force=True)
        tokB = [P.last["act"], P.last["dve"], P.last["pe"]]
        if debug:
            P.op("sp", lambda e: e.dma_start(out=dbg_mix, in_=av(OFF_M, SZ_M)), waits=tokB, ev=ev_out)

        t_g = {}
        xr_free = [None, None]
        xr_n = [0]
        ft_free = [None, None, None]
        ost_free = [None, None]
        act_free = [[None, None], [None, None]]
        gu_b = banks[0:4]
        dn_b = banks[4:8]
        tb2 = banks[4:6]
        dn_rr = [0]
        out_toks = []
        NG = NFF // 2
        for hf in range(2):
            n2_trs = {}

            def norm2_chain(s):
                gs = hf * 8 + s
                n2_trs[s] = norm_chain(h1[:, s, :], 128, gs, ssC, vvC, lnC, rsC, junkC, P.last["dve"])

            def norm2_f(s):
                gs = hf * 8 + s
                b = s % 3
                ta = P.op("dve", lambda e: e.scalar_tensor_tensor(out=ft[b], in0=h1[:, s, :], scalar=rsC[:, gs:gs + 1], in1=g2bc,
                                                                  op0=ALU.mult, op1=ALU.mult), waits=[n2_trs[s], t_g["gf"], ft_free[b]], ev=ev_dve)
                return ta

            ft_ready = {}
            xq = []
            for dq in range(4):
                w = [aT_dead[0]] if hf == 0 else [E_tok[s] for s in range(8)]
                src_ap = x[hf * 1024:(hf + 1) * 1024, dq * 512:(dq + 1) * 512].rearrange("(s p) c -> p s c", p=128)
                xq.append(P.op("sp", lambda e, dq=dq, src_ap=src_ap: e.dma_start(out=h1[:, :, dq * 512:(dq + 1) * 512], in_=src_ap), waits=w, ev=ev_xh[dq]))
                if hf == 0 and dq == 0:
                    t_g["g2"] = P.op("sp", lambda e: e.dma_start(out=g2bc, in_=g2.broadcast_to([128, D])), waits=tokB, ev=ev_sp)
                    t_g["gf"] = P.op("sp", lambda e: e.dma_start(out=gfbc, in_=gf.broadcast_to([128, D])), ev=ev_sp)
            for dq in range(3):
                frs = [banks[s].acquire() for s in range(8)]
                toks = [None] * 8
                for kg in range(4):
                    sl = ring_load(w_out_v[:, kg * 4:(kg + 1) * 4, dq * 512:(dq + 1) * 512], 4)
                    tok = None
                    for s in range(8):
                        gs = hf * 8 + s
                        for kk in range(4):
                            k = kg * 4 + kk
                            l_ap = mixT[:, k, gs * 128:(gs + 1) * 128]
                            r_ap = sl.ap[:, kk, :]
                            bank = banks[s]
                            last = (k == 15)
                            rel = (s == 7 and kk == 3)
                            ev = bank.ev if last else (ev_rel if rel else None)
                            tok = P.op("pe", lambda e, bank=bank, l_ap=l_ap, r_ap=r_ap, k=k: e.matmul(bank.ps[:, :], lhsT=l_ap, rhs=r_ap, start=(k == 0), stop=(k == 15)),
                                       waits=[sl.tok, (tok_h0[0] if (hf == 0 and tok_h0[0] is not None) else tokB), frs[s] if k == 0 else None], ev=ev)
                        if kg == 3:
                            toks[s] = tok
                    ring_release(sl, tok)
                for s in range(8):
                    dst = h1[:, s, dq * 512:(dq + 1) * 512]
                    td = P.op("dve", lambda e, s=s, dst=dst: e.tensor_tensor(out=dst, in0=banks[s].ps[:, :], in1=dst, op=ALU.add), waits=[toks[s], xq[dq]], ev=ev_dve)
                    banks[s].free.append(td)
            for dq in range(3, 4):
                slots = [ring_load(w_out_v[:, kg * 4:(kg + 1) * 4, dq * 512:(dq + 1) * 512], 4) for kg in range(4)]
                for s in range(8):
                    gs = hf * 8 + s
                    bank = gu_b[mm_rr[0] % 4]
                    mm_rr[0] += 1
                    tok = acc16(bank, 512, lambda k, gs=gs: mixT[:, k, gs * 128:(gs + 1) * 128], lambda k: slots[k // 4].ap[:, k % 4, :],
                                [sl.tok for sl in slots] + tokB, bank.ev)
                    dst = h1[:, s, dq * 512:(dq + 1) * 512]
                    td = P.op("dve", lambda e, bank=bank, dst=dst: e.tensor_tensor(out=dst, in0=bank.ps[:, :], in1=dst, op=ALU.add), waits=[tok, xq[dq]], ev=ev_dve)
                    bank.free.append(td)
                    norm2_chain(s)
                    if s >= 2:
                        pt, evs = transposes(ft[(s - 2) % 3], 128, mixT, (hf * 8 + s - 2) * 128, tb2, ft_ready[s - 2])
                        ft_free[(s - 2) % 3] = pt
                        if s == 5:
                            tokC_n0 = [P.last["act"], P.last["dve"]]
                    if s >= 1:
                        ft_ready[s - 1] = norm2_f(s - 1)
                ft_ready[7] = norm2_f(7)
                for sl in slots:
                    ring_release(sl, tok)

            sg_rr = [0]
            sg_free = [None, None]
            act_rdy = {}

            def GU(g):
                for ci in range(2):
                    c = 2 * g + ci
                    sl_g = ring_load(w_gate_v[:, :, c * 128:(c + 1) * 128], 16)
                    sl_u = ring_load(w_up_v[:, :, c * 128:(c + 1) * 128], 16)
                    for n in range(2):
                        c0 = hf * 1024 + n * 512
                        res = []
                        for wi, sl in enumerate((sl_g, sl_u)):
                            bank = gu_b[mm_rr[0] % 4]
                            mm_rr[0] += 1
                            tok = acc16(bank, 512, lambda k, sl=sl: sl.ap[:, k, :], lambda k: mixT[:, k, c0:c0 + 512], [sl.tok] + tokC, bank.ev)
                            if n == 1:
                                ring_release(sl, tok)
                            res.append((bank, tok))
                            if wi == 0:
                                yield
                        sb_i = sg_rr[0] % 2
                        sg_rr[0] += 1
                        (bg_, tg_), (bu_, tu_) = res
                        t1 = P.op("act", lambda e, bg_=bg_, sb_i=sb_i: e.activation(out=sg[sb_i], in_=bg_.ps[:, :], func=AF.Silu),
                                  waits=[tg_, sg_free[sb_i]], ev=ev_act)
                        bg_.free.append(t1)
                        dst = actT[g % 2][ci][:, n * 512:(n + 1) * 512]
                        t2 = P.op("dve", lambda e, bu_=bu_, sb_i=sb_i, dst=dst: e.tensor_tensor(out=dst, in0=sg[sb_i], in1=bu_.ps[:, :], op=ALU.mult),
                                  waits=[t1, tu_, act_free[g % 2][ci]], ev=ev_dve)
                        bu_.free.append(t2)
                        sg_free[sb_i] = t2
                        act_rdy[(g, ci)] = t2
                        yield

            def DOWN(gl, with_E):
                chunks = [(g, ci) for g in gl for ci in range(2)]
                wd = [ring_load(w_down[(2 * g + ci) * 128:(2 * g + ci + 1) * 128, :]) for (g, ci) in chunks]
                nch = len(chunks)
                tok = None
                for s in range(8):
                    for dq in range(4):
                        bank = dn_b[dn_rr[0] % 4]
                        dn_rr[0] += 1
                        fr = bank.acquire()
                        for j, (g, ci) in enumerate(chunks):
                            sl = wd[j]
                            l_ap = actT[g % 2][ci][:, s * 128:(s + 1) * 128]
                            r_ap = sl.ap[:, dq * 512:(dq + 1) * 512]
                            tok = P.op("pe", lambda e, bank=bank, j=j, l_ap=l_ap, r_ap=r_ap: e.matmul(bank.ps[:, :], lhsT=l_ap, rhs=r_ap, start=(j == 0), stop=(j == nch - 1)),
                                       waits=[sl.tok, act_rdy[(g, ci)], fr if j == 0 else None], ev=(bank.ev if j == nch - 1 else None))
                        dst = h1[:, s, dq * 512:(dq + 1) * 512]
                        td = P.op("dve", lambda e, bank=bank, dst=dst: e.tensor_tensor(out=dst, in0=bank.ps[:, :], in1=dst, op=ALU.add), waits=[tok], ev=ev_dve)
                        bank.free.append(td)
                        yield
                    if with_E:
                        E_a(s)
                        if s >= 1:
                            E_b(s - 1)
                if with_E:
                    E_b(7)
                for g in gl:
                    act_free[g % 2][0] = tok
                    act_free[g % 2][1] = tok
                for sl in wd:
                    ring_release(sl, tok)

            e_trs = {}

            def E_a(s):
                gs = hf * 8 + s
                e_trs[s] = norm_chain(h1[:, s, :], 128, gs, ssF, vvF, lnF, rsF, junkE, P.last["dve"])

            def E_b(s):
                gs = hf * 8 + s
                ob = s % 2
                to = P.op("dve", lambda e: e.scalar_tensor_tensor(out=ost[ob], in0=h1[:, s, :], scalar=rsF[:, gs:gs + 1], in1=gfbc,
                                                                  op0=ALU.mult, op1=ALU.mult), waits=[e_trs[s], t_g["gf"], ost_free[ob], P.last["pe"]], ev=ev_dve)
                tp = None
                E_tok[s] = [to]
                tw = P.op("sp", lambda e: e.dma_start(out=out[gs * 128:(gs + 1) * 128, :], in_=ost[ob]), waits=[to, tp], ev=ev_outb[ob])
                ost_free[ob] = tw
                out_toks.append(tw)

            junkE = xv(28672, 4096)

            def drain(gen, n):
                for _ in range(n):
                    try:
                        next(gen)
                    except StopIteration:
                        return

            tokC = tokC_n0
            gu0 = GU(0)
            next(gu0)
            next(gu0)
            for s2 in (6, 7):
                pt, evs = transposes(ft[s2 % 3], 128, mixT, (hf * 8 + s2) * 128, tb2, ft_ready[s2])
                ft_free[s2 % 3] = pt
            tokC = [P.last["act"], P.last["dve"]]
            dn = None
            for g in range(NG):
                gu = gu0 if g == 0 else GU(g)
                sched = [6, 6, 6, 6, 4, 4, 0, 0]
                for u in range(8):
                    if g == 0 and u < 2:
                        continue
                    next(gu)
                    if dn is not None:
                        drain(dn, sched[u])
                for _ in gu:
                    pass
                if dn is not None:
                    for _ in dn:
                        pass
                dn = DOWN([g], False) if g < NG - 2 else None
            for _ in DOWN([NG - 2, NG - 1], True):
                pass
            ft_free = [ost_free[0], ost_free[0], ost_free[1]]
            act_free = [[None, None], [None, None]]
        if debug:
            P.op("sp", lambda e: e.dma_start(out=dbg_h1, in_=av(OFF_A, 8 * D * 2).bitcast(F32)), waits=[P.last["dve"], P.last["pe"]], ev=ev_out)

        with nc.Block() as block:
            @block.tensor
            def _(e):
                P.emit("pe", e)

            @block.scalar
            def _(e):
                P.emit("act", e)

            @block.vector
            def _(e):
                P.emit("dve", e)

            @block.gpsimd
            def _(e):
                P.emit("pool", e)

            @block.sync
            def _(e):
                P.emit("sp", e)
                for evo in (ev_outb[0], ev_outb[1], ev_out):
                    if evo.n > 0:
                        e.wait_ge(evo.sem, evo.n)
    return nc


def _consts():
    ident = np.eye(128, dtype=np.float32)
    tri = (np.arange(128)[:, None] <= np.arange(128)[None, :]).astype(np.float32)
    qpos = (NM + np.arange(S)).astype(np.int64)
    qaug = np.stack([np.ones(S), np.ones(S), (qpos % 128), (qpos // 128) * 128]).astype(np.float32)
    kpos = np.concatenate([NM + np.arange(S), np.arange(NM)]).astype(np.int64)
    base = np.stack([(kpos % 128), (kpos // 128) * 128, -np.ones(T), -np.ones(T)]).astype(np.float32)
    kaug = np.zeros((NH, 4, T), np.float32)
    for h in range(NH):
        c = 8.0 * 2.0 ** (-(h + 1))
        kaug[h] = base * c
    return ident, tri, qaug, kaug


_NC_CACHE = {}


def kernel(x, meta, norm1_g, w_in, lambda_q1, lambda_k1, lambda_q2, lambda_k2, head_g, conv_w, w_out,
           norm2_g, w_gate, w_up, w_down, norm_f_g):
    f = lambda a: np.ascontiguousarray(np.asarray(a, dtype=np.float32))
    x = f(x)
    B = x.shape[0]
    if "nc" not in _NC_CACHE:
        _NC_CACHE["nc"] = build_program()
    nc = _NC_CACHE["nc"]
    ident, tri, qaug, kaug = _consts()
    shared = {
        "meta": f(meta), "norm1_g": f(norm1_g).reshape(1, D), "w_in": f(w_in)[0],
        "lambda_q1": f(lambda_q1).reshape(1, 64), "lambda_k1": f(lambda_k1).reshape(1, 64),
        "lambda_q2": f(lambda_q2).reshape(1, 64), "lambda_k2": f(lambda_k2).reshape(1, 64),
        "head_g": f(head_g).reshape(1, 128), "conv_w": f(conv_w)[0], "w_out": f(w_out)[0],
        "norm2_g": f(norm2_g).reshape(1, D), "w_gate": f(w_gate)[0], "w_up": f(w_up)[0], "w_down": f(w_down)[0],
        "norm_f_g": f(norm_f_g).reshape(1, D),
        "c_ident": ident, "c_tri": tri, "c_qaug": qaug, "c_kaug": kaug,
    }
    in_maps = [dict(shared, x=x[b]) for b in range(B)]
    res = run_bass_kernel_spmd(nc, in_maps, core_ids=list(range(B)))
    return np.stack([r["out"] for r in res.results], axis=0)
```
